# Optimizing a Trainium2 kernel written in Bass

```python
import math
import jax, jax.numpy as jnp
from jax import lax
import numpy as np

D_MODEL = 1024
BATCH = 16
SEQ = 2048
DEPTH = 2

N_A_LAYERS = DEPTH // 2
N_B_LAYERS = DEPTH - N_A_LAYERS
N_DENSE_LAYERS = (DEPTH + 1) // 2
N_MOE_LAYERS = DEPTH // 2

SSM_GROUP_CH = 16
SSM_GROUPS = D_MODEL // SSM_GROUP_CH
SSM_STATE = 64
DT_MIN = 1e-3
DT_MAX = 1e-1

N_HEADS = 16
HEAD_DIM = 64
N_KV_HEADS = 4
Q_PER_KV = N_HEADS // N_KV_HEADS
KV_DIM = N_KV_HEADS * HEAD_DIM
WINDOW = 128
BLOCK = 128
ROT_DIM = HEAD_DIM // 4
ROPE_THETA = 500000.0
MAX_POS_OFFSET = 4096

FFN_DIM = 2816
N_EXPERTS = 8
TOP_K = 2
EXPERT_DIM = 1024

DEEPNORM_ALPHA = (2 * DEPTH) ** 0.25
DEEPNORM_BETA = (8 * DEPTH) ** -0.25
LN_EPS = 1e-5

kernel_name = "yoco_s5_swa_sink_moe_deepnorm"


def layer_norm(x, g, b):
    xf = x.astype(jnp.float32)
    mu = jnp.mean(xf, axis=-1, keepdims=True)
    xc = xf - mu
    var = jnp.mean(xc * xc, axis=-1, keepdims=True)
    return (xc * lax.rsqrt(var + LN_EPS) * g.astype(jnp.float32) + b.astype(jnp.float32)).astype(x.dtype)


def _ssm_combine(e1, e2):
    a1r, a1i, b1r, b1i = e1
    a2r, a2i, b2r, b2i = e2
    ar = a2r * a1r - a2i * a1i
    ai = a2r * a1i + a2i * a1r
    br = a2r * b1r - a2i * b1i + b2r
    bi = a2r * b1i + a2i * b1r + b2i
    return (ar, ai, br, bi)


def s5_mixer(u, lam_re, lam_im, log_step, b_re, b_im, c_re, c_im, d_skip, w_glu):
    bsz, seq, _ = u.shape
    uf = u.astype(jnp.float32)
    ug = uf.reshape(bsz, seq, SSM_GROUPS, SSM_GROUP_CH)
    lr = lam_re.astype(jnp.float32)
    li = lam_im.astype(jnp.float32)
    dt = jnp.exp(log_step.astype(jnp.float32))[:, None]
    mag = jnp.exp(lr * dt)
    ar = mag * jnp.cos(li * dt)
    ai = mag * jnp.sin(li * dt)
    nr = ar - 1.0
    den = lr * lr + li * li
    kr = (nr * lr + ai * li) / den
    ki = (ai * lr - nr * li) / den
    br = b_re.astype(jnp.float32)
    bi = b_im.astype(jnp.float32)
    bbar_r = kr[..., None] * br - ki[..., None] * bi
    bbar_i = kr[..., None] * bi + ki[..., None] * br
    bu_r = jnp.einsum('blgc,gpc->blgp', ug, bbar_r)
    bu_i = jnp.einsum('blgc,gpc->blgp', ug, bbar_i)
    a_r = jnp.broadcast_to(ar[None, None], (1, seq, SSM_GROUPS, SSM_STATE))
    a_i = jnp.broadcast_to(ai[None, None], (1, seq, SSM_GROUPS, SSM_STATE))
    _, _, h_r, h_i = lax.associative_scan(_ssm_combine, (a_r, a_i, bu_r, bu_i), axis=1)
    y = (jnp.einsum('blgp,gcp->blgc', h_r, c_re.astype(jnp.float32))
         - jnp.einsum('blgp,gcp->blgc', h_i, c_im.astype(jnp.float32)))
    y = y.reshape(bsz, seq, D_MODEL) + d_skip.astype(jnp.float32) * uf
    g = jax.nn.gelu(y).astype(u.dtype)
    val, gate = jnp.split(g @ w_glu, 2, axis=-1)
    return val * jax.nn.sigmoid(gate)


def rope_tables(positions):
    inv_freq = ROPE_THETA ** (-jnp.arange(0, ROT_DIM, 2, dtype=jnp.float32) / ROT_DIM)
    ang = positions.astype(jnp.float32)[..., None] * inv_freq
    return jnp.cos(ang)[:, :, None, :], jnp.sin(ang)[:, :, None, :]


def apply_partial_rope(t, cos, sin):
    half = ROT_DIM // 2
    r = t[..., :ROT_DIM].astype(jnp.float32)
    x1, x2 = r[..., :half], r[..., half:]
    rot = jnp.concatenate([x1 * cos - x2 * sin, x2 * cos + x1 * sin], axis=-1).astype(t.dtype)
    return jnp.concatenate([rot, t[..., ROT_DIM:]], axis=-1)


def shared_kv(h, kv_w, cos, sin):
    bsz, seq, _ = h.shape
    k, v = jnp.split(h @ kv_w, 2, axis=-1)
    k = apply_partial_rope(k.reshape(bsz, seq, N_KV_HEADS, HEAD_DIM), cos, sin)
    v = v.reshape(bsz, seq, N_KV_HEADS, HEAD_DIM)
    return k, v


def _band(t):
    prev = jnp.pad(t[:, :-1], ((0, 0), (1, 0), (0, 0), (0, 0), (0, 0)))
    return jnp.concatenate([prev, t], axis=2)


def sliding_window_attention(h, k, v, w_q, sinks, w_out, cos, sin):
    bsz, seq, _ = h.shape
    nb = seq // BLOCK
    q = apply_partial_rope((h @ w_q).reshape(bsz, seq, N_HEADS, HEAD_DIM), cos, sin)
    q = q.reshape(bsz, nb, BLOCK, N_KV_HEADS, Q_PER_KV, HEAD_DIM)
    kb = _band(k.reshape(bsz, nb, BLOCK, N_KV_HEADS, HEAD_DIM))
    vb = _band(v.reshape(bsz, nb, BLOCK, N_KV_HEADS, HEAD_DIM))
    s = jnp.einsum('bnqkgd,bnskd->bnkgqs', q, kb).astype(jnp.float32) * (HEAD_DIM ** -0.5)
    q_idx = jnp.arange(BLOCK)[:, None]
    s_idx = jnp.arange(2 * BLOCK)[None, :]
    rel = q_idx + BLOCK - s_idx
    valid = (rel >= 0) & (rel < WINDOW)
    has_prev = (jnp.arange(nb)[:, None, None] > 0) | (s_idx[None] >= BLOCK)
    mask = valid[None] & has_prev
    s = jnp.where(mask[None, :, None, None], s, -jnp.inf)
    sink = sinks.astype(jnp.float32).reshape(N_KV_HEADS, Q_PER_KV)[None, None, :, :, None, None]
    m = jnp.maximum(jnp.max(s, axis=-1, keepdims=True), sink)
    p = jnp.exp(s - m)
    denom = jnp.sum(p, axis=-1, keepdims=True) + jnp.exp(sink - m)
    p = (p / denom).astype(v.dtype)
    o = jnp.einsum('bnkgqs,bnskd->bnqkgd', p, vb).reshape(bsz, seq, N_HEADS * HEAD_DIM)
    return o @ w_out


def swiglu(t, w_gate, w_up, w_down):
    return (jax.nn.silu(t @ w_gate) * (t @ w_up)) @ w_down


def moe_swiglu(h, w_router, b_router, w_gate, w_up, w_down):
    bsz, seq, d = h.shape
    t = h.reshape(-1, d)
    logits = (t @ w_router).astype(jnp.float32) + b_router.astype(jnp.float32)
    top_vals, top_idx = lax.top_k(logits, TOP_K)
    top_w = jax.nn.softmax(top_vals, axis=-1)
    combine = jnp.sum(jax.nn.one_hot(top_idx, N_EXPERTS, dtype=jnp.float32) * top_w[..., None], axis=1).astype(h.dtype)
    out = jnp.zeros_like(t)
    for e in range(N_EXPERTS):
        out = out + combine[:, e:e + 1] * swiglu(t, w_gate[e], w_up[e], w_down[e])
    return out.reshape(bsz, seq, d)


def setup_inputs(seed: int = 0) -> dict:
    key = jax.random.key(seed)
    ks = iter(jax.random.split(key, 40))
    f32 = jnp.float32

    def nrm(shape, scale):
        return jax.random.normal(next(ks), shape, f32) * scale

    beta = DEEPNORM_BETA
    x = nrm((BATCH, SEQ, D_MODEL), 1.0)
    offs = jax.random.randint(next(ks), (BATCH, 1), 0, MAX_POS_OFFSET, dtype=jnp.int32)
    positions = (offs + jnp.arange(SEQ, dtype=jnp.int32)[None, :]).astype(jnp.int32)
    ln_g = 1.0 + nrm((DEPTH, 2, D_MODEL), 0.02)
    ln_b = nrm((DEPTH, 2, D_MODEL), 0.02)
    ssm_lambda_re = -0.5 + nrm((N_A_LAYERS, SSM_GROUPS, SSM_STATE), 0.01)
    ssm_lambda_im = math.pi * jnp.arange(SSM_STATE, dtype=f32)[None, None, :] + nrm((N_A_LAYERS, SSM_GROUPS, SSM_STATE), 0.01)
    ssm_log_step = jax.random.uniform(next(ks), (N_A_LAYERS, SSM_GROUPS), f32, math.log(DT_MIN), math.log(DT_MAX))
    ssm_b_re = nrm((N_A_LAYERS, SSM_GROUPS, SSM_STATE, SSM_GROUP_CH), (2 * SSM_GROUP_CH) ** -0.5)
    ssm_b_im = nrm((N_A_LAYERS, SSM_GROUPS, SSM_STATE, SSM_GROUP_CH), (2 * SSM_GROUP_CH) ** -0.5)
    ssm_c_re = nrm((N_A_LAYERS, SSM_GROUPS, SSM_GROUP_CH, SSM_STATE), (2 * SSM_STATE) ** -0.5)
    ssm_c_im = nrm((N_A_LAYERS, SSM_GROUPS, SSM_GROUP_CH, SSM_STATE), (2 * SSM_STATE) ** -0.5)
    ssm_d = nrm((N_A_LAYERS, D_MODEL), 1.0)
    ssm_w_glu = jnp.concatenate([nrm((N_A_LAYERS, D_MODEL, D_MODEL), D_MODEL ** -0.5 * beta),
                                 nrm((N_A_LAYERS, D_MODEL, D_MODEL), D_MODEL ** -0.5)], axis=-1)
    kv_w = jnp.concatenate([nrm((D_MODEL, KV_DIM), D_MODEL ** -0.5),
                            nrm((D_MODEL, KV_DIM), D_MODEL ** -0.5 * beta)], axis=-1)
    attn_w_q = nrm((N_B_LAYERS, D_MODEL, N_HEADS * HEAD_DIM), D_MODEL ** -0.5)
    attn_sinks = nrm((N_B_LAYERS, N_HEADS), 0.5)
    attn_w_out = nrm((N_B_LAYERS, N_HEADS * HEAD_DIM, D_MODEL), (N_HEADS * HEAD_DIM) ** -0.5 * beta)
    ffn_w_gate = nrm((N_DENSE_LAYERS, D_MODEL, FFN_DIM), D_MODEL ** -0.5 * beta)
    ffn_w_up = nrm((N_DENSE_LAYERS, D_MODEL, FFN_DIM), D_MODEL ** -0.5 * beta)
    ffn_w_down = nrm((N_DENSE_LAYERS, FFN_DIM, D_MODEL), FFN_DIM ** -0.5 * beta)
    moe_w_router = nrm((N_MOE_LAYERS, D_MODEL, N_EXPERTS), D_MODEL ** -0.5)
    moe_b_router = nrm((N_MOE_LAYERS, N_EXPERTS), 0.01)
    moe_w_gate = nrm((N_MOE_LAYERS, N_EXPERTS, D_MODEL, EXPERT_DIM), D_MODEL ** -0.5 * beta)
    moe_w_up = nrm((N_MOE_LAYERS, N_EXPERTS, D_MODEL, EXPERT_DIM), D_MODEL ** -0.5 * beta)
    moe_w_down = nrm((N_MOE_LAYERS, N_EXPERTS, EXPERT_DIM, D_MODEL), EXPERT_DIM ** -0.5 * beta)
    return {"x": x, "positions": positions, "ln_g": ln_g, "ln_b": ln_b,
            "ssm_lambda_re": ssm_lambda_re, "ssm_lambda_im": ssm_lambda_im, "ssm_log_step": ssm_log_step,
            "ssm_b_re": ssm_b_re, "ssm_b_im": ssm_b_im, "ssm_c_re": ssm_c_re, "ssm_c_im": ssm_c_im,
            "ssm_d": ssm_d, "ssm_w_glu": ssm_w_glu, "kv_w": kv_w,
            "attn_w_q": attn_w_q, "attn_sinks": attn_sinks, "attn_w_out": attn_w_out,
            "ffn_w_gate": ffn_w_gate, "ffn_w_up": ffn_w_up, "ffn_w_down": ffn_w_down,
            "moe_w_router": moe_w_router, "moe_b_router": moe_b_router,
            "moe_w_gate": moe_w_gate, "moe_w_up": moe_w_up, "moe_w_down": moe_w_down}


def reference(x, positions, ln_g, ln_b, ssm_lambda_re, ssm_lambda_im, ssm_log_step,
              ssm_b_re, ssm_b_im, ssm_c_re, ssm_c_im, ssm_d, ssm_w_glu, kv_w,
              attn_w_q, attn_sinks, attn_w_out, ffn_w_gate, ffn_w_up, ffn_w_down,
              moe_w_router, moe_b_router, moe_w_gate, moe_w_up, moe_w_down):
    cos, sin = rope_tables(positions)
    h = x
    k_shared = None
    v_shared = None
    for layer in range(DEPTH):
        if layer < N_A_LAYERS:
            a = layer
            mix = s5_mixer(h, ssm_lambda_re[a], ssm_lambda_im[a], ssm_log_step[a],
                           ssm_b_re[a], ssm_b_im[a], ssm_c_re[a], ssm_c_im[a], ssm_d[a], ssm_w_glu[a])
        else:
            bl = layer - N_A_LAYERS
            mix = sliding_window_attention(h, k_shared, v_shared, attn_w_q[bl], attn_sinks[bl],
                                           attn_w_out[bl], cos, sin)
        h = layer_norm(DEEPNORM_ALPHA * h + mix, ln_g[layer, 0], ln_b[layer, 0])
        c = layer // 2
        if layer % 2 == 0:
            ff = swiglu(h, ffn_w_gate[c], ffn_w_up[c], ffn_w_down[c])
        else:
            ff = moe_swiglu(h, moe_w_router[c], moe_b_router[c], moe_w_gate[c], moe_w_up[c], moe_w_down[c])
        h = layer_norm(DEEPNORM_ALPHA * h + ff, ln_g[layer, 1], ln_b[layer, 1])
        if layer == N_A_LAYERS - 1 and N_B_LAYERS > 0:
            k_shared, v_shared = shared_kv(h, kv_w, cos, sin)
    return h
```

```python
import contextlib
import math
import numpy as np
import concourse.bass as bass
import concourse.mybir as mybir
from concourse.alu_op_type import AluOpType as ALU
from concourse.bass_utils import run_bass_kernel_spmd

F32 = mybir.dt.float32
BF16 = mybir.dt.bfloat16
I32 = mybir.dt.int32
AF = mybir.ActivationFunctionType
AX = mybir.AxisListType

NCORES = 8
D = 1024
L = 2048
NSEQ = 2
TOK = NSEQ * L
NT = TOK // 128
FF = 2816
NE = 8
ED = 1024
ALPHA = float(4.0 ** 0.25)
LN_EPS = 1e-5
MLIST = [0, -1, -2, -3, -4, -5, -6, -7] + list(range(16)) + [16, 32, 64, 128, 256, 512, 1024]
NM = len(MLIST)
SEM_ROLL = 3000


class Ev:
    __slots__ = ("sem", "val", "eng", "ref")

    def __init__(self, sem, val, eng, ref=None):
        self.sem = sem
        self.val = val
        self.eng = eng
        self.ref = ref


class Buf:
    __slots__ = ("name", "w", "r", "excl")

    def __init__(self, name="", excl=False):
        self.name = name
        self.w = None
        self.r = {}
        self.excl = excl


class Ctx:
    def __init__(self, nc, stack):
        self.nc = nc
        self.stack = stack
        self.engs = {"pe": nc.tensor, "act": nc.scalar, "dve": nc.vector, "pool": nc.gpsimd, "sp": nc.sync}
        self.sem = {}
        self.cnt = {}
        self.nsem = 0
        self.dsems = []
        for k in self.engs:
            self._new_eng_sem(k)
        self.waited = {k: {} for k in self.engs}
        self.pending = {k: [] for k in self.engs}
        self.ninst = {k: 0 for k in self.engs}
        self.last = {k: None for k in self.engs}

    def _new_sem(self, name):
        self.nsem += 1
        return self.stack.enter_context(self.nc.semaphore(f"{name}_{self.nsem}"))

    def _new_eng_sem(self, k):
        self.sem[k] = self._new_sem("e" + k)
        self.cnt[k] = 0

    def dma_sem(self, name="d"):
        s = [self._new_sem(name), 0]
        self.dsems.append(s)
        return s

    def _wait(self, k, ev):
        if ev is None:
            return
        if ev.eng == "pe" and k == "pe":
            return
        if ev.val is None:
            raise RuntimeError("dependency on an instruction without inc")
        w = self.waited[k]
        sid = id(ev.sem)
        val = ev.ref[1] if ev.ref is not None else ev.val
        if w.get(sid, 0) >= val:
            return
        self.engs[k].wait_ge(ev.sem, val)
        self.ninst[k] += 1
        w[sid] = val

    def _deps(self, k, reads, writes):
        for b in reads:
            self._wait(k, b.w)
            if b.excl:
                for kk, e in b.r.items():
                    if kk != k:
                        self._wait(k, e)
        for b in writes:
            self._wait(k, b.w)
            for e in b.r.values():
                self._wait(k, e)

    def _commit(self, ev, reads, writes):
        key = id(ev.sem) if ev.eng == "dma" else ev.eng
        for b in reads:
            b.r[key] = ev
        for b in writes:
            b.w = ev
            b.r = {}

    def op(self, k, fn, reads=(), writes=(), inc=True):
        self._deps(k, reads, writes)
        inst = fn(self.engs[k])
        self.ninst[k] += 1
        if inc:
            if self.cnt[k] >= SEM_ROLL:
                self._new_eng_sem(k)
            self.cnt[k] += 1
            inst.then_inc(self.sem[k], 1)
            ev = Ev(self.sem[k], self.cnt[k], k)
            for p in self.pending[k]:
                p.sem = ev.sem
                p.val = ev.val
            self.pending[k] = []
            self.last[k] = ev
        else:
            ev = Ev(None, None, k)
            self.pending[k].append(ev)
        self._commit(ev, reads, writes)
        return ev

    def dma(self, k, out, in_, reads=(), writes=(), sem=None, **kw):
        self._deps(k, reads, writes)
        inst = self.engs[k].dma_start(out=out, in_=in_, **kw)
        self.ninst[k] += 1
        sem[1] += 16
        inst.then_inc(sem[0], 16)
        ev = Ev(sem[0], sem[1], "dma", sem)
        self._commit(ev, reads, writes)
        return ev

    def barrier(self, engs=("pe", "act", "dve", "pool", "sp")):
        for k in engs:
            assert not self.pending[k]
        for k in engs:
            for k2 in engs:
                if k2 != k and self.last[k2] is not None:
                    self._wait(k, self.last[k2])
            for s in self.dsems:
                if s[1] > 0:
                    self._wait(k, Ev(s[0], s[1], "dma", s))


LAST_CTX = None


def build(stop_after=None):
    global LAST_CTX
    nc = bass.Bass("TRN2", target_bir_lowering=False)
    dram_in = lambda name, shape, dt=F32: nc.dram_tensor(name, list(shape), dt, kind="ExternalInput").ap()
    x_d = dram_in("x", [TOK, D])
    pos_d = dram_in("pos", [128, NT], I32)
    lnG_d = dram_in("ln_g", [128, 4, D])
    lnB_d = dram_in("ln_b", [128, 4, D])
    lamr_d = dram_in("lam_re", [128, 64])
    lami_d = dram_in("lam_im", [128, 64])
    lstep_d = dram_in("lstep", [128, 64])
    BA_d = dram_in("s5_ba", [128, 64, 16])
    BB_d = dram_in("s5_bb", [128, 64, 16])
    CA_d = dram_in("s5_ca", [128, 64, 16])
    CB_d = dram_in("s5_cb", [128, 64, 16])
    dsk_d = dram_in("s5_d", [128, D])
    wglu_d = dram_in("w_glu", [D, 2 * D])
    kvw_d = dram_in("kv_w", [D, 512])
    wq_d = dram_in("w_q", [D, D])
    sink_d = dram_in("sinks", [128, 16])
    wo_d = dram_in("w_out", [D, D])
    fg_d = dram_in("ffn_g", [D, FF])
    fu_d = dram_in("ffn_u", [D, FF])
    fd_d = dram_in("ffn_d", [FF, D])
    wr_d = dram_in("w_router", [D, NE])
    br_d = dram_in("b_router", [128, NE])
    mg_d = dram_in("moe_g", [NE, D, ED])
    mu_d = dram_in("moe_u", [NE, D, ED])
    md_d = dram_in("moe_d", [NE, ED, D])
    invf_d = dram_in("inv_freq", [128, 8])
    out_d = nc.dram_tensor("out", [TOK, D], F32, kind="ExternalOutput").ap()
    dbg = stop_after is not None
    G_d = nc.dram_tensor("G", [TOK, D], BF16, kind="ExternalOutput" if stop_after == "A" else "Internal").ap()
    H1_d = nc.dram_tensor("H1", [TOK, D], F32, kind="ExternalOutput" if stop_after == "B" else "Internal").ap()
    H2_d = nc.dram_tensor("H2", [TOK, D], F32, kind="ExternalOutput" if stop_after == "C" else "Internal").ap()
    H3_d = nc.dram_tensor("H3", [TOK, D], F32, kind="ExternalOutput" if stop_after == "E" else "Internal").ap()
    if stop_after == "P":
        dbgP = nc.dram_tensor("dbgP", [128, 2 * NM + 2, 64], F32, kind="ExternalOutput").ap()
        dbgT = nc.dram_tensor("dbgT", [128, 3, 64, 128], BF16, kind="ExternalOutput").ap()

    with contextlib.ExitStack() as st:
        c = Ctx(nc, st)
        LAST_CTX = c
        uniq = [0]

        def sbt(stack, name, shape, dt):
            uniq[0] += 1
            return stack.enter_context(nc.sbuf_tensor(f"{name}_{uniq[0]}", list(shape), dt))
        PS = [st.enter_context(nc.psum_tensor(f"ps{i}", [128, 512], F32)) for i in range(8)]
        PSB = [Buf(f"ps{i}", excl=True) for i in range(8)]

        identf = sbt(st, "identf", [128, 128], F32); b_identf = Buf()
        identb = sbt(st, "identb", [128, 128], BF16); b_identb = Buf()
        c.op("pool", lambda e: e.memset(identf[:], 0.0), writes=[b_identf])
        c.op("pool", lambda e: e.affine_select(out=identf[:], in_=identf[:], pattern=[[-1, 128]], compare_op=ALU.not_equal,
                                               fill=1.0, base=0, channel_multiplier=1), reads=[b_identf], writes=[b_identf])
        c.op("pool", lambda e: e.tensor_copy(out=identb[:], in_=identf[:]), reads=[b_identf], writes=[b_identb])
        out_sem = c.dma_sem("out")

        with contextlib.ExitStack() as sa:
            Tm = sbt(sa, "Tm", [128, 64, 128], BF16); b_Tm = Buf()
            MinT = sbt(sa, "MinT", [128, 64, 128], BF16); b_MinT = Buf()
            Mout = sbt(sa, "Mout", [128, 64, 128], BF16); b_Mout = Buf()
            PA = sbt(sa, "PA", [128, 8, 64], F32); b_PA = Buf()
            PB = sbt(sa, "PB", [128, 8, 64], F32); b_PB = Buf()
            PAl = sbt(sa, "PAl", [128, 8, 64], F32); b_PAl = Buf()
            PBl = sbt(sa, "PBl", [128, 8, 64], F32); b_PBl = Buf()
            D1b = sbt(sa, "D1b", [128, 128], BF16); b_D1 = Buf()
            D2b = sbt(sa, "D2b", [128, 128], BF16); b_D2 = Buf()
            dB = sbt(sa, "dB", [128, D], F32); b_dB = Buf()
            ld0 = c.dma_sem("ld0")
            c.dma("sp", dB[:], dsk_d[:, :], writes=[b_dB], sem=ld0)
            with contextlib.ExitStack() as s0:
                lr = sbt(s0, "lr", [128, 64], F32); b_lr = Buf()
                li = sbt(s0, "li", [128, 64], F32); b_li = Buf()
                ls = sbt(s0, "ls", [128, 64], F32); b_ls = Buf()
                BAt = sbt(s0, "BAt", [128, 64, 16], F32); b_BA = Buf()
                BBt = sbt(s0, "BBt", [128, 64, 16], F32); b_BB = Buf()
                CAt = sbt(s0, "CAt", [128, 64, 16], F32); b_CA = Buf()
                CBt = sbt(s0, "CBt", [128, 64, 16], F32); b_CB = Buf()
                c.dma("sp", lr[:], lamr_d[:, :], writes=[b_lr], sem=ld0)
                c.dma("sp", li[:], lami_d[:, :], writes=[b_li], sem=ld0)
                c.dma("sp", ls[:], lstep_d[:, :], writes=[b_ls], sem=ld0)
                c.dma("act", BAt[:], BA_d[:, :, :], writes=[b_BA], sem=ld0)
                c.dma("act", BBt[:], BB_d[:, :, :], writes=[b_BB], sem=ld0)
                c.dma("act", CAt[:], CA_d[:, :, :], writes=[b_CA], sem=ld0)
                c.dma("act", CBt[:], CB_d[:, :, :], writes=[b_CB], sem=ld0)

                sgn = sbt(s0, "sgn", [128, 1], F32); b_sgn = Buf()
                sgn2 = sbt(s0, "sgn2", [128, 1], F32); b_sgn2 = Buf()
                c.op("pool", lambda e: e.memset(sgn[0:64, :], -1.0), writes=[b_sgn])
                c.op("pool", lambda e: e.memset(sgn[64:128, :], 1.0), writes=[b_sgn])
                c.op("pool", lambda e: e.memset(sgn2[0:64, :], 1.0), writes=[b_sgn2])
                c.op("pool", lambda e: e.memset(sgn2[64:128, :], -1.0), writes=[b_sgn2])
                D2f = sbt(s0, "D2f", [128, 128], F32); b_D2f = Buf()
                cmask = sbt(s0, "cmask", [128, 128], F32); b_cm = Buf()
                c.op("pool", lambda e: e.memset(D2f[:], 0.0), writes=[b_D2f])
                c.op("pool", lambda e: e.affine_select(out=D2f[:], in_=D2f[:], pattern=[[-1, 128]], compare_op=ALU.not_equal,
                                                       fill=1.0, base=64, channel_multiplier=1), reads=[b_D2f], writes=[b_D2f])
                c.op("pool", lambda e: e.affine_select(out=D2f[:], in_=D2f[:], pattern=[[-1, 128]], compare_op=ALU.not_equal,
                                                       fill=1.0, base=-64, channel_multiplier=1), reads=[b_D2f], writes=[b_D2f])
                c.op("pool", lambda e: e.tensor_copy(out=D2b[:], in_=D2f[:]), reads=[b_D2f], writes=[b_D2])
                c.op("pool", lambda e: e.tensor_copy(out=D1b[:], in_=identf[:]), reads=[b_identf], writes=[b_D1])
                c.op("pool", lambda e: e.memset(cmask[:], 1.0), writes=[b_cm])
                c.op("pool", lambda e: e.affine_select(out=cmask[:].rearrange("p (i o) -> p i o", i=8),
                                                       in_=cmask[:].rearrange("p (i o) -> p i o", i=8),
                                                       pattern=[[16, 8], [0, 16]], compare_op=ALU.is_ge,
                                                       fill=0.0, base=15, channel_multiplier=-1), reads=[b_cm], writes=[b_cm])

                s0a = contextlib.ExitStack()
                s0a.__enter__()
                def T3(name, stk=None):
                    return sbt(s0a if stk is None else stk, name, [128, NM, 64], F32), Buf(name)
                Pr, b_Pr = T3("Pr", s0)
                Pi, b_Pi = T3("Pi", s0)
                Mtab, b_Mt = T3("Mtab")
                for j, m in enumerate(MLIST):
                    c.op("pool", lambda e, j=j, m=m: e.memset(Mtab[:, j, :], float(m)), writes=[b_Mt])
                dt_ = sbt(s0a, "dt", [128, 64], F32); b_dt = Buf()
                lrdt = sbt(s0a, "lrdt", [128, 64], F32); b_lrdt = Buf()
                lidt = sbt(s0a, "lidt", [128, 64], F32); b_lidt = Buf()
                c.op("act", lambda e: e.activation(out=dt_[:], in_=ls[:], func=AF.Exp), reads=[b_ls], writes=[b_dt])
                c.op("dve", lambda e: e.tensor_tensor(out=lrdt[:], in0=lr[:], in1=dt_[:], op=ALU.mult), reads=[b_lr, b_dt], writes=[b_lrdt])
                c.op("dve", lambda e: e.tensor_tensor(out=lidt[:], in0=li[:], in1=dt_[:], op=ALU.mult), reads=[b_li, b_dt], writes=[b_lidt])
                bc3 = lambda t: t[:].unsqueeze(1).broadcast_to([128, NM, 64])
                Et, b_E = T3("Et")
                mag, b_mag = T3("mag")
                Ang, b_Ang = T3("Ang")
                c.op("dve", lambda e: e.tensor_tensor(out=Et[:], in0=Mtab[:], in1=bc3(lrdt), op=ALU.mult), reads=[b_Mt, b_lrdt], writes=[b_E])
                c.op("act", lambda e: e.activation(out=mag[:], in_=Et[:], func=AF.Exp), reads=[b_E], writes=[b_mag])
                c.op("dve", lambda e: e.tensor_tensor(out=Ang[:], in0=Mtab[:], in1=bc3(lidt), op=ALU.mult), reads=[b_Mt, b_lidt], writes=[b_Ang])
                kq, b_kq = T3("kq")
                ki_ = sbt(s0a, "ki", [128, NM, 64], I32); b_ki = Buf()
                kf, b_kf = T3("kf")
                yv, b_y = T3("yv")
                tw, b_tw = T3("tw")
                C1 = 6.28125
                C2 = float(2.0 * math.pi - 6.28125)
                PI = float(math.pi)
                TWO_PI = float(2.0 * math.pi)
                c.op("dve", lambda e: e.tensor_scalar(out=kq[:], in0=Ang[:], scalar1=float(1.0 / TWO_PI), scalar2=None, op0=ALU.mult), reads=[b_Ang], writes=[b_kq])
                c.op("dve", lambda e: e.tensor_copy(out=ki_[:], in_=kq[:]), reads=[b_kq], writes=[b_ki])
                c.op("dve", lambda e: e.tensor_copy(out=kf[:], in_=ki_[:]), reads=[b_ki], writes=[b_kf])
                c.op("dve", lambda e: e.scalar_tensor_tensor(out=yv[:], in0=kf[:], scalar=-C1, in1=Ang[:], op0=ALU.mult, op1=ALU.add), reads=[b_kf, b_Ang], writes=[b_y])
                c.op("dve", lambda e: e.scalar_tensor_tensor(out=yv[:], in0=kf[:], scalar=-C2, in1=yv[:], op0=ALU.mult, op1=ALU.add), reads=[b_kf, b_y], writes=[b_y])

                def wrap(t, b_t):
                    c.op("dve", lambda e: e.tensor_scalar(out=tw[:], in0=t[:], scalar1=-PI, scalar2=TWO_PI, op0=ALU.is_lt, op1=ALU.mult), reads=[b_t], writes=[b_tw])
                    c.op("dve", lambda e: e.tensor_tensor(out=t[:], in0=t[:], in1=tw[:], op=ALU.add), reads=[b_t, b_tw], writes=[b_t])
                    c.op("dve", lambda e: e.tensor_scalar(out=tw[:], in0=t[:], scalar1=PI, scalar2=-TWO_PI, op0=ALU.is_gt, op1=ALU.mult), reads=[b_t], writes=[b_tw])
                    c.op("dve", lambda e: e.tensor_tensor(out=t[:], in0=t[:], in1=tw[:], op=ALU.add), reads=[b_t, b_tw], writes=[b_t])
                wrap(yv, b_y)
                wrap(yv, b_y)
                sn, b_sn = T3("sn")
                cs, b_cs = T3("cs")
                yc, b_yc = T3("yc")
                c.op("act", lambda e: e.activation(out=sn[:], in_=yv[:], func=AF.Sin), reads=[b_y], writes=[b_sn])
                c.op("dve", lambda e: e.tensor_scalar(out=yc[:], in0=yv[:], scalar1=float(PI / 2), scalar2=None, op0=ALU.add), reads=[b_y], writes=[b_yc])
                wrap(yc, b_yc)
                c.op("act", lambda e: e.activation(out=cs[:], in_=yc[:], func=AF.Sin), reads=[b_yc], writes=[b_cs])
                c.op("dve", lambda e: e.tensor_tensor(out=Pr[:], in0=mag[:], in1=cs[:], op=ALU.mult), reads=[b_mag, b_cs], writes=[b_Pr])
                c.op("dve", lambda e: e.tensor_tensor(out=Pi[:], in0=mag[:], in1=sn[:], op=ALU.mult), reads=[b_mag, b_sn], writes=[b_Pi])
                c.barrier()
                s0a.__exit__(None, None, None)
                J1 = 9
                t64 = lambda name: (sbt(s0, name, [128, 64], F32), Buf(name))
                nr, b_nr = t64("nr"); den, b_den = t64("den"); t1, b_t1 = t64("t1"); t2, b_t2 = t64("t2")
                kr, b_kr = t64("kr"); kim, b_kim = t64("kim"); rden, b_rden = t64("rden")
                c.op("dve", lambda e: e.tensor_scalar(out=nr[:], in0=Pr[:, J1, :], scalar1=-1.0, scalar2=None, op0=ALU.add), reads=[b_Pr], writes=[b_nr])
                c.op("dve", lambda e: e.tensor_tensor(out=t1[:], in0=lr[:], in1=lr[:], op=ALU.mult), reads=[b_lr], writes=[b_t1])
                c.op("dve", lambda e: e.tensor_tensor(out=t2[:], in0=li[:], in1=li[:], op=ALU.mult), reads=[b_li], writes=[b_t2])
                c.op("dve", lambda e: e.tensor_tensor(out=den[:], in0=t1[:], in1=t2[:], op=ALU.add), reads=[b_t1, b_t2], writes=[b_den])
                c.op("dve", lambda e: e.reciprocal(out=rden[:], in_=den[:]), reads=[b_den], writes=[b_rden])
                c.op("dve", lambda e: e.tensor_tensor(out=t1[:], in0=nr[:], in1=lr[:], op=ALU.mult), reads=[b_nr, b_lr, b_den], writes=[b_t1])
                c.op("dve", lambda e: e.tensor_tensor(out=t2[:], in0=Pi[:, J1, :], in1=li[:], op=ALU.mult), reads=[b_Pi, b_li, b_den], writes=[b_t2])
                c.op("dve", lambda e: e.tensor_tensor(out=t1[:], in0=t1[:], in1=t2[:], op=ALU.add), reads=[b_t1, b_t2], writes=[b_t1])
                c.op("dve", lambda e: e.tensor_tensor(out=kr[:], in0=t1[:], in1=rden[:], op=ALU.mult), reads=[b_t1, b_rden], writes=[b_kr])
                c.op("dve", lambda e: e.tensor_tensor(out=t1[:], in0=Pi[:, J1, :], in1=lr[:], op=ALU.mult), reads=[b_Pi, b_lr, b_kr], writes=[b_t1])
                c.op("dve", lambda e: e.tensor_tensor(out=t2[:], in0=nr[:], in1=li[:], op=ALU.mult), reads=[b_nr, b_li, b_kr], writes=[b_t2])
                c.op("dve", lambda e: e.tensor_tensor(out=t1[:], in0=t1[:], in1=t2[:], op=ALU.subtract), reads=[b_t1, b_t2], writes=[b_t1])
                c.op("dve", lambda e: e.tensor_tensor(out=kim[:], in0=t1[:], in1=rden[:], op=ALU.mult), reads=[b_t1, b_rden], writes=[b_kim])
                Qr = sbt(s0, "Qr", [128, 8, 64], F32); b_Qr = Buf()
                Qi = sbt(s0, "Qi", [128, 8, 64], F32); b_Qi = Buf()
                q1 = sbt(s0, "q1", [128, 8, 64], F32); b_q1 = Buf()
                bc8 = lambda t: t[:].unsqueeze(1).broadcast_to([128, 8, 64])
                c.op("dve", lambda e: e.tensor_tensor(out=Qr[:], in0=Pr[:, 0:8, :], in1=bc8(kr), op=ALU.mult), reads=[b_Pr, b_kr], writes=[b_Qr])
                c.op("dve", lambda e: e.tensor_tensor(out=q1[:], in0=Pi[:, 0:8, :], in1=bc8(kim), op=ALU.mult), reads=[b_Pi, b_kim], writes=[b_q1])
                c.op("dve", lambda e: e.tensor_tensor(out=Qr[:], in0=Qr[:], in1=q1[:], op=ALU.subtract), reads=[b_Qr, b_q1], writes=[b_Qr])
                c.op("dve", lambda e: e.tensor_tensor(out=Qi[:], in0=Pi[:, 0:8, :], in1=bc8(kr), op=ALU.mult), reads=[b_Pi, b_kr], writes=[b_Qi])
                c.op("dve", lambda e: e.tensor_tensor(out=q1[:], in0=Pr[:, 0:8, :], in1=bc8(kim), op=ALU.mult), reads=[b_Pr, b_kim, b_Qr], writes=[b_q1])
                c.op("dve", lambda e: e.tensor_tensor(out=Qi[:], in0=Qi[:], in1=q1[:], op=ALU.add), reads=[b_Qi, b_q1], writes=[b_Qi])
                c.op("dve", lambda e: e.tensor_scalar(out=Qi[:], in0=Qi[:], scalar1=sgn[:, 0:1], scalar2=None, op0=ALU.mult), reads=[b_Qi, b_sgn], writes=[b_Qi])
                SIDX = [16, 24, 25, 26, 27, 28, 29, 30]
                for k, ix in enumerate(SIDX):
                    c.op("pool", lambda e, k=k, ix=ix: e.tensor_copy(out=PA[:, k, :], in_=Pr[:, ix, :]), reads=[b_Pr], writes=[b_PA])
                    c.op("dve", lambda e, k=k, ix=ix: e.tensor_scalar(out=PB[:, k, :], in0=Pi[:, ix, :], scalar1=sgn2[:, 0:1], scalar2=None, op0=ALU.mult), reads=[b_Pi, b_sgn2], writes=[b_PB])
                PXb = sbt(s0, "PXb", [128, 8, 64], BF16); b_PXb = Buf()
                PXf = sbt(s0, "PXf", [128, 8, 64], F32); b_PXf = Buf()
                for (Pt, b_Pt, Pl, b_Pl) in ((PA, b_PA, PAl, b_PAl), (PB, b_PB, PBl, b_PBl)):
                    c.op("dve", lambda e, Pt=Pt: e.tensor_copy(out=PXb[:], in_=Pt[:]), reads=[b_Pt], writes=[b_PXb])
                    c.op("dve", lambda e: e.tensor_copy(out=PXf[:], in_=PXb[:]), reads=[b_PXb], writes=[b_PXf])
                    c.op("dve", lambda e, Pt=Pt, Pl=Pl: e.tensor_tensor(out=Pl[:], in0=Pt[:], in1=PXf[:], op=ALU.subtract), reads=[b_Pt, b_PXf], writes=[b_Pl])
                    c.op("dve", lambda e, Pt=Pt: e.tensor_copy(out=Pt[:], in_=PXf[:]), reads=[b_PXf, b_Pl], writes=[b_Pt])
                Prs = sbt(s0, "Prs", [128, 16, 64], F32); b_Prs = Buf()
                c.op("dve", lambda e: e.tensor_scalar(out=Prs[:], in0=Pr[:, 8:24, :], scalar1=sgn2[:, 0:1], scalar2=None, op0=ALU.mult), reads=[b_Pr, b_sgn2], writes=[b_Prs])
                X = sbt(s0, "X", [128, 32, 8, 16], F32); b_X = Buf()
                X2 = sbt(s0, "X2", [128, 32, 8, 16], F32); b_X2 = Buf()
                YY = sbt(s0, "YY", [128, 32, 16, 16], F32); b_YY = Buf()
                Y2 = sbt(s0, "Y2", [128, 32, 16, 16], F32); b_Y2 = Buf()
                for hg in range(2):
                    gs = slice(hg * 32, hg * 32 + 32)
                    qv = lambda t: t[:, :, gs].rearrange("p j g -> p g j").unsqueeze(3).broadcast_to([128, 32, 8, 16])
                    bv = lambda t: t[:, gs, :].unsqueeze(2).broadcast_to([128, 32, 8, 16])
                    pv = lambda ap: ap.rearrange("p m g -> p g m").unsqueeze(3).broadcast_to([128, 32, 16, 16])
                    cv = lambda t: t[:, gs, :].unsqueeze(2).broadcast_to([128, 32, 16, 16])
                    c.op("dve", lambda e: e.tensor_tensor(out=X[:], in0=qv(Qr), in1=bv(BAt), op=ALU.mult), reads=[b_Qr, b_BA], writes=[b_X])
                    c.op("pool", lambda e: e.tensor_tensor(out=X2[:], in0=qv(Qi), in1=bv(BBt), op=ALU.mult), reads=[b_Qi, b_BB], writes=[b_X2])
                    c.op("dve", lambda e: e.tensor_tensor(out=X[:], in0=X[:], in1=X2[:], op=ALU.add), reads=[b_X, b_X2], writes=[b_X])
                    c.op("dve", lambda e: e.tensor_tensor(out=YY[:], in0=pv(Prs[:, :, gs]), in1=cv(CAt), op=ALU.mult), reads=[b_Prs, b_CA], writes=[b_YY])
                    c.op("pool", lambda e: e.tensor_tensor(out=Y2[:], in0=pv(Pi[:, 8:24, gs]), in1=cv(CBt), op=ALU.mult), reads=[b_Pi, b_CB], writes=[b_Y2])
                    c.op("dve", lambda e: e.tensor_tensor(out=YY[:], in0=YY[:], in1=Y2[:], op=ALU.subtract), reads=[b_YY, b_Y2], writes=[b_YY])
                    c.op("act", lambda e: e.copy(out=Mout[:, gs, :].rearrange("p g (m o) -> p g m o", m=8), in_=YY[:, :, 8:16, :]), reads=[b_YY], writes=[b_Mout])
                    for q4 in range(8):
                        pa, pb = q4 % 2, 2 + (q4 % 2)
                        G0 = hg * 32 + q4 * 4
                        for gg in range(4):
                            g = q4 * 4 + gg
                            c.op("pe", lambda e, g=g, gg=gg, pa=pa: e.matmul(PS[pa][:, gg * 128:(gg + 1) * 128],
                                                                             lhsT=X[:, g, :, :].rearrange("p j c -> p (j c)"),
                                                                             rhs=YY[:, g, 0:8, :].rearrange("p m o -> p (m o)"),
                                                                             start=True, stop=True),
                                 reads=[b_X, b_YY], writes=[PSB[pa]], inc=(gg == 3))
                        c.op("dve", lambda e, G0=G0, pa=pa: e.tensor_tensor(out=Tm[:, G0:G0 + 4, :],
                                                                            in0=PS[pa][:].rearrange("p (g n) -> p g n", g=4),
                                                                            in1=cmask[:].unsqueeze(1).broadcast_to([128, 4, 128]), op=ALU.mult),
                             reads=[PSB[pa], b_cm], writes=[b_Tm])
                        for gg in range(4):
                            g = q4 * 4 + gg
                            c.op("pe", lambda e, g=g, gg=gg, pb=pb: e.transpose(out=PS[pb][:, gg * 128:(gg + 1) * 128],
                                                                                in_=X[:, g, :, :].rearrange("p j c -> p (j c)"),
                                                                                identity=identf[:]),
                                 reads=[b_X, b_identf], writes=[PSB[pb]], inc=(gg == 3))
                        c.op("act", lambda e, G0=G0, pb=pb: e.copy(out=MinT[:, G0:G0 + 4, :],
                                                                   in_=PS[pb][:].rearrange("p (g n) -> p g n", g=4)),
                             reads=[PSB[pb]], writes=[b_MinT])
                if stop_after == "P":
                    ds = c.dma_sem("dbg")
                    c.dma("sp", dbgP[:, 0:NM, :], Pr[:], reads=[b_Pr], sem=ds)
                    c.dma("sp", dbgP[:, NM:2 * NM, :], Pi[:], reads=[b_Pi], sem=ds)
                    c.dma("sp", dbgP[:, 2 * NM, :], kr[:], reads=[b_kr], sem=ds)
                    c.dma("sp", dbgP[:, 2 * NM + 1, :], kim[:], reads=[b_kim], sem=ds)
                    c.dma("sp", dbgT[:, 0, :, :], Tm[:], reads=[b_Tm], sem=ds)
                    c.dma("sp", dbgT[:, 1, :, :], MinT[:], reads=[b_MinT], sem=ds)
                    c.dma("sp", dbgT[:, 2, :, :], Mout[:], reads=[b_Mout], sem=ds)
                    nc.sync.wait_ge(ds[0], ds[1])
                    return nc
                c.barrier()
            with contextlib.ExitStack() as s1:
                xv = x_d.rearrange("(ct ch j) d -> ch ct j d", ct=4, ch=128, j=8)
                Gv = G_d.rearrange("(ct ch j) d -> ch ct j d", ct=4, ch=128, j=8)
                NXB = 2
                XB = [sbt(s1, f"XB{i}", [128, 4, 8, 128], F32) for i in range(NXB)]; b_XB = [Buf() for _ in range(NXB)]
                xsem = [c.dma_sem("xs") for _ in range(NXB)]
                XR = [sbt(s1, "XR0", [128, 4, 8, 128], BF16)] * NXB; b_XR = [Buf()] * NXB
                U = [sbt(s1, "U0", [128, 8, 512], BF16)] * NXB; b_U = [Buf()] * NXB
                YB = [sbt(s1, "YB0", [128, 4, 8, 128], F32)] * NXB; b_YB = [Buf()] * NXB
                GB = [sbt(s1, "GB0", [128, 4, 8, 128], BF16)] * NXB; b_GB = [Buf()] * NXB
                gsem = [c.dma_sem("gs")] * NXB
                SC = [[sbt(s1, f"SC{i}_{k}", [128, 128], BF16) for k in range(8)] for i in range(8)]
                b_SC = [[Buf() for k in range(8)] for i in range(8)]
                SCl = [[sbt(s1, f"SCl{i}_{k}", [128, 128], BF16) for k in range(8)] for i in range(8)]
                b_SCl = [[Buf() for k in range(8)] for i in range(8)]
                SCt_ = [sbt(s1, f"SCt{k}", [128, 128], BF16) for k in range(16)]
                b_SCt_ = [Buf() for k in range(16)]
                SCt = [SCt_] * 8
                b_SCt = [b_SCt_] * 8
                Hsb = [[sbt(s1, f"Hsb{i}_{k}", [128, 512], BF16) for k in range(2)] for i in range(4)]
                b_Hsb = [[Buf() for k in range(2)] for i in range(4)]
                Hp = [sbt(s1, f"Hp{i}", [128, 2, 256], BF16) for i in range(4)]; b_Hp = [Buf() for _ in range(4)]
                Ysb = [sbt(s1, f"Ysb{i}", [128, 512], F32) for i in range(4)]; b_Ysb = [Buf() for _ in range(4)]
                for i in range(4):
                    c.op("pool", lambda e, i=i: e.memset(Hp[i][:], 0.0), writes=[b_Hp[i]])
                g_ev = []
                for gb in range(8):
                    s = gb % NXB
                    for ct in range(4):
                        c.dma("sp" if ct % 2 == 0 else "act", XB[s][:, ct, :, :], xv[:, ct, :, gb * 128:(gb + 1) * 128],
                              writes=[b_XB[s]], sem=xsem[s])
                    for ct in range(4):
                        c.op("pool", lambda e, ct=ct, s=s: e.tensor_copy(
                            out=XR[s][:, ct, :, :].rearrange("p gl (j c) -> p gl j c", j=8),
                            in_=XB[s][:, ct, :, :].rearrange("p j (gl c) -> p gl j c", c=16)),
                             reads=[b_XB[s]], writes=[b_XR[s]])
                    for gl in range(8):
                        pb = 6 + (gl % 2)
                        psb16 = PS[pb][:].bitcast(BF16)
                        for ct in range(4):
                            c.op("pe", lambda e, ct=ct, gl=gl, s=s, psb16=psb16: e.transpose(
                                out=psb16[:, ct * 128:(ct + 1) * 128], in_=XR[s][:, ct, gl, :], identity=identb[:]),
                                 reads=[b_XR[s], b_identb], writes=[PSB[pb]], inc=(ct == 3))
                        c.op("act", lambda e, gl=gl, s=s, psb16=psb16: e.copy(out=U[s][:, gl, :], in_=psb16[:, 0:512]),
                             reads=[PSB[pb]], writes=[b_U[s]])
                    for quad in range(2):
                        for gg in range(4):
                            gl = quad * 4 + gg
                            g = gb * 8 + gl
                            si = gl
                            for k in range(8):
                                c.op("pool", lambda e, g=g, k=k, si=si: e.tensor_scalar(
                                    out=SCt[si][k][:], in0=D1b[:], scalar1=PA[:, k, g:g + 1], scalar2=0.0, op0=ALU.mult, op1=ALU.add),
                                     reads=[b_D1, b_PA], writes=[b_SCt[si][k]])
                                c.op("dve", lambda e, g=g, k=k, si=si: e.scalar_tensor_tensor(
                                    out=SC[si][k][:], in0=D2b[:], scalar=PB[:, k, g:g + 1], in1=SCt[si][k][:], op0=ALU.mult, op1=ALU.add),
                                     reads=[b_D2, b_PB, b_SCt[si][k]], writes=[b_SC[si][k]])
                                c.op("pool", lambda e, g=g, k=k, si=si: e.tensor_scalar(
                                    out=SCt[si][8 + k][:], in0=D1b[:], scalar1=PAl[:, k, g:g + 1], scalar2=0.0, op0=ALU.mult, op1=ALU.add),
                                     reads=[b_D1, b_PAl], writes=[b_SCt[si][8 + k]])
                                c.op("dve", lambda e, g=g, k=k, si=si: e.scalar_tensor_tensor(
                                    out=SCl[si][k][:], in0=D2b[:], scalar=PBl[:, k, g:g + 1], in1=SCt[si][8 + k][:], op0=ALU.mult, op1=ALU.add),
                                     reads=[b_D2, b_PBl, b_SCt[si][8 + k]], writes=[b_SCl[si][k]])
                        for gg in range(4):
                            gl = quad * 4 + gg
                            g = gb * 8 + gl
                            c.op("pe", lambda e, g=g, gl=gl, gg=gg, s=s: e.matmul(PS[gg][:], lhsT=MinT[:, g, :], rhs=U[s][:, gl, :], start=True, stop=True),
                                 reads=[b_MinT, b_U[s]], writes=[PSB[gg]])
                        for k in range(8):
                            sh = 1 << k
                            for gg in range(4):
                                eng = "act" if gg % 2 == 0 else "dve"
                                if eng == "act":
                                    c.op("act", lambda e, gg=gg, k=k: e.copy(out=Hsb[gg][k % 2][:], in_=PS[gg][:]),
                                         reads=[PSB[gg]], writes=[b_Hsb[gg][k % 2]])
                                else:
                                    c.op("dve", lambda e, gg=gg, k=k: e.tensor_copy(out=Hsb[gg][k % 2][:], in_=PS[gg][:]),
                                         reads=[PSB[gg]], writes=[b_Hsb[gg][k % 2]])
                            for gg in range(4):
                                gl = quad * 4 + gg
                                c.op("pe", lambda e, gg=gg, gl=gl, k=k, sh=sh: e.matmul(
                                    PS[gg][:].rearrange("p (s n) -> p s n", s=2)[:, :, sh:256],
                                    lhsT=SC[gl][k][:],
                                    rhs=Hsb[gg][k % 2][:].rearrange("p (s n) -> p s n", s=2)[:, :, 0:256 - sh],
                                    start=False, stop=True, skip_group_check=True),
                                     reads=[b_SC[gl][k], b_Hsb[gg][k % 2]], writes=[PSB[gg]], inc=False)
                                c.op("pe", lambda e, gg=gg, gl=gl, k=k, sh=sh: e.matmul(
                                    PS[gg][:].rearrange("p (s n) -> p s n", s=2)[:, :, sh:256],
                                    lhsT=SCl[gl][k][:],
                                    rhs=Hsb[gg][k % 2][:].rearrange("p (s n) -> p s n", s=2)[:, :, 0:256 - sh],
                                    start=False, stop=True, skip_group_check=True),
                                     reads=[b_SCl[gl][k], b_Hsb[gg][k % 2]], writes=[PSB[gg]])
                        for gg in range(4):
                            eng = "act" if gg % 2 == 0 else "dve"
                            src = PS[gg][:].rearrange("p (s n) -> p s n", s=2)[:, :, 0:255]
                            if eng == "act":
                                c.op("act", lambda e, gg=gg, src=src: e.copy(out=Hp[gg][:, :, 1:256], in_=src), reads=[PSB[gg]], writes=[b_Hp[gg]])
                            else:
                                c.op("dve", lambda e, gg=gg, src=src: e.tensor_copy(out=Hp[gg][:, :, 1:256], in_=src), reads=[PSB[gg]], writes=[b_Hp[gg]])
                        for gg in range(4):
                            gl = quad * 4 + gg
                            g = gb * 8 + gl
                            pb = 4 + (gg % 2)
                            c.op("pe", lambda e, g=g, gl=gl, pb=pb, s=s: e.matmul(PS[pb][:], lhsT=Tm[:, g, :], rhs=U[s][:, gl, :], start=True, stop=False),
                                 reads=[b_Tm, b_U[s]], writes=[PSB[pb]], inc=False)
                            c.op("pe", lambda e, g=g, gg=gg, pb=pb: e.matmul(PS[pb][:], lhsT=Mout[:, g, :], rhs=Hp[gg][:].rearrange("p s n -> p (s n)"), start=False, stop=True),
                                 reads=[b_Mout, b_Hp[gg]], writes=[PSB[pb]])
                            if gg % 2 == 0:
                                c.op("dve", lambda e, gg=gg, pb=pb: e.tensor_copy(out=Ysb[gg][:], in_=PS[pb][:]), reads=[PSB[pb]], writes=[b_Ysb[gg]])
                            else:
                                c.op("act", lambda e, gg=gg, pb=pb: e.copy(out=Ysb[gg][:], in_=PS[pb][:]), reads=[PSB[pb]], writes=[b_Ysb[gg]])
                            pt = 6 + (gg % 2)
                            for ct in range(4):
                                c.op("pe", lambda e, gg=gg, ct=ct, pt=pt: e.transpose(out=PS[pt][:, ct * 128:(ct + 1) * 128],
                                                                                      in_=Ysb[gg][:, ct * 128:(ct + 1) * 128], identity=identf[:]),
                                     reads=[b_Ysb[gg], b_identf], writes=[PSB[pt]], inc=(ct == 3))
                            ydst = YB[s][:, :, :, gl * 16:(gl + 1) * 16]
                            ysrc = PS[pt][:].rearrange("p (ct i o) -> p ct i o", ct=4, i=8)
                            if gg % 2 == 0:
                                c.op("act", lambda e, ydst=ydst, ysrc=ysrc: e.copy(out=ydst, in_=ysrc), reads=[PSB[pt]], writes=[b_YB[s]])
                            else:
                                c.op("dve", lambda e, ydst=ydst, ysrc=ysrc: e.tensor_copy(out=ydst, in_=ysrc), reads=[PSB[pt]], writes=[b_YB[s]])
                    xbv = XB[s][:].rearrange("p ct j d -> p (ct j) d")
                    ybv = YB[s][:].rearrange("p ct j d -> p (ct j) d")
                    dbv = dB[:, gb * 128:(gb + 1) * 128].unsqueeze(1).broadcast_to([128, 32, 128])
                    c.op("pool", lambda e, xbv=xbv, dbv=dbv: e.tensor_tensor(out=xbv, in0=xbv, in1=dbv, op=ALU.mult),
                         reads=[b_XB[s], b_dB], writes=[b_XB[s]])
                    c.op("pool", lambda e, xbv=xbv, ybv=ybv: e.tensor_tensor(out=ybv, in0=ybv, in1=xbv, op=ALU.add),
                         reads=[b_XB[s], b_YB[s]], writes=[b_YB[s]])
                    c.op("act", lambda e, s=s: e.activation(out=GB[s][:], in_=YB[s][:], func=AF.Gelu_apprx_tanh),
                         reads=[b_YB[s]], writes=[b_GB[s]])
                    for ct in range(4):
                        g_ev.append(c.dma("sp", Gv[:, ct, :, gb * 128:(gb + 1) * 128], GB[s][:, ct, :, :], reads=[b_GB[s]], sem=gsem[s]))
                c.barrier()
        if stop_after == "A":
            nc.sync.wait_ge(gsem[0][0], gsem[0][1])
            return nc
        class Ring:
            def __init__(self, stack, name, n, shape, dt):
                self.t = [sbt(stack, f"{name}{i}", shape, dt) for i in range(n)]
                self.b = [Buf(f"{name}{i}") for i in range(n)]
                self.i = 0
                self.n = n

            def next(self):
                k = self.i % self.n
                self.i += 1
                return self.t[k], self.b[k]

        bank_ctr = [0]

        def nbank():
            k = bank_ctr[0] % 8
            bank_ctr[0] += 1
            return k

        def load_weight(stack, name, src, K, N, queue="pool"):
            kc = K // 128
            t = sbt(stack, name, [128, kc, N], BF16)
            b = Buf(name)
            sem = c.dma_sem(name)
            srcv = src.rearrange("(kc p) n -> p kc n", p=128)
            for k in range(kc):
                for n0 in range(0, N, 2048):
                    n1 = min(N, n0 + 2048)
                    c.dma(queue, t[:, k, n0:n1], srcv[:, k, n0:n1], writes=[b], sem=sem)
            return t, b

        def load_ln(stack, idx):
            g = sbt(stack, f"lng{idx}", [128, D], F32); bg = Buf()
            bt = sbt(stack, f"lnb{idx}", [128, D], F32); bb = Buf()
            sem = c.dma_sem("ln")
            c.dma("sp", g[:], lnG_d[:, idx, :], writes=[bg], sem=sem)
            c.dma("sp", bt[:], lnB_d[:, idx, :], writes=[bb], sem=sem)
            return g, bg, bt, bb

        def layernorm(stack_rings, r, b_r, lng, b_lng, lnb, b_lnb):
            stats, b_st = stack_rings["stats"].next()
            mv, b_mv = stack_rings["mv"].next()
            sd, b_sd = stack_rings["sd"].next()
            for hh in range(2):
                c.op("dve", lambda e, hh=hh: e.bn_stats(out=stats[:, hh, :], in_=r[:, hh * 512:(hh + 1) * 512]), reads=[b_r], writes=[b_st])
            c.op("dve", lambda e: e.bn_aggr(out=mv[:], in_=stats[:].rearrange("p a b -> p (a b)")), reads=[b_st], writes=[b_mv])
            c.op("dve", lambda e: e.tensor_scalar(out=sd[:, 0:1], in0=mv[:, 1:2], scalar1=float(LN_EPS), scalar2=None, op0=ALU.add), reads=[b_mv], writes=[b_sd])
            c.op("act", lambda e: e.activation(out=sd[:, 1:2], in_=sd[:, 0:1], func=AF.Sqrt), reads=[b_sd], writes=[b_sd])
            c.op("dve", lambda e: e.reciprocal(out=sd[:, 2:3], in_=sd[:, 1:2]), reads=[b_sd], writes=[b_sd])
            c.op("dve", lambda e: e.tensor_scalar(out=sd[:, 3:4], in0=mv[:, 0:1], scalar1=sd[:, 2:3], scalar2=-1.0, op0=ALU.mult, op1=ALU.mult), reads=[b_mv, b_sd], writes=[b_sd])
            c.op("act", lambda e: e.activation(out=r[:], in_=r[:], func=AF.Identity, scale=sd[:, 2:3], bias=sd[:, 3:4]), reads=[b_r, b_sd], writes=[b_r])
            c.op("pool", lambda e: e.tensor_tensor(out=r[:], in0=r[:], in1=lng[:], op=ALU.mult), reads=[b_r, b_lng], writes=[b_r])
            c.op("dve", lambda e: e.tensor_tensor(out=r[:], in0=r[:], in1=lnb[:], op=ALU.add), reads=[b_r, b_lnb], writes=[b_r])

        def ln_rings(stack):
            return {"stats": Ring(stack, "lnst", 3, [128, 2, 6], F32), "mv": Ring(stack, "lnmv", 3, [128, 2], F32),
                    "sd": Ring(stack, "lnsd", 3, [128, 4], F32)}

        def transpose_tile(src, b_src, dst_fn, b_dst, evac_eng="act"):
            pb = nbank()
            p16 = PS[pb][:].bitcast(BF16)
            for kc in range(8):
                c.op("pe", lambda e, kc=kc: e.transpose(out=p16[:, kc * 128:(kc + 1) * 128], in_=src[:, kc * 128:(kc + 1) * 128], identity=identb[:]),
                     reads=[b_src, b_identb], writes=[PSB[pb]], inc=(kc == 7))
            dst = dst_fn()
            if evac_eng == "act":
                c.op("act", lambda e: e.copy(out=dst, in_=p16[:].rearrange("p (k t) -> p k t", k=8)), reads=[PSB[pb]], writes=[b_dst])
            else:
                c.op("dve", lambda e: e.tensor_copy(out=dst, in_=p16[:].rearrange("p (k t) -> p k t", k=8)), reads=[PSB[pb]], writes=[b_dst])

        with contextlib.ExitStack() as sB:
            Wglu, b_Wglu = load_weight(sB, "Wglu", wglu_d, D, 2 * D)
            lng, b_lng, lnb, b_lnb = load_ln(sB, 0)
            rings = ln_rings(sB)
            gt_r = Ring(sB, "gt", 2, [128, D], BF16); gsemB = [c.dma_sem("gB") for _ in range(2)]
            xt_r = Ring(sB, "xt", 2, [128, D], F32); xsemB = [c.dma_sem("xB") for _ in range(2)]
            gT_r = Ring(sB, "gT", 2, [128, 8, 128], BF16)
            sg_r = Ring(sB, "sg", 2, [128, D], F32)
            r_r = Ring(sB, "rB", 3, [128, D], F32); osemB = [c.dma_sem("oB") for _ in range(3)]
            for t in range(NT):
                gt, b_gt = gt_r.next(); xt, b_xt = xt_r.next(); gT, b_gT = gT_r.next(); sg, b_sg = sg_r.next(); r, b_r = r_r.next()
                c.dma("sp", gt[:], G_d[t * 128:(t + 1) * 128, :], writes=[b_gt], sem=gsemB[t % 2])
                c.dma("act", xt[:], x_d[t * 128:(t + 1) * 128, :], writes=[b_xt], sem=xsemB[t % 2])
                transpose_tile(gt, b_gt, lambda: gT[:], b_gT)
                banks = [nbank() for _ in range(4)]
                for nb_ in range(4):
                    for kc in range(8):
                        c.op("pe", lambda e, nb_=nb_, kc=kc: e.matmul(PS[banks[nb_]][:], lhsT=gT[:, kc, :], rhs=Wglu[:, kc, nb_ * 512:(nb_ + 1) * 512],
                                                                  start=(kc == 0), stop=(kc == 7)),
                             reads=[b_gT, b_Wglu], writes=[PSB[banks[nb_]]], inc=(kc == 7))
                for hh in range(2):
                    c.op("act", lambda e, hh=hh: e.activation(out=sg[:, hh * 512:(hh + 1) * 512], in_=PS[banks[2 + hh]][:], func=AF.Sigmoid),
                         reads=[PSB[banks[2 + hh]]], writes=[b_sg])
                    c.op("dve", lambda e, hh=hh: e.tensor_tensor(out=sg[:, hh * 512:(hh + 1) * 512], in0=PS[banks[hh]][:], in1=sg[:, hh * 512:(hh + 1) * 512], op=ALU.mult),
                         reads=[PSB[banks[hh]], b_sg], writes=[b_sg])
                c.op("dve", lambda e: e.scalar_tensor_tensor(out=r[:], in0=xt[:], scalar=ALPHA, in1=sg[:], op0=ALU.mult, op1=ALU.add),
                     reads=[b_xt, b_sg], writes=[b_r])
                layernorm(rings, r, b_r, lng, b_lng, lnb, b_lnb)
                c.dma("sp", H1_d[t * 128:(t + 1) * 128, :], r[:], reads=[b_r], sem=osemB[t % 3])
            c.barrier()
        if stop_after == "B":
            return nc

        with contextlib.ExitStack() as sC:
            Wg, b_Wg = load_weight(sC, "Wg", fg_d, D, FF)
            Wu, b_Wu = load_weight(sC, "Wu", fu_d, D, FF)
            Wd, b_Wd = load_weight(sC, "Wd", fd_d, FF, D)
            lng, b_lng, lnb, b_lnb = load_ln(sC, 1)
            rings = ln_rings(sC)
            ht_r = Ring(sC, "htC", 2, [128, D], BF16); hsemC = [c.dma_sem("hC") for _ in range(2)]
            hres_r = Ring(sC, "hresC", 2, [128, D], F32); rsemC = [c.dma_sem("rC") for _ in range(2)]
            hT = sbt(sC, "hTC", [128, 8, 512], BF16); b_hT = Buf()
            h1T = sbt(sC, "h1TC", [128, FF // 128, 512], BF16); b_h1T = Buf()
            s_r = Ring(sC, "sC", 2, [128, 512], BF16)
            r_r = Ring(sC, "rC", 2, [128, D], F32); osemC = [c.dma_sem("oC") for _ in range(2)]
            for st_ in range(TOK // 512):
                for sub in range(4):
                    t = st_ * 4 + sub
                    ht, b_ht = ht_r.next()
                    c.dma("pool", ht[:], H1_d[t * 128:(t + 1) * 128, :], writes=[b_ht], sem=hsemC[t % 2])
                    transpose_tile(ht, b_ht, lambda sub=sub: hT[:, :, sub * 128:(sub + 1) * 128], b_hT, evac_eng="act" if sub % 2 == 0 else "dve")
                for fc in range(FF // 128):
                    pg, pu = nbank(), nbank()
                    for (W, bW, pb) in ((Wg, b_Wg, pg), (Wu, b_Wu, pu)):
                        for kc in range(8):
                            c.op("pe", lambda e, W=W, pb=pb, kc=kc, fc=fc: e.matmul(PS[pb][:], lhsT=W[:, kc, fc * 128:(fc + 1) * 128], rhs=hT[:, kc, :],
                                                                              start=(kc == 0), stop=(kc == 7)),
                                 reads=[bW, b_hT], writes=[PSB[pb]], inc=(kc == 7))
                    sb_, b_s = s_r.next()
                    c.op("act", lambda e, pg=pg, sb_=sb_: e.activation(out=sb_[:], in_=PS[pg][:], func=AF.Silu), reads=[PSB[pg]], writes=[b_s])
                    c.op("dve", lambda e, pu=pu, sb_=sb_, fc=fc: e.tensor_tensor(out=h1T[:, fc, :], in0=PS[pu][:], in1=sb_[:], op=ALU.mult),
                         reads=[PSB[pu], b_s], writes=[b_h1T])
                for sub in range(4):
                    t = st_ * 4 + sub
                    hres, b_hres = hres_r.next(); r, b_r = r_r.next()
                    c.dma("act", hres[:], H1_d[t * 128:(t + 1) * 128, :], writes=[b_hres], sem=rsemC[t % 2])
                    po = [nbank(), nbank()]
                    for nb_ in range(2):
                        for fc in range(FF // 128):
                            c.op("pe", lambda e, nb_=nb_, fc=fc, sub=sub: e.matmul(PS[po[nb_]][:], lhsT=h1T[:, fc, sub * 128:(sub + 1) * 128],
                                                                             rhs=Wd[:, fc, nb_ * 512:(nb_ + 1) * 512], start=(fc == 0), stop=(fc == FF // 128 - 1)),
                                 reads=[b_h1T, b_Wd], writes=[PSB[po[nb_]]], inc=(fc == FF // 128 - 1))
                        c.op("dve", lambda e, nb_=nb_: e.scalar_tensor_tensor(out=r[:, nb_ * 512:(nb_ + 1) * 512], in0=hres[:, nb_ * 512:(nb_ + 1) * 512], scalar=ALPHA,
                                                                         in1=PS[po[nb_]][:], op0=ALU.mult, op1=ALU.add),
                             reads=[b_hres, PSB[po[nb_]]], writes=[b_r])
                    layernorm(rings, r, b_r, lng, b_lng, lnb, b_lnb)
                    c.dma("sp", H2_d[t * 128:(t + 1) * 128, :], r[:], reads=[b_r], sem=osemC[t % 2])
            c.barrier()
        if stop_after == "C":
            return nc

        with contextlib.ExitStack() as sE:
            Wq, b_Wq = load_weight(sE, "Wq", wq_d, D, D)
            Wkv, b_Wkv = load_weight(sE, "Wkv", kvw_d, D, 512)
            Wo, b_Wo = load_weight(sE, "Wo", wo_d, D, D)
            lng, b_lng, lnb, b_lnb = load_ln(sE, 2)
            rings = ln_rings(sE)
            semE = c.dma_sem("cE")
            posi = sbt(sE, "posi", [128, NT], I32); b_posi = Buf()
            invf = sbt(sE, "invf", [128, 8], F32); b_invf = Buf()
            sk = sbt(sE, "sk", [128, 16], F32); b_sk = Buf()
            c.dma("sp", posi[:], pos_d[:, :], writes=[b_posi], sem=semE)
            c.dma("sp", invf[:], invf_d[:, :], writes=[b_invf], sem=semE)
            c.dma("sp", sk[:], sink_d[:, :], writes=[b_sk], sem=semE)
            posf = sbt(sE, "posf", [128, NT], F32); b_posf = Buf()
            angE = sbt(sE, "angE", [128, NT, 8], F32); b_angE = Buf()
            kqE = sbt(sE, "kqE", [128, NT, 8], F32); b_kqE = Buf()
            kiE = sbt(sE, "kiE", [128, NT, 8], I32); b_kiE = Buf()
            twE = sbt(sE, "twE", [128, NT, 8], F32); b_twE = Buf()
            ycE = sbt(sE, "ycE", [128, NT, 8], F32); b_ycE = Buf()
            cosT = sbt(sE, "cosT", [128, NT, 8], F32); b_cosT = Buf()
            sinT = sbt(sE, "sinT", [128, NT, 8], F32); b_sinT = Buf()
            esk = sbt(sE, "esk", [128, 16], F32); b_esk = Buf()
            c.op("act", lambda e: e.activation(out=esk[:], in_=sk[:], func=AF.Exp), reads=[b_sk], writes=[b_esk])
            c.op("dve", lambda e: e.tensor_copy(out=posf[:], in_=posi[:]), reads=[b_posi], writes=[b_posf])
            c.op("dve", lambda e: e.tensor_tensor(out=angE[:], in0=posf[:].unsqueeze(2).broadcast_to([128, NT, 8]),
                                                  in1=invf[:].unsqueeze(1).broadcast_to([128, NT, 8]), op=ALU.mult), reads=[b_posf, b_invf], writes=[b_angE])
            c.op("dve", lambda e: e.tensor_scalar(out=kqE[:], in0=angE[:], scalar1=float(1.0 / (2 * math.pi)), scalar2=None, op0=ALU.mult), reads=[b_angE], writes=[b_kqE])
            c.op("dve", lambda e: e.tensor_copy(out=kiE[:], in_=kqE[:]), reads=[b_kqE], writes=[b_kiE])
            c.op("dve", lambda e: e.tensor_copy(out=kqE[:], in_=kiE[:]), reads=[b_kiE], writes=[b_kqE])
            c.op("dve", lambda e: e.scalar_tensor_tensor(out=angE[:], in0=kqE[:], scalar=-6.28125, in1=angE[:], op0=ALU.mult, op1=ALU.add), reads=[b_kqE, b_angE], writes=[b_angE])
            c.op("dve", lambda e: e.scalar_tensor_tensor(out=angE[:], in0=kqE[:], scalar=-float(2.0 * math.pi - 6.28125), in1=angE[:], op0=ALU.mult, op1=ALU.add), reads=[b_kqE, b_angE], writes=[b_angE])

            def wrapE(t, b_t):
                PI_ = float(math.pi)
                c.op("dve", lambda e: e.tensor_scalar(out=twE[:], in0=t[:], scalar1=-PI_, scalar2=2 * PI_, op0=ALU.is_lt, op1=ALU.mult), reads=[b_t], writes=[b_twE])
                c.op("dve", lambda e: e.tensor_tensor(out=t[:], in0=t[:], in1=twE[:], op=ALU.add), reads=[b_t, b_twE], writes=[b_t])
                c.op("dve", lambda e: e.tensor_scalar(out=twE[:], in0=t[:], scalar1=PI_, scalar2=-2 * PI_, op0=ALU.is_gt, op1=ALU.mult), reads=[b_t], writes=[b_twE])
                c.op("dve", lambda e: e.tensor_tensor(out=t[:], in0=t[:], in1=twE[:], op=ALU.add), reads=[b_t, b_twE], writes=[b_t])
            wrapE(angE, b_angE)
            wrapE(angE, b_angE)
            c.op("act", lambda e: e.activation(out=sinT[:], in_=angE[:], func=AF.Sin), reads=[b_angE], writes=[b_sinT])
            c.op("dve", lambda e: e.tensor_scalar(out=ycE[:], in0=angE[:], scalar1=float(math.pi / 2), scalar2=None, op0=ALU.add), reads=[b_angE], writes=[b_ycE])
            wrapE(ycE, b_ycE)
            c.op("act", lambda e: e.activation(out=cosT[:], in_=ycE[:], func=AF.Sin), reads=[b_ycE], writes=[b_cosT])
            mprev = sbt(sE, "mprev", [128, 128], BF16); b_mprev = Buf()
            mcur = sbt(sE, "mcur", [128, 128], BF16); b_mcur = Buf()
            c.op("pool", lambda e: e.memset(mprev[:], 1.0), writes=[b_mprev])
            c.op("pool", lambda e: e.affine_select(out=mprev[:], in_=mprev[:], pattern=[[-1, 128]], compare_op=ALU.is_gt, fill=0.0, base=0, channel_multiplier=1),
                 reads=[b_mprev], writes=[b_mprev])
            c.op("pool", lambda e: e.memset(mcur[:], 1.0), writes=[b_mcur])
            c.op("pool", lambda e: e.affine_select(out=mcur[:], in_=mcur[:], pattern=[[1, 128]], compare_op=ALU.is_ge, fill=0.0, base=0, channel_multiplier=-1),
                 reads=[b_mcur], writes=[b_mcur])
            KT = [sbt(sE, f"KT{i}", [128, 8, 128], BF16) for i in range(2)]; b_KT = [Buf() for _ in range(2)]
            Ksp = [sbt(sE, f"Ksp{i}", [128, 4, 2, 128], BF16) for i in range(2)]; b_Ksp = [Buf() for _ in range(2)]
            for i in range(2):
                c.op("pool", lambda e, i=i: e.memset(Ksp[i][:], 0.0), writes=[b_Ksp[i]])
            Va = [sbt(sE, f"Va{i}", [128, 4, 65], BF16) for i in range(2)]; b_Va = [Buf() for _ in range(2)]
            for i in range(2):
                c.op("pool", lambda e, i=i: e.memset(Va[i][:], 1.0), writes=[b_Va[i]])
            ht_r = Ring(sE, "htE", 2, [128, D], BF16); hsemE = [c.dma_sem("hE") for _ in range(2)]
            hres_r = Ring(sE, "hresE", 2, [128, D], F32); rsemE = [c.dma_sem("rE") for _ in range(2)]
            hT_r = Ring(sE, "hTE", 2, [128, 8, 128], BF16)
            Qs_r = Ring(sE, "Qs", 2, [128, 16, 64], BF16)
            Ks_r = Ring(sE, "Ks", 2, [128, 4, 64], BF16)
            QT_r = Ring(sE, "QT", 2, [128, 8, 128], BF16)
            ro_r = Ring(sE, "ro", 4, [128, 8, 8], F32)
            E_r = Ring(sE, "E", 4, [128, 512], BF16)
            O_r = Ring(sE, "O", 2, [128, 16, 64], BF16)
            OT_r = Ring(sE, "OT", 2, [128, 8, 128], BF16)
            dn_r = Ring(sE, "dn", 4, [128, 8], F32)
            r_r = Ring(sE, "rE", 2, [128, D], F32); osemE = [c.dma_sem("oE") for _ in range(2)]

            def rope(psrc, nh, dst, b_dst, pbuf, t):
                cb = cosT[:, t, :].unsqueeze(1).broadcast_to([128, nh, 8])
                sb2 = sinT[:, t, :].unsqueeze(1).broadcast_to([128, nh, 8])
                q1 = psrc[:, :, 0:8]; q2 = psrc[:, :, 8:16]
                ta, b_ta = ro_r.next(); tb, b_tb = ro_r.next()
                c.op("dve", lambda e: e.tensor_tensor(out=ta[:, 0:nh, :], in0=q1, in1=cb, op=ALU.mult), reads=[pbuf, b_cosT], writes=[b_ta])
                c.op("dve", lambda e: e.tensor_tensor(out=tb[:, 0:nh, :], in0=q2, in1=sb2, op=ALU.mult), reads=[pbuf, b_sinT], writes=[b_tb])
                c.op("dve", lambda e: e.tensor_tensor(out=dst[:, :, 0:8], in0=ta[:, 0:nh, :], in1=tb[:, 0:nh, :], op=ALU.subtract), reads=[b_ta, b_tb], writes=[b_dst])
                tc_, b_tc = ro_r.next(); td, b_td = ro_r.next()
                c.op("dve", lambda e: e.tensor_tensor(out=tc_[:, 0:nh, :], in0=q2, in1=cb, op=ALU.mult), reads=[pbuf, b_cosT], writes=[b_tc])
                c.op("dve", lambda e: e.tensor_tensor(out=td[:, 0:nh, :], in0=q1, in1=sb2, op=ALU.mult), reads=[pbuf, b_sinT], writes=[b_td])
                c.op("dve", lambda e: e.tensor_tensor(out=dst[:, :, 8:16], in0=tc_[:, 0:nh, :], in1=td[:, 0:nh, :], op=ALU.add), reads=[b_tc, b_td], writes=[b_dst])

            for t in range(NT):
                nblk = t % 16
                ht, b_ht = ht_r.next(); hres, b_hres = hres_r.next(); hT, b_hT = hT_r.next()
                c.dma("pool", ht[:], H2_d[t * 128:(t + 1) * 128, :], writes=[b_ht], sem=hsemE[t % 2])
                c.dma("act", hres[:], H2_d[t * 128:(t + 1) * 128, :], writes=[b_hres], sem=rsemE[t % 2])
                transpose_tile(ht, b_ht, lambda: hT[:], b_hT)
                pq = [nbank(), nbank()]; pkv = nbank()
                for nb_ in range(2):
                    for kc in range(8):
                        c.op("pe", lambda e, nb_=nb_, kc=kc: e.matmul(PS[pq[nb_]][:], lhsT=hT[:, kc, :], rhs=Wq[:, kc, nb_ * 512:(nb_ + 1) * 512], start=(kc == 0), stop=(kc == 7)),
                             reads=[b_hT, b_Wq], writes=[PSB[pq[nb_]]], inc=(kc == 7))
                for kc in range(8):
                    c.op("pe", lambda e, kc=kc: e.matmul(PS[pkv][:], lhsT=hT[:, kc, :], rhs=Wkv[:, kc, :], start=(kc == 0), stop=(kc == 7)),
                         reads=[b_hT, b_Wkv], writes=[PSB[pkv]], inc=(kc == 7))
                Qs, b_Qs = Qs_r.next(); Ks, b_Ks = Ks_r.next(); QT, b_QT = QT_r.next()
                slot = t % 2
                for nb_ in range(2):
                    pv_ = PS[pq[nb_]][:].rearrange("p (h d) -> p h d", h=8)
                    c.op("act", lambda e, nb_=nb_, pv_=pv_: e.copy(out=Qs[:, nb_ * 8:(nb_ + 1) * 8, :], in_=pv_), reads=[PSB[pq[nb_]]], writes=[b_Qs])
                    rope(pv_, 8, Qs[:, nb_ * 8:(nb_ + 1) * 8, :], b_Qs, PSB[pq[nb_]], t)
                kvv = PS[pkv][:].rearrange("p (h d) -> p h d", h=8)
                c.op("act", lambda e: e.copy(out=Ks[:], in_=kvv[:, 0:4, :]), reads=[PSB[pkv]], writes=[b_Ks])
                rope(kvv[:, 0:4, :], 4, Ks[:], b_Ks, PSB[pkv], t)
                c.op("act", lambda e: e.copy(out=Va[slot][:, :, 0:64], in_=kvv[:, 4:8, :]), reads=[PSB[pkv]], writes=[b_Va[slot]])
                transpose_tile(Qs[:].rearrange("p h d -> p (h d)"), b_Qs, lambda: QT[:], b_QT, evac_eng="act")
                c.op("pool", lambda e: e.tensor_copy(out=Ksp[slot][:, :, 0, 0:64], in_=Ks[:]), reads=[b_Ks], writes=[b_Ksp[slot]])
                c.op("pool", lambda e: e.tensor_copy(out=Ksp[slot][:, :, 1, 64:128], in_=Ks[:]), reads=[b_Ks], writes=[b_Ksp[slot]])
                transpose_tile(Ksp[slot][:].rearrange("p g v d -> p (g v d)"), b_Ksp[slot], lambda: KT[slot][:], b_KT[slot], evac_eng="dve")
                O, b_O = O_r.next()
                kbs = ([(1 - slot, mprev, b_mprev)] if nblk > 0 else []) + [(slot, mcur, b_mcur)]
                for g in range(4):
                    Es = []
                    for (sl, mk, b_mk) in kbs:
                        ps_ = nbank()
                        for var in range(2):
                            c.op("pe", lambda e, ps_=ps_, sl=sl, g=g, var=var: e.matmul(PS[ps_][:, var * 256:(var + 1) * 256], lhsT=KT[sl][:, g * 2 + var, :],
                                                                                  rhs=QT[:, 2 * g:2 * g + 2, :].rearrange("p h t -> p (h t)"), start=True, stop=True),
                                 reads=[b_KT[sl], b_QT], writes=[PSB[ps_]], inc=(var == 1))
                        E, b_E = E_r.next()
                        c.op("act", lambda e, ps_=ps_, E=E: e.activation(out=E[:], in_=PS[ps_][:], func=AF.Exp, scale=0.125), reads=[PSB[ps_]], writes=[b_E])
                        c.op("pool", lambda e, E=E, mk=mk: e.tensor_tensor(out=E[:].rearrange("p (h t) -> p h t", h=4), in0=E[:].rearrange("p (h t) -> p h t", h=4),
                                                                       in1=mk[:].unsqueeze(1).broadcast_to([128, 4, 128]), op=ALU.mult), reads=[b_E, b_mk], writes=[b_E])
                        Es.append((E, b_E, sl))
                    po_ = nbank()
                    for hl in range(4):
                        for i, (E, b_E, sl) in enumerate(Es):
                            c.op("pe", lambda e, po_=po_, hl=hl, E=E, sl=sl, g=g, i=i: e.matmul(PS[po_][:, hl * 65:(hl + 1) * 65], lhsT=E[:, hl * 128:(hl + 1) * 128], rhs=Va[sl][:, g, :],
                                                                                     start=(i == 0), stop=(i == len(Es) - 1)),
                                 reads=[b_E, b_Va[sl]], writes=[PSB[po_]], inc=(hl == 3 and i == len(Es) - 1))
                    dn, b_dn = dn_r.next()
                    ov = PS[po_][:, 0:260].rearrange("p (v pr d) -> p v pr d", v=2, pr=2)
                    c.op("dve", lambda e, ov=ov, dn=dn, g=g: e.tensor_tensor(out=dn[:, 0:4].rearrange("p (v pr) -> p v pr", v=2), in0=ov[:, :, :, 64],
                                                                         in1=esk[:, 4 * g:4 * g + 4].rearrange("p (pr v) -> p v pr", v=2), op=ALU.add), reads=[PSB[po_], b_esk], writes=[b_dn])
                    c.op("dve", lambda e, dn=dn: e.reciprocal(out=dn[:, 4:8], in_=dn[:, 0:4]), reads=[b_dn], writes=[b_dn])
                    for var in range(2):
                        c.op("dve", lambda e, ov=ov, dn=dn, g=g, var=var: e.tensor_tensor(
                            out=O[:, 4 * g:4 * g + 4, :].rearrange("p (pr v) d -> p v pr d", v=2)[:, var, :, :], in0=ov[:, var, :, 0:64],
                            in1=dn[:, 4 + 2 * var:6 + 2 * var].unsqueeze(2).broadcast_to([128, 2, 64]), op=ALU.mult),
                             reads=[PSB[po_], b_dn], writes=[b_O])
                OT, b_OT = OT_r.next()
                transpose_tile(O[:].rearrange("p h d -> p (h d)"), b_O, lambda: OT[:], b_OT, evac_eng="act")
                r, b_r = r_r.next()
                po = [nbank(), nbank()]
                for nb_ in range(2):
                    for kc in range(8):
                        c.op("pe", lambda e, nb_=nb_, kc=kc: e.matmul(PS[po[nb_]][:], lhsT=OT[:, kc, :], rhs=Wo[:, kc, nb_ * 512:(nb_ + 1) * 512], start=(kc == 0), stop=(kc == 7)),
                             reads=[b_OT, b_Wo], writes=[PSB[po[nb_]]], inc=(kc == 7))
                    c.op("dve", lambda e, nb_=nb_: e.scalar_tensor_tensor(out=r[:, nb_ * 512:(nb_ + 1) * 512], in0=hres[:, nb_ * 512:(nb_ + 1) * 512], scalar=ALPHA,
                                                                     in1=PS[po[nb_]][:], op0=ALU.mult, op1=ALU.add), reads=[b_hres, PSB[po[nb_]]], writes=[b_r])
                layernorm(rings, r, b_r, lng, b_lng, lnb, b_lnb)
                c.dma("sp", H3_d[t * 128:(t + 1) * 128, :], r[:], reads=[b_r], sem=osemE[t % 2])
            c.barrier()
        if stop_after == "E":
            return nc

        with contextlib.ExitStack() as sF:
            lng, b_lng, lnb, b_lnb = load_ln(sF, 3)
            rings = ln_rings(sF)
            semF = c.dma_sem("cF")
            Wr = sbt(sF, "Wr", [128, 8, NE], F32); b_Wr = Buf()
            brt = sbt(sF, "brt", [128, NE], F32); b_brt = Buf()
            c.dma("sp", Wr[:], wr_d.rearrange("(kc p) n -> p kc n", p=128), writes=[b_Wr], sem=semF)
            c.dma("sp", brt[:], br_d[:, :], writes=[b_brt], sem=semF)
            STK = 1024
            NSUB = STK // 128
            WG = [sbt(sF, f"WGe{i}", [128, 8, ED], BF16) for i in range(2)]; b_WG = [Buf() for _ in range(2)]
            WU = [sbt(sF, f"WUe{i}", [128, 8, ED], BF16) for i in range(2)]; b_WU = [Buf() for _ in range(2)]
            WD = [sbt(sF, f"WDe{i}", [128, 8, D], BF16) for i in range(2)]; b_WD = [Buf() for _ in range(2)]
            wsem = [c.dma_sem("wF") for _ in range(2)]
            hTm = sbt(sF, "hTm", [128, 8, STK], BF16); b_hTm = Buf()
            h1m = sbt(sF, "h1m", [128, 8, STK], BF16); b_h1m = Buf()
            acc = sbt(sF, "acc", [128, NSUB, D], F32); b_acc = [Buf() for _ in range(NSUB)]
            comb = sbt(sF, "comb", [128, NSUB, NE], F32); b_comb = [Buf() for _ in range(NSUB)]
            h3_r = Ring(sF, "h3F", 2, [128, D], F32); h3sem = [c.dma_sem("h3F") for _ in range(2)]
            h3T_r = Ring(sF, "h3T", 2, [128, 8, 128], F32)
            lg_r = Ring(sF, "lg", 3, [128, 4, NE], F32)
            s_r = Ring(sF, "sF", 3, [128, 512], BF16)
            osemF = [c.dma_sem("oF") for _ in range(2)]
            n_w = 0

            def issue_weights(e_idx, slot):
                wsem[slot] = c.dma_sem("wF")
                for kc in range(8):
                    c.dma("pool", WG[slot][:, kc, :], mg_d[e_idx, kc * 128:(kc + 1) * 128, :], writes=[b_WG[slot]], sem=wsem[slot])
                    c.dma("pool", WU[slot][:, kc, :], mu_d[e_idx, kc * 128:(kc + 1) * 128, :], writes=[b_WU[slot]], sem=wsem[slot])
                for kc in range(8):
                    c.dma("pool", WD[slot][:, kc, :], md_d[e_idx, kc * 128:(kc + 1) * 128, :], writes=[b_WD[slot]], sem=wsem[slot])

            issue_weights(0, 0)
            for st_ in range(TOK // STK):
                h3sem = [c.dma_sem("h3F") for _ in range(2)]
                for sub in range(NSUB):
                    t = st_ * NSUB + sub
                    h3, b_h3 = h3_r.next(); h3T, b_h3T = h3T_r.next()
                    c.dma("sp", h3[:], H3_d[t * 128:(t + 1) * 128, :], writes=[b_h3], sem=h3sem[t % 2])
                    pt = [nbank(), nbank()]
                    for kc in range(8):
                        c.op("pe", lambda e, kc=kc: e.transpose(out=PS[pt[kc // 4]][:, (kc % 4) * 128:(kc % 4 + 1) * 128], in_=h3[:, kc * 128:(kc + 1) * 128], identity=identf[:]),
                             reads=[b_h3, b_identf], writes=[PSB[pt[kc // 4]]], inc=(kc % 4 == 3))
                    for hh in range(2):
                        src = PS[pt[hh]][:].rearrange("p (k t) -> p k t", k=4)
                        c.op("act", lambda e, hh=hh, src=src: e.copy(out=h3T[:, hh * 4:(hh + 1) * 4, :], in_=src), reads=[PSB[pt[hh]]], writes=[b_h3T])
                        c.op("dve", lambda e, hh=hh, src=src, sub=sub: e.tensor_copy(out=hTm[:, hh * 4:(hh + 1) * 4, sub * 128:(sub + 1) * 128], in_=src), reads=[PSB[pt[hh]]], writes=[b_hTm])
                    pl = nbank()
                    for kc in range(8):
                        c.op("pe", lambda e, kc=kc: e.matmul(PS[pl][:, 0:NE], lhsT=h3T[:, kc, :], rhs=Wr[:, kc, :], start=(kc == 0), stop=(kc == 7)),
                             reads=[b_h3T, b_Wr], writes=[PSB[pl]], inc=(kc == 7))
                    lg, b_lg = lg_r.next()
                    c.op("dve", lambda e: e.tensor_tensor(out=lg[:, 0, :], in0=PS[pl][:, 0:NE], in1=brt[:], op=ALU.add), reads=[PSB[pl], b_brt], writes=[b_lg])
                    c.op("dve", lambda e: e.max(out=lg[:, 1, :], in_=lg[:, 0, :]), reads=[b_lg], writes=[b_lg])
                    c.op("dve", lambda e: e.tensor_scalar(out=lg[:, 2, :], in0=lg[:, 0, :], scalar1=lg[:, 1, 1:2], scalar2=None, op0=ALU.is_ge), reads=[b_lg], writes=[b_lg])
                    c.op("dve", lambda e: e.tensor_scalar(out=lg[:, 1, 2:3], in0=lg[:, 1, 0:1], scalar1=-1.0, scalar2=None, op0=ALU.mult), reads=[b_lg], writes=[b_lg])
                    c.op("act", lambda e: e.activation(out=lg[:, 3, :], in_=lg[:, 0, :], func=AF.Exp, bias=lg[:, 1, 2:3], scale=1.0), reads=[b_lg], writes=[b_lg])
                    c.op("dve", lambda e: e.tensor_tensor(out=lg[:, 3, :], in0=lg[:, 3, :], in1=lg[:, 2, :], op=ALU.mult), reads=[b_lg], writes=[b_lg])
                    c.op("dve", lambda e: e.tensor_reduce(out=lg[:, 1, 3:4], in_=lg[:, 3, :], axis=AX.X, op=ALU.add), reads=[b_lg], writes=[b_lg])
                    c.op("dve", lambda e: e.reciprocal(out=lg[:, 1, 4:5], in_=lg[:, 1, 3:4]), reads=[b_lg], writes=[b_lg])
                    c.op("dve", lambda e, sub=sub: e.tensor_scalar(out=comb[:, sub, :], in0=lg[:, 3, :], scalar1=lg[:, 1, 4:5], scalar2=None, op0=ALU.mult), reads=[b_lg], writes=[b_comb[sub]])
                for ex in range(NE):
                    slot = n_w % 2
                    n_w += 1
                    nxt = (st_ * NE + ex + 1)
                    if nxt < (TOK // STK) * NE:
                        issue_weights(nxt % NE, 1 - slot)
                    for half in range(STK // 512):
                        for fc in range(8):
                            pg, pu = nbank(), nbank()
                            for (W, bW, pb) in ((WG[slot], b_WG[slot], pg), (WU[slot], b_WU[slot], pu)):
                                for kc in range(8):
                                    c.op("pe", lambda e, W=W, pb=pb, kc=kc, fc=fc, half=half: e.matmul(PS[pb][:], lhsT=W[:, kc, fc * 128:(fc + 1) * 128], rhs=hTm[:, kc, half * 512:(half + 1) * 512],
                                                                                                 start=(kc == 0), stop=(kc == 7)),
                                         reads=[bW, b_hTm], writes=[PSB[pb]], inc=(kc == 7))
                            sb_, b_s = s_r.next()
                            c.op("act", lambda e, pg=pg, sb_=sb_: e.activation(out=sb_[:], in_=PS[pg][:], func=AF.Silu), reads=[PSB[pg]], writes=[b_s])
                            c.op("dve", lambda e, pu=pu, sb_=sb_, fc=fc, half=half: e.tensor_tensor(out=h1m[:, fc, half * 512:(half + 1) * 512], in0=PS[pu][:], in1=sb_[:], op=ALU.mult),
                                 reads=[PSB[pu], b_s], writes=[b_h1m])
                    for sub in range(NSUB):
                        for nb_ in range(2):
                            po_ = nbank()
                            for fc in range(8):
                                c.op("pe", lambda e, po_=po_, fc=fc, sub=sub, nb_=nb_, slot=slot: e.matmul(PS[po_][:], lhsT=h1m[:, fc, sub * 128:(sub + 1) * 128],
                                                                                                 rhs=WD[slot][:, fc, nb_ * 512:(nb_ + 1) * 512], start=(fc == 0), stop=(fc == 7)),
                                     reads=[b_h1m, b_WD[slot]], writes=[PSB[po_]], inc=(fc == 7))
                            av = acc[:, sub, nb_ * 512:(nb_ + 1) * 512]
                            if ex == 0:
                                c.op("dve", lambda e, po_=po_, av=av, sub=sub, ex=ex: e.tensor_scalar(out=av, in0=PS[po_][:], scalar1=comb[:, sub, ex:ex + 1], scalar2=None, op0=ALU.mult),
                                     reads=[PSB[po_], b_comb[sub]], writes=[b_acc[sub]])
                            else:
                                c.op("dve", lambda e, po_=po_, av=av, sub=sub, ex=ex: e.scalar_tensor_tensor(out=av, in0=PS[po_][:], scalar=comb[:, sub, ex:ex + 1], in1=av, op0=ALU.mult, op1=ALU.add),
                                     reads=[PSB[po_], b_comb[sub], b_acc[sub]], writes=[b_acc[sub]])
                for sub in range(NSUB):
                    t = st_ * NSUB + sub
                    h3, b_h3 = h3_r.next()
                    c.dma("sp", h3[:], H3_d[t * 128:(t + 1) * 128, :], writes=[b_h3], sem=h3sem[t % 2])
                    c.op("dve", lambda e, sub=sub: e.scalar_tensor_tensor(out=acc[:, sub, :], in0=h3[:], scalar=ALPHA, in1=acc[:, sub, :], op0=ALU.mult, op1=ALU.add),
                         reads=[b_h3, b_acc[sub]], writes=[b_acc[sub]])
                    layernorm(rings, acc[:, sub, :], b_acc[sub], lng, b_lng, lnb, b_lnb)
                    c.dma("sp", out_d[t * 128:(t + 1) * 128, :], acc[:, sub, :], reads=[b_acc[sub]], sem=osemF[t % 2])
            for k in ("sp",):
                for s_ in osemF:
                    nc.sync.wait_ge(s_[0], s_[1])
    return nc


def make_in_maps(inp):
    f = lambda a: np.ascontiguousarray(np.asarray(a), dtype=np.float32)
    x = f(inp["x"])
    pos = np.asarray(inp["positions"]).astype(np.int32)
    ln_g = f(inp["ln_g"]).reshape(4, D)
    ln_b = f(inp["ln_b"]).reshape(4, D)
    bcast = lambda a: np.ascontiguousarray(np.broadcast_to(a[None], (128,) + a.shape))
    lam_re = f(inp["ssm_lambda_re"])[0]; lam_im = f(inp["ssm_lambda_im"])[0]
    two = lambda a: np.ascontiguousarray(np.concatenate([a, a], axis=0))
    b_re = f(inp["ssm_b_re"])[0].transpose(1, 0, 2); b_im = f(inp["ssm_b_im"])[0].transpose(1, 0, 2)
    c_re = f(inp["ssm_c_re"])[0].transpose(2, 0, 1); c_im = f(inp["ssm_c_im"])[0].transpose(2, 0, 1)
    cat = lambda a, b: np.ascontiguousarray(np.concatenate([a, b], axis=0))
    inv_freq = (500000.0 ** (-np.arange(0, 16, 2, dtype=np.float32) / 16.0)).astype(np.float32)
    shared = {
        "ln_g": bcast(ln_g), "ln_b": bcast(ln_b),
        "lam_re": two(lam_re.T), "lam_im": two(lam_im.T), "lstep": bcast(f(inp["ssm_log_step"])[0]),
        "s5_ba": cat(b_re, b_im), "s5_bb": cat(b_im, b_re), "s5_ca": cat(c_re, c_im), "s5_cb": cat(c_im, c_re),
        "s5_d": bcast(f(inp["ssm_d"])[0]),
        "w_glu": f(inp["ssm_w_glu"])[0], "kv_w": f(inp["kv_w"]), "w_q": f(inp["attn_w_q"])[0],
        "sinks": bcast(f(inp["attn_sinks"])[0]), "w_out": f(inp["attn_w_out"])[0],
        "ffn_g": f(inp["ffn_w_gate"])[0], "ffn_u": f(inp["ffn_w_up"])[0], "ffn_d": f(inp["ffn_w_down"])[0],
        "w_router": f(inp["moe_w_router"])[0], "b_router": bcast(f(inp["moe_b_router"])[0]),
        "moe_g": f(inp["moe_w_gate"])[0], "moe_u": f(inp["moe_w_up"])[0], "moe_d": f(inp["moe_w_down"])[0],
        "inv_freq": bcast(inv_freq),
    }
    maps = []
    for i in range(NCORES):
        m = dict(shared)
        m["x"] = np.ascontiguousarray(x[2 * i:2 * i + 2].reshape(TOK, D))
        m["pos"] = np.ascontiguousarray(pos[2 * i:2 * i + 2].reshape(NT, 128).T)
        maps.append(m)
    return maps


def kernel(**inputs):
    nc = build()
    maps = make_in_maps(inputs)
    res = run_bass_kernel_spmd(nc, maps, core_ids=list(range(NCORES)))
    out = np.stack([np.asarray(r["out"]).reshape(NSEQ, L, D) for r in res.results], axis=0)
    return out.reshape(NCORES * NSEQ, L, D).astype(np.float32)
```

```python
import contextlib
import math
import numpy as np
import concourse.bass as bass
import concourse.mybir as mybir
from concourse.alu_op_type import AluOpType as ALU
from concourse.bass_utils import run_bass_kernel_spmd

F32 = mybir.dt.float32
BF16 = mybir.dt.bfloat16
I32 = mybir.dt.int32
AF = mybir.ActivationFunctionType
AX = mybir.AxisListType

NCORES = 8
D = 1024
L = 2048
NSEQ = 2
TOK = NSEQ * L
NT = TOK // 128
FF = 2816
NE = 8
ED = 1024
ALPHA = float(4.0 ** 0.25)
LN_EPS = 1e-5
MLIST = [0, -1, -2, -3, -4, -5, -6, -7] + list(range(16)) + [16, 32, 64, 128, 256, 512, 1024]
NM = len(MLIST)
SEM_ROLL = 3000


class Ev:
    __slots__ = ("sem", "val", "eng", "ref")

    def __init__(self, sem, val, eng, ref=None):
        self.sem = sem
        self.val = val
        self.eng = eng
        self.ref = ref


class Buf:
    __slots__ = ("name", "w", "r", "excl")

    def __init__(self, name="", excl=False):
        self.name = name
        self.w = None
        self.r = {}
        self.excl = excl


class Ctx:
    def __init__(self, nc, stack):
        self.nc = nc
        self.stack = stack
        self.engs = {"pe": nc.tensor, "act": nc.scalar, "dve": nc.vector, "pool": nc.gpsimd, "sp": nc.sync}
        self.sem = {}
        self.cnt = {}
        self.nsem = 0
        self.dsems = []
        for k in self.engs:
            self._new_eng_sem(k)
        self.waited = {k: {} for k in self.engs}
        self.pending = {k: [] for k in self.engs}
        self.ninst = {k: 0 for k in self.engs}
        self.last = {k: None for k in self.engs}

    def _new_sem(self, name):
        self.nsem += 1
        return self.stack.enter_context(self.nc.semaphore(f"{name}_{self.nsem}"))

    def _new_eng_sem(self, k):
        self.sem[k] = self._new_sem("e" + k)
        self.cnt[k] = 0

    def dma_sem(self, name="d"):
        s = [self._new_sem(name), 0]
        self.dsems.append(s)
        return s

    def _wait(self, k, ev):
        if ev is None:
            return
        if ev.eng == "pe" and k == "pe":
            return
        if ev.val is None:
            raise RuntimeError("dependency on an instruction without inc")
        w = self.waited[k]
        sid = id(ev.sem)
        val = ev.ref[1] if ev.ref is not None else ev.val
        if w.get(sid, 0) >= val:
            return
        self.engs[k].wait_ge(ev.sem, val)
        self.ninst[k] += 1
        w[sid] = val

    def _deps(self, k, reads, writes):
        for b in reads:
            self._wait(k, b.w)
            if b.excl:
                for kk, e in b.r.items():
                    if kk != k:
                        self._wait(k, e)
        for b in writes:
            self._wait(k, b.w)
            for e in b.r.values():
                self._wait(k, e)

    def _commit(self, ev, reads, writes):
        key = id(ev.sem) if ev.eng == "dma" else ev.eng
        for b in reads:
            b.r[key] = ev
        for b in writes:
            b.w = ev
            b.r = {}

    def op(self, k, fn, reads=(), writes=(), inc=True):
        self._deps(k, reads, writes)
        inst = fn(self.engs[k])
        self.ninst[k] += 1
        if inc:
            if self.cnt[k] >= SEM_ROLL:
                self._new_eng_sem(k)
            self.cnt[k] += 1
            inst.then_inc(self.sem[k], 1)
            ev = Ev(self.sem[k], self.cnt[k], k)
            for p in self.pending[k]:
                p.sem = ev.sem
                p.val = ev.val
            self.pending[k] = []
            self.last[k] = ev
        else:
            ev = Ev(None, None, k)
            self.pending[k].append(ev)
        self._commit(ev, reads, writes)
        return ev

    def dma(self, k, out, in_, reads=(), writes=(), sem=None, **kw):
        self._deps(k, reads, writes)
        inst = self.engs[k].dma_start(out=out, in_=in_, **kw)
        self.ninst[k] += 1
        sem[1] += 16
        inst.then_inc(sem[0], 16)
        ev = Ev(sem[0], sem[1], "dma", sem)
        self._commit(ev, reads, writes)
        return ev

    def barrier(self, engs=("pe", "act", "dve", "pool", "sp")):
        for k in engs:
            assert not self.pending[k]
        for k in engs:
            for k2 in engs:
                if k2 != k and self.last[k2] is not None:
                    self._wait(k, self.last[k2])
            for s in self.dsems:
                if s[1] > 0:
                    self._wait(k, Ev(s[0], s[1], "dma", s))


LAST_CTX = None


def build(stop_after=None):
    global LAST_CTX
    nc = bass.Bass("TRN2", target_bir_lowering=False)
    dram_in = lambda name, shape, dt=F32: nc.dram_tensor(name, list(shape), dt, kind="ExternalInput").ap()
    x_d = dram_in("x", [TOK, D])
    pos_d = dram_in("pos", [128, NT], I32)
    lnG_d = dram_in("ln_g", [128, 4, D])
    lnB_d = dram_in("ln_b", [128, 4, D])
    lamr_d = dram_in("lam_re", [128, 64])
    lami_d = dram_in("lam_im", [128, 64])
    lstep_d = dram_in("lstep", [128, 64])
    BA_d = dram_in("s5_ba", [128, 64, 16])
    BB_d = dram_in("s5_bb", [128, 64, 16])
    CA_d = dram_in("s5_ca", [128, 64, 16])
    CB_d = dram_in("s5_cb", [128, 64, 16])
    dsk_d = dram_in("s5_d", [128, D])
    wglu_d = dram_in("w_glu", [D, 2 * D])
    kvw_d = dram_in("kv_w", [D, 512])
    wq_d = dram_in("w_q", [D, D])
    sink_d = dram_in("sinks", [128, 16])
    wo_d = dram_in("w_out", [D, D])
    fg_d = dram_in("ffn_g", [D, FF])
    fu_d = dram_in("ffn_u", [D, FF])
    fd_d = dram_in("ffn_d", [FF, D])
    wr_d = dram_in("w_router", [D, NE])
    br_d = dram_in("b_router", [128, NE])
    mg_d = dram_in("moe_g", [NE, D, ED])
    mu_d = dram_in("moe_u", [NE, D, ED])
    md_d = dram_in("moe_d", [NE, ED, D])
    invf_d = dram_in("inv_freq", [128, 8])
    out_d = nc.dram_tensor("out", [TOK, D], F32, kind="ExternalOutput").ap()
    dbg = stop_after is not None
    G_d = nc.dram_tensor("G", [TOK, D], BF16, kind="ExternalOutput" if stop_after == "A" else "Internal").ap()
    H1_d = nc.dram_tensor("H1", [TOK, D], F32, kind="ExternalOutput" if stop_after == "B" else "Internal").ap()
    H2_d = nc.dram_tensor("H2", [TOK, D], F32, kind="ExternalOutput" if stop_after == "C" else "Internal").ap()
    H3_d = nc.dram_tensor("H3", [TOK, D], F32, kind="ExternalOutput" if stop_after == "E" else "Internal").ap()
    if stop_after == "P":
        dbgP = nc.dram_tensor("dbgP", [128, 2 * NM + 2, 64], F32, kind="ExternalOutput").ap()
        dbgT = nc.dram_tensor("dbgT", [128, 3, 64, 128], BF16, kind="ExternalOutput").ap()

    with contextlib.ExitStack() as st:
        c = Ctx(nc, st)
        LAST_CTX = c
        uniq = [0]

        def sbt(stack, name, shape, dt):
            uniq[0] += 1
            return stack.enter_context(nc.sbuf_tensor(f"{name}_{uniq[0]}", list(shape), dt))
        PS = [st.enter_context(nc.psum_tensor(f"ps{i}", [128, 512], F32)) for i in range(8)]
        PSB = [Buf(f"ps{i}", excl=True) for i in range(8)]

        identf = sbt(st, "identf", [128, 128], F32); b_identf = Buf()
        identb = sbt(st, "identb", [128, 128], BF16); b_identb = Buf()
        c.op("pool", lambda e: e.memset(identf[:], 0.0), writes=[b_identf])
        c.op("pool", lambda e: e.affine_select(out=identf[:], in_=identf[:], pattern=[[-1, 128]], compare_op=ALU.not_equal,
                                               fill=1.0, base=0, channel_multiplier=1), reads=[b_identf], writes=[b_identf])
        c.op("pool", lambda e: e.tensor_copy(out=identb[:], in_=identf[:]), reads=[b_identf], writes=[b_identb])
        out_sem = c.dma_sem("out")
        mhalf = sbt(st, "mhalf", [128, 1], F32); b_mhalf = Buf()
        c.op("pool", lambda e: e.memset(mhalf[:], -0.5), writes=[b_mhalf])

        with contextlib.ExitStack() as sa:
            Tm = sbt(sa, "Tm", [128, 64, 128], BF16); b_Tm = Buf()
            MinT = sbt(sa, "MinT", [128, 64, 128], BF16); b_MinT = Buf()
            Mout = sbt(sa, "Mout", [128, 64, 128], BF16); b_Mout = Buf()
            PA = sbt(sa, "PA", [128, 8, 64], F32); b_PA = Buf()
            PB = sbt(sa, "PB", [128, 8, 64], F32); b_PB = Buf()
            PAl = sbt(sa, "PAl", [128, 8, 64], F32); b_PAl = Buf()
            PBl = sbt(sa, "PBl", [128, 8, 64], F32); b_PBl = Buf()
            D1b = sbt(sa, "D1b", [128, 128], BF16); b_D1 = Buf()
            D2b = sbt(sa, "D2b", [128, 128], BF16); b_D2 = Buf()
            dB = sbt(sa, "dB", [128, D], F32); b_dB = Buf()
            ld0 = c.dma_sem("ld0")
            c.dma("sp", dB[:], dsk_d[:, :], writes=[b_dB], sem=ld0)
            with contextlib.ExitStack() as s0:
                lr = sbt(s0, "lr", [128, 64], F32); b_lr = Buf()
                li = sbt(s0, "li", [128, 64], F32); b_li = Buf()
                ls = sbt(s0, "ls", [128, 64], F32); b_ls = Buf()
                BAt = sbt(s0, "BAt", [128, 64, 16], F32); b_BA = Buf()
                BBt = sbt(s0, "BBt", [128, 64, 16], F32); b_BB = Buf()
                CAt = sbt(s0, "CAt", [128, 64, 16], F32); b_CA = Buf()
                CBt = sbt(s0, "CBt", [128, 64, 16], F32); b_CB = Buf()
                c.dma("sp", lr[:], lamr_d[:, :], writes=[b_lr], sem=ld0)
                c.dma("sp", li[:], lami_d[:, :], writes=[b_li], sem=ld0)
                c.dma("sp", ls[:], lstep_d[:, :], writes=[b_ls], sem=ld0)
                c.dma("act", BAt[:], BA_d[:, :, :], writes=[b_BA], sem=ld0)
                c.dma("act", BBt[:], BB_d[:, :, :], writes=[b_BB], sem=ld0)
                c.dma("act", CAt[:], CA_d[:, :, :], writes=[b_CA], sem=ld0)
                c.dma("act", CBt[:], CB_d[:, :, :], writes=[b_CB], sem=ld0)

                sgn = sbt(s0, "sgn", [128, 1], F32); b_sgn = Buf()
                sgn2 = sbt(s0, "sgn2", [128, 1], F32); b_sgn2 = Buf()
                c.op("pool", lambda e: e.memset(sgn[0:64, :], -1.0), writes=[b_sgn])
                c.op("pool", lambda e: e.memset(sgn[64:128, :], 1.0), writes=[b_sgn])
                c.op("pool", lambda e: e.memset(sgn2[0:64, :], 1.0), writes=[b_sgn2])
                c.op("pool", lambda e: e.memset(sgn2[64:128, :], -1.0), writes=[b_sgn2])
                D2f = sbt(s0, "D2f", [128, 128], F32); b_D2f = Buf()
                cmask = sbt(s0, "cmask", [128, 128], F32); b_cm = Buf()
                c.op("pool", lambda e: e.memset(D2f[:], 0.0), writes=[b_D2f])
                c.op("pool", lambda e: e.affine_select(out=D2f[:], in_=D2f[:], pattern=[[-1, 128]], compare_op=ALU.not_equal,
                                                       fill=1.0, base=64, channel_multiplier=1), reads=[b_D2f], writes=[b_D2f])
                c.op("pool", lambda e: e.affine_select(out=D2f[:], in_=D2f[:], pattern=[[-1, 128]], compare_op=ALU.not_equal,
                                                       fill=1.0, base=-64, channel_multiplier=1), reads=[b_D2f], writes=[b_D2f])
                c.op("pool", lambda e: e.tensor_copy(out=D2b[:], in_=D2f[:]), reads=[b_D2f], writes=[b_D2])
                c.op("pool", lambda e: e.tensor_copy(out=D1b[:], in_=identf[:]), reads=[b_identf], writes=[b_D1])
                c.op("pool", lambda e: e.memset(cmask[:], 1.0), writes=[b_cm])
                c.op("pool", lambda e: e.affine_select(out=cmask[:].rearrange("p (i o) -> p i o", i=8),
                                                       in_=cmask[:].rearrange("p (i o) -> p i o", i=8),
                                                       pattern=[[16, 8], [0, 16]], compare_op=ALU.is_ge,
                                                       fill=0.0, base=15, channel_multiplier=-1), reads=[b_cm], writes=[b_cm])

                s0a = contextlib.ExitStack()
                s0a.__enter__()
                def T3(name, stk=None):
                    return sbt(s0a if stk is None else stk, name, [128, NM, 64], F32), Buf(name)
                Pr, b_Pr = T3("Pr", s0)
                Pi, b_Pi = T3("Pi", s0)
                Mtab, b_Mt = T3("Mtab")
                for j, m in enumerate(MLIST):
                    c.op("pool", lambda e, j=j, m=m: e.memset(Mtab[:, j, :], float(m)), writes=[b_Mt])
                dt_ = sbt(s0a, "dt", [128, 64], F32); b_dt = Buf()
                lrdt = sbt(s0a, "lrdt", [128, 64], F32); b_lrdt = Buf()
                lidt = sbt(s0a, "lidt", [128, 64], F32); b_lidt = Buf()
                c.op("act", lambda e: e.activation(out=dt_[:], in_=ls[:], func=AF.Exp), reads=[b_ls], writes=[b_dt])
                c.op("dve", lambda e: e.tensor_tensor(out=lrdt[:], in0=lr[:], in1=dt_[:], op=ALU.mult), reads=[b_lr, b_dt], writes=[b_lrdt])
                c.op("dve", lambda e: e.tensor_tensor(out=lidt[:], in0=li[:], in1=dt_[:], op=ALU.mult), reads=[b_li, b_dt], writes=[b_lidt])
                bc3 = lambda t: t[:].unsqueeze(1).broadcast_to([128, NM, 64])
                Et, b_E = T3("Et")
                mag, b_mag = T3("mag")
                Ang, b_Ang = T3("Ang")
                c.op("dve", lambda e: e.tensor_tensor(out=Et[:], in0=Mtab[:], in1=bc3(lrdt), op=ALU.mult), reads=[b_Mt, b_lrdt], writes=[b_E])
                c.op("act", lambda e: e.activation(out=mag[:], in_=Et[:], func=AF.Exp), reads=[b_E], writes=[b_mag])
                c.op("dve", lambda e: e.tensor_tensor(out=Ang[:], in0=Mtab[:], in1=bc3(lidt), op=ALU.mult), reads=[b_Mt, b_lidt], writes=[b_Ang])
                kq, b_kq = T3("kq")
                ki_ = sbt(s0a, "ki", [128, NM, 64], I32); b_ki = Buf()
                kf, b_kf = T3("kf")
                yv, b_y = T3("yv")
                tw, b_tw = T3("tw")
                C1 = 6.28125
                C2 = float(2.0 * math.pi - 6.28125)
                PI = float(math.pi)
                TWO_PI = float(2.0 * math.pi)
                c.op("dve", lambda e: e.tensor_scalar(out=kq[:], in0=Ang[:], scalar1=float(1.0 / TWO_PI), scalar2=None, op0=ALU.mult), reads=[b_Ang], writes=[b_kq])
                c.op("dve", lambda e: e.tensor_copy(out=ki_[:], in_=kq[:]), reads=[b_kq], writes=[b_ki])
                c.op("dve", lambda e: e.tensor_copy(out=kf[:], in_=ki_[:]), reads=[b_ki], writes=[b_kf])
                c.op("dve", lambda e: e.scalar_tensor_tensor(out=yv[:], in0=kf[:], scalar=-C1, in1=Ang[:], op0=ALU.mult, op1=ALU.add), reads=[b_kf, b_Ang], writes=[b_y])
                c.op("dve", lambda e: e.scalar_tensor_tensor(out=yv[:], in0=kf[:], scalar=-C2, in1=yv[:], op0=ALU.mult, op1=ALU.add), reads=[b_kf, b_y], writes=[b_y])

                def wrap(t, b_t):
                    c.op("dve", lambda e: e.tensor_scalar(out=tw[:], in0=t[:], scalar1=-PI, scalar2=TWO_PI, op0=ALU.is_lt, op1=ALU.mult), reads=[b_t], writes=[b_tw])
                    c.op("dve", lambda e: e.tensor_tensor(out=t[:], in0=t[:], in1=tw[:], op=ALU.add), reads=[b_t, b_tw], writes=[b_t])
                    c.op("dve", lambda e: e.tensor_scalar(out=tw[:], in0=t[:], scalar1=PI, scalar2=-TWO_PI, op0=ALU.is_gt, op1=ALU.mult), reads=[b_t], writes=[b_tw])
                    c.op("dve", lambda e: e.tensor_tensor(out=t[:], in0=t[:], in1=tw[:], op=ALU.add), reads=[b_t, b_tw], writes=[b_t])
                wrap(yv, b_y)
                wrap(yv, b_y)
                sn, b_sn = T3("sn")
                cs, b_cs = T3("cs")
                yc, b_yc = T3("yc")
                c.op("act", lambda e: e.activation(out=sn[:], in_=yv[:], func=AF.Sin), reads=[b_y], writes=[b_sn])
                c.op("dve", lambda e: e.tensor_scalar(out=yc[:], in0=yv[:], scalar1=float(PI / 2), scalar2=None, op0=ALU.add), reads=[b_y], writes=[b_yc])
                wrap(yc, b_yc)
                c.op("act", lambda e: e.activation(out=cs[:], in_=yc[:], func=AF.Sin), reads=[b_yc], writes=[b_cs])
                c.op("dve", lambda e: e.tensor_tensor(out=Pr[:], in0=mag[:], in1=cs[:], op=ALU.mult), reads=[b_mag, b_cs], writes=[b_Pr])
                c.op("dve", lambda e: e.tensor_tensor(out=Pi[:], in0=mag[:], in1=sn[:], op=ALU.mult), reads=[b_mag, b_sn], writes=[b_Pi])
                c.barrier()
                s0a.__exit__(None, None, None)
                J1 = 9
                t64 = lambda name: (sbt(s0, name, [128, 64], F32), Buf(name))
                nr, b_nr = t64("nr"); den, b_den = t64("den"); t1, b_t1 = t64("t1"); t2, b_t2 = t64("t2")
                kr, b_kr = t64("kr"); kim, b_kim = t64("kim"); rden, b_rden = t64("rden")
                c.op("dve", lambda e: e.tensor_scalar(out=nr[:], in0=Pr[:, J1, :], scalar1=-1.0, scalar2=None, op0=ALU.add), reads=[b_Pr], writes=[b_nr])
                c.op("dve", lambda e: e.tensor_tensor(out=t1[:], in0=lr[:], in1=lr[:], op=ALU.mult), reads=[b_lr], writes=[b_t1])
                c.op("dve", lambda e: e.tensor_tensor(out=t2[:], in0=li[:], in1=li[:], op=ALU.mult), reads=[b_li], writes=[b_t2])
                c.op("dve", lambda e: e.tensor_tensor(out=den[:], in0=t1[:], in1=t2[:], op=ALU.add), reads=[b_t1, b_t2], writes=[b_den])
                c.op("dve", lambda e: e.reciprocal(out=rden[:], in_=den[:]), reads=[b_den], writes=[b_rden])
                c.op("dve", lambda e: e.tensor_tensor(out=t1[:], in0=nr[:], in1=lr[:], op=ALU.mult), reads=[b_nr, b_lr, b_den], writes=[b_t1])
                c.op("dve", lambda e: e.tensor_tensor(out=t2[:], in0=Pi[:, J1, :], in1=li[:], op=ALU.mult), reads=[b_Pi, b_li, b_den], writes=[b_t2])
                c.op("dve", lambda e: e.tensor_tensor(out=t1[:], in0=t1[:], in1=t2[:], op=ALU.add), reads=[b_t1, b_t2], writes=[b_t1])
                c.op("dve", lambda e: e.tensor_tensor(out=kr[:], in0=t1[:], in1=rden[:], op=ALU.mult), reads=[b_t1, b_rden], writes=[b_kr])
                c.op("dve", lambda e: e.tensor_tensor(out=t1[:], in0=Pi[:, J1, :], in1=lr[:], op=ALU.mult), reads=[b_Pi, b_lr, b_kr], writes=[b_t1])
                c.op("dve", lambda e: e.tensor_tensor(out=t2[:], in0=nr[:], in1=li[:], op=ALU.mult), reads=[b_nr, b_li, b_kr], writes=[b_t2])
                c.op("dve", lambda e: e.tensor_tensor(out=t1[:], in0=t1[:], in1=t2[:], op=ALU.subtract), reads=[b_t1, b_t2], writes=[b_t1])
                c.op("dve", lambda e: e.tensor_tensor(out=kim[:], in0=t1[:], in1=rden[:], op=ALU.mult), reads=[b_t1, b_rden], writes=[b_kim])
                Qr = sbt(s0, "Qr", [128, 8, 64], F32); b_Qr = Buf()
                Qi = sbt(s0, "Qi", [128, 8, 64], F32); b_Qi = Buf()
                q1 = sbt(s0, "q1", [128, 8, 64], F32); b_q1 = Buf()
                bc8 = lambda t: t[:].unsqueeze(1).broadcast_to([128, 8, 64])
                c.op("dve", lambda e: e.tensor_tensor(out=Qr[:], in0=Pr[:, 0:8, :], in1=bc8(kr), op=ALU.mult), reads=[b_Pr, b_kr], writes=[b_Qr])
                c.op("dve", lambda e: e.tensor_tensor(out=q1[:], in0=Pi[:, 0:8, :], in1=bc8(kim), op=ALU.mult), reads=[b_Pi, b_kim], writes=[b_q1])
                c.op("dve", lambda e: e.tensor_tensor(out=Qr[:], in0=Qr[:], in1=q1[:], op=ALU.subtract), reads=[b_Qr, b_q1], writes=[b_Qr])
                c.op("dve", lambda e: e.tensor_tensor(out=Qi[:], in0=Pi[:, 0:8, :], in1=bc8(kr), op=ALU.mult), reads=[b_Pi, b_kr], writes=[b_Qi])
                c.op("dve", lambda e: e.tensor_tensor(out=q1[:], in0=Pr[:, 0:8, :], in1=bc8(kim), op=ALU.mult), reads=[b_Pr, b_kim, b_Qr], writes=[b_q1])
                c.op("dve", lambda e: e.tensor_tensor(out=Qi[:], in0=Qi[:], in1=q1[:], op=ALU.add), reads=[b_Qi, b_q1], writes=[b_Qi])
                c.op("dve", lambda e: e.tensor_scalar(out=Qi[:], in0=Qi[:], scalar1=sgn[:, 0:1], scalar2=None, op0=ALU.mult), reads=[b_Qi, b_sgn], writes=[b_Qi])
                SIDX = [16, 24, 25, 26, 27, 28, 29, 30]
                for k, ix in enumerate(SIDX):
                    c.op("pool", lambda e, k=k, ix=ix: e.tensor_copy(out=PA[:, k, :], in_=Pr[:, ix, :]), reads=[b_Pr], writes=[b_PA])
                    c.op("dve", lambda e, k=k, ix=ix: e.tensor_scalar(out=PB[:, k, :], in0=Pi[:, ix, :], scalar1=sgn2[:, 0:1], scalar2=None, op0=ALU.mult), reads=[b_Pi, b_sgn2], writes=[b_PB])
                PXb = sbt(s0, "PXb", [128, 8, 64], BF16); b_PXb = Buf()
                PXf = sbt(s0, "PXf", [128, 8, 64], F32); b_PXf = Buf()
                for (Pt, b_Pt, Pl, b_Pl) in ((PA, b_PA, PAl, b_PAl), (PB, b_PB, PBl, b_PBl)):
                    c.op("dve", lambda e, Pt=Pt: e.tensor_copy(out=PXb[:], in_=Pt[:]), reads=[b_Pt], writes=[b_PXb])
                    c.op("dve", lambda e: e.tensor_copy(out=PXf[:], in_=PXb[:]), reads=[b_PXb], writes=[b_PXf])
                    c.op("dve", lambda e, Pt=Pt, Pl=Pl: e.tensor_tensor(out=Pl[:], in0=Pt[:], in1=PXf[:], op=ALU.subtract), reads=[b_Pt, b_PXf], writes=[b_Pl])
                    c.op("dve", lambda e, Pt=Pt: e.tensor_copy(out=Pt[:], in_=PXf[:]), reads=[b_PXf, b_Pl], writes=[b_Pt])
                Prs = sbt(s0, "Prs", [128, 16, 64], F32); b_Prs = Buf()
                c.op("dve", lambda e: e.tensor_scalar(out=Prs[:], in0=Pr[:, 8:24, :], scalar1=sgn2[:, 0:1], scalar2=None, op0=ALU.mult), reads=[b_Pr, b_sgn2], writes=[b_Prs])
                X = sbt(s0, "X", [128, 32, 8, 16], F32); b_X = Buf()
                X2 = sbt(s0, "X2", [128, 32, 8, 16], F32); b_X2 = Buf()
                YY = sbt(s0, "YY", [128, 32, 16, 16], F32); b_YY = Buf()
                Y2 = sbt(s0, "Y2", [128, 32, 16, 16], F32); b_Y2 = Buf()
                for hg in range(2):
                    gs = slice(hg * 32, hg * 32 + 32)
                    qv = lambda t: t[:, :, gs].rearrange("p j g -> p g j").unsqueeze(3).broadcast_to([128, 32, 8, 16])
                    bv = lambda t: t[:, gs, :].unsqueeze(2).broadcast_to([128, 32, 8, 16])
                    pv = lambda ap: ap.rearrange("p m g -> p g m").unsqueeze(3).broadcast_to([128, 32, 16, 16])
                    cv = lambda t: t[:, gs, :].unsqueeze(2).broadcast_to([128, 32, 16, 16])
                    c.op("dve", lambda e: e.tensor_tensor(out=X[:], in0=qv(Qr), in1=bv(BAt), op=ALU.mult), reads=[b_Qr, b_BA], writes=[b_X])
                    c.op("pool", lambda e: e.tensor_tensor(out=X2[:], in0=qv(Qi), in1=bv(BBt), op=ALU.mult), reads=[b_Qi, b_BB], writes=[b_X2])
                    c.op("dve", lambda e: e.tensor_tensor(out=X[:], in0=X[:], in1=X2[:], op=ALU.add), reads=[b_X, b_X2], writes=[b_X])
                    c.op("dve", lambda e: e.tensor_tensor(out=YY[:], in0=pv(Prs[:, :, gs]), in1=cv(CAt), op=ALU.mult), reads=[b_Prs, b_CA], writes=[b_YY])
                    c.op("pool", lambda e: e.tensor_tensor(out=Y2[:], in0=pv(Pi[:, 8:24, gs]), in1=cv(CBt), op=ALU.mult), reads=[b_Pi, b_CB], writes=[b_Y2])
                    c.op("dve", lambda e: e.tensor_tensor(out=YY[:], in0=YY[:], in1=Y2[:], op=ALU.subtract), reads=[b_YY, b_Y2], writes=[b_YY])
                    c.op("act", lambda e: e.copy(out=Mout[:, gs, :].rearrange("p g (m o) -> p g m o", m=8), in_=YY[:, :, 8:16, :]), reads=[b_YY], writes=[b_Mout])
                    for q4 in range(8):
                        pa, pb = q4 % 2, 2 + (q4 % 2)
                        G0 = hg * 32 + q4 * 4
                        for gg in range(4):
                            g = q4 * 4 + gg
                            c.op("pe", lambda e, g=g, gg=gg, pa=pa: e.matmul(PS[pa][:, gg * 128:(gg + 1) * 128],
                                                                             lhsT=X[:, g, :, :].rearrange("p j c -> p (j c)"),
                                                                             rhs=YY[:, g, 0:8, :].rearrange("p m o -> p (m o)"),
                                                                             start=True, stop=True),
                                 reads=[b_X, b_YY], writes=[PSB[pa]], inc=(gg == 3))
                        c.op("dve", lambda e, G0=G0, pa=pa: e.tensor_tensor(out=Tm[:, G0:G0 + 4, :],
                                                                            in0=PS[pa][:].rearrange("p (g n) -> p g n", g=4),
                                                                            in1=cmask[:].unsqueeze(1).broadcast_to([128, 4, 128]), op=ALU.mult),
                             reads=[PSB[pa], b_cm], writes=[b_Tm])
                        for gg in range(4):
                            g = q4 * 4 + gg
                            c.op("pe", lambda e, g=g, gg=gg, pb=pb: e.transpose(out=PS[pb][:, gg * 128:(gg + 1) * 128],
                                                                                in_=X[:, g, :, :].rearrange("p j c -> p (j c)"),
                                                                                identity=identf[:]),
                                 reads=[b_X, b_identf], writes=[PSB[pb]], inc=(gg == 3))
                        c.op("act", lambda e, G0=G0, pb=pb: e.copy(out=MinT[:, G0:G0 + 4, :],
                                                                   in_=PS[pb][:].rearrange("p (g n) -> p g n", g=4)),
                             reads=[PSB[pb]], writes=[b_MinT])
                if stop_after == "P":
                    ds = c.dma_sem("dbg")
                    c.dma("sp", dbgP[:, 0:NM, :], Pr[:], reads=[b_Pr], sem=ds)
                    c.dma("sp", dbgP[:, NM:2 * NM, :], Pi[:], reads=[b_Pi], sem=ds)
                    c.dma("sp", dbgP[:, 2 * NM, :], kr[:], reads=[b_kr], sem=ds)
                    c.dma("sp", dbgP[:, 2 * NM + 1, :], kim[:], reads=[b_kim], sem=ds)
                    c.dma("sp", dbgT[:, 0, :, :], Tm[:], reads=[b_Tm], sem=ds)
                    c.dma("sp", dbgT[:, 1, :, :], MinT[:], reads=[b_MinT], sem=ds)
                    c.dma("sp", dbgT[:, 2, :, :], Mout[:], reads=[b_Mout], sem=ds)
                    nc.sync.wait_ge(ds[0], ds[1])
                    return nc
                c.barrier()
            with contextlib.ExitStack() as s1:
                xv = x_d.rearrange("(ct ch j) d -> ch ct j d", ct=4, ch=128, j=8)
                Gv = G_d.rearrange("(ct ch j) d -> ch ct j d", ct=4, ch=128, j=8)
                NXB = 2
                XB = [sbt(s1, f"XB{i}", [128, 4, 8, 128], F32) for i in range(NXB)]; b_XB = [Buf() for _ in range(NXB)]
                xsem = [c.dma_sem("xs") for _ in range(NXB)]
                XR = [sbt(s1, "XR0", [128, 4, 8, 128], BF16)] * NXB; b_XR = [Buf()] * NXB
                U = [sbt(s1, "U0", [128, 8, 512], BF16)] * NXB; b_U = [Buf()] * NXB
                YB = [sbt(s1, "YB0", [128, 4, 8, 128], F32)] * NXB; b_YB = [Buf()] * NXB
                GB = [sbt(s1, "GB0", [128, 4, 8, 128], BF16)] * NXB; b_GB = [Buf()] * NXB
                gsem = [c.dma_sem("gs")] * NXB
                SC = [[sbt(s1, f"SC{i}_{k}", [128, 128], BF16) for k in range(8)] for i in range(8)]
                b_SC = [[Buf() for k in range(8)] for i in range(8)]
                SCl = [[sbt(s1, f"SCl{i}_{k}", [128, 128], BF16) for k in range(8)] for i in range(8)]
                b_SCl = [[Buf() for k in range(8)] for i in range(8)]
                SCt_ = [sbt(s1, f"SCt{k}", [128, 128], BF16) for k in range(16)]
                b_SCt_ = [Buf() for k in range(16)]
                SCt = [SCt_] * 8
                b_SCt = [b_SCt_] * 8
                Hsb = [[sbt(s1, f"Hsb{i}_{k}", [128, 512], BF16) for k in range(2)] for i in range(4)]
                b_Hsb = [[Buf() for k in range(2)] for i in range(4)]
                Hp = [sbt(s1, f"Hp{i}", [128, 2, 256], BF16) for i in range(4)]; b_Hp = [Buf() for _ in range(4)]
                Ysb = [sbt(s1, f"Ysb{i}", [128, 512], F32) for i in range(4)]; b_Ysb = [Buf() for _ in range(4)]
                for i in range(4):
                    c.op("pool", lambda e, i=i: e.memset(Hp[i][:], 0.0), writes=[b_Hp[i]])
                g_ev = []
                for gb in range(8):
                    s = gb % NXB
                    for ct in range(4):
                        c.dma("sp" if ct % 2 == 0 else "act", XB[s][:, ct, :, :], xv[:, ct, :, gb * 128:(gb + 1) * 128],
                              writes=[b_XB[s]], sem=xsem[s])
                    for ct in range(4):
                        c.op("pool", lambda e, ct=ct, s=s: e.tensor_copy(
                            out=XR[s][:, ct, :, :].rearrange("p gl (j c) -> p gl j c", j=8),
                            in_=XB[s][:, ct, :, :].rearrange("p j (gl c) -> p gl j c", c=16)),
                             reads=[b_XB[s]], writes=[b_XR[s]])
                    for gl in range(8):
                        pb = 6 + (gl % 2)
                        psb16 = PS[pb][:].bitcast(BF16)
                        for ct in range(4):
                            c.op("pe", lambda e, ct=ct, gl=gl, s=s, psb16=psb16: e.transpose(
                                out=psb16[:, ct * 128:(ct + 1) * 128], in_=XR[s][:, ct, gl, :], identity=identb[:]),
                                 reads=[b_XR[s], b_identb], writes=[PSB[pb]], inc=(ct == 3))
                        c.op("act", lambda e, gl=gl, s=s, psb16=psb16: e.copy(out=U[s][:, gl, :], in_=psb16[:, 0:512]),
                             reads=[PSB[pb]], writes=[b_U[s]])
                    for quad in range(2):
                        for gg in range(4):
                            gl = quad * 4 + gg
                            g = gb * 8 + gl
                            si = gl
                            for k in range(8):
                                c.op("pool", lambda e, g=g, k=k, si=si: e.tensor_scalar(
                                    out=SCt[si][k][:], in0=D1b[:], scalar1=PA[:, k, g:g + 1], scalar2=0.0, op0=ALU.mult, op1=ALU.add),
                                     reads=[b_D1, b_PA], writes=[b_SCt[si][k]])
                                c.op("dve", lambda e, g=g, k=k, si=si: e.scalar_tensor_tensor(
                                    out=SC[si][k][:], in0=D2b[:], scalar=PB[:, k, g:g + 1], in1=SCt[si][k][:], op0=ALU.mult, op1=ALU.add),
                                     reads=[b_D2, b_PB, b_SCt[si][k]], writes=[b_SC[si][k]])
                                c.op("pool", lambda e, g=g, k=k, si=si: e.tensor_scalar(
                                    out=SCt[si][8 + k][:], in0=D1b[:], scalar1=PAl[:, k, g:g + 1], scalar2=0.0, op0=ALU.mult, op1=ALU.add),
                                     reads=[b_D1, b_PAl], writes=[b_SCt[si][8 + k]])
                                c.op("dve", lambda e, g=g, k=k, si=si: e.scalar_tensor_tensor(
                                    out=SCl[si][k][:], in0=D2b[:], scalar=PBl[:, k, g:g + 1], in1=SCt[si][8 + k][:], op0=ALU.mult, op1=ALU.add),
                                     reads=[b_D2, b_PBl, b_SCt[si][8 + k]], writes=[b_SCl[si][k]])
                        for gg in range(4):
                            gl = quad * 4 + gg
                            g = gb * 8 + gl
                            c.op("pe", lambda e, g=g, gl=gl, gg=gg, s=s: e.matmul(PS[gg][:], lhsT=MinT[:, g, :], rhs=U[s][:, gl, :], start=True, stop=True),
                                 reads=[b_MinT, b_U[s]], writes=[PSB[gg]])
                        for k in range(8):
                            sh = 1 << k
                            for gg in range(4):
                                eng = "act" if gg % 2 == 0 else "dve"
                                if eng == "act":
                                    c.op("act", lambda e, gg=gg, k=k: e.copy(out=Hsb[gg][k % 2][:], in_=PS[gg][:]),
                                         reads=[PSB[gg]], writes=[b_Hsb[gg][k % 2]])
                                else:
                                    c.op("dve", lambda e, gg=gg, k=k: e.tensor_copy(out=Hsb[gg][k % 2][:], in_=PS[gg][:]),
                                         reads=[PSB[gg]], writes=[b_Hsb[gg][k % 2]])
                            for gg in range(4):
                                gl = quad * 4 + gg
                                for (SCx, b_SCx, last) in ((SC, b_SC, False), (SCl, b_SCl, True)):
                                    for sq in range(2):
                                        c.op("pe", lambda e, gg=gg, gl=gl, k=k, sh=sh, sq=sq, SCx=SCx: e.matmul(
                                            PS[gg][:, sq * 256 + sh:(sq + 1) * 256],
                                            lhsT=SCx[gl][k][:],
                                            rhs=Hsb[gg][k % 2][:, sq * 256:(sq + 1) * 256 - sh],
                                            start=False, stop=True, skip_group_check=True),
                                             reads=[b_SCx[gl][k], b_Hsb[gg][k % 2]], writes=[PSB[gg]], inc=(last and sq == 1))
                        for gg in range(4):
                            eng = "act" if gg % 2 == 0 else "dve"
                            src = PS[gg][:].rearrange("p (s n) -> p s n", s=2)[:, :, 0:255]
                            if eng == "act":
                                c.op("act", lambda e, gg=gg, src=src: e.copy(out=Hp[gg][:, :, 1:256], in_=src), reads=[PSB[gg]], writes=[b_Hp[gg]])
                            else:
                                c.op("dve", lambda e, gg=gg, src=src: e.tensor_copy(out=Hp[gg][:, :, 1:256], in_=src), reads=[PSB[gg]], writes=[b_Hp[gg]])
                        for gg in range(4):
                            gl = quad * 4 + gg
                            g = gb * 8 + gl
                            pb = 4 + (gg % 2)
                            c.op("pe", lambda e, g=g, gl=gl, pb=pb, s=s: e.matmul(PS[pb][:], lhsT=Tm[:, g, :], rhs=U[s][:, gl, :], start=True, stop=False),
                                 reads=[b_Tm, b_U[s]], writes=[PSB[pb]], inc=False)
                            c.op("pe", lambda e, g=g, gg=gg, pb=pb: e.matmul(PS[pb][:], lhsT=Mout[:, g, :], rhs=Hp[gg][:].rearrange("p s n -> p (s n)"), start=False, stop=True),
                                 reads=[b_Mout, b_Hp[gg]], writes=[PSB[pb]])
                            if gg % 2 == 0:
                                c.op("dve", lambda e, gg=gg, pb=pb: e.tensor_copy(out=Ysb[gg][:], in_=PS[pb][:]), reads=[PSB[pb]], writes=[b_Ysb[gg]])
                            else:
                                c.op("act", lambda e, gg=gg, pb=pb: e.copy(out=Ysb[gg][:], in_=PS[pb][:]), reads=[PSB[pb]], writes=[b_Ysb[gg]])
                            pt = 6 + (gg % 2)
                            for ct in range(4):
                                c.op("pe", lambda e, gg=gg, ct=ct, pt=pt: e.transpose(out=PS[pt][:, ct * 128:(ct + 1) * 128],
                                                                                      in_=Ysb[gg][:, ct * 128:(ct + 1) * 128], identity=identf[:]),
                                     reads=[b_Ysb[gg], b_identf], writes=[PSB[pt]], inc=(ct == 3))
                            ydst = YB[s][:, :, :, gl * 16:(gl + 1) * 16]
                            ysrc = PS[pt][:].rearrange("p (ct i o) -> p ct i o", ct=4, i=8)
                            if gg % 2 == 0:
                                c.op("act", lambda e, ydst=ydst, ysrc=ysrc: e.copy(out=ydst, in_=ysrc), reads=[PSB[pt]], writes=[b_YB[s]])
                            else:
                                c.op("dve", lambda e, ydst=ydst, ysrc=ysrc: e.tensor_copy(out=ydst, in_=ysrc), reads=[PSB[pt]], writes=[b_YB[s]])
                    xbv = XB[s][:].rearrange("p ct j d -> p (ct j) d")
                    ybv = YB[s][:].rearrange("p ct j d -> p (ct j) d")
                    dbv = dB[:, gb * 128:(gb + 1) * 128].unsqueeze(1).broadcast_to([128, 32, 128])
                    c.op("pool", lambda e, xbv=xbv, dbv=dbv: e.tensor_tensor(out=xbv, in0=xbv, in1=dbv, op=ALU.mult),
                         reads=[b_XB[s], b_dB], writes=[b_XB[s]])
                    c.op("pool", lambda e, xbv=xbv, ybv=ybv: e.tensor_tensor(out=ybv, in0=ybv, in1=xbv, op=ALU.add),
                         reads=[b_XB[s], b_YB[s]], writes=[b_YB[s]])
                    c.op("act", lambda e, s=s: e.activation(out=GB[s][:], in_=YB[s][:], func=AF.Gelu_apprx_tanh),
                         reads=[b_YB[s]], writes=[b_GB[s]])
                    for ct in range(4):
                        g_ev.append(c.dma("sp", Gv[:, ct, :, gb * 128:(gb + 1) * 128], GB[s][:, ct, :, :], reads=[b_GB[s]], sem=gsem[s]))
                c.barrier()
        if stop_after == "A":
            nc.sync.wait_ge(gsem[0][0], gsem[0][1])
            return nc
        class Ring:
            def __init__(self, stack, name, n, shape, dt):
                self.t = [sbt(stack, f"{name}{i}", shape, dt) for i in range(n)]
                self.b = [Buf(f"{name}{i}") for i in range(n)]
                self.i = 0
                self.n = n

            def next(self):
                k = self.i % self.n
                self.i += 1
                return self.t[k], self.b[k]

        bank_ctr = [0]

        def nbank():
            k = bank_ctr[0] % 8
            bank_ctr[0] += 1
            return k

        def load_weight(stack, name, src, K, N, queue="pool"):
            kc = K // 128
            t = sbt(stack, name, [128, kc, N], BF16)
            b = Buf(name)
            sem = c.dma_sem(name)
            srcv = src.rearrange("(kc p) n -> p kc n", p=128)
            for k in range(kc):
                for n0 in range(0, N, 2048):
                    n1 = min(N, n0 + 2048)
                    c.dma(queue, t[:, k, n0:n1], srcv[:, k, n0:n1], writes=[b], sem=sem)
            return t, b

        def load_ln(stack, idx):
            g = sbt(stack, f"lng{idx}", [128, D], F32); bg = Buf()
            bt = sbt(stack, f"lnb{idx}", [128, D], F32); bb = Buf()
            sem = c.dma_sem("ln")
            c.dma("sp", g[:], lnG_d[:, idx, :], writes=[bg], sem=sem)
            c.dma("sp", bt[:], lnB_d[:, idx, :], writes=[bb], sem=sem)
            return g, bg, bt, bb

        def layernorm(stack_rings, r, b_r, lng, b_lng, lnb, b_lnb):
            stats, b_st = stack_rings["stats"].next()
            mv, b_mv = stack_rings["mv"].next()
            sd, b_sd = stack_rings["sd"].next()
            for hh in range(2):
                c.op("dve", lambda e, hh=hh: e.bn_stats(out=stats[:, hh, :], in_=r[:, hh * 512:(hh + 1) * 512]), reads=[b_r], writes=[b_st])
            c.op("dve", lambda e: e.bn_aggr(out=mv[:], in_=stats[:].rearrange("p a b -> p (a b)")), reads=[b_st], writes=[b_mv])
            c.op("dve", lambda e: e.tensor_scalar(out=sd[:, 0:1], in0=mv[:, 1:2], scalar1=float(LN_EPS), scalar2=None, op0=ALU.add), reads=[b_mv], writes=[b_sd])
            c.op("pool", lambda e: e.tensor_tensor(out=sd[:, 2:3], in0=sd[:, 0:1], in1=mhalf[:, 0:1], op=ALU.pow), reads=[b_sd, b_mhalf], writes=[b_sd])
            c.op("dve", lambda e: e.tensor_scalar(out=sd[:, 3:4], in0=mv[:, 0:1], scalar1=sd[:, 2:3], scalar2=-1.0, op0=ALU.mult, op1=ALU.mult), reads=[b_mv, b_sd], writes=[b_sd])
            c.op("dve", lambda e: e.tensor_scalar(out=r[:], in0=r[:], scalar1=sd[:, 2:3], scalar2=sd[:, 3:4], op0=ALU.mult, op1=ALU.add), reads=[b_r, b_sd], writes=[b_r])
            c.op("pool", lambda e: e.tensor_tensor(out=r[:], in0=r[:], in1=lng[:], op=ALU.mult), reads=[b_r, b_lng], writes=[b_r])
            c.op("dve", lambda e: e.tensor_tensor(out=r[:], in0=r[:], in1=lnb[:], op=ALU.add), reads=[b_r, b_lnb], writes=[b_r])

        def ln_rings(stack):
            return {"stats": Ring(stack, "lnst", 3, [128, 2, 6], F32), "mv": Ring(stack, "lnmv", 3, [128, 2], F32),
                    "sd": Ring(stack, "lnsd", 3, [128, 4], F32)}

        def transpose_tile(src, b_src, dst_fn, b_dst, evac_eng="act"):
            pb = nbank()
            p16 = PS[pb][:].bitcast(BF16)
            for kc in range(8):
                c.op("pe", lambda e, kc=kc: e.transpose(out=p16[:, kc * 128:(kc + 1) * 128], in_=src[:, kc * 128:(kc + 1) * 128], identity=identb[:]),
                     reads=[b_src, b_identb], writes=[PSB[pb]], inc=(kc == 7))
            dst = dst_fn()
            if evac_eng == "act":
                c.op("act", lambda e: e.copy(out=dst, in_=p16[:].rearrange("p (k t) -> p k t", k=8)), reads=[PSB[pb]], writes=[b_dst])
            else:
                c.op("dve", lambda e: e.tensor_copy(out=dst, in_=p16[:].rearrange("p (k t) -> p k t", k=8)), reads=[PSB[pb]], writes=[b_dst])

        with contextlib.ExitStack() as sB:
            Wglu, b_Wglu = load_weight(sB, "Wglu", wglu_d, D, 2 * D)
            lng, b_lng, lnb, b_lnb = load_ln(sB, 0)
            rings = ln_rings(sB)
            gt_r = Ring(sB, "gt", 2, [128, D], BF16); gsemB = [c.dma_sem("gB") for _ in range(2)]
            xt_r = Ring(sB, "xt", 3, [128, D], F32); xsemB = [c.dma_sem("xB") for _ in range(3)]
            gT_r = Ring(sB, "gT", 2, [128, 8, 128], BF16)
            sg_r = Ring(sB, "sg", 3, [128, D], F32)
            r_r = Ring(sB, "rB", 3, [128, D], F32); osemB = [c.dma_sem("oB") for _ in range(3)]
            def B_stage1(t):
                gt, b_gt = gt_r.next(); xt, b_xt = xt_r.next(); gT, b_gT = gT_r.next(); sg, b_sg = sg_r.next()
                c.dma("sp", gt[:], G_d[t * 128:(t + 1) * 128, :], writes=[b_gt], sem=gsemB[t % 2])
                c.dma("act", xt[:], x_d[t * 128:(t + 1) * 128, :], writes=[b_xt], sem=xsemB[t % 3])
                transpose_tile(gt, b_gt, lambda: gT[:], b_gT)
                banks = [nbank() for _ in range(4)]
                for nb_ in range(4):
                    for kc in range(8):
                        c.op("pe", lambda e, nb_=nb_, kc=kc: e.matmul(PS[banks[nb_]][:], lhsT=gT[:, kc, :], rhs=Wglu[:, kc, nb_ * 512:(nb_ + 1) * 512],
                                                                  start=(kc == 0), stop=(kc == 7)),
                             reads=[b_gT, b_Wglu], writes=[PSB[banks[nb_]]], inc=(kc == 7))
                for hh in range(2):
                    c.op("act", lambda e, hh=hh: e.activation(out=sg[:, hh * 512:(hh + 1) * 512], in_=PS[banks[2 + hh]][:], func=AF.Sigmoid),
                         reads=[PSB[banks[2 + hh]]], writes=[b_sg])
                    c.op("dve", lambda e, hh=hh: e.tensor_tensor(out=sg[:, hh * 512:(hh + 1) * 512], in0=PS[banks[hh]][:], in1=sg[:, hh * 512:(hh + 1) * 512], op=ALU.mult),
                         reads=[PSB[banks[hh]], b_sg], writes=[b_sg])
                return (t, xt, b_xt, sg, b_sg)

            def B_stage2(ctx):
                t, xt, b_xt, sg, b_sg = ctx
                r, b_r = r_r.next()
                c.op("dve", lambda e: e.scalar_tensor_tensor(out=r[:], in0=xt[:], scalar=ALPHA, in1=sg[:], op0=ALU.mult, op1=ALU.add),
                     reads=[b_xt, b_sg], writes=[b_r])
                layernorm(rings, r, b_r, lng, b_lng, lnb, b_lnb)
                c.dma("pool", H1_d[t * 128:(t + 1) * 128, :], r[:], reads=[b_r], sem=osemB[t % 3])

            pend = None
            for t in range(NT):
                ctx = B_stage1(t)
                if pend is not None:
                    B_stage2(pend)
                pend = ctx
            B_stage2(pend)
            c.barrier()
        if stop_after == "B":
            return nc

        with contextlib.ExitStack() as sC:
            Wg, b_Wg = load_weight(sC, "Wg", fg_d, D, FF)
            Wu, b_Wu = load_weight(sC, "Wu", fu_d, D, FF)
            Wd, b_Wd = load_weight(sC, "Wd", fd_d, FF, D)
            lng, b_lng, lnb, b_lnb = load_ln(sC, 1)
            rings = ln_rings(sC)
            ht_r = Ring(sC, "htC", 2, [128, D], BF16); hsemC = [c.dma_sem("hC") for _ in range(2)]
            hres_r = Ring(sC, "hresC", 2, [128, D], F32); rsemC = [c.dma_sem("rC") for _ in range(2)]
            hT = sbt(sC, "hTC", [128, 8, 512], BF16); b_hT = Buf()
            h1T = sbt(sC, "h1TC", [128, FF // 128, 512], BF16); b_h1T = Buf()
            s_r = Ring(sC, "sC", 2, [128, 512], BF16)
            r_r = Ring(sC, "rC", 2, [128, D], F32); osemC = [c.dma_sem("oC") for _ in range(2)]
            for st_ in range(TOK // 512):
                for sub in range(4):
                    t = st_ * 4 + sub
                    ht, b_ht = ht_r.next()
                    c.dma("pool", ht[:], H1_d[t * 128:(t + 1) * 128, :], writes=[b_ht], sem=hsemC[t % 2])
                    transpose_tile(ht, b_ht, lambda sub=sub: hT[:, :, sub * 128:(sub + 1) * 128], b_hT, evac_eng="act" if sub % 2 == 0 else "dve")
                for fc in range(FF // 128):
                    pg, pu = nbank(), nbank()
                    for (W, bW, pb) in ((Wg, b_Wg, pg), (Wu, b_Wu, pu)):
                        for kc in range(8):
                            c.op("pe", lambda e, W=W, pb=pb, kc=kc, fc=fc: e.matmul(PS[pb][:], lhsT=W[:, kc, fc * 128:(fc + 1) * 128], rhs=hT[:, kc, :],
                                                                              start=(kc == 0), stop=(kc == 7)),
                                 reads=[bW, b_hT], writes=[PSB[pb]], inc=(kc == 7))
                    sb_, b_s = s_r.next()
                    c.op("act", lambda e, pg=pg, sb_=sb_: e.activation(out=sb_[:], in_=PS[pg][:], func=AF.Silu), reads=[PSB[pg]], writes=[b_s])
                    c.op("dve", lambda e, pu=pu, sb_=sb_, fc=fc: e.tensor_tensor(out=h1T[:, fc, :], in0=PS[pu][:], in1=sb_[:], op=ALU.mult),
                         reads=[PSB[pu], b_s], writes=[b_h1T])
                for sub in range(4):
                    t = st_ * 4 + sub
                    hres, b_hres = hres_r.next(); r, b_r = r_r.next()
                    c.dma("act", hres[:], H1_d[t * 128:(t + 1) * 128, :], writes=[b_hres], sem=rsemC[t % 2])
                    po = [nbank(), nbank()]
                    for nb_ in range(2):
                        for fc in range(FF // 128):
                            c.op("pe", lambda e, nb_=nb_, fc=fc, sub=sub: e.matmul(PS[po[nb_]][:], lhsT=h1T[:, fc, sub * 128:(sub + 1) * 128],
                                                                             rhs=Wd[:, fc, nb_ * 512:(nb_ + 1) * 512], start=(fc == 0), stop=(fc == FF // 128 - 1)),
                                 reads=[b_h1T, b_Wd], writes=[PSB[po[nb_]]], inc=(fc == FF // 128 - 1))
                        c.op("dve", lambda e, nb_=nb_: e.scalar_tensor_tensor(out=r[:, nb_ * 512:(nb_ + 1) * 512], in0=hres[:, nb_ * 512:(nb_ + 1) * 512], scalar=ALPHA,
                                                                         in1=PS[po[nb_]][:], op0=ALU.mult, op1=ALU.add),
                             reads=[b_hres, PSB[po[nb_]]], writes=[b_r])
                    layernorm(rings, r, b_r, lng, b_lng, lnb, b_lnb)
                    c.dma("sp", H2_d[t * 128:(t + 1) * 128, :], r[:], reads=[b_r], sem=osemC[t % 2])
            c.barrier()
        if stop_after == "C":
            return nc

        with contextlib.ExitStack() as sE:
            Wq, b_Wq = load_weight(sE, "Wq", wq_d, D, D)
            Wkv, b_Wkv = load_weight(sE, "Wkv", kvw_d, D, 512)
            Wo, b_Wo = load_weight(sE, "Wo", wo_d, D, D)
            lng, b_lng, lnb, b_lnb = load_ln(sE, 2)
            rings = ln_rings(sE)
            semE = c.dma_sem("cE")
            posi = sbt(sE, "posi", [128, NT], I32); b_posi = Buf()
            invf = sbt(sE, "invf", [128, 8], F32); b_invf = Buf()
            sk = sbt(sE, "sk", [128, 16], F32); b_sk = Buf()
            c.dma("sp", posi[:], pos_d[:, :], writes=[b_posi], sem=semE)
            c.dma("sp", invf[:], invf_d[:, :], writes=[b_invf], sem=semE)
            c.dma("sp", sk[:], sink_d[:, :], writes=[b_sk], sem=semE)
            posf = sbt(sE, "posf", [128, NT], F32); b_posf = Buf()
            angE = sbt(sE, "angE", [128, NT, 8], F32); b_angE = Buf()
            kqE = sbt(sE, "kqE", [128, NT, 8], F32); b_kqE = Buf()
            kiE = sbt(sE, "kiE", [128, NT, 8], I32); b_kiE = Buf()
            twE = sbt(sE, "twE", [128, NT, 8], F32); b_twE = Buf()
            ycE = sbt(sE, "ycE", [128, NT, 8], F32); b_ycE = Buf()
            cosT = sbt(sE, "cosT", [128, NT, 8], F32); b_cosT = Buf()
            sinT = sbt(sE, "sinT", [128, NT, 8], F32); b_sinT = Buf()
            esk = sbt(sE, "esk", [128, 16], F32); b_esk = Buf()
            c.op("act", lambda e: e.activation(out=esk[:], in_=sk[:], func=AF.Exp), reads=[b_sk], writes=[b_esk])
            c.op("dve", lambda e: e.tensor_copy(out=posf[:], in_=posi[:]), reads=[b_posi], writes=[b_posf])
            c.op("dve", lambda e: e.tensor_tensor(out=angE[:], in0=posf[:].unsqueeze(2).broadcast_to([128, NT, 8]),
                                                  in1=invf[:].unsqueeze(1).broadcast_to([128, NT, 8]), op=ALU.mult), reads=[b_posf, b_invf], writes=[b_angE])
            c.op("dve", lambda e: e.tensor_scalar(out=kqE[:], in0=angE[:], scalar1=float(1.0 / (2 * math.pi)), scalar2=None, op0=ALU.mult), reads=[b_angE], writes=[b_kqE])
            c.op("dve", lambda e: e.tensor_copy(out=kiE[:], in_=kqE[:]), reads=[b_kqE], writes=[b_kiE])
            c.op("dve", lambda e: e.tensor_copy(out=kqE[:], in_=kiE[:]), reads=[b_kiE], writes=[b_kqE])
            c.op("dve", lambda e: e.scalar_tensor_tensor(out=angE[:], in0=kqE[:], scalar=-6.28125, in1=angE[:], op0=ALU.mult, op1=ALU.add), reads=[b_kqE, b_angE], writes=[b_angE])
            c.op("dve", lambda e: e.scalar_tensor_tensor(out=angE[:], in0=kqE[:], scalar=-float(2.0 * math.pi - 6.28125), in1=angE[:], op0=ALU.mult, op1=ALU.add), reads=[b_kqE, b_angE], writes=[b_angE])

            def wrapE(t, b_t):
                PI_ = float(math.pi)
                c.op("dve", lambda e: e.tensor_scalar(out=twE[:], in0=t[:], scalar1=-PI_, scalar2=2 * PI_, op0=ALU.is_lt, op1=ALU.mult), reads=[b_t], writes=[b_twE])
                c.op("dve", lambda e: e.tensor_tensor(out=t[:], in0=t[:], in1=twE[:], op=ALU.add), reads=[b_t, b_twE], writes=[b_t])
                c.op("dve", lambda e: e.tensor_scalar(out=twE[:], in0=t[:], scalar1=PI_, scalar2=-2 * PI_, op0=ALU.is_gt, op1=ALU.mult), reads=[b_t], writes=[b_twE])
                c.op("dve", lambda e: e.tensor_tensor(out=t[:], in0=t[:], in1=twE[:], op=ALU.add), reads=[b_t, b_twE], writes=[b_t])
            wrapE(angE, b_angE)
            wrapE(angE, b_angE)
            c.op("act", lambda e: e.activation(out=sinT[:], in_=angE[:], func=AF.Sin), reads=[b_angE], writes=[b_sinT])
            c.op("dve", lambda e: e.tensor_scalar(out=ycE[:], in0=angE[:], scalar1=float(math.pi / 2), scalar2=None, op0=ALU.add), reads=[b_angE], writes=[b_ycE])
            wrapE(ycE, b_ycE)
            c.op("act", lambda e: e.activation(out=cosT[:], in_=ycE[:], func=AF.Sin), reads=[b_ycE], writes=[b_cosT])
            mprev = sbt(sE, "mprev", [128, 128], BF16); b_mprev = Buf()
            mcur = sbt(sE, "mcur", [128, 128], BF16); b_mcur = Buf()
            c.op("pool", lambda e: e.memset(mprev[:], 1.0), writes=[b_mprev])
            c.op("pool", lambda e: e.affine_select(out=mprev[:], in_=mprev[:], pattern=[[-1, 128]], compare_op=ALU.is_gt, fill=0.0, base=0, channel_multiplier=1),
                 reads=[b_mprev], writes=[b_mprev])
            c.op("pool", lambda e: e.memset(mcur[:], 1.0), writes=[b_mcur])
            c.op("pool", lambda e: e.affine_select(out=mcur[:], in_=mcur[:], pattern=[[1, 128]], compare_op=ALU.is_ge, fill=0.0, base=0, channel_multiplier=-1),
                 reads=[b_mcur], writes=[b_mcur])
            KT = [sbt(sE, f"KT{i}", [128, 8, 128], BF16) for i in range(2)]; b_KT = [Buf() for _ in range(2)]
            Ksp = [sbt(sE, f"Ksp{i}", [128, 4, 2, 128], BF16) for i in range(2)]; b_Ksp = [Buf() for _ in range(2)]
            for i in range(2):
                c.op("pool", lambda e, i=i: e.memset(Ksp[i][:], 0.0), writes=[b_Ksp[i]])
            Va = [sbt(sE, f"Va{i}", [128, 4, 65], BF16) for i in range(2)]; b_Va = [Buf() for _ in range(2)]
            for i in range(2):
                c.op("pool", lambda e, i=i: e.memset(Va[i][:], 1.0), writes=[b_Va[i]])
            ht_r = Ring(sE, "htE", 2, [128, D], BF16); hsemE = [c.dma_sem("hE") for _ in range(2)]
            hres_r = Ring(sE, "hresE", 2, [128, D], F32); rsemE = [c.dma_sem("rE") for _ in range(2)]
            hT_r = Ring(sE, "hTE", 2, [128, 8, 128], BF16)
            Qs_r = Ring(sE, "Qs", 2, [128, 16, 64], BF16)
            Ks_r = Ring(sE, "Ks", 2, [128, 4, 64], BF16)
            QT_r = Ring(sE, "QT", 2, [128, 8, 128], BF16)
            ro_r = Ring(sE, "ro", 4, [128, 8, 8], F32)
            E_r = Ring(sE, "E", 4, [128, 512], BF16)
            O_r = Ring(sE, "O", 2, [128, 16, 64], BF16)
            OT_r = Ring(sE, "OT", 2, [128, 8, 128], BF16)
            dn_r = Ring(sE, "dn", 4, [128, 8], F32)
            r_r = Ring(sE, "rE", 2, [128, D], F32); osemE = [c.dma_sem("oE") for _ in range(2)]

            def rope(psrc, nh, dst, b_dst, pbuf, t):
                cb = cosT[:, t, :].unsqueeze(1).broadcast_to([128, nh, 8])
                sb2 = sinT[:, t, :].unsqueeze(1).broadcast_to([128, nh, 8])
                q1 = psrc[:, :, 0:8]; q2 = psrc[:, :, 8:16]
                ta, b_ta = ro_r.next(); tb, b_tb = ro_r.next()
                c.op("dve", lambda e: e.tensor_tensor(out=ta[:, 0:nh, :], in0=q1, in1=cb, op=ALU.mult), reads=[pbuf, b_cosT], writes=[b_ta])
                c.op("dve", lambda e: e.tensor_tensor(out=tb[:, 0:nh, :], in0=q2, in1=sb2, op=ALU.mult), reads=[pbuf, b_sinT], writes=[b_tb])
                c.op("dve", lambda e: e.tensor_tensor(out=dst[:, :, 0:8], in0=ta[:, 0:nh, :], in1=tb[:, 0:nh, :], op=ALU.subtract), reads=[b_ta, b_tb], writes=[b_dst])
                tc_, b_tc = ro_r.next(); td, b_td = ro_r.next()
                c.op("dve", lambda e: e.tensor_tensor(out=tc_[:, 0:nh, :], in0=q2, in1=cb, op=ALU.mult), reads=[pbuf, b_cosT], writes=[b_tc])
                c.op("dve", lambda e: e.tensor_tensor(out=td[:, 0:nh, :], in0=q1, in1=sb2, op=ALU.mult), reads=[pbuf, b_sinT], writes=[b_td])
                c.op("dve", lambda e: e.tensor_tensor(out=dst[:, :, 8:16], in0=tc_[:, 0:nh, :], in1=td[:, 0:nh, :], op=ALU.add), reads=[b_tc, b_td], writes=[b_dst])

            for t in range(NT):
                nblk = t % 16
                ht, b_ht = ht_r.next(); hres, b_hres = hres_r.next(); hT, b_hT = hT_r.next()
                c.dma("pool", ht[:], H2_d[t * 128:(t + 1) * 128, :], writes=[b_ht], sem=hsemE[t % 2])
                c.dma("act", hres[:], H2_d[t * 128:(t + 1) * 128, :], writes=[b_hres], sem=rsemE[t % 2])
                transpose_tile(ht, b_ht, lambda: hT[:], b_hT)
                pq = [nbank(), nbank()]; pkv = nbank()
                for nb_ in range(2):
                    for kc in range(8):
                        c.op("pe", lambda e, nb_=nb_, kc=kc: e.matmul(PS[pq[nb_]][:], lhsT=hT[:, kc, :], rhs=Wq[:, kc, nb_ * 512:(nb_ + 1) * 512], start=(kc == 0), stop=(kc == 7)),
                             reads=[b_hT, b_Wq], writes=[PSB[pq[nb_]]], inc=(kc == 7))
                for kc in range(8):
                    c.op("pe", lambda e, kc=kc: e.matmul(PS[pkv][:], lhsT=hT[:, kc, :], rhs=Wkv[:, kc, :], start=(kc == 0), stop=(kc == 7)),
                         reads=[b_hT, b_Wkv], writes=[PSB[pkv]], inc=(kc == 7))
                Qs, b_Qs = Qs_r.next(); Ks, b_Ks = Ks_r.next(); QT, b_QT = QT_r.next()
                slot = t % 2
                for nb_ in range(2):
                    pv_ = PS[pq[nb_]][:].rearrange("p (h d) -> p h d", h=8)
                    c.op("act", lambda e, nb_=nb_, pv_=pv_: e.copy(out=Qs[:, nb_ * 8:(nb_ + 1) * 8, :], in_=pv_), reads=[PSB[pq[nb_]]], writes=[b_Qs])
                    rope(pv_, 8, Qs[:, nb_ * 8:(nb_ + 1) * 8, :], b_Qs, PSB[pq[nb_]], t)
                kvv = PS[pkv][:].rearrange("p (h d) -> p h d", h=8)
                c.op("act", lambda e: e.copy(out=Ks[:], in_=kvv[:, 0:4, :]), reads=[PSB[pkv]], writes=[b_Ks])
                rope(kvv[:, 0:4, :], 4, Ks[:], b_Ks, PSB[pkv], t)
                c.op("act", lambda e: e.copy(out=Va[slot][:, :, 0:64], in_=kvv[:, 4:8, :]), reads=[PSB[pkv]], writes=[b_Va[slot]])
                transpose_tile(Qs[:].rearrange("p h d -> p (h d)"), b_Qs, lambda: QT[:], b_QT, evac_eng="act")
                c.op("pool", lambda e: e.tensor_copy(out=Ksp[slot][:, :, 0, 0:64], in_=Ks[:]), reads=[b_Ks], writes=[b_Ksp[slot]])
                c.op("pool", lambda e: e.tensor_copy(out=Ksp[slot][:, :, 1, 64:128], in_=Ks[:]), reads=[b_Ks], writes=[b_Ksp[slot]])
                transpose_tile(Ksp[slot][:].rearrange("p g v d -> p (g v d)"), b_Ksp[slot], lambda: KT[slot][:], b_KT[slot], evac_eng="dve")
                O, b_O = O_r.next()
                kbs = ([(1 - slot, mprev, b_mprev)] if nblk > 0 else []) + [(slot, mcur, b_mcur)]
                for g in range(4):
                    Es = []
                    for (sl, mk, b_mk) in kbs:
                        ps_ = nbank()
                        for var in range(2):
                            c.op("pe", lambda e, ps_=ps_, sl=sl, g=g, var=var: e.matmul(PS[ps_][:, var * 256:(var + 1) * 256], lhsT=KT[sl][:, g * 2 + var, :],
                                                                                  rhs=QT[:, 2 * g:2 * g + 2, :].rearrange("p h t -> p (h t)"), start=True, stop=True),
                                 reads=[b_KT[sl], b_QT], writes=[PSB[ps_]], inc=(var == 1))
                        E, b_E = E_r.next()
                        c.op("act", lambda e, ps_=ps_, E=E: e.activation(out=E[:], in_=PS[ps_][:], func=AF.Exp, scale=0.125), reads=[PSB[ps_]], writes=[b_E])
                        c.op("pool", lambda e, E=E, mk=mk: e.tensor_tensor(out=E[:].rearrange("p (h t) -> p h t", h=4), in0=E[:].rearrange("p (h t) -> p h t", h=4),
                                                                       in1=mk[:].unsqueeze(1).broadcast_to([128, 4, 128]), op=ALU.mult), reads=[b_E, b_mk], writes=[b_E])
                        Es.append((E, b_E, sl))
                    po_ = nbank()
                    for hl in range(4):
                        for i, (E, b_E, sl) in enumerate(Es):
                            c.op("pe", lambda e, po_=po_, hl=hl, E=E, sl=sl, g=g, i=i: e.matmul(PS[po_][:, hl * 65:(hl + 1) * 65], lhsT=E[:, hl * 128:(hl + 1) * 128], rhs=Va[sl][:, g, :],
                                                                                     start=(i == 0), stop=(i == len(Es) - 1)),
                                 reads=[b_E, b_Va[sl]], writes=[PSB[po_]], inc=(hl == 3 and i == len(Es) - 1))
                    dn, b_dn = dn_r.next()
                    ov = PS[po_][:, 0:260].rearrange("p (v pr d) -> p v pr d", v=2, pr=2)
                    c.op("dve", lambda e, ov=ov, dn=dn, g=g: e.tensor_tensor(out=dn[:, 0:4].rearrange("p (v pr) -> p v pr", v=2), in0=ov[:, :, :, 64],
                                                                         in1=esk[:, 4 * g:4 * g + 4].rearrange("p (pr v) -> p v pr", v=2), op=ALU.add), reads=[PSB[po_], b_esk], writes=[b_dn])
                    c.op("dve", lambda e, dn=dn: e.reciprocal(out=dn[:, 4:8], in_=dn[:, 0:4]), reads=[b_dn], writes=[b_dn])
                    for var in range(2):
                        c.op("dve", lambda e, ov=ov, dn=dn, g=g, var=var: e.tensor_tensor(
                            out=O[:, 4 * g:4 * g + 4, :].rearrange("p (pr v) d -> p v pr d", v=2)[:, var, :, :], in0=ov[:, var, :, 0:64],
                            in1=dn[:, 4 + 2 * var:6 + 2 * var].unsqueeze(2).broadcast_to([128, 2, 64]), op=ALU.mult),
                             reads=[PSB[po_], b_dn], writes=[b_O])
                OT, b_OT = OT_r.next()
                transpose_tile(O[:].rearrange("p h d -> p (h d)"), b_O, lambda: OT[:], b_OT, evac_eng="act")
                r, b_r = r_r.next()
                po = [nbank(), nbank()]
                for nb_ in range(2):
                    for kc in range(8):
                        c.op("pe", lambda e, nb_=nb_, kc=kc: e.matmul(PS[po[nb_]][:], lhsT=OT[:, kc, :], rhs=Wo[:, kc, nb_ * 512:(nb_ + 1) * 512], start=(kc == 0), stop=(kc == 7)),
                             reads=[b_OT, b_Wo], writes=[PSB[po[nb_]]], inc=(kc == 7))
                    c.op("dve", lambda e, nb_=nb_: e.scalar_tensor_tensor(out=r[:, nb_ * 512:(nb_ + 1) * 512], in0=hres[:, nb_ * 512:(nb_ + 1) * 512], scalar=ALPHA,
                                                                     in1=PS[po[nb_]][:], op0=ALU.mult, op1=ALU.add), reads=[b_hres, PSB[po[nb_]]], writes=[b_r])
                layernorm(rings, r, b_r, lng, b_lng, lnb, b_lnb)
                c.dma("sp", H3_d[t * 128:(t + 1) * 128, :], r[:], reads=[b_r], sem=osemE[t % 2])
            c.barrier()
        if stop_after == "E":
            return nc

        with contextlib.ExitStack() as sF:
            lng, b_lng, lnb, b_lnb = load_ln(sF, 3)
            rings = ln_rings(sF)
            semF = c.dma_sem("cF")
            Wr = sbt(sF, "Wr", [128, 8, NE], F32); b_Wr = Buf()
            brt = sbt(sF, "brt", [128, NE], F32); b_brt = Buf()
            c.dma("sp", Wr[:], wr_d.rearrange("(kc p) n -> p kc n", p=128), writes=[b_Wr], sem=semF)
            c.dma("sp", brt[:], br_d[:, :], writes=[b_brt], sem=semF)
            STK = 1024
            NSUB = STK // 128
            WG = [sbt(sF, f"WGe{i}", [128, 8, ED], BF16) for i in range(2)]; b_WG = [Buf() for _ in range(2)]
            WU = [sbt(sF, f"WUe{i}", [128, 8, ED], BF16) for i in range(2)]; b_WU = [Buf() for _ in range(2)]
            WD = [sbt(sF, f"WDe{i}", [128, 8, D], BF16) for i in range(2)]; b_WD = [Buf() for _ in range(2)]
            wsem = [c.dma_sem("wF") for _ in range(2)]
            hTm = sbt(sF, "hTm", [128, 8, STK], BF16); b_hTm = Buf()
            h1m = sbt(sF, "h1m", [128, 8, STK], BF16); b_h1m = Buf()
            acc = sbt(sF, "acc", [128, NSUB, D], F32); b_acc = [Buf() for _ in range(NSUB)]
            comb = sbt(sF, "comb", [128, NSUB, NE], F32); b_comb = [Buf() for _ in range(NSUB)]
            h3_r = Ring(sF, "h3F", 2, [128, D], F32); h3sem = [c.dma_sem("h3F") for _ in range(2)]
            h3T_r = Ring(sF, "h3T", 2, [128, 8, 128], F32)
            lg_r = Ring(sF, "lg", 3, [128, 4, NE], F32)
            s_r = Ring(sF, "sF", 3, [128, 512], BF16)
            osemF = [c.dma_sem("oF") for _ in range(2)]
            n_w = 0

            def issue_weights(e_idx, slot):
                wsem[slot] = c.dma_sem("wF")
                for kc in range(8):
                    c.dma("pool", WG[slot][:, kc, :], mg_d[e_idx, kc * 128:(kc + 1) * 128, :], writes=[b_WG[slot]], sem=wsem[slot])
                    c.dma("pool", WU[slot][:, kc, :], mu_d[e_idx, kc * 128:(kc + 1) * 128, :], writes=[b_WU[slot]], sem=wsem[slot])
                for kc in range(8):
                    c.dma("pool", WD[slot][:, kc, :], md_d[e_idx, kc * 128:(kc + 1) * 128, :], writes=[b_WD[slot]], sem=wsem[slot])

            issue_weights(0, 0)
            for st_ in range(TOK // STK):
                h3sem = [c.dma_sem("h3F") for _ in range(2)]
                for sub in range(NSUB):
                    t = st_ * NSUB + sub
                    h3, b_h3 = h3_r.next(); h3T, b_h3T = h3T_r.next()
                    c.dma("sp", h3[:], H3_d[t * 128:(t + 1) * 128, :], writes=[b_h3], sem=h3sem[t % 2])
                    pt = [nbank(), nbank()]
                    for kc in range(8):
                        c.op("pe", lambda e, kc=kc: e.transpose(out=PS[pt[kc // 4]][:, (kc % 4) * 128:(kc % 4 + 1) * 128], in_=h3[:, kc * 128:(kc + 1) * 128], identity=identf[:]),
                             reads=[b_h3, b_identf], writes=[PSB[pt[kc // 4]]], inc=(kc % 4 == 3))
                    for hh in range(2):
                        src = PS[pt[hh]][:].rearrange("p (k t) -> p k t", k=4)
                        c.op("act", lambda e, hh=hh, src=src: e.copy(out=h3T[:, hh * 4:(hh + 1) * 4, :], in_=src), reads=[PSB[pt[hh]]], writes=[b_h3T])
                        c.op("dve", lambda e, hh=hh, src=src, sub=sub: e.tensor_copy(out=hTm[:, hh * 4:(hh + 1) * 4, sub * 128:(sub + 1) * 128], in_=src), reads=[PSB[pt[hh]]], writes=[b_hTm])
                    pl = nbank()
                    for kc in range(8):
                        c.op("pe", lambda e, kc=kc: e.matmul(PS[pl][:, 0:NE], lhsT=h3T[:, kc, :], rhs=Wr[:, kc, :], start=(kc == 0), stop=(kc == 7)),
                             reads=[b_h3T, b_Wr], writes=[PSB[pl]], inc=(kc == 7))
                    lg, b_lg = lg_r.next()
                    c.op("dve", lambda e: e.tensor_tensor(out=lg[:, 0, :], in0=PS[pl][:, 0:NE], in1=brt[:], op=ALU.add), reads=[PSB[pl], b_brt], writes=[b_lg])
                    c.op("dve", lambda e: e.max(out=lg[:, 1, :], in_=lg[:, 0, :]), reads=[b_lg], writes=[b_lg])
                    c.op("dve", lambda e: e.tensor_scalar(out=lg[:, 2, :], in0=lg[:, 0, :], scalar1=lg[:, 1, 1:2], scalar2=None, op0=ALU.is_ge), reads=[b_lg], writes=[b_lg])
                    c.op("dve", lambda e: e.tensor_scalar(out=lg[:, 1, 2:3], in0=lg[:, 1, 0:1], scalar1=-1.0, scalar2=None, op0=ALU.mult), reads=[b_lg], writes=[b_lg])
                    c.op("act", lambda e: e.activation(out=lg[:, 3, :], in_=lg[:, 0, :], func=AF.Exp, bias=lg[:, 1, 2:3], scale=1.0), reads=[b_lg], writes=[b_lg])
                    c.op("dve", lambda e: e.tensor_tensor(out=lg[:, 3, :], in0=lg[:, 3, :], in1=lg[:, 2, :], op=ALU.mult), reads=[b_lg], writes=[b_lg])
                    c.op("dve", lambda e: e.tensor_reduce(out=lg[:, 1, 3:4], in_=lg[:, 3, :], axis=AX.X, op=ALU.add), reads=[b_lg], writes=[b_lg])
                    c.op("dve", lambda e: e.reciprocal(out=lg[:, 1, 4:5], in_=lg[:, 1, 3:4]), reads=[b_lg], writes=[b_lg])
                    c.op("dve", lambda e, sub=sub: e.tensor_scalar(out=comb[:, sub, :], in0=lg[:, 3, :], scalar1=lg[:, 1, 4:5], scalar2=None, op0=ALU.mult), reads=[b_lg], writes=[b_comb[sub]])
                for ex in range(NE):
                    slot = n_w % 2
                    n_w += 1
                    nxt = (st_ * NE + ex + 1)
                    if nxt < (TOK // STK) * NE:
                        issue_weights(nxt % NE, 1 - slot)
                    for half in range(STK // 512):
                        for fc in range(8):
                            pg, pu = nbank(), nbank()
                            for (W, bW, pb) in ((WG[slot], b_WG[slot], pg), (WU[slot], b_WU[slot], pu)):
                                for kc in range(8):
                                    c.op("pe", lambda e, W=W, pb=pb, kc=kc, fc=fc, half=half: e.matmul(PS[pb][:], lhsT=W[:, kc, fc * 128:(fc + 1) * 128], rhs=hTm[:, kc, half * 512:(half + 1) * 512],
                                                                                                 start=(kc == 0), stop=(kc == 7)),
                                         reads=[bW, b_hTm], writes=[PSB[pb]], inc=(kc == 7))
                            sb_, b_s = s_r.next()
                            c.op("act", lambda e, pg=pg, sb_=sb_: e.activation(out=sb_[:], in_=PS[pg][:], func=AF.Silu), reads=[PSB[pg]], writes=[b_s])
                            c.op("dve", lambda e, pu=pu, sb_=sb_, fc=fc, half=half: e.tensor_tensor(out=h1m[:, fc, half * 512:(half + 1) * 512], in0=PS[pu][:], in1=sb_[:], op=ALU.mult),
                                 reads=[PSB[pu], b_s], writes=[b_h1m])
                    for sub in range(NSUB):
                        for nb_ in range(2):
                            po_ = nbank()
                            for fc in range(8):
                                c.op("pe", lambda e, po_=po_, fc=fc, sub=sub, nb_=nb_, slot=slot: e.matmul(PS[po_][:], lhsT=h1m[:, fc, sub * 128:(sub + 1) * 128],
                                                                                                 rhs=WD[slot][:, fc, nb_ * 512:(nb_ + 1) * 512], start=(fc == 0), stop=(fc == 7)),
                                     reads=[b_h1m, b_WD[slot]], writes=[PSB[po_]], inc=(fc == 7))
                            av = acc[:, sub, nb_ * 512:(nb_ + 1) * 512]
                            if ex == 0:
                                c.op("dve", lambda e, po_=po_, av=av, sub=sub, ex=ex: e.tensor_scalar(out=av, in0=PS[po_][:], scalar1=comb[:, sub, ex:ex + 1], scalar2=None, op0=ALU.mult),
                                     reads=[PSB[po_], b_comb[sub]], writes=[b_acc[sub]])
                            else:
                                c.op("dve", lambda e, po_=po_, av=av, sub=sub, ex=ex: e.scalar_tensor_tensor(out=av, in0=PS[po_][:], scalar=comb[:, sub, ex:ex + 1], in1=av, op0=ALU.mult, op1=ALU.add),
                                     reads=[PSB[po_], b_comb[sub], b_acc[sub]], writes=[b_acc[sub]])
                for sub in range(NSUB):
                    t = st_ * NSUB + sub
                    h3, b_h3 = h3_r.next()
                    c.dma("sp", h3[:], H3_d[t * 128:(t + 1) * 128, :], writes=[b_h3], sem=h3sem[t % 2])
                    c.op("dve", lambda e, sub=sub: e.scalar_tensor_tensor(out=acc[:, sub, :], in0=h3[:], scalar=ALPHA, in1=acc[:, sub, :], op0=ALU.mult, op1=ALU.add),
                         reads=[b_h3, b_acc[sub]], writes=[b_acc[sub]])
                    layernorm(rings, acc[:, sub, :], b_acc[sub], lng, b_lng, lnb, b_lnb)
                    c.dma("act", out_d[t * 128:(t + 1) * 128, :], acc[:, sub, :], reads=[b_acc[sub]], sem=osemF[t % 2])
            for k in ("sp",):
                for s_ in osemF:
                    nc.sync.wait_ge(s_[0], s_[1])
    return nc


def make_in_maps(inp):
    f = lambda a: np.ascontiguousarray(np.asarray(a), dtype=np.float32)
    x = f(inp["x"])
    pos = np.asarray(inp["positions"]).astype(np.int32)
    ln_g = f(inp["ln_g"]).reshape(4, D)
    ln_b = f(inp["ln_b"]).reshape(4, D)
    bcast = lambda a: np.ascontiguousarray(np.broadcast_to(a[None], (128,) + a.shape))
    lam_re = f(inp["ssm_lambda_re"])[0]; lam_im = f(inp["ssm_lambda_im"])[0]
    two = lambda a: np.ascontiguousarray(np.concatenate([a, a], axis=0))
    b_re = f(inp["ssm_b_re"])[0].transpose(1, 0, 2); b_im = f(inp["ssm_b_im"])[0].transpose(1, 0, 2)
    c_re = f(inp["ssm_c_re"])[0].transpose(2, 0, 1); c_im = f(inp["ssm_c_im"])[0].transpose(2, 0, 1)
    cat = lambda a, b: np.ascontiguousarray(np.concatenate([a, b], axis=0))
    inv_freq = (500000.0 ** (-np.arange(0, 16, 2, dtype=np.float32) / 16.0)).astype(np.float32)
    shared = {
        "ln_g": bcast(ln_g), "ln_b": bcast(ln_b),
        "lam_re": two(lam_re.T), "lam_im": two(lam_im.T), "lstep": bcast(f(inp["ssm_log_step"])[0]),
        "s5_ba": cat(b_re, b_im), "s5_bb": cat(b_im, b_re), "s5_ca": cat(c_re, c_im), "s5_cb": cat(c_im, c_re),
        "s5_d": bcast(f(inp["ssm_d"])[0]),
        "w_glu": f(inp["ssm_w_glu"])[0], "kv_w": f(inp["kv_w"]), "w_q": f(inp["attn_w_q"])[0],
        "sinks": bcast(f(inp["attn_sinks"])[0]), "w_out": f(inp["attn_w_out"])[0],
        "ffn_g": f(inp["ffn_w_gate"])[0], "ffn_u": f(inp["ffn_w_up"])[0], "ffn_d": f(inp["ffn_w_down"])[0],
        "w_router": f(inp["moe_w_router"])[0], "b_router": bcast(f(inp["moe_b_router"])[0]),
        "moe_g": f(inp["moe_w_gate"])[0], "moe_u": f(inp["moe_w_up"])[0], "moe_d": f(inp["moe_w_down"])[0],
        "inv_freq": bcast(inv_freq),
    }
    maps = []
    for i in range(NCORES):
        m = dict(shared)
        m["x"] = np.ascontiguousarray(x[2 * i:2 * i + 2].reshape(TOK, D))
        m["pos"] = np.ascontiguousarray(pos[2 * i:2 * i + 2].reshape(NT, 128).T)
        maps.append(m)
    return maps


def kernel(**inputs):
    nc = build()
    maps = make_in_maps(inputs)
    res = run_bass_kernel_spmd(nc, maps, core_ids=list(range(NCORES)))
    out = np.stack([np.asarray(r["out"]).reshape(NSEQ, L, D) for r in res.results], axis=0)
    return out.reshape(NCORES * NSEQ, L, D).astype(np.float32)
```

```python
import contextlib
import math
import numpy as np
import concourse.bass as bass
import concourse.mybir as mybir
from concourse.alu_op_type import AluOpType as ALU
from concourse.bass_utils import run_bass_kernel_spmd

F32 = mybir.dt.float32
BF16 = mybir.dt.bfloat16
I32 = mybir.dt.int32
AF = mybir.ActivationFunctionType
AX = mybir.AxisListType

NCORES = 8
D = 1024
L = 2048
NSEQ = 2
TOK = NSEQ * L
NT = TOK // 128
FF = 2816
NE = 8
ED = 1024
ALPHA = float(4.0 ** 0.25)
LN_EPS = 1e-5
MLIST = [0, -1, -2, -3, -4, -5, -6, -7] + list(range(16)) + [16, 32, 64, 128, 256, 512, 1024]
NM = len(MLIST)
SEM_ROLL = 3000


class Ev:
    __slots__ = ("sem", "val", "eng", "ref")

    def __init__(self, sem, val, eng, ref=None):
        self.sem = sem
        self.val = val
        self.eng = eng
        self.ref = ref


class Buf:
    __slots__ = ("name", "w", "r", "excl")

    def __init__(self, name="", excl=False):
        self.name = name
        self.w = None
        self.r = {}
        self.excl = excl


class Ctx:
    def __init__(self, nc, stack):
        self.nc = nc
        self.stack = stack
        self.engs = {"pe": nc.tensor, "act": nc.scalar, "dve": nc.vector, "pool": nc.gpsimd, "sp": nc.sync}
        self.sem = {}
        self.cnt = {}
        self.nsem = 0
        self.dsems = []
        for k in self.engs:
            self._new_eng_sem(k)
        self.waited = {k: {} for k in self.engs}
        self.pending = {k: [] for k in self.engs}
        self.ninst = {k: 0 for k in self.engs}
        self.last = {k: None for k in self.engs}

    def _new_sem(self, name):
        self.nsem += 1
        return self.stack.enter_context(self.nc.semaphore(f"{name}_{self.nsem}"))

    def _new_eng_sem(self, k):
        self.sem[k] = self._new_sem("e" + k)
        self.cnt[k] = 0

    def dma_sem(self, name="d"):
        s = [self._new_sem(name), 0]
        self.dsems.append(s)
        return s

    def _wait(self, k, ev):
        if ev is None:
            return
        if ev.eng == "pe" and k == "pe":
            return
        if ev.val is None:
            raise RuntimeError("dependency on an instruction without inc")
        w = self.waited[k]
        sid = id(ev.sem)
        val = ev.ref[1] if ev.ref is not None else ev.val
        if w.get(sid, 0) >= val:
            return
        self.engs[k].wait_ge(ev.sem, val)
        self.ninst[k] += 1
        w[sid] = val

    def _deps(self, k, reads, writes):
        for b in reads:
            self._wait(k, b.w)
            if b.excl:
                for kk, e in b.r.items():
                    if kk != k:
                        self._wait(k, e)
        for b in writes:
            self._wait(k, b.w)
            for e in b.r.values():
                self._wait(k, e)

    def _commit(self, ev, reads, writes):
        key = id(ev.sem) if ev.eng == "dma" else ev.eng
        for b in reads:
            b.r[key] = ev
        for b in writes:
            b.w = ev
            b.r = {}

    def op(self, k, fn, reads=(), writes=(), inc=True):
        self._deps(k, reads, writes)
        inst = fn(self.engs[k])
        self.ninst[k] += 1
        if inc:
            if self.cnt[k] >= SEM_ROLL:
                self._new_eng_sem(k)
            self.cnt[k] += 1
            inst.then_inc(self.sem[k], 1)
            ev = Ev(self.sem[k], self.cnt[k], k)
            for p in self.pending[k]:
                p.sem = ev.sem
                p.val = ev.val
            self.pending[k] = []
            self.last[k] = ev
        else:
            ev = Ev(None, None, k)
            self.pending[k].append(ev)
        self._commit(ev, reads, writes)
        return ev

    def dma(self, k, out, in_, reads=(), writes=(), sem=None, **kw):
        self._deps(k, reads, writes)
        inst = self.engs[k].dma_start(out=out, in_=in_, **kw)
        self.ninst[k] += 1
        sem[1] += 16
        inst.then_inc(sem[0], 16)
        ev = Ev(sem[0], sem[1], "dma", sem)
        self._commit(ev, reads, writes)
        return ev

    def barrier(self, engs=("pe", "act", "dve", "pool", "sp")):
        for k in engs:
            assert not self.pending[k]
        for k in engs:
            for k2 in engs:
                if k2 != k and self.last[k2] is not None:
                    self._wait(k, self.last[k2])
            for s in self.dsems:
                if s[1] > 0:
                    self._wait(k, Ev(s[0], s[1], "dma", s))


LAST_CTX = None


def build(stop_after=None):
    global LAST_CTX
    nc = bass.Bass("TRN2", target_bir_lowering=False)
    dram_in = lambda name, shape, dt=F32: nc.dram_tensor(name, list(shape), dt, kind="ExternalInput").ap()
    x_d = dram_in("x", [TOK, D])
    pos_d = dram_in("pos", [128, NT], I32)
    lnG_d = dram_in("ln_g", [128, 4, D])
    lnB_d = dram_in("ln_b", [128, 4, D])
    lamr_d = dram_in("lam_re", [128, 64])
    lami_d = dram_in("lam_im", [128, 64])
    lstep_d = dram_in("lstep", [128, 64])
    BA_d = dram_in("s5_ba", [128, 64, 16])
    BB_d = dram_in("s5_bb", [128, 64, 16])
    CA_d = dram_in("s5_ca", [128, 64, 16])
    CB_d = dram_in("s5_cb", [128, 64, 16])
    dsk_d = dram_in("s5_d", [128, D])
    wglu_d = dram_in("w_glu", [D, 2 * D])
    kvw_d = dram_in("kv_w", [D, 512])
    wq_d = dram_in("w_q", [D, D])
    sink_d = dram_in("sinks", [128, 16])
    wo_d = dram_in("w_out", [D, D])
    fg_d = dram_in("ffn_g", [D, FF])
    fu_d = dram_in("ffn_u", [D, FF])
    fd_d = dram_in("ffn_d", [FF, D])
    wr_d = dram_in("w_router", [D, NE])
    br_d = dram_in("b_router", [128, NE])
    mg_d = dram_in("moe_g", [NE, D, ED])
    mu_d = dram_in("moe_u", [NE, D, ED])
    md_d = dram_in("moe_d", [NE, ED, D])
    invf_d = dram_in("inv_freq", [128, 8])
    out_d = nc.dram_tensor("out", [TOK, D], F32, kind="ExternalOutput").ap()
    dbg = stop_after is not None
    G_d = nc.dram_tensor("G", [TOK, D], BF16, kind="ExternalOutput" if stop_after == "A" else "Internal").ap()
    H1_d = nc.dram_tensor("H1", [TOK, D], F32, kind="ExternalOutput" if stop_after == "B" else "Internal").ap()
    H2_d = nc.dram_tensor("H2", [TOK, D], F32, kind="ExternalOutput" if stop_after == "C" else "Internal").ap()
    H3_d = nc.dram_tensor("H3", [TOK, D], F32, kind="ExternalOutput" if stop_after == "E" else "Internal").ap()
    if stop_after == "P":
        dbgP = nc.dram_tensor("dbgP", [128, 2 * NM + 2, 64], F32, kind="ExternalOutput").ap()
        dbgT = nc.dram_tensor("dbgT", [128, 3, 64, 128], BF16, kind="ExternalOutput").ap()

    with contextlib.ExitStack() as st:
        c = Ctx(nc, st)
        LAST_CTX = c
        uniq = [0]

        def sbt(stack, name, shape, dt):
            uniq[0] += 1
            return stack.enter_context(nc.sbuf_tensor(f"{name}_{uniq[0]}", list(shape), dt))
        PS = [st.enter_context(nc.psum_tensor(f"ps{i}", [128, 512], F32)) for i in range(8)]
        PSB = [Buf(f"ps{i}", excl=True) for i in range(8)]

        identf = sbt(st, "identf", [128, 128], F32); b_identf = Buf()
        identb = sbt(st, "identb", [128, 128], BF16); b_identb = Buf()
        c.op("pool", lambda e: e.memset(identf[:], 0.0), writes=[b_identf])
        c.op("pool", lambda e: e.affine_select(out=identf[:], in_=identf[:], pattern=[[-1, 128]], compare_op=ALU.not_equal,
                                               fill=1.0, base=0, channel_multiplier=1), reads=[b_identf], writes=[b_identf])
        c.op("pool", lambda e: e.tensor_copy(out=identb[:], in_=identf[:]), reads=[b_identf], writes=[b_identb])
        out_sem = c.dma_sem("out")
        mhalf = sbt(st, "mhalf", [128, 1], F32); b_mhalf = Buf()
        c.op("pool", lambda e: e.memset(mhalf[:], -0.5), writes=[b_mhalf])

        with contextlib.ExitStack() as sa:
            Tm = sbt(sa, "Tm", [128, 64, 128], BF16); b_Tm = Buf()
            MinT = sbt(sa, "MinT", [128, 64, 128], BF16); b_MinT = Buf()
            Mout = sbt(sa, "Mout", [128, 64, 128], BF16); b_Mout = Buf()
            PA = sbt(sa, "PA", [128, 8, 64], F32); b_PA = Buf()
            PB = sbt(sa, "PB", [128, 8, 64], F32); b_PB = Buf()
            D1b = sbt(sa, "D1b", [128, 128], BF16); b_D1 = Buf()
            D2b = sbt(sa, "D2b", [128, 128], BF16); b_D2 = Buf()
            dB = sbt(sa, "dB", [128, D], F32); b_dB = Buf()
            ld0 = c.dma_sem("ld0")
            c.dma("sp", dB[:], dsk_d[:, :], writes=[b_dB], sem=ld0)
            with contextlib.ExitStack() as s0:
                lr = sbt(s0, "lr", [128, 64], F32); b_lr = Buf()
                li = sbt(s0, "li", [128, 64], F32); b_li = Buf()
                ls = sbt(s0, "ls", [128, 64], F32); b_ls = Buf()
                BAt = sbt(s0, "BAt", [128, 64, 16], F32); b_BA = Buf()
                BBt = sbt(s0, "BBt", [128, 64, 16], F32); b_BB = Buf()
                CAt = sbt(s0, "CAt", [128, 64, 16], F32); b_CA = Buf()
                CBt = sbt(s0, "CBt", [128, 64, 16], F32); b_CB = Buf()
                c.dma("sp", lr[:], lamr_d[:, :], writes=[b_lr], sem=ld0)
                c.dma("sp", li[:], lami_d[:, :], writes=[b_li], sem=ld0)
                c.dma("sp", ls[:], lstep_d[:, :], writes=[b_ls], sem=ld0)
                c.dma("act", BAt[:], BA_d[:, :, :], writes=[b_BA], sem=ld0)
                c.dma("act", BBt[:], BB_d[:, :, :], writes=[b_BB], sem=ld0)
                c.dma("act", CAt[:], CA_d[:, :, :], writes=[b_CA], sem=ld0)
                c.dma("act", CBt[:], CB_d[:, :, :], writes=[b_CB], sem=ld0)

                sgn = sbt(s0, "sgn", [128, 1], F32); b_sgn = Buf()
                sgn2 = sbt(s0, "sgn2", [128, 1], F32); b_sgn2 = Buf()
                c.op("pool", lambda e: e.memset(sgn[0:64, :], -1.0), writes=[b_sgn])
                c.op("pool", lambda e: e.memset(sgn[64:128, :], 1.0), writes=[b_sgn])
                c.op("pool", lambda e: e.memset(sgn2[0:64, :], 1.0), writes=[b_sgn2])
                c.op("pool", lambda e: e.memset(sgn2[64:128, :], -1.0), writes=[b_sgn2])
                D2f = sbt(s0, "D2f", [128, 128], F32); b_D2f = Buf()
                cmask = sbt(s0, "cmask", [128, 128], F32); b_cm = Buf()
                c.op("pool", lambda e: e.memset(D2f[:], 0.0), writes=[b_D2f])
                c.op("pool", lambda e: e.affine_select(out=D2f[:], in_=D2f[:], pattern=[[-1, 128]], compare_op=ALU.not_equal,
                                                       fill=1.0, base=64, channel_multiplier=1), reads=[b_D2f], writes=[b_D2f])
                c.op("pool", lambda e: e.affine_select(out=D2f[:], in_=D2f[:], pattern=[[-1, 128]], compare_op=ALU.not_equal,
                                                       fill=1.0, base=-64, channel_multiplier=1), reads=[b_D2f], writes=[b_D2f])
                c.op("pool", lambda e: e.tensor_copy(out=D2b[:], in_=D2f[:]), reads=[b_D2f], writes=[b_D2])
                c.op("pool", lambda e: e.tensor_copy(out=D1b[:], in_=identf[:]), reads=[b_identf], writes=[b_D1])
                c.op("pool", lambda e: e.memset(cmask[:], 1.0), writes=[b_cm])
                c.op("pool", lambda e: e.affine_select(out=cmask[:].rearrange("p (i o) -> p i o", i=8),
                                                       in_=cmask[:].rearrange("p (i o) -> p i o", i=8),
                                                       pattern=[[16, 8], [0, 16]], compare_op=ALU.is_ge,
                                                       fill=0.0, base=15, channel_multiplier=-1), reads=[b_cm], writes=[b_cm])

                s0a = contextlib.ExitStack()
                s0a.__enter__()
                def T3(name, stk=None):
                    return sbt(s0a if stk is None else stk, name, [128, NM, 64], F32), Buf(name)
                Pr, b_Pr = T3("Pr", s0)
                Pi, b_Pi = T3("Pi", s0)
                Mtab, b_Mt = T3("Mtab")
                for j, m in enumerate(MLIST):
                    c.op("pool", lambda e, j=j, m=m: e.memset(Mtab[:, j, :], float(m)), writes=[b_Mt])
                dt_ = sbt(s0a, "dt", [128, 64], F32); b_dt = Buf()
                lrdt = sbt(s0a, "lrdt", [128, 64], F32); b_lrdt = Buf()
                lidt = sbt(s0a, "lidt", [128, 64], F32); b_lidt = Buf()
                c.op("act", lambda e: e.activation(out=dt_[:], in_=ls[:], func=AF.Exp), reads=[b_ls], writes=[b_dt])
                c.op("dve", lambda e: e.tensor_tensor(out=lrdt[:], in0=lr[:], in1=dt_[:], op=ALU.mult), reads=[b_lr, b_dt], writes=[b_lrdt])
                c.op("dve", lambda e: e.tensor_tensor(out=lidt[:], in0=li[:], in1=dt_[:], op=ALU.mult), reads=[b_li, b_dt], writes=[b_lidt])
                bc3 = lambda t: t[:].unsqueeze(1).broadcast_to([128, NM, 64])
                Et, b_E = T3("Et")
                mag, b_mag = T3("mag")
                Ang, b_Ang = T3("Ang")
                c.op("dve", lambda e: e.tensor_tensor(out=Et[:], in0=Mtab[:], in1=bc3(lrdt), op=ALU.mult), reads=[b_Mt, b_lrdt], writes=[b_E])
                c.op("act", lambda e: e.activation(out=mag[:], in_=Et[:], func=AF.Exp), reads=[b_E], writes=[b_mag])
                c.op("dve", lambda e: e.tensor_tensor(out=Ang[:], in0=Mtab[:], in1=bc3(lidt), op=ALU.mult), reads=[b_Mt, b_lidt], writes=[b_Ang])
                kq, b_kq = T3("kq")
                ki_ = sbt(s0a, "ki", [128, NM, 64], I32); b_ki = Buf()
                kf, b_kf = T3("kf")
                yv, b_y = T3("yv")
                tw, b_tw = T3("tw")
                C1 = 6.28125
                C2 = float(2.0 * math.pi - 6.28125)
                PI = float(math.pi)
                TWO_PI = float(2.0 * math.pi)
                c.op("dve", lambda e: e.tensor_scalar(out=kq[:], in0=Ang[:], scalar1=float(1.0 / TWO_PI), scalar2=None, op0=ALU.mult), reads=[b_Ang], writes=[b_kq])
                c.op("dve", lambda e: e.tensor_copy(out=ki_[:], in_=kq[:]), reads=[b_kq], writes=[b_ki])
                c.op("dve", lambda e: e.tensor_copy(out=kf[:], in_=ki_[:]), reads=[b_ki], writes=[b_kf])
                c.op("dve", lambda e: e.scalar_tensor_tensor(out=yv[:], in0=kf[:], scalar=-C1, in1=Ang[:], op0=ALU.mult, op1=ALU.add), reads=[b_kf, b_Ang], writes=[b_y])
                c.op("dve", lambda e: e.scalar_tensor_tensor(out=yv[:], in0=kf[:], scalar=-C2, in1=yv[:], op0=ALU.mult, op1=ALU.add), reads=[b_kf, b_y], writes=[b_y])

                def wrap(t, b_t):
                    c.op("dve", lambda e: e.tensor_scalar(out=tw[:], in0=t[:], scalar1=-PI, scalar2=TWO_PI, op0=ALU.is_lt, op1=ALU.mult), reads=[b_t], writes=[b_tw])
                    c.op("dve", lambda e: e.tensor_tensor(out=t[:], in0=t[:], in1=tw[:], op=ALU.add), reads=[b_t, b_tw], writes=[b_t])
                    c.op("dve", lambda e: e.tensor_scalar(out=tw[:], in0=t[:], scalar1=PI, scalar2=-TWO_PI, op0=ALU.is_gt, op1=ALU.mult), reads=[b_t], writes=[b_tw])
                    c.op("dve", lambda e: e.tensor_tensor(out=t[:], in0=t[:], in1=tw[:], op=ALU.add), reads=[b_t, b_tw], writes=[b_t])
                wrap(yv, b_y)
                wrap(yv, b_y)
                sn, b_sn = T3("sn")
                cs, b_cs = T3("cs")
                yc, b_yc = T3("yc")
                c.op("act", lambda e: e.activation(out=sn[:], in_=yv[:], func=AF.Sin), reads=[b_y], writes=[b_sn])
                c.op("dve", lambda e: e.tensor_scalar(out=yc[:], in0=yv[:], scalar1=float(PI / 2), scalar2=None, op0=ALU.add), reads=[b_y], writes=[b_yc])
                wrap(yc, b_yc)
                c.op("act", lambda e: e.activation(out=cs[:], in_=yc[:], func=AF.Sin), reads=[b_yc], writes=[b_cs])
                c.op("dve", lambda e: e.tensor_tensor(out=Pr[:], in0=mag[:], in1=cs[:], op=ALU.mult), reads=[b_mag, b_cs], writes=[b_Pr])
                c.op("dve", lambda e: e.tensor_tensor(out=Pi[:], in0=mag[:], in1=sn[:], op=ALU.mult), reads=[b_mag, b_sn], writes=[b_Pi])
                c.barrier()
                s0a.__exit__(None, None, None)
                J1 = 9
                t64 = lambda name: (sbt(s0, name, [128, 64], F32), Buf(name))
                nr, b_nr = t64("nr"); den, b_den = t64("den"); t1, b_t1 = t64("t1"); t2, b_t2 = t64("t2")
                kr, b_kr = t64("kr"); kim, b_kim = t64("kim"); rden, b_rden = t64("rden")
                c.op("dve", lambda e: e.tensor_scalar(out=nr[:], in0=Pr[:, J1, :], scalar1=-1.0, scalar2=None, op0=ALU.add), reads=[b_Pr], writes=[b_nr])
                c.op("dve", lambda e: e.tensor_tensor(out=t1[:], in0=lr[:], in1=lr[:], op=ALU.mult), reads=[b_lr], writes=[b_t1])
                c.op("dve", lambda e: e.tensor_tensor(out=t2[:], in0=li[:], in1=li[:], op=ALU.mult), reads=[b_li], writes=[b_t2])
                c.op("dve", lambda e: e.tensor_tensor(out=den[:], in0=t1[:], in1=t2[:], op=ALU.add), reads=[b_t1, b_t2], writes=[b_den])
                c.op("dve", lambda e: e.reciprocal(out=rden[:], in_=den[:]), reads=[b_den], writes=[b_rden])
                c.op("dve", lambda e: e.tensor_tensor(out=t1[:], in0=nr[:], in1=lr[:], op=ALU.mult), reads=[b_nr, b_lr, b_den], writes=[b_t1])
                c.op("dve", lambda e: e.tensor_tensor(out=t2[:], in0=Pi[:, J1, :], in1=li[:], op=ALU.mult), reads=[b_Pi, b_li, b_den], writes=[b_t2])
                c.op("dve", lambda e: e.tensor_tensor(out=t1[:], in0=t1[:], in1=t2[:], op=ALU.add), reads=[b_t1, b_t2], writes=[b_t1])
                c.op("dve", lambda e: e.tensor_tensor(out=kr[:], in0=t1[:], in1=rden[:], op=ALU.mult), reads=[b_t1, b_rden], writes=[b_kr])
                c.op("dve", lambda e: e.tensor_tensor(out=t1[:], in0=Pi[:, J1, :], in1=lr[:], op=ALU.mult), reads=[b_Pi, b_lr, b_kr], writes=[b_t1])
                c.op("dve", lambda e: e.tensor_tensor(out=t2[:], in0=nr[:], in1=li[:], op=ALU.mult), reads=[b_nr, b_li, b_kr], writes=[b_t2])
                c.op("dve", lambda e: e.tensor_tensor(out=t1[:], in0=t1[:], in1=t2[:], op=ALU.subtract), reads=[b_t1, b_t2], writes=[b_t1])
                c.op("dve", lambda e: e.tensor_tensor(out=kim[:], in0=t1[:], in1=rden[:], op=ALU.mult), reads=[b_t1, b_rden], writes=[b_kim])
                Qr = sbt(s0, "Qr", [128, 8, 64], F32); b_Qr = Buf()
                Qi = sbt(s0, "Qi", [128, 8, 64], F32); b_Qi = Buf()
                q1 = sbt(s0, "q1", [128, 8, 64], F32); b_q1 = Buf()
                bc8 = lambda t: t[:].unsqueeze(1).broadcast_to([128, 8, 64])
                c.op("dve", lambda e: e.tensor_tensor(out=Qr[:], in0=Pr[:, 0:8, :], in1=bc8(kr), op=ALU.mult), reads=[b_Pr, b_kr], writes=[b_Qr])
                c.op("dve", lambda e: e.tensor_tensor(out=q1[:], in0=Pi[:, 0:8, :], in1=bc8(kim), op=ALU.mult), reads=[b_Pi, b_kim], writes=[b_q1])
                c.op("dve", lambda e: e.tensor_tensor(out=Qr[:], in0=Qr[:], in1=q1[:], op=ALU.subtract), reads=[b_Qr, b_q1], writes=[b_Qr])
                c.op("dve", lambda e: e.tensor_tensor(out=Qi[:], in0=Pi[:, 0:8, :], in1=bc8(kr), op=ALU.mult), reads=[b_Pi, b_kr], writes=[b_Qi])
                c.op("dve", lambda e: e.tensor_tensor(out=q1[:], in0=Pr[:, 0:8, :], in1=bc8(kim), op=ALU.mult), reads=[b_Pr, b_kim, b_Qr], writes=[b_q1])
                c.op("dve", lambda e: e.tensor_tensor(out=Qi[:], in0=Qi[:], in1=q1[:], op=ALU.add), reads=[b_Qi, b_q1], writes=[b_Qi])
                c.op("dve", lambda e: e.tensor_scalar(out=Qi[:], in0=Qi[:], scalar1=sgn[:, 0:1], scalar2=None, op0=ALU.mult), reads=[b_Qi, b_sgn], writes=[b_Qi])
                SIDX = [16, 24, 25, 26, 27, 28, 29, 30]
                for k, ix in enumerate(SIDX):
                    c.op("pool", lambda e, k=k, ix=ix: e.tensor_copy(out=PA[:, k, :], in_=Pr[:, ix, :]), reads=[b_Pr], writes=[b_PA])
                    c.op("dve", lambda e, k=k, ix=ix: e.tensor_scalar(out=PB[:, k, :], in0=Pi[:, ix, :], scalar1=sgn2[:, 0:1], scalar2=None, op0=ALU.mult), reads=[b_Pi, b_sgn2], writes=[b_PB])
                Prs = sbt(s0, "Prs", [128, 16, 64], F32); b_Prs = Buf()
                c.op("dve", lambda e: e.tensor_scalar(out=Prs[:], in0=Pr[:, 8:24, :], scalar1=sgn2[:, 0:1], scalar2=None, op0=ALU.mult), reads=[b_Pr, b_sgn2], writes=[b_Prs])
                X = sbt(s0, "X", [128, 32, 8, 16], F32); b_X = Buf()
                X2 = sbt(s0, "X2", [128, 32, 8, 16], F32); b_X2 = Buf()
                YY = sbt(s0, "YY", [128, 32, 16, 16], F32); b_YY = Buf()
                Y2 = sbt(s0, "Y2", [128, 32, 16, 16], F32); b_Y2 = Buf()
                for hg in range(2):
                    gs = slice(hg * 32, hg * 32 + 32)
                    qv = lambda t: t[:, :, gs].rearrange("p j g -> p g j").unsqueeze(3).broadcast_to([128, 32, 8, 16])
                    bv = lambda t: t[:, gs, :].unsqueeze(2).broadcast_to([128, 32, 8, 16])
                    pv = lambda ap: ap.rearrange("p m g -> p g m").unsqueeze(3).broadcast_to([128, 32, 16, 16])
                    cv = lambda t: t[:, gs, :].unsqueeze(2).broadcast_to([128, 32, 16, 16])
                    c.op("dve", lambda e: e.tensor_tensor(out=X[:], in0=qv(Qr), in1=bv(BAt), op=ALU.mult), reads=[b_Qr, b_BA], writes=[b_X])
                    c.op("pool", lambda e: e.tensor_tensor(out=X2[:], in0=qv(Qi), in1=bv(BBt), op=ALU.mult), reads=[b_Qi, b_BB], writes=[b_X2])
                    c.op("dve", lambda e: e.tensor_tensor(out=X[:], in0=X[:], in1=X2[:], op=ALU.add), reads=[b_X, b_X2], writes=[b_X])
                    c.op("dve", lambda e: e.tensor_tensor(out=YY[:], in0=pv(Prs[:, :, gs]), in1=cv(CAt), op=ALU.mult), reads=[b_Prs, b_CA], writes=[b_YY])
                    c.op("pool", lambda e: e.tensor_tensor(out=Y2[:], in0=pv(Pi[:, 8:24, gs]), in1=cv(CBt), op=ALU.mult), reads=[b_Pi, b_CB], writes=[b_Y2])
                    c.op("dve", lambda e: e.tensor_tensor(out=YY[:], in0=YY[:], in1=Y2[:], op=ALU.subtract), reads=[b_YY, b_Y2], writes=[b_YY])
                    c.op("act", lambda e: e.copy(out=Mout[:, gs, :].rearrange("p g (m o) -> p g m o", m=8), in_=YY[:, :, 8:16, :]), reads=[b_YY], writes=[b_Mout])
                    for q4 in range(8):
                        pa, pb = q4 % 2, 2 + (q4 % 2)
                        G0 = hg * 32 + q4 * 4
                        for gg in range(4):
                            g = q4 * 4 + gg
                            c.op("pe", lambda e, g=g, gg=gg, pa=pa: e.matmul(PS[pa][:, gg * 128:(gg + 1) * 128],
                                                                             lhsT=X[:, g, :, :].rearrange("p j c -> p (j c)"),
                                                                             rhs=YY[:, g, 0:8, :].rearrange("p m o -> p (m o)"),
                                                                             start=True, stop=True),
                                 reads=[b_X, b_YY], writes=[PSB[pa]], inc=(gg == 3))
                        c.op("dve", lambda e, G0=G0, pa=pa: e.tensor_tensor(out=Tm[:, G0:G0 + 4, :],
                                                                            in0=PS[pa][:].rearrange("p (g n) -> p g n", g=4),
                                                                            in1=cmask[:].unsqueeze(1).broadcast_to([128, 4, 128]), op=ALU.mult),
                             reads=[PSB[pa], b_cm], writes=[b_Tm])
                        for gg in range(4):
                            g = q4 * 4 + gg
                            c.op("pe", lambda e, g=g, gg=gg, pb=pb: e.transpose(out=PS[pb][:, gg * 128:(gg + 1) * 128],
                                                                                in_=X[:, g, :, :].rearrange("p j c -> p (j c)"),
                                                                                identity=identf[:]),
                                 reads=[b_X, b_identf], writes=[PSB[pb]], inc=(gg == 3))
                        c.op("act", lambda e, G0=G0, pb=pb: e.copy(out=MinT[:, G0:G0 + 4, :],
                                                                   in_=PS[pb][:].rearrange("p (g n) -> p g n", g=4)),
                             reads=[PSB[pb]], writes=[b_MinT])
                if stop_after == "P":
                    ds = c.dma_sem("dbg")
                    c.dma("sp", dbgP[:, 0:NM, :], Pr[:], reads=[b_Pr], sem=ds)
                    c.dma("sp", dbgP[:, NM:2 * NM, :], Pi[:], reads=[b_Pi], sem=ds)
                    c.dma("sp", dbgP[:, 2 * NM, :], kr[:], reads=[b_kr], sem=ds)
                    c.dma("sp", dbgP[:, 2 * NM + 1, :], kim[:], reads=[b_kim], sem=ds)
                    c.dma("sp", dbgT[:, 0, :, :], Tm[:], reads=[b_Tm], sem=ds)
                    c.dma("sp", dbgT[:, 1, :, :], MinT[:], reads=[b_MinT], sem=ds)
                    c.dma("sp", dbgT[:, 2, :, :], Mout[:], reads=[b_Mout], sem=ds)
                    nc.sync.wait_ge(ds[0], ds[1])
                    return nc
                c.barrier()
            with contextlib.ExitStack() as s1:
                xv = x_d.rearrange("(ct ch j) d -> ch ct j d", ct=4, ch=128, j=8)
                Gv = G_d.rearrange("(ct ch j) d -> ch ct j d", ct=4, ch=128, j=8)
                NXB = 2
                XB = [sbt(s1, f"XB{i}", [128, 4, 8, 128], F32) for i in range(NXB)]; b_XB = [Buf() for _ in range(NXB)]
                xsem = [c.dma_sem("xs") for _ in range(NXB)]
                XR = [sbt(s1, "XR0", [128, 4, 8, 128], BF16)] * NXB; b_XR = [Buf()] * NXB
                U = [sbt(s1, "U0", [128, 8, 512], BF16)] * NXB; b_U = [Buf()] * NXB
                YB = [sbt(s1, "YB0", [128, 4, 8, 128], F32)] * NXB; b_YB = [Buf()] * NXB
                GB = [sbt(s1, "GB0", [128, 4, 8, 128], BF16)] * NXB; b_GB = [Buf()] * NXB
                gsem = [c.dma_sem("gs")] * NXB
                Hr = [[sbt(s1, f"Hr{i}_{k}", [128, 512], BF16) for k in range(2)] for i in range(4)]
                b_Hr = [[Buf() for k in range(2)] for i in range(4)]
                Hi = [[sbt(s1, f"Hi{i}_{k}", [128, 512], BF16) for k in range(2)] for i in range(4)]
                b_Hi = [[Buf() for k in range(2)] for i in range(4)]
                Hp = [sbt(s1, f"Hp{i}", [128, 2, 256], BF16) for i in range(4)]; b_Hp = [Buf() for _ in range(4)]
                Ysb = [sbt(s1, f"Ysb{i}", [128, 512], F32) for i in range(4)]; b_Ysb = [Buf() for _ in range(4)]
                for i in range(4):
                    c.op("pool", lambda e, i=i: e.memset(Hp[i][:], 0.0), writes=[b_Hp[i]])
                g_ev = []
                for gb in range(8):
                    s = gb % NXB
                    for ct in range(4):
                        c.dma("sp" if ct % 2 == 0 else "act", XB[s][:, ct, :, :], xv[:, ct, :, gb * 128:(gb + 1) * 128],
                              writes=[b_XB[s]], sem=xsem[s])
                    for ct in range(4):
                        c.op("pool", lambda e, ct=ct, s=s: e.tensor_copy(
                            out=XR[s][:, ct, :, :].rearrange("p gl (j c) -> p gl j c", j=8),
                            in_=XB[s][:, ct, :, :].rearrange("p j (gl c) -> p gl j c", c=16)),
                             reads=[b_XB[s]], writes=[b_XR[s]])
                    for gl in range(8):
                        pb = 6 + (gl % 2)
                        psb16 = PS[pb][:].bitcast(BF16)
                        for ct in range(4):
                            c.op("pe", lambda e, ct=ct, gl=gl, s=s, psb16=psb16: e.transpose(
                                out=psb16[:, ct * 128:(ct + 1) * 128], in_=XR[s][:, ct, gl, :], identity=identb[:]),
                                 reads=[b_XR[s], b_identb], writes=[PSB[pb]], inc=(ct == 3))
                        c.op("act", lambda e, gl=gl, s=s, psb16=psb16: e.copy(out=U[s][:, gl, :], in_=psb16[:, 0:512]),
                             reads=[PSB[pb]], writes=[b_U[s]])
                    for quad in range(2):
                        for gg in range(4):
                            gl = quad * 4 + gg
                            g = gb * 8 + gl
                            c.op("pe", lambda e, g=g, gl=gl, gg=gg, s=s: e.matmul(PS[gg][:], lhsT=MinT[:, g, :], rhs=U[s][:, gl, :], start=True, stop=True),
                                 reads=[b_MinT, b_U[s]], writes=[PSB[gg]])
                        for k in range(8):
                            sh = 1 << k
                            for gg in range(4):
                                g = gb * 8 + quad * 4 + gg
                                c.op("act", lambda e, gg=gg, k=k, g=g: e.activation(out=Hr[gg][k % 2][:], in_=PS[gg][:], func=AF.Identity, scale=PA[:, k, g:g + 1]),
                                     reads=[PSB[gg], b_PA], writes=[b_Hr[gg][k % 2]])
                                c.op("dve", lambda e, gg=gg, k=k, g=g: e.tensor_scalar(out=Hi[gg][k % 2][:], in0=PS[gg][:], scalar1=PB[:, k, g:g + 1], scalar2=None, op0=ALU.mult),
                                     reads=[PSB[gg], b_PB], writes=[b_Hi[gg][k % 2]])
                            for gg in range(4):
                                for (Dm, b_Dm, Hx, b_Hx, last) in ((D1b, b_D1, Hr, b_Hr, False), (D2b, b_D2, Hi, b_Hi, True)):
                                    for sq in range(2):
                                        c.op("pe", lambda e, gg=gg, k=k, sh=sh, sq=sq, Dm=Dm, Hx=Hx: e.matmul(
                                            PS[gg][:, sq * 256 + sh:(sq + 1) * 256],
                                            lhsT=Dm[:],
                                            rhs=Hx[gg][k % 2][:, sq * 256:(sq + 1) * 256 - sh],
                                            start=False, stop=True, skip_group_check=True),
                                             reads=[b_Dm, b_Hx[gg][k % 2]], writes=[PSB[gg]], inc=(last and sq == 1))
                        for gg in range(4):
                            eng = "act" if gg % 2 == 0 else "dve"
                            src = PS[gg][:].rearrange("p (s n) -> p s n", s=2)[:, :, 0:255]
                            if eng == "act":
                                c.op("act", lambda e, gg=gg, src=src: e.copy(out=Hp[gg][:, :, 1:256], in_=src), reads=[PSB[gg]], writes=[b_Hp[gg]])
                            else:
                                c.op("dve", lambda e, gg=gg, src=src: e.tensor_copy(out=Hp[gg][:, :, 1:256], in_=src), reads=[PSB[gg]], writes=[b_Hp[gg]])
                        for gg in range(4):
                            gl = quad * 4 + gg
                            g = gb * 8 + gl
                            pb = 4 + (gg % 2)
                            c.op("pe", lambda e, g=g, gl=gl, pb=pb, s=s: e.matmul(PS[pb][:], lhsT=Tm[:, g, :], rhs=U[s][:, gl, :], start=True, stop=False),
                                 reads=[b_Tm, b_U[s]], writes=[PSB[pb]], inc=False)
                            c.op("pe", lambda e, g=g, gg=gg, pb=pb: e.matmul(PS[pb][:], lhsT=Mout[:, g, :], rhs=Hp[gg][:].rearrange("p s n -> p (s n)"), start=False, stop=True),
                                 reads=[b_Mout, b_Hp[gg]], writes=[PSB[pb]])
                            if gg % 2 == 0:
                                c.op("dve", lambda e, gg=gg, pb=pb: e.tensor_copy(out=Ysb[gg][:], in_=PS[pb][:]), reads=[PSB[pb]], writes=[b_Ysb[gg]])
                            else:
                                c.op("act", lambda e, gg=gg, pb=pb: e.copy(out=Ysb[gg][:], in_=PS[pb][:]), reads=[PSB[pb]], writes=[b_Ysb[gg]])
                            pt = 6 + (gg % 2)
                            for ct in range(4):
                                c.op("pe", lambda e, gg=gg, ct=ct, pt=pt: e.transpose(out=PS[pt][:, ct * 128:(ct + 1) * 128],
                                                                                      in_=Ysb[gg][:, ct * 128:(ct + 1) * 128], identity=identf[:]),
                                     reads=[b_Ysb[gg], b_identf], writes=[PSB[pt]], inc=(ct == 3))
                            ydst = YB[s][:, :, :, gl * 16:(gl + 1) * 16]
                            ysrc = PS[pt][:].rearrange("p (ct i o) -> p ct i o", ct=4, i=8)
                            if gg % 2 == 0:
                                c.op("act", lambda e, ydst=ydst, ysrc=ysrc: e.copy(out=ydst, in_=ysrc), reads=[PSB[pt]], writes=[b_YB[s]])
                            else:
                                c.op("dve", lambda e, ydst=ydst, ysrc=ysrc: e.tensor_copy(out=ydst, in_=ysrc), reads=[PSB[pt]], writes=[b_YB[s]])
                    xbv = XB[s][:].rearrange("p ct j d -> p (ct j) d")
                    ybv = YB[s][:].rearrange("p ct j d -> p (ct j) d")
                    dbv = dB[:, gb * 128:(gb + 1) * 128].unsqueeze(1).broadcast_to([128, 32, 128])
                    c.op("pool", lambda e, xbv=xbv, dbv=dbv: e.tensor_tensor(out=xbv, in0=xbv, in1=dbv, op=ALU.mult),
                         reads=[b_XB[s], b_dB], writes=[b_XB[s]])
                    c.op("pool", lambda e, xbv=xbv, ybv=ybv: e.tensor_tensor(out=ybv, in0=ybv, in1=xbv, op=ALU.add),
                         reads=[b_XB[s], b_YB[s]], writes=[b_YB[s]])
                    c.op("act", lambda e, s=s: e.activation(out=GB[s][:], in_=YB[s][:], func=AF.Gelu_apprx_tanh),
                         reads=[b_YB[s]], writes=[b_GB[s]])
                    for ct in range(4):
                        g_ev.append(c.dma("sp", Gv[:, ct, :, gb * 128:(gb + 1) * 128], GB[s][:, ct, :, :], reads=[b_GB[s]], sem=gsem[s]))
                c.barrier()
        if stop_after == "A":
            nc.sync.wait_ge(gsem[0][0], gsem[0][1])
            return nc
        class Ring:
            def __init__(self, stack, name, n, shape, dt):
                self.t = [sbt(stack, f"{name}{i}", shape, dt) for i in range(n)]
                self.b = [Buf(f"{name}{i}") for i in range(n)]
                self.i = 0
                self.n = n

            def next(self):
                k = self.i % self.n
                self.i += 1
                return self.t[k], self.b[k]

        bank_ctr = [0]

        def nbank():
            k = bank_ctr[0] % 8
            bank_ctr[0] += 1
            return k

        def load_weight(stack, name, src, K, N, queue="pool"):
            kc = K // 128
            t = sbt(stack, name, [128, kc, N], BF16)
            b = Buf(name)
            sem = c.dma_sem(name)
            srcv = src.rearrange("(kc p) n -> p kc n", p=128)
            for k in range(kc):
                for n0 in range(0, N, 2048):
                    n1 = min(N, n0 + 2048)
                    c.dma(queue, t[:, k, n0:n1], srcv[:, k, n0:n1], writes=[b], sem=sem)
            return t, b

        def load_ln(stack, idx):
            g = sbt(stack, f"lng{idx}", [128, D], F32); bg = Buf()
            bt = sbt(stack, f"lnb{idx}", [128, D], F32); bb = Buf()
            sem = c.dma_sem("ln")
            c.dma("sp", g[:], lnG_d[:, idx, :], writes=[bg], sem=sem)
            c.dma("sp", bt[:], lnB_d[:, idx, :], writes=[bb], sem=sem)
            return g, bg, bt, bb

        def layernorm(stack_rings, r, b_r, lng, b_lng, lnb, b_lnb):
            stats, b_st = stack_rings["stats"].next()
            mv, b_mv = stack_rings["mv"].next()
            sd, b_sd = stack_rings["sd"].next()
            for hh in range(2):
                c.op("dve", lambda e, hh=hh: e.bn_stats(out=stats[:, hh, :], in_=r[:, hh * 512:(hh + 1) * 512]), reads=[b_r], writes=[b_st])
            c.op("dve", lambda e: e.bn_aggr(out=mv[:], in_=stats[:].rearrange("p a b -> p (a b)")), reads=[b_st], writes=[b_mv])
            c.op("dve", lambda e: e.tensor_scalar(out=sd[:, 0:1], in0=mv[:, 1:2], scalar1=float(LN_EPS), scalar2=None, op0=ALU.add), reads=[b_mv], writes=[b_sd])
            c.op("pool", lambda e: e.tensor_tensor(out=sd[:, 2:3], in0=sd[:, 0:1], in1=mhalf[:, 0:1], op=ALU.pow), reads=[b_sd, b_mhalf], writes=[b_sd])
            c.op("dve", lambda e: e.tensor_scalar(out=sd[:, 3:4], in0=mv[:, 0:1], scalar1=sd[:, 2:3], scalar2=-1.0, op0=ALU.mult, op1=ALU.mult), reads=[b_mv, b_sd], writes=[b_sd])
            c.op("dve", lambda e: e.tensor_scalar(out=r[:], in0=r[:], scalar1=sd[:, 2:3], scalar2=sd[:, 3:4], op0=ALU.mult, op1=ALU.add), reads=[b_r, b_sd], writes=[b_r])
            c.op("pool", lambda e: e.tensor_tensor(out=r[:], in0=r[:], in1=lng[:], op=ALU.mult), reads=[b_r, b_lng], writes=[b_r])
            c.op("dve", lambda e: e.tensor_tensor(out=r[:], in0=r[:], in1=lnb[:], op=ALU.add), reads=[b_r, b_lnb], writes=[b_r])

        def ln_rings(stack):
            return {"stats": Ring(stack, "lnst", 3, [128, 2, 6], F32), "mv": Ring(stack, "lnmv", 3, [128, 2], F32),
                    "sd": Ring(stack, "lnsd", 3, [128, 4], F32)}

        def transpose_tile(src, b_src, dst_fn, b_dst, evac_eng="act"):
            pb = nbank()
            p16 = PS[pb][:].bitcast(BF16)
            for kc in range(8):
                c.op("pe", lambda e, kc=kc: e.transpose(out=p16[:, kc * 128:(kc + 1) * 128], in_=src[:, kc * 128:(kc + 1) * 128], identity=identb[:]),
                     reads=[b_src, b_identb], writes=[PSB[pb]], inc=(kc == 7))
            dst = dst_fn()
            if evac_eng == "act":
                c.op("act", lambda e: e.copy(out=dst, in_=p16[:].rearrange("p (k t) -> p k t", k=8)), reads=[PSB[pb]], writes=[b_dst])
            else:
                c.op("dve", lambda e: e.tensor_copy(out=dst, in_=p16[:].rearrange("p (k t) -> p k t", k=8)), reads=[PSB[pb]], writes=[b_dst])

        with contextlib.ExitStack() as sB:
            Wglu, b_Wglu = load_weight(sB, "Wglu", wglu_d, D, 2 * D)
            lng, b_lng, lnb, b_lnb = load_ln(sB, 0)
            rings = ln_rings(sB)
            gt_r = Ring(sB, "gt", 2, [128, D], BF16); gsemB = [c.dma_sem("gB") for _ in range(2)]
            xt_r = Ring(sB, "xt", 3, [128, D], F32); xsemB = [c.dma_sem("xB") for _ in range(3)]
            gT_r = Ring(sB, "gT", 2, [128, 8, 128], BF16)
            sg_r = Ring(sB, "sg", 3, [128, D], F32)
            r_r = Ring(sB, "rB", 3, [128, D], F32); osemB = [c.dma_sem("oB") for _ in range(3)]
            def B_stage1(t):
                gt, b_gt = gt_r.next(); xt, b_xt = xt_r.next(); gT, b_gT = gT_r.next(); sg, b_sg = sg_r.next()
                c.dma("sp", gt[:], G_d[t * 128:(t + 1) * 128, :], writes=[b_gt], sem=gsemB[t % 2])
                c.dma("act", xt[:], x_d[t * 128:(t + 1) * 128, :], writes=[b_xt], sem=xsemB[t % 3])
                transpose_tile(gt, b_gt, lambda: gT[:], b_gT)
                banks = [nbank() for _ in range(4)]
                for nb_ in range(4):
                    for kc in range(8):
                        c.op("pe", lambda e, nb_=nb_, kc=kc: e.matmul(PS[banks[nb_]][:], lhsT=gT[:, kc, :], rhs=Wglu[:, kc, nb_ * 512:(nb_ + 1) * 512],
                                                                  start=(kc == 0), stop=(kc == 7)),
                             reads=[b_gT, b_Wglu], writes=[PSB[banks[nb_]]], inc=(kc == 7))
                for hh in range(2):
                    c.op("act", lambda e, hh=hh: e.activation(out=sg[:, hh * 512:(hh + 1) * 512], in_=PS[banks[2 + hh]][:], func=AF.Sigmoid),
                         reads=[PSB[banks[2 + hh]]], writes=[b_sg])
                    c.op("dve", lambda e, hh=hh: e.tensor_tensor(out=sg[:, hh * 512:(hh + 1) * 512], in0=PS[banks[hh]][:], in1=sg[:, hh * 512:(hh + 1) * 512], op=ALU.mult),
                         reads=[PSB[banks[hh]], b_sg], writes=[b_sg])
                return (t, xt, b_xt, sg, b_sg)

            def B_stage2(ctx):
                t, xt, b_xt, sg, b_sg = ctx
                r, b_r = r_r.next()
                c.op("dve", lambda e: e.scalar_tensor_tensor(out=r[:], in0=xt[:], scalar=ALPHA, in1=sg[:], op0=ALU.mult, op1=ALU.add),
                     reads=[b_xt, b_sg], writes=[b_r])
                layernorm(rings, r, b_r, lng, b_lng, lnb, b_lnb)
                c.dma("pool", H1_d[t * 128:(t + 1) * 128, :], r[:], reads=[b_r], sem=osemB[t % 3])

            pend = None
            for t in range(NT):
                ctx = B_stage1(t)
                if pend is not None:
                    B_stage2(pend)
                pend = ctx
            B_stage2(pend)
            c.barrier()
        if stop_after == "B":
            return nc

        with contextlib.ExitStack() as sC:
            Wg, b_Wg = load_weight(sC, "Wg", fg_d, D, FF)
            Wu, b_Wu = load_weight(sC, "Wu", fu_d, D, FF)
            Wd, b_Wd = load_weight(sC, "Wd", fd_d, FF, D)
            lng, b_lng, lnb, b_lnb = load_ln(sC, 1)
            rings = ln_rings(sC)
            ht_r = Ring(sC, "htC", 2, [128, D], BF16); hsemC = [c.dma_sem("hC") for _ in range(2)]
            hres_r = Ring(sC, "hresC", 2, [128, D], F32); rsemC = [c.dma_sem("rC") for _ in range(2)]
            hT = sbt(sC, "hTC", [128, 8, 512], BF16); b_hT = Buf()
            h1T = sbt(sC, "h1TC", [128, FF // 128, 512], BF16); b_h1T = Buf()
            s_r = Ring(sC, "sC", 2, [128, 512], BF16)
            r_r = Ring(sC, "rC", 2, [128, D], F32); osemC = [c.dma_sem("oC") for _ in range(2)]
            for st_ in range(TOK // 512):
                for sub in range(4):
                    t = st_ * 4 + sub
                    ht, b_ht = ht_r.next()
                    c.dma("pool", ht[:], H1_d[t * 128:(t + 1) * 128, :], writes=[b_ht], sem=hsemC[t % 2])
                    transpose_tile(ht, b_ht, lambda sub=sub: hT[:, :, sub * 128:(sub + 1) * 128], b_hT, evac_eng="act" if sub % 2 == 0 else "dve")
                for fc in range(FF // 128):
                    pg, pu = nbank(), nbank()
                    for (W, bW, pb) in ((Wg, b_Wg, pg), (Wu, b_Wu, pu)):
                        for kc in range(8):
                            c.op("pe", lambda e, W=W, pb=pb, kc=kc, fc=fc: e.matmul(PS[pb][:], lhsT=W[:, kc, fc * 128:(fc + 1) * 128], rhs=hT[:, kc, :],
                                                                              start=(kc == 0), stop=(kc == 7)),
                                 reads=[bW, b_hT], writes=[PSB[pb]], inc=(kc == 7))
                    sb_, b_s = s_r.next()
                    c.op("act", lambda e, pg=pg, sb_=sb_: e.activation(out=sb_[:], in_=PS[pg][:], func=AF.Silu), reads=[PSB[pg]], writes=[b_s])
                    c.op("dve", lambda e, pu=pu, sb_=sb_, fc=fc: e.tensor_tensor(out=h1T[:, fc, :], in0=PS[pu][:], in1=sb_[:], op=ALU.mult),
                         reads=[PSB[pu], b_s], writes=[b_h1T])
                for sub in range(4):
                    t = st_ * 4 + sub
                    hres, b_hres = hres_r.next(); r, b_r = r_r.next()
                    c.dma("act", hres[:], H1_d[t * 128:(t + 1) * 128, :], writes=[b_hres], sem=rsemC[t % 2])
                    po = [nbank(), nbank()]
                    for nb_ in range(2):
                        for fc in range(FF // 128):
                            c.op("pe", lambda e, nb_=nb_, fc=fc, sub=sub: e.matmul(PS[po[nb_]][:], lhsT=h1T[:, fc, sub * 128:(sub + 1) * 128],
                                                                             rhs=Wd[:, fc, nb_ * 512:(nb_ + 1) * 512], start=(fc == 0), stop=(fc == FF // 128 - 1)),
                                 reads=[b_h1T, b_Wd], writes=[PSB[po[nb_]]], inc=(fc == FF // 128 - 1))
                        c.op("dve", lambda e, nb_=nb_: e.scalar_tensor_tensor(out=r[:, nb_ * 512:(nb_ + 1) * 512], in0=hres[:, nb_ * 512:(nb_ + 1) * 512], scalar=ALPHA,
                                                                         in1=PS[po[nb_]][:], op0=ALU.mult, op1=ALU.add),
                             reads=[b_hres, PSB[po[nb_]]], writes=[b_r])
                    layernorm(rings, r, b_r, lng, b_lng, lnb, b_lnb)
                    c.dma("sp", H2_d[t * 128:(t + 1) * 128, :], r[:], reads=[b_r], sem=osemC[t % 2])
            c.barrier()
        if stop_after == "C":
            return nc

        with contextlib.ExitStack() as sE:
            Wq, b_Wq = load_weight(sE, "Wq", wq_d, D, D)
            Wkv, b_Wkv = load_weight(sE, "Wkv", kvw_d, D, 512)
            Wo, b_Wo = load_weight(sE, "Wo", wo_d, D, D)
            lng, b_lng, lnb, b_lnb = load_ln(sE, 2)
            rings = ln_rings(sE)
            semE = c.dma_sem("cE")
            posi = sbt(sE, "posi", [128, NT], I32); b_posi = Buf()
            invf = sbt(sE, "invf", [128, 8], F32); b_invf = Buf()
            sk = sbt(sE, "sk", [128, 16], F32); b_sk = Buf()
            c.dma("sp", posi[:], pos_d[:, :], writes=[b_posi], sem=semE)
            c.dma("sp", invf[:], invf_d[:, :], writes=[b_invf], sem=semE)
            c.dma("sp", sk[:], sink_d[:, :], writes=[b_sk], sem=semE)
            posf = sbt(sE, "posf", [128, NT], F32); b_posf = Buf()
            angE = sbt(sE, "angE", [128, NT, 8], F32); b_angE = Buf()
            kqE = sbt(sE, "kqE", [128, NT, 8], F32); b_kqE = Buf()
            kiE = sbt(sE, "kiE", [128, NT, 8], I32); b_kiE = Buf()
            twE = sbt(sE, "twE", [128, NT, 8], F32); b_twE = Buf()
            ycE = sbt(sE, "ycE", [128, NT, 8], F32); b_ycE = Buf()
            cosT = sbt(sE, "cosT", [128, NT, 8], F32); b_cosT = Buf()
            sinT = sbt(sE, "sinT", [128, NT, 8], F32); b_sinT = Buf()
            esk = sbt(sE, "esk", [128, 16], F32); b_esk = Buf()
            c.op("act", lambda e: e.activation(out=esk[:], in_=sk[:], func=AF.Exp), reads=[b_sk], writes=[b_esk])
            c.op("dve", lambda e: e.tensor_copy(out=posf[:], in_=posi[:]), reads=[b_posi], writes=[b_posf])
            c.op("dve", lambda e: e.tensor_tensor(out=angE[:], in0=posf[:].unsqueeze(2).broadcast_to([128, NT, 8]),
                                                  in1=invf[:].unsqueeze(1).broadcast_to([128, NT, 8]), op=ALU.mult), reads=[b_posf, b_invf], writes=[b_angE])
            c.op("dve", lambda e: e.tensor_scalar(out=kqE[:], in0=angE[:], scalar1=float(1.0 / (2 * math.pi)), scalar2=None, op0=ALU.mult), reads=[b_angE], writes=[b_kqE])
            c.op("dve", lambda e: e.tensor_copy(out=kiE[:], in_=kqE[:]), reads=[b_kqE], writes=[b_kiE])
            c.op("dve", lambda e: e.tensor_copy(out=kqE[:], in_=kiE[:]), reads=[b_kiE], writes=[b_kqE])
            c.op("dve", lambda e: e.scalar_tensor_tensor(out=angE[:], in0=kqE[:], scalar=-6.28125, in1=angE[:], op0=ALU.mult, op1=ALU.add), reads=[b_kqE, b_angE], writes=[b_angE])
            c.op("dve", lambda e: e.scalar_tensor_tensor(out=angE[:], in0=kqE[:], scalar=-float(2.0 * math.pi - 6.28125), in1=angE[:], op0=ALU.mult, op1=ALU.add), reads=[b_kqE, b_angE], writes=[b_angE])

            def wrapE(t, b_t):
                PI_ = float(math.pi)
                c.op("dve", lambda e: e.tensor_scalar(out=twE[:], in0=t[:], scalar1=-PI_, scalar2=2 * PI_, op0=ALU.is_lt, op1=ALU.mult), reads=[b_t], writes=[b_twE])
                c.op("dve", lambda e: e.tensor_tensor(out=t[:], in0=t[:], in1=twE[:], op=ALU.add), reads=[b_t, b_twE], writes=[b_t])
                c.op("dve", lambda e: e.tensor_scalar(out=twE[:], in0=t[:], scalar1=PI_, scalar2=-2 * PI_, op0=ALU.is_gt, op1=ALU.mult), reads=[b_t], writes=[b_twE])
                c.op("dve", lambda e: e.tensor_tensor(out=t[:], in0=t[:], in1=twE[:], op=ALU.add), reads=[b_t, b_twE], writes=[b_t])
            wrapE(angE, b_angE)
            wrapE(angE, b_angE)
            c.op("act", lambda e: e.activation(out=sinT[:], in_=angE[:], func=AF.Sin), reads=[b_angE], writes=[b_sinT])
            c.op("dve", lambda e: e.tensor_scalar(out=ycE[:], in0=angE[:], scalar1=float(math.pi / 2), scalar2=None, op0=ALU.add), reads=[b_angE], writes=[b_ycE])
            wrapE(ycE, b_ycE)
            c.op("act", lambda e: e.activation(out=cosT[:], in_=ycE[:], func=AF.Sin), reads=[b_ycE], writes=[b_cosT])
            mprev = sbt(sE, "mprev", [128, 128], BF16); b_mprev = Buf()
            mcur = sbt(sE, "mcur", [128, 128], BF16); b_mcur = Buf()
            c.op("pool", lambda e: e.memset(mprev[:], 1.0), writes=[b_mprev])
            c.op("pool", lambda e: e.affine_select(out=mprev[:], in_=mprev[:], pattern=[[-1, 128]], compare_op=ALU.is_gt, fill=0.0, base=0, channel_multiplier=1),
                 reads=[b_mprev], writes=[b_mprev])
            c.op("pool", lambda e: e.memset(mcur[:], 1.0), writes=[b_mcur])
            c.op("pool", lambda e: e.affine_select(out=mcur[:], in_=mcur[:], pattern=[[1, 128]], compare_op=ALU.is_ge, fill=0.0, base=0, channel_multiplier=-1),
                 reads=[b_mcur], writes=[b_mcur])
            KT = [sbt(sE, f"KT{i}", [128, 8, 128], BF16) for i in range(3)]; b_KT = [Buf() for _ in range(3)]
            Ksp = [sbt(sE, f"Ksp{i}", [128, 4, 2, 128], BF16) for i in range(3)]; b_Ksp = [Buf() for _ in range(3)]
            for i in range(3):
                c.op("pool", lambda e, i=i: e.memset(Ksp[i][:], 0.0), writes=[b_Ksp[i]])
            Va = [sbt(sE, f"Va{i}", [128, 4, 65], BF16) for i in range(3)]; b_Va = [Buf() for _ in range(3)]
            for i in range(3):
                c.op("pool", lambda e, i=i: e.memset(Va[i][:], 1.0), writes=[b_Va[i]])
            ht_r = Ring(sE, "htE", 3, [128, D], BF16); hsemE = [c.dma_sem("hE") for _ in range(3)]
            hres_r = Ring(sE, "hresE", 4, [128, D], F32); rsemE = [c.dma_sem("rE") for _ in range(4)]
            hT_r = Ring(sE, "hTE", 2, [128, 8, 128], BF16)
            Qs_r = Ring(sE, "Qs", 2, [128, 16, 64], BF16)
            Ks_r = Ring(sE, "Ks", 2, [128, 4, 64], BF16)
            QT_r = Ring(sE, "QT", 3, [128, 8, 128], BF16)
            ro_r = Ring(sE, "ro", 4, [128, 8, 8], F32)
            E_r = Ring(sE, "E", 6, [128, 512], BF16)
            O_r = Ring(sE, "O", 3, [128, 16, 64], BF16)
            OT_r = Ring(sE, "OT", 2, [128, 8, 128], BF16)
            dn_r = Ring(sE, "dn", 4, [128, 8], F32)
            r_r = Ring(sE, "rE", 3, [128, D], F32); osemE = [c.dma_sem("oE") for _ in range(3)]

            def rope(psrc, nh, dst, b_dst, pbuf, t):
                cb = cosT[:, t, :].unsqueeze(1).broadcast_to([128, nh, 8])
                sb2 = sinT[:, t, :].unsqueeze(1).broadcast_to([128, nh, 8])
                q1 = psrc[:, :, 0:8]; q2 = psrc[:, :, 8:16]
                ta, b_ta = ro_r.next(); tb, b_tb = ro_r.next()
                c.op("dve", lambda e: e.tensor_tensor(out=ta[:, 0:nh, :], in0=q1, in1=cb, op=ALU.mult), reads=[pbuf, b_cosT], writes=[b_ta])
                c.op("dve", lambda e: e.tensor_tensor(out=tb[:, 0:nh, :], in0=q2, in1=sb2, op=ALU.mult), reads=[pbuf, b_sinT], writes=[b_tb])
                c.op("dve", lambda e: e.tensor_tensor(out=dst[:, :, 0:8], in0=ta[:, 0:nh, :], in1=tb[:, 0:nh, :], op=ALU.subtract), reads=[b_ta, b_tb], writes=[b_dst])
                tc_, b_tc = ro_r.next(); td, b_td = ro_r.next()
                c.op("dve", lambda e: e.tensor_tensor(out=tc_[:, 0:nh, :], in0=q2, in1=cb, op=ALU.mult), reads=[pbuf, b_cosT], writes=[b_tc])
                c.op("dve", lambda e: e.tensor_tensor(out=td[:, 0:nh, :], in0=q1, in1=sb2, op=ALU.mult), reads=[pbuf, b_sinT], writes=[b_td])
                c.op("dve", lambda e: e.tensor_tensor(out=dst[:, :, 8:16], in0=tc_[:, 0:nh, :], in1=td[:, 0:nh, :], op=ALU.add), reads=[b_tc, b_td], writes=[b_dst])

            def E_A1(t):
                ht, b_ht = ht_r.next(); hres, b_hres = hres_r.next(); hT, b_hT = hT_r.next()
                c.dma("pool", ht[:], H2_d[t * 128:(t + 1) * 128, :], writes=[b_ht], sem=hsemE[t % 3])
                c.dma("act", hres[:], H2_d[t * 128:(t + 1) * 128, :], writes=[b_hres], sem=rsemE[t % 4])
                transpose_tile(ht, b_ht, lambda: hT[:], b_hT)
                pq = [nbank(), nbank()]; pkv = nbank()
                for nb_ in range(2):
                    for kc in range(8):
                        c.op("pe", lambda e, nb_=nb_, kc=kc: e.matmul(PS[pq[nb_]][:], lhsT=hT[:, kc, :], rhs=Wq[:, kc, nb_ * 512:(nb_ + 1) * 512], start=(kc == 0), stop=(kc == 7)),
                             reads=[b_hT, b_Wq], writes=[PSB[pq[nb_]]], inc=(kc == 7))
                for kc in range(8):
                    c.op("pe", lambda e, kc=kc: e.matmul(PS[pkv][:], lhsT=hT[:, kc, :], rhs=Wkv[:, kc, :], start=(kc == 0), stop=(kc == 7)),
                         reads=[b_hT, b_Wkv], writes=[PSB[pkv]], inc=(kc == 7))
                return dict(t=t, nblk=t % 16, hres=hres, b_hres=b_hres, pq=pq, pkv=pkv)

            def E_A2a(cx):
                t = cx['t']; pq = cx['pq']; pkv = cx['pkv']
                Qs, b_Qs = Qs_r.next(); Ks, b_Ks = Ks_r.next()
                slot = t % 3
                for nb_ in range(2):
                    pv_ = PS[pq[nb_]][:].rearrange("p (h d) -> p h d", h=8)
                    c.op("act", lambda e, nb_=nb_, pv_=pv_: e.copy(out=Qs[:, nb_ * 8:(nb_ + 1) * 8, :], in_=pv_), reads=[PSB[pq[nb_]]], writes=[b_Qs])
                    rope(pv_, 8, Qs[:, nb_ * 8:(nb_ + 1) * 8, :], b_Qs, PSB[pq[nb_]], t)
                kvv = PS[pkv][:].rearrange("p (h d) -> p h d", h=8)
                c.op("act", lambda e: e.copy(out=Ks[:], in_=kvv[:, 0:4, :]), reads=[PSB[pkv]], writes=[b_Ks])
                rope(kvv[:, 0:4, :], 4, Ks[:], b_Ks, PSB[pkv], t)
                c.op("act", lambda e: e.copy(out=Va[slot][:, :, 0:64], in_=kvv[:, 4:8, :]), reads=[PSB[pkv]], writes=[b_Va[slot]])
                c.op("pool", lambda e: e.tensor_copy(out=Ksp[slot][:, :, 0, 0:64], in_=Ks[:]), reads=[b_Ks], writes=[b_Ksp[slot]])
                c.op("pool", lambda e: e.tensor_copy(out=Ksp[slot][:, :, 1, 64:128], in_=Ks[:]), reads=[b_Ks], writes=[b_Ksp[slot]])
                cx['Qs'] = Qs; cx['b_Qs'] = b_Qs

            def E_A2b(cx):
                t = cx['t']; Qs = cx['Qs']; b_Qs = cx['b_Qs']
                slot = t % 3
                QT, b_QT = QT_r.next()
                transpose_tile(Qs[:].rearrange("p h d -> p (h d)"), b_Qs, lambda: QT[:], b_QT, evac_eng="act")
                transpose_tile(Ksp[slot][:].rearrange("p g v d -> p (g v d)"), b_Ksp[slot], lambda: KT[slot][:], b_KT[slot], evac_eng="dve")
                cx['QT'] = QT; cx['b_QT'] = b_QT

            def E_B(cx):
                t = cx['t']; nblk = cx['nblk']; QT = cx['QT']; b_QT = cx['b_QT']
                O, b_O = O_r.next()
                kbs = ([((t - 1) % 3, mprev, b_mprev)] if nblk > 0 else []) + [(t % 3, mcur, b_mcur)]

                def scores(g):
                    Es = []
                    for (sl, mk, b_mk) in kbs:
                        ps_ = nbank()
                        for var in range(2):
                            c.op("pe", lambda e, ps_=ps_, sl=sl, g=g, var=var: e.matmul(PS[ps_][:, var * 256:(var + 1) * 256], lhsT=KT[sl][:, g * 2 + var, :],
                                                                                  rhs=QT[:, 2 * g:2 * g + 2, :].rearrange("p h t -> p (h t)"), start=True, stop=True),
                                 reads=[b_KT[sl], b_QT], writes=[PSB[ps_]], inc=(var == 1))
                        E, b_E = E_r.next()
                        c.op("act", lambda e, ps_=ps_, E=E: e.activation(out=E[:], in_=PS[ps_][:], func=AF.Exp, scale=0.125), reads=[PSB[ps_]], writes=[b_E])
                        c.op("pool", lambda e, E=E, mk=mk: e.tensor_tensor(out=E[:].rearrange("p (h t) -> p h t", h=4), in0=E[:].rearrange("p (h t) -> p h t", h=4),
                                                                       in1=mk[:].unsqueeze(1).broadcast_to([128, 4, 128]), op=ALU.mult), reads=[b_E, b_mk], writes=[b_E])
                        Es.append((E, b_E, sl))
                    return Es

                def pv(g, Es):
                    po_ = nbank()
                    for hl in range(4):
                        for i, (E, b_E, sl) in enumerate(Es):
                            c.op("pe", lambda e, po_=po_, hl=hl, E=E, sl=sl, g=g, i=i: e.matmul(PS[po_][:, hl * 65:(hl + 1) * 65], lhsT=E[:, hl * 128:(hl + 1) * 128], rhs=Va[sl][:, g, :],
                                                                                     start=(i == 0), stop=(i == len(Es) - 1)),
                                 reads=[b_E, b_Va[sl]], writes=[PSB[po_]], inc=(hl == 3 and i == len(Es) - 1))
                    dn, b_dn = dn_r.next()
                    ov = PS[po_][:, 0:260].rearrange("p (v pr d) -> p v pr d", v=2, pr=2)
                    c.op("dve", lambda e: e.tensor_tensor(out=dn[:, 0:4].rearrange("p (v pr) -> p v pr", v=2), in0=ov[:, :, :, 64],
                                                          in1=esk[:, 4 * g:4 * g + 4].rearrange("p (pr v) -> p v pr", v=2), op=ALU.add), reads=[PSB[po_], b_esk], writes=[b_dn])
                    c.op("dve", lambda e: e.reciprocal(out=dn[:, 4:8], in_=dn[:, 0:4]), reads=[b_dn], writes=[b_dn])
                    for var in range(2):
                        c.op("dve", lambda e, var=var: e.tensor_tensor(
                            out=O[:, 4 * g:4 * g + 4, :].rearrange("p (pr v) d -> p v pr d", v=2)[:, var, :, :], in0=ov[:, var, :, 0:64],
                            in1=dn[:, 4 + 2 * var:6 + 2 * var].unsqueeze(2).broadcast_to([128, 2, 64]), op=ALU.mult),
                             reads=[PSB[po_], b_dn], writes=[b_O])

                prev = None
                for g in range(4):
                    Es = scores(g)
                    if prev is not None:
                        pv(*prev)
                    prev = (g, Es)
                pv(*prev)
                cx['O'] = O; cx['b_O'] = b_O

            def E_C(cx):
                t = cx['t']; hres = cx['hres']; b_hres = cx['b_hres']; O = cx['O']; b_O = cx['b_O']
                OT, b_OT = OT_r.next()
                transpose_tile(O[:].rearrange("p h d -> p (h d)"), b_O, lambda: OT[:], b_OT, evac_eng="act")
                r, b_r = r_r.next()
                po = [nbank(), nbank()]
                for nb_ in range(2):
                    for kc in range(8):
                        c.op("pe", lambda e, nb_=nb_, kc=kc: e.matmul(PS[po[nb_]][:], lhsT=OT[:, kc, :], rhs=Wo[:, kc, nb_ * 512:(nb_ + 1) * 512], start=(kc == 0), stop=(kc == 7)),
                             reads=[b_OT, b_Wo], writes=[PSB[po[nb_]]], inc=(kc == 7))
                    c.op("dve", lambda e, nb_=nb_: e.scalar_tensor_tensor(out=r[:, nb_ * 512:(nb_ + 1) * 512], in0=hres[:, nb_ * 512:(nb_ + 1) * 512], scalar=ALPHA,
                                                                     in1=PS[po[nb_]][:], op0=ALU.mult, op1=ALU.add), reads=[b_hres, PSB[po[nb_]]], writes=[b_r])
                layernorm(rings, r, b_r, lng, b_lng, lnb, b_lnb)
                c.dma("sp", H3_d[t * 128:(t + 1) * 128, :], r[:], reads=[b_r], sem=osemE[t % 3])

            cxs = {}
            for i in range(NT + 2):
                if i < NT:
                    cxs[i] = E_A1(i)
                    E_A2a(cxs[i])
                if 0 <= i - 2 < NT:
                    E_C(cxs.pop(i - 2))
                if 0 <= i - 1 < NT:
                    E_B(cxs[i - 1])
                if i < NT:
                    E_A2b(cxs[i])
            c.barrier()
        if stop_after == "E":
            return nc

        with contextlib.ExitStack() as sF:
            lng, b_lng, lnb, b_lnb = load_ln(sF, 3)
            rings = ln_rings(sF)
            semF = c.dma_sem("cF")
            Wr = sbt(sF, "Wr", [128, 8, NE], F32); b_Wr = Buf()
            brt = sbt(sF, "brt", [128, NE], F32); b_brt = Buf()
            c.dma("sp", Wr[:], wr_d.rearrange("(kc p) n -> p kc n", p=128), writes=[b_Wr], sem=semF)
            c.dma("sp", brt[:], br_d[:, :], writes=[b_brt], sem=semF)
            STK = 1024
            NSUB = STK // 128
            WG = [sbt(sF, f"WGe{i}", [128, 8, ED], BF16) for i in range(2)]; b_WG = [Buf() for _ in range(2)]
            WU = [sbt(sF, f"WUe{i}", [128, 8, ED], BF16) for i in range(2)]; b_WU = [Buf() for _ in range(2)]
            WD = [sbt(sF, f"WDe{i}", [128, 8, D], BF16) for i in range(2)]; b_WD = [Buf() for _ in range(2)]
            wsem = [c.dma_sem("wF") for _ in range(2)]
            hTm = sbt(sF, "hTm", [128, 8, STK], BF16); b_hTm = Buf()
            h1m = sbt(sF, "h1m", [128, 8, STK], BF16); b_h1m = Buf()
            acc = sbt(sF, "acc", [128, NSUB, D], F32); b_acc = [Buf() for _ in range(NSUB)]
            comb = sbt(sF, "comb", [128, NSUB, NE], F32); b_comb = [Buf() for _ in range(NSUB)]
            h3_r = Ring(sF, "h3F", 2, [128, D], F32); h3sem = [c.dma_sem("h3F") for _ in range(2)]
            h3T_r = Ring(sF, "h3T", 2, [128, 8, 128], F32)
            lg_r = Ring(sF, "lg", 3, [128, 4, NE], F32)
            s_r = Ring(sF, "sF", 3, [128, 512], BF16)
            osemF = [c.dma_sem("oF") for _ in range(2)]
            n_w = 0

            def issue_weights(e_idx, slot):
                wsem[slot] = c.dma_sem("wF")
                for kc in range(8):
                    c.dma("pool", WG[slot][:, kc, :], mg_d[e_idx, kc * 128:(kc + 1) * 128, :], writes=[b_WG[slot]], sem=wsem[slot])
                    c.dma("pool", WU[slot][:, kc, :], mu_d[e_idx, kc * 128:(kc + 1) * 128, :], writes=[b_WU[slot]], sem=wsem[slot])
                for kc in range(8):
                    c.dma("pool", WD[slot][:, kc, :], md_d[e_idx, kc * 128:(kc + 1) * 128, :], writes=[b_WD[slot]], sem=wsem[slot])

            issue_weights(0, 0)
            for st_ in range(TOK // STK):
                h3sem = [c.dma_sem("h3F") for _ in range(2)]
                for sub in range(NSUB):
                    t = st_ * NSUB + sub
                    h3, b_h3 = h3_r.next(); h3T, b_h3T = h3T_r.next()
                    c.dma("sp", h3[:], H3_d[t * 128:(t + 1) * 128, :], writes=[b_h3], sem=h3sem[t % 2])
                    pt = [nbank(), nbank()]
                    for kc in range(8):
                        c.op("pe", lambda e, kc=kc: e.transpose(out=PS[pt[kc // 4]][:, (kc % 4) * 128:(kc % 4 + 1) * 128], in_=h3[:, kc * 128:(kc + 1) * 128], identity=identf[:]),
                             reads=[b_h3, b_identf], writes=[PSB[pt[kc // 4]]], inc=(kc % 4 == 3))
                    for hh in range(2):
                        src = PS[pt[hh]][:].rearrange("p (k t) -> p k t", k=4)
                        c.op("act", lambda e, hh=hh, src=src: e.copy(out=h3T[:, hh * 4:(hh + 1) * 4, :], in_=src), reads=[PSB[pt[hh]]], writes=[b_h3T])
                        c.op("dve", lambda e, hh=hh, src=src, sub=sub: e.tensor_copy(out=hTm[:, hh * 4:(hh + 1) * 4, sub * 128:(sub + 1) * 128], in_=src), reads=[PSB[pt[hh]]], writes=[b_hTm])
                    pl = nbank()
                    for kc in range(8):
                        c.op("pe", lambda e, kc=kc: e.matmul(PS[pl][:, 0:NE], lhsT=h3T[:, kc, :], rhs=Wr[:, kc, :], start=(kc == 0), stop=(kc == 7)),
                             reads=[b_h3T, b_Wr], writes=[PSB[pl]], inc=(kc == 7))
                    lg, b_lg = lg_r.next()
                    c.op("dve", lambda e: e.tensor_tensor(out=lg[:, 0, :], in0=PS[pl][:, 0:NE], in1=brt[:], op=ALU.add), reads=[PSB[pl], b_brt], writes=[b_lg])
                    c.op("dve", lambda e: e.max(out=lg[:, 1, :], in_=lg[:, 0, :]), reads=[b_lg], writes=[b_lg])
                    c.op("dve", lambda e: e.tensor_scalar(out=lg[:, 2, :], in0=lg[:, 0, :], scalar1=lg[:, 1, 1:2], scalar2=None, op0=ALU.is_ge), reads=[b_lg], writes=[b_lg])
                    c.op("dve", lambda e: e.tensor_scalar(out=lg[:, 1, 2:3], in0=lg[:, 1, 0:1], scalar1=-1.0, scalar2=None, op0=ALU.mult), reads=[b_lg], writes=[b_lg])
                    c.op("act", lambda e: e.activation(out=lg[:, 3, :], in_=lg[:, 0, :], func=AF.Exp, bias=lg[:, 1, 2:3], scale=1.0), reads=[b_lg], writes=[b_lg])
                    c.op("dve", lambda e: e.tensor_tensor(out=lg[:, 3, :], in0=lg[:, 3, :], in1=lg[:, 2, :], op=ALU.mult), reads=[b_lg], writes=[b_lg])
                    c.op("dve", lambda e: e.tensor_reduce(out=lg[:, 1, 3:4], in_=lg[:, 3, :], axis=AX.X, op=ALU.add), reads=[b_lg], writes=[b_lg])
                    c.op("dve", lambda e: e.reciprocal(out=lg[:, 1, 4:5], in_=lg[:, 1, 3:4]), reads=[b_lg], writes=[b_lg])
                    c.op("dve", lambda e, sub=sub: e.tensor_scalar(out=comb[:, sub, :], in0=lg[:, 3, :], scalar1=lg[:, 1, 4:5], scalar2=None, op0=ALU.mult), reads=[b_lg], writes=[b_comb[sub]])
                for ex in range(NE):
                    slot = n_w % 2
                    n_w += 1
                    nxt = (st_ * NE + ex + 1)
                    if nxt < (TOK // STK) * NE:
                        issue_weights(nxt % NE, 1 - slot)
                    for half in range(STK // 512):
                        for fc in range(8):
                            pg, pu = nbank(), nbank()
                            for (W, bW, pb) in ((WG[slot], b_WG[slot], pg), (WU[slot], b_WU[slot], pu)):
                                for kc in range(8):
                                    c.op("pe", lambda e, W=W, pb=pb, kc=kc, fc=fc, half=half: e.matmul(PS[pb][:], lhsT=W[:, kc, fc * 128:(fc + 1) * 128], rhs=hTm[:, kc, half * 512:(half + 1) * 512],
                                                                                                 start=(kc == 0), stop=(kc == 7)),
                                         reads=[bW, b_hTm], writes=[PSB[pb]], inc=(kc == 7))
                            sb_, b_s = s_r.next()
                            c.op("act", lambda e, pg=pg, sb_=sb_: e.activation(out=sb_[:], in_=PS[pg][:], func=AF.Silu), reads=[PSB[pg]], writes=[b_s])
                            c.op("dve", lambda e, pu=pu, sb_=sb_, fc=fc, half=half: e.tensor_tensor(out=h1m[:, fc, half * 512:(half + 1) * 512], in0=PS[pu][:], in1=sb_[:], op=ALU.mult),
                                 reads=[PSB[pu], b_s], writes=[b_h1m])
                    for sub in range(NSUB):
                        for nb_ in range(2):
                            po_ = nbank()
                            for fc in range(8):
                                c.op("pe", lambda e, po_=po_, fc=fc, sub=sub, nb_=nb_, slot=slot: e.matmul(PS[po_][:], lhsT=h1m[:, fc, sub * 128:(sub + 1) * 128],
                                                                                                 rhs=WD[slot][:, fc, nb_ * 512:(nb_ + 1) * 512], start=(fc == 0), stop=(fc == 7)),
                                     reads=[b_h1m, b_WD[slot]], writes=[PSB[po_]], inc=(fc == 7))
                            av = acc[:, sub, nb_ * 512:(nb_ + 1) * 512]
                            if ex == 0:
                                c.op("dve", lambda e, po_=po_, av=av, sub=sub, ex=ex: e.tensor_scalar(out=av, in0=PS[po_][:], scalar1=comb[:, sub, ex:ex + 1], scalar2=None, op0=ALU.mult),
                                     reads=[PSB[po_], b_comb[sub]], writes=[b_acc[sub]])
                            else:
                                c.op("dve", lambda e, po_=po_, av=av, sub=sub, ex=ex: e.scalar_tensor_tensor(out=av, in0=PS[po_][:], scalar=comb[:, sub, ex:ex + 1], in1=av, op0=ALU.mult, op1=ALU.add),
                                     reads=[PSB[po_], b_comb[sub], b_acc[sub]], writes=[b_acc[sub]])
                for sub in range(NSUB):
                    t = st_ * NSUB + sub
                    h3, b_h3 = h3_r.next()
                    c.dma("sp", h3[:], H3_d[t * 128:(t + 1) * 128, :], writes=[b_h3], sem=h3sem[t % 2])
                    c.op("dve", lambda e, sub=sub: e.scalar_tensor_tensor(out=acc[:, sub, :], in0=h3[:], scalar=ALPHA, in1=acc[:, sub, :], op0=ALU.mult, op1=ALU.add),
                         reads=[b_h3, b_acc[sub]], writes=[b_acc[sub]])
                    layernorm(rings, acc[:, sub, :], b_acc[sub], lng, b_lng, lnb, b_lnb)
                    c.dma("act", out_d[t * 128:(t + 1) * 128, :], acc[:, sub, :], reads=[b_acc[sub]], sem=osemF[t % 2])
            for k in ("sp",):
                for s_ in osemF:
                    nc.sync.wait_ge(s_[0], s_[1])
    return nc


def make_in_maps(inp):
    f = lambda a: np.ascontiguousarray(np.asarray(a), dtype=np.float32)
    x = f(inp["x"])
    pos = np.asarray(inp["positions"]).astype(np.int32)
    ln_g = f(inp["ln_g"]).reshape(4, D)
    ln_b = f(inp["ln_b"]).reshape(4, D)
    bcast = lambda a: np.ascontiguousarray(np.broadcast_to(a[None], (128,) + a.shape))
    lam_re = f(inp["ssm_lambda_re"])[0]; lam_im = f(inp["ssm_lambda_im"])[0]
    two = lambda a: np.ascontiguousarray(np.concatenate([a, a], axis=0))
    b_re = f(inp["ssm_b_re"])[0].transpose(1, 0, 2); b_im = f(inp["ssm_b_im"])[0].transpose(1, 0, 2)
    c_re = f(inp["ssm_c_re"])[0].transpose(2, 0, 1); c_im = f(inp["ssm_c_im"])[0].transpose(2, 0, 1)
    cat = lambda a, b: np.ascontiguousarray(np.concatenate([a, b], axis=0))
    inv_freq = (500000.0 ** (-np.arange(0, 16, 2, dtype=np.float32) / 16.0)).astype(np.float32)
    shared = {
        "ln_g": bcast(ln_g), "ln_b": bcast(ln_b),
        "lam_re": two(lam_re.T), "lam_im": two(lam_im.T), "lstep": bcast(f(inp["ssm_log_step"])[0]),
        "s5_ba": cat(b_re, b_im), "s5_bb": cat(b_im, b_re), "s5_ca": cat(c_re, c_im), "s5_cb": cat(c_im, c_re),
        "s5_d": bcast(f(inp["ssm_d"])[0]),
        "w_glu": f(inp["ssm_w_glu"])[0], "kv_w": f(inp["kv_w"]), "w_q": f(inp["attn_w_q"])[0],
        "sinks": bcast(f(inp["attn_sinks"])[0]), "w_out": f(inp["attn_w_out"])[0],
        "ffn_g": f(inp["ffn_w_gate"])[0], "ffn_u": f(inp["ffn_w_up"])[0], "ffn_d": f(inp["ffn_w_down"])[0],
        "w_router": f(inp["moe_w_router"])[0], "b_router": bcast(f(inp["moe_b_router"])[0]),
        "moe_g": f(inp["moe_w_gate"])[0], "moe_u": f(inp["moe_w_up"])[0], "moe_d": f(inp["moe_w_down"])[0],
        "inv_freq": bcast(inv_freq),
    }
    maps = []
    for i in range(NCORES):
        m = dict(shared)
        m["x"] = np.ascontiguousarray(x[2 * i:2 * i + 2].reshape(TOK, D))
        m["pos"] = np.ascontiguousarray(pos[2 * i:2 * i + 2].reshape(NT, 128).T)
        maps.append(m)
    return maps


def kernel(**inputs):
    nc = build()
    maps = make_in_maps(inputs)
    res = run_bass_kernel_spmd(nc, maps, core_ids=list(range(NCORES)))
    out = np.stack([np.asarray(r["out"]).reshape(NSEQ, L, D) for r in res.results], axis=0)
    return out.reshape(NCORES * NSEQ, L, D).astype(np.float32)
```

```python
import contextlib
import math
import numpy as np
import concourse.bass as bass
import concourse.mybir as mybir
from concourse.alu_op_type import AluOpType as ALU
from concourse.bass_utils import run_bass_kernel_spmd

F32 = mybir.dt.float32
BF16 = mybir.dt.bfloat16
I32 = mybir.dt.int32
AF = mybir.ActivationFunctionType
AX = mybir.AxisListType

NCORES = 8
D = 1024
L = 2048
NSEQ = 2
TOK = NSEQ * L
NT = TOK // 128
FF = 2816
NE = 8
ED = 1024
ALPHA = float(4.0 ** 0.25)
LN_EPS = 1e-5
MLIST = [0, -1, -2, -3, -4, -5, -6, -7] + list(range(16)) + [16, 32, 64, 128, 256, 512, 1024]
NM = len(MLIST)
SEM_ROLL = 6000


class Ev:
    __slots__ = ("sem", "val", "eng", "ref")

    def __init__(self, sem, val, eng, ref=None):
        self.sem = sem
        self.val = val
        self.eng = eng
        self.ref = ref


class Buf:
    __slots__ = ("name", "w", "r", "excl")

    def __init__(self, name="", excl=False):
        self.name = name
        self.w = None
        self.r = {}
        self.excl = excl


class Ctx:
    def __init__(self, nc, stack):
        self.nc = nc
        self.stack = stack
        self.engs = {"pe": nc.tensor, "act": nc.scalar, "dve": nc.vector, "pool": nc.gpsimd, "sp": nc.sync}
        self.sem = {}
        self.cnt = {}
        self.nsem = 0
        self.dsems = []
        for k in self.engs:
            self._new_eng_sem(k)
        self.waited = {k: {} for k in self.engs}
        self.pending = {k: [] for k in self.engs}
        self.ninst = {k: 0 for k in self.engs}
        self.last = {k: None for k in self.engs}

    def _new_sem(self, name):
        self.nsem += 1
        return self.stack.enter_context(self.nc.semaphore(f"{name}_{self.nsem}"))

    def _new_eng_sem(self, k):
        self.sem[k] = self._new_sem("e" + k)
        self.cnt[k] = 0

    def dma_sem(self, name="d"):
        s = [self._new_sem(name), 0]
        self.dsems.append(s)
        return s

    def _wait(self, k, ev):
        if ev is None:
            return
        if ev.eng == "pe" and k == "pe":
            return
        if ev.val is None:
            raise RuntimeError("dependency on an instruction without inc")
        w = self.waited[k]
        sid = id(ev.sem)
        val = ev.ref[1] if ev.ref is not None else ev.val
        if w.get(sid, 0) >= val:
            return
        self.engs[k].wait_ge(ev.sem, val)
        self.ninst[k] += 1
        w[sid] = val

    def _deps(self, k, reads, writes):
        for b in reads:
            self._wait(k, b.w)
            if b.excl:
                for kk, e in b.r.items():
                    if kk != k:
                        self._wait(k, e)
        for b in writes:
            self._wait(k, b.w)
            for e in b.r.values():
                self._wait(k, e)

    def _commit(self, ev, reads, writes):
        key = id(ev.sem) if ev.eng == "dma" else ev.eng
        for b in reads:
            b.r[key] = ev
        for b in writes:
            b.w = ev
            b.r = {}

    def op(self, k, fn, reads=(), writes=(), inc=True):
        self._deps(k, reads, writes)
        inst = fn(self.engs[k])
        self.ninst[k] += 1
        if inc:
            if self.cnt[k] >= SEM_ROLL:
                self._new_eng_sem(k)
            self.cnt[k] += 1
            inst.then_inc(self.sem[k], 1)
            ev = Ev(self.sem[k], self.cnt[k], k)
            for p in self.pending[k]:
                p.sem = ev.sem
                p.val = ev.val
            self.pending[k] = []
            self.last[k] = ev
        else:
            ev = Ev(None, None, k)
            self.pending[k].append(ev)
        self._commit(ev, reads, writes)
        return ev

    def dma(self, k, out, in_, reads=(), writes=(), sem=None, **kw):
        self._deps(k, reads, writes)
        inst = self.engs[k].dma_start(out=out, in_=in_, **kw)
        self.ninst[k] += 1
        sem[1] += 16
        inst.then_inc(sem[0], 16)
        ev = Ev(sem[0], sem[1], "dma", sem)
        self._commit(ev, reads, writes)
        return ev

    def barrier(self, engs=("pe", "act", "dve", "pool", "sp")):
        for k in engs:
            assert not self.pending[k]
        for k in engs:
            for k2 in engs:
                if k2 != k and self.last[k2] is not None:
                    self._wait(k, self.last[k2])
            for s in self.dsems:
                if s[1] > 0:
                    self._wait(k, Ev(s[0], s[1], "dma", s))


LAST_CTX = None


def build(stop_after=None):
    global LAST_CTX
    nc = bass.Bass("TRN2", target_bir_lowering=False)
    dram_in = lambda name, shape, dt=F32: nc.dram_tensor(name, list(shape), dt, kind="ExternalInput").ap()
    x_d = dram_in("x", [TOK, D])
    pos_d = dram_in("pos", [128, NT], I32)
    lnG_d = dram_in("ln_g", [128, 4, D])
    lnB_d = dram_in("ln_b", [128, 4, D])
    lamr_d = dram_in("lam_re", [128, 64])
    lami_d = dram_in("lam_im", [128, 64])
    lstep_d = dram_in("lstep", [128, 64])
    BA_d = dram_in("s5_ba", [128, 64, 16])
    BB_d = dram_in("s5_bb", [128, 64, 16])
    CA_d = dram_in("s5_ca", [128, 64, 16])
    CB_d = dram_in("s5_cb", [128, 64, 16])
    dsk_d = dram_in("s5_d", [128, D])
    wglu_d = dram_in("w_glu", [D, 2 * D])
    kvw_d = dram_in("kv_w", [D, 512])
    wq_d = dram_in("w_q", [D, D])
    sink_d = dram_in("sinks", [128, 16])
    wo_d = dram_in("w_out", [D, D])
    fg_d = dram_in("ffn_g", [D, FF])
    fu_d = dram_in("ffn_u", [D, FF])
    fd_d = dram_in("ffn_d", [FF, D])
    wr_d = dram_in("w_router", [D, NE])
    br_d = dram_in("b_router", [128, NE])
    mg_d = dram_in("moe_g", [NE, D, ED])
    mu_d = dram_in("moe_u", [NE, D, ED])
    md_d = dram_in("moe_d", [NE, ED, D])
    invf_d = dram_in("inv_freq", [128, 8])
    out_d = nc.dram_tensor("out", [TOK, D], F32, kind="ExternalOutput").ap()
    dbg = stop_after is not None
    G_d = nc.dram_tensor("G", [TOK, D], BF16, kind="ExternalOutput" if stop_after == "A" else "Internal").ap()
    H1_d = nc.dram_tensor("H1", [TOK, D], F32, kind="ExternalOutput" if stop_after == "B" else "Internal").ap()
    H2_d = nc.dram_tensor("H2", [TOK, D], F32, kind="ExternalOutput" if stop_after == "C" else "Internal").ap()
    H3_d = nc.dram_tensor("H3", [TOK, D], F32, kind="ExternalOutput" if stop_after == "E" else "Internal").ap()
    if stop_after == "P":
        dbgP = nc.dram_tensor("dbgP", [128, 2 * NM + 2, 64], F32, kind="ExternalOutput").ap()
        dbgT = nc.dram_tensor("dbgT", [128, 3, 64, 128], BF16, kind="ExternalOutput").ap()

    with contextlib.ExitStack() as st:
        c = Ctx(nc, st)
        LAST_CTX = c
        uniq = [0]

        def sbt(stack, name, shape, dt):
            uniq[0] += 1
            return stack.enter_context(nc.sbuf_tensor(f"{name}_{uniq[0]}", list(shape), dt))
        PS = [st.enter_context(nc.psum_tensor(f"ps{i}", [128, 512], F32)) for i in range(8)]
        PSB = [Buf(f"ps{i}", excl=True) for i in range(8)]

        identf = sbt(st, "identf", [128, 128], F32); b_identf = Buf()
        identb = sbt(st, "identb", [128, 128], BF16); b_identb = Buf()
        c.op("pool", lambda e: e.memset(identf[:], 0.0), writes=[b_identf])
        c.op("pool", lambda e: e.affine_select(out=identf[:], in_=identf[:], pattern=[[-1, 128]], compare_op=ALU.not_equal,
                                               fill=1.0, base=0, channel_multiplier=1), reads=[b_identf], writes=[b_identf])
        c.op("pool", lambda e: e.tensor_copy(out=identb[:], in_=identf[:]), reads=[b_identf], writes=[b_identb])
        out_sem = c.dma_sem("out")
        mhalf = sbt(st, "mhalf", [128, 1], F32); b_mhalf = Buf()
        c.op("pool", lambda e: e.memset(mhalf[:], -0.5), writes=[b_mhalf])

        with contextlib.ExitStack() as sa:
            Tm = sbt(sa, "Tm", [128, 64, 128], BF16); b_Tm = Buf()
            MinT = sbt(sa, "MinT", [128, 64, 128], BF16); b_MinT = Buf()
            Mout = sbt(sa, "Mout", [128, 64, 128], BF16); b_Mout = Buf()
            PA = sbt(sa, "PA", [128, 8, 64], F32); b_PA = Buf()
            PB = sbt(sa, "PB", [128, 8, 64], F32); b_PB = Buf()
            D1b = sbt(sa, "D1b", [128, 128], BF16); b_D1 = Buf()
            D2b = sbt(sa, "D2b", [128, 128], BF16); b_D2 = Buf()
            dB = sbt(sa, "dB", [128, D], F32); b_dB = Buf()
            ld0 = c.dma_sem("ld0")
            c.dma("sp", dB[:], dsk_d[:, :], writes=[b_dB], sem=ld0)
            with contextlib.ExitStack() as s0:
                lr = sbt(s0, "lr", [128, 64], F32); b_lr = Buf()
                li = sbt(s0, "li", [128, 64], F32); b_li = Buf()
                ls = sbt(s0, "ls", [128, 64], F32); b_ls = Buf()
                BAt = sbt(s0, "BAt", [128, 64, 16], F32); b_BA = Buf()
                BBt = sbt(s0, "BBt", [128, 64, 16], F32); b_BB = Buf()
                CAt = sbt(s0, "CAt", [128, 64, 16], F32); b_CA = Buf()
                CBt = sbt(s0, "CBt", [128, 64, 16], F32); b_CB = Buf()
                c.dma("sp", lr[:], lamr_d[:, :], writes=[b_lr], sem=ld0)
                c.dma("sp", li[:], lami_d[:, :], writes=[b_li], sem=ld0)
                c.dma("sp", ls[:], lstep_d[:, :], writes=[b_ls], sem=ld0)
                c.dma("act", BAt[:], BA_d[:, :, :], writes=[b_BA], sem=ld0)
                c.dma("act", BBt[:], BB_d[:, :, :], writes=[b_BB], sem=ld0)
                c.dma("act", CAt[:], CA_d[:, :, :], writes=[b_CA], sem=ld0)
                c.dma("act", CBt[:], CB_d[:, :, :], writes=[b_CB], sem=ld0)

                sgn = sbt(s0, "sgn", [128, 1], F32); b_sgn = Buf()
                sgn2 = sbt(s0, "sgn2", [128, 1], F32); b_sgn2 = Buf()
                c.op("pool", lambda e: e.memset(sgn[0:64, :], -1.0), writes=[b_sgn])
                c.op("pool", lambda e: e.memset(sgn[64:128, :], 1.0), writes=[b_sgn])
                c.op("pool", lambda e: e.memset(sgn2[0:64, :], 1.0), writes=[b_sgn2])
                c.op("pool", lambda e: e.memset(sgn2[64:128, :], -1.0), writes=[b_sgn2])
                D2f = sbt(s0, "D2f", [128, 128], F32); b_D2f = Buf()
                cmask = sbt(s0, "cmask", [128, 128], F32); b_cm = Buf()
                c.op("pool", lambda e: e.memset(D2f[:], 0.0), writes=[b_D2f])
                c.op("pool", lambda e: e.affine_select(out=D2f[:], in_=D2f[:], pattern=[[-1, 128]], compare_op=ALU.not_equal,
                                                       fill=1.0, base=64, channel_multiplier=1), reads=[b_D2f], writes=[b_D2f])
                c.op("pool", lambda e: e.affine_select(out=D2f[:], in_=D2f[:], pattern=[[-1, 128]], compare_op=ALU.not_equal,
                                                       fill=1.0, base=-64, channel_multiplier=1), reads=[b_D2f], writes=[b_D2f])
                c.op("pool", lambda e: e.tensor_copy(out=D2b[:], in_=D2f[:]), reads=[b_D2f], writes=[b_D2])
                c.op("pool", lambda e: e.tensor_copy(out=D1b[:], in_=identf[:]), reads=[b_identf], writes=[b_D1])
                c.op("pool", lambda e: e.memset(cmask[:], 1.0), writes=[b_cm])
                c.op("pool", lambda e: e.affine_select(out=cmask[:].rearrange("p (i o) -> p i o", i=8),
                                                       in_=cmask[:].rearrange("p (i o) -> p i o", i=8),
                                                       pattern=[[16, 8], [0, 16]], compare_op=ALU.is_ge,
                                                       fill=0.0, base=15, channel_multiplier=-1), reads=[b_cm], writes=[b_cm])

                s0a = contextlib.ExitStack()
                s0a.__enter__()
                def T3(name, stk=None):
                    return sbt(s0a if stk is None else stk, name, [128, NM, 64], F32), Buf(name)
                Pr, b_Pr = T3("Pr", s0)
                Pi, b_Pi = T3("Pi", s0)
                Mtab, b_Mt = T3("Mtab")
                for j, m in enumerate(MLIST):
                    c.op("pool", lambda e, j=j, m=m: e.memset(Mtab[:, j, :], float(m)), writes=[b_Mt])
                dt_ = sbt(s0a, "dt", [128, 64], F32); b_dt = Buf()
                lrdt = sbt(s0a, "lrdt", [128, 64], F32); b_lrdt = Buf()
                lidt = sbt(s0a, "lidt", [128, 64], F32); b_lidt = Buf()
                c.op("act", lambda e: e.activation(out=dt_[:], in_=ls[:], func=AF.Exp), reads=[b_ls], writes=[b_dt])
                c.op("dve", lambda e: e.tensor_tensor(out=lrdt[:], in0=lr[:], in1=dt_[:], op=ALU.mult), reads=[b_lr, b_dt], writes=[b_lrdt])
                c.op("dve", lambda e: e.tensor_tensor(out=lidt[:], in0=li[:], in1=dt_[:], op=ALU.mult), reads=[b_li, b_dt], writes=[b_lidt])
                bc3 = lambda t: t[:].unsqueeze(1).broadcast_to([128, NM, 64])
                Et, b_E = T3("Et")
                mag, b_mag = T3("mag")
                Ang, b_Ang = T3("Ang")
                c.op("dve", lambda e: e.tensor_tensor(out=Et[:], in0=Mtab[:], in1=bc3(lrdt), op=ALU.mult), reads=[b_Mt, b_lrdt], writes=[b_E])
                c.op("act", lambda e: e.activation(out=mag[:], in_=Et[:], func=AF.Exp), reads=[b_E], writes=[b_mag])
                c.op("dve", lambda e: e.tensor_tensor(out=Ang[:], in0=Mtab[:], in1=bc3(lidt), op=ALU.mult), reads=[b_Mt, b_lidt], writes=[b_Ang])
                kq, b_kq = T3("kq")
                ki_ = sbt(s0a, "ki", [128, NM, 64], I32); b_ki = Buf()
                kf, b_kf = T3("kf")
                yv, b_y = T3("yv")
                tw, b_tw = T3("tw")
                C1 = 6.28125
                C2 = float(2.0 * math.pi - 6.28125)
                PI = float(math.pi)
                TWO_PI = float(2.0 * math.pi)
                c.op("dve", lambda e: e.tensor_scalar(out=kq[:], in0=Ang[:], scalar1=float(1.0 / TWO_PI), scalar2=None, op0=ALU.mult), reads=[b_Ang], writes=[b_kq])
                c.op("dve", lambda e: e.tensor_copy(out=ki_[:], in_=kq[:]), reads=[b_kq], writes=[b_ki])
                c.op("dve", lambda e: e.tensor_copy(out=kf[:], in_=ki_[:]), reads=[b_ki], writes=[b_kf])
                c.op("dve", lambda e: e.scalar_tensor_tensor(out=yv[:], in0=kf[:], scalar=-C1, in1=Ang[:], op0=ALU.mult, op1=ALU.add), reads=[b_kf, b_Ang], writes=[b_y])
                c.op("dve", lambda e: e.scalar_tensor_tensor(out=yv[:], in0=kf[:], scalar=-C2, in1=yv[:], op0=ALU.mult, op1=ALU.add), reads=[b_kf, b_y], writes=[b_y])

                def wrap(t, b_t):
                    c.op("dve", lambda e: e.tensor_scalar(out=tw[:], in0=t[:], scalar1=-PI, scalar2=TWO_PI, op0=ALU.is_lt, op1=ALU.mult), reads=[b_t], writes=[b_tw])
                    c.op("dve", lambda e: e.tensor_tensor(out=t[:], in0=t[:], in1=tw[:], op=ALU.add), reads=[b_t, b_tw], writes=[b_t])
                    c.op("dve", lambda e: e.tensor_scalar(out=tw[:], in0=t[:], scalar1=PI, scalar2=-TWO_PI, op0=ALU.is_gt, op1=ALU.mult), reads=[b_t], writes=[b_tw])
                    c.op("dve", lambda e: e.tensor_tensor(out=t[:], in0=t[:], in1=tw[:], op=ALU.add), reads=[b_t, b_tw], writes=[b_t])
                wrap(yv, b_y)
                wrap(yv, b_y)
                sn, b_sn = T3("sn")
                cs, b_cs = T3("cs")
                yc, b_yc = T3("yc")
                c.op("act", lambda e: e.activation(out=sn[:], in_=yv[:], func=AF.Sin), reads=[b_y], writes=[b_sn])
                c.op("dve", lambda e: e.tensor_scalar(out=yc[:], in0=yv[:], scalar1=float(PI / 2), scalar2=None, op0=ALU.add), reads=[b_y], writes=[b_yc])
                wrap(yc, b_yc)
                c.op("act", lambda e: e.activation(out=cs[:], in_=yc[:], func=AF.Sin), reads=[b_yc], writes=[b_cs])
                c.op("dve", lambda e: e.tensor_tensor(out=Pr[:], in0=mag[:], in1=cs[:], op=ALU.mult), reads=[b_mag, b_cs], writes=[b_Pr])
                c.op("dve", lambda e: e.tensor_tensor(out=Pi[:], in0=mag[:], in1=sn[:], op=ALU.mult), reads=[b_mag, b_sn], writes=[b_Pi])
                c.barrier()
                s0a.__exit__(None, None, None)
                J1 = 9
                t64 = lambda name: (sbt(s0, name, [128, 64], F32), Buf(name))
                nr, b_nr = t64("nr"); den, b_den = t64("den"); t1, b_t1 = t64("t1"); t2, b_t2 = t64("t2")
                kr, b_kr = t64("kr"); kim, b_kim = t64("kim"); rden, b_rden = t64("rden")
                c.op("dve", lambda e: e.tensor_scalar(out=nr[:], in0=Pr[:, J1, :], scalar1=-1.0, scalar2=None, op0=ALU.add), reads=[b_Pr], writes=[b_nr])
                c.op("dve", lambda e: e.tensor_tensor(out=t1[:], in0=lr[:], in1=lr[:], op=ALU.mult), reads=[b_lr], writes=[b_t1])
                c.op("dve", lambda e: e.tensor_tensor(out=t2[:], in0=li[:], in1=li[:], op=ALU.mult), reads=[b_li], writes=[b_t2])
                c.op("dve", lambda e: e.tensor_tensor(out=den[:], in0=t1[:], in1=t2[:], op=ALU.add), reads=[b_t1, b_t2], writes=[b_den])
                c.op("dve", lambda e: e.reciprocal(out=rden[:], in_=den[:]), reads=[b_den], writes=[b_rden])
                c.op("dve", lambda e: e.tensor_tensor(out=t1[:], in0=nr[:], in1=lr[:], op=ALU.mult), reads=[b_nr, b_lr, b_den], writes=[b_t1])
                c.op("dve", lambda e: e.tensor_tensor(out=t2[:], in0=Pi[:, J1, :], in1=li[:], op=ALU.mult), reads=[b_Pi, b_li, b_den], writes=[b_t2])
                c.op("dve", lambda e: e.tensor_tensor(out=t1[:], in0=t1[:], in1=t2[:], op=ALU.add), reads=[b_t1, b_t2], writes=[b_t1])
                c.op("dve", lambda e: e.tensor_tensor(out=kr[:], in0=t1[:], in1=rden[:], op=ALU.mult), reads=[b_t1, b_rden], writes=[b_kr])
                c.op("dve", lambda e: e.tensor_tensor(out=t1[:], in0=Pi[:, J1, :], in1=lr[:], op=ALU.mult), reads=[b_Pi, b_lr, b_kr], writes=[b_t1])
                c.op("dve", lambda e: e.tensor_tensor(out=t2[:], in0=nr[:], in1=li[:], op=ALU.mult), reads=[b_nr, b_li, b_kr], writes=[b_t2])
                c.op("dve", lambda e: e.tensor_tensor(out=t1[:], in0=t1[:], in1=t2[:], op=ALU.subtract), reads=[b_t1, b_t2], writes=[b_t1])
                c.op("dve", lambda e: e.tensor_tensor(out=kim[:], in0=t1[:], in1=rden[:], op=ALU.mult), reads=[b_t1, b_rden], writes=[b_kim])
                Qr = sbt(s0, "Qr", [128, 8, 64], F32); b_Qr = Buf()
                Qi = sbt(s0, "Qi", [128, 8, 64], F32); b_Qi = Buf()
                q1 = sbt(s0, "q1", [128, 8, 64], F32); b_q1 = Buf()
                bc8 = lambda t: t[:].unsqueeze(1).broadcast_to([128, 8, 64])
                c.op("dve", lambda e: e.tensor_tensor(out=Qr[:], in0=Pr[:, 0:8, :], in1=bc8(kr), op=ALU.mult), reads=[b_Pr, b_kr], writes=[b_Qr])
                c.op("dve", lambda e: e.tensor_tensor(out=q1[:], in0=Pi[:, 0:8, :], in1=bc8(kim), op=ALU.mult), reads=[b_Pi, b_kim], writes=[b_q1])
                c.op("dve", lambda e: e.tensor_tensor(out=Qr[:], in0=Qr[:], in1=q1[:], op=ALU.subtract), reads=[b_Qr, b_q1], writes=[b_Qr])
                c.op("dve", lambda e: e.tensor_tensor(out=Qi[:], in0=Pi[:, 0:8, :], in1=bc8(kr), op=ALU.mult), reads=[b_Pi, b_kr], writes=[b_Qi])
                c.op("dve", lambda e: e.tensor_tensor(out=q1[:], in0=Pr[:, 0:8, :], in1=bc8(kim), op=ALU.mult), reads=[b_Pr, b_kim, b_Qr], writes=[b_q1])
                c.op("dve", lambda e: e.tensor_tensor(out=Qi[:], in0=Qi[:], in1=q1[:], op=ALU.add), reads=[b_Qi, b_q1], writes=[b_Qi])
                c.op("dve", lambda e: e.tensor_scalar(out=Qi[:], in0=Qi[:], scalar1=sgn[:, 0:1], scalar2=None, op0=ALU.mult), reads=[b_Qi, b_sgn], writes=[b_Qi])
                SIDX = [16, 24, 25, 26, 27, 28, 29, 30]
                for k, ix in enumerate(SIDX):
                    c.op("pool", lambda e, k=k, ix=ix: e.tensor_copy(out=PA[:, k, :], in_=Pr[:, ix, :]), reads=[b_Pr], writes=[b_PA])
                    c.op("dve", lambda e, k=k, ix=ix: e.tensor_scalar(out=PB[:, k, :], in0=Pi[:, ix, :], scalar1=sgn2[:, 0:1], scalar2=None, op0=ALU.mult), reads=[b_Pi, b_sgn2], writes=[b_PB])
                Prs = sbt(s0, "Prs", [128, 16, 64], F32); b_Prs = Buf()
                c.op("dve", lambda e: e.tensor_scalar(out=Prs[:], in0=Pr[:, 8:24, :], scalar1=sgn2[:, 0:1], scalar2=None, op0=ALU.mult), reads=[b_Pr, b_sgn2], writes=[b_Prs])
                X = sbt(s0, "X", [128, 32, 8, 16], F32); b_X = Buf()
                X2 = sbt(s0, "X2", [128, 32, 8, 16], F32); b_X2 = Buf()
                YY = sbt(s0, "YY", [128, 32, 16, 16], F32); b_YY = Buf()
                Y2 = sbt(s0, "Y2", [128, 32, 16, 16], F32); b_Y2 = Buf()
                for hg in range(2):
                    gs = slice(hg * 32, hg * 32 + 32)
                    qv = lambda t: t[:, :, gs].rearrange("p j g -> p g j").unsqueeze(3).broadcast_to([128, 32, 8, 16])
                    bv = lambda t: t[:, gs, :].unsqueeze(2).broadcast_to([128, 32, 8, 16])
                    pv = lambda ap: ap.rearrange("p m g -> p g m").unsqueeze(3).broadcast_to([128, 32, 16, 16])
                    cv = lambda t: t[:, gs, :].unsqueeze(2).broadcast_to([128, 32, 16, 16])
                    c.op("dve", lambda e: e.tensor_tensor(out=X[:], in0=qv(Qr), in1=bv(BAt), op=ALU.mult), reads=[b_Qr, b_BA], writes=[b_X])
                    c.op("pool", lambda e: e.tensor_tensor(out=X2[:], in0=qv(Qi), in1=bv(BBt), op=ALU.mult), reads=[b_Qi, b_BB], writes=[b_X2])
                    c.op("dve", lambda e: e.tensor_tensor(out=X[:], in0=X[:], in1=X2[:], op=ALU.add), reads=[b_X, b_X2], writes=[b_X])
                    c.op("dve", lambda e: e.tensor_tensor(out=YY[:], in0=pv(Prs[:, :, gs]), in1=cv(CAt), op=ALU.mult), reads=[b_Prs, b_CA], writes=[b_YY])
                    c.op("pool", lambda e: e.tensor_tensor(out=Y2[:], in0=pv(Pi[:, 8:24, gs]), in1=cv(CBt), op=ALU.mult), reads=[b_Pi, b_CB], writes=[b_Y2])
                    c.op("dve", lambda e: e.tensor_tensor(out=YY[:], in0=YY[:], in1=Y2[:], op=ALU.subtract), reads=[b_YY, b_Y2], writes=[b_YY])
                    c.op("act", lambda e: e.copy(out=Mout[:, gs, :].rearrange("p g (m o) -> p g m o", m=8), in_=YY[:, :, 8:16, :]), reads=[b_YY], writes=[b_Mout])
                    for q4 in range(8):
                        pa, pb = q4 % 2, 2 + (q4 % 2)
                        G0 = hg * 32 + q4 * 4
                        for gg in range(4):
                            g = q4 * 4 + gg
                            c.op("pe", lambda e, g=g, gg=gg, pa=pa: e.matmul(PS[pa][:, gg * 128:(gg + 1) * 128],
                                                                             lhsT=X[:, g, :, :].rearrange("p j c -> p (j c)"),
                                                                             rhs=YY[:, g, 0:8, :].rearrange("p m o -> p (m o)"),
                                                                             start=True, stop=True),
                                 reads=[b_X, b_YY], writes=[PSB[pa]], inc=(gg == 3))
                        c.op("dve", lambda e, G0=G0, pa=pa: e.tensor_tensor(out=Tm[:, G0:G0 + 4, :],
                                                                            in0=PS[pa][:].rearrange("p (g n) -> p g n", g=4),
                                                                            in1=cmask[:].unsqueeze(1).broadcast_to([128, 4, 128]), op=ALU.mult),
                             reads=[PSB[pa], b_cm], writes=[b_Tm])
                        for gg in range(4):
                            g = q4 * 4 + gg
                            c.op("pe", lambda e, g=g, gg=gg, pb=pb: e.transpose(out=PS[pb][:, gg * 128:(gg + 1) * 128],
                                                                                in_=X[:, g, :, :].rearrange("p j c -> p (j c)"),
                                                                                identity=identf[:]),
                                 reads=[b_X, b_identf], writes=[PSB[pb]], inc=(gg == 3))
                        c.op("act", lambda e, G0=G0, pb=pb: e.copy(out=MinT[:, G0:G0 + 4, :],
                                                                   in_=PS[pb][:].rearrange("p (g n) -> p g n", g=4)),
                             reads=[PSB[pb]], writes=[b_MinT])
                if stop_after == "P":
                    ds = c.dma_sem("dbg")
                    c.dma("sp", dbgP[:, 0:NM, :], Pr[:], reads=[b_Pr], sem=ds)
                    c.dma("sp", dbgP[:, NM:2 * NM, :], Pi[:], reads=[b_Pi], sem=ds)
                    c.dma("sp", dbgP[:, 2 * NM, :], kr[:], reads=[b_kr], sem=ds)
                    c.dma("sp", dbgP[:, 2 * NM + 1, :], kim[:], reads=[b_kim], sem=ds)
                    c.dma("sp", dbgT[:, 0, :, :], Tm[:], reads=[b_Tm], sem=ds)
                    c.dma("sp", dbgT[:, 1, :, :], MinT[:], reads=[b_MinT], sem=ds)
                    c.dma("sp", dbgT[:, 2, :, :], Mout[:], reads=[b_Mout], sem=ds)
                    nc.sync.wait_ge(ds[0], ds[1])
                    return nc
                c.barrier()
            with contextlib.ExitStack() as s1:
                xv = x_d.rearrange("(ct ch j) d -> ch ct j d", ct=4, ch=128, j=8)
                Gv = G_d.rearrange("(ct ch j) d -> ch ct j d", ct=4, ch=128, j=8)
                NXB = 2
                XB = [sbt(s1, f"XB{i}", [128, 4, 8, 128], F32) for i in range(NXB)]; b_XB = [Buf() for _ in range(NXB)]
                xsem = [c.dma_sem("xs") for _ in range(NXB)]
                XR = [sbt(s1, "XR0", [128, 4, 8, 128], BF16)] * NXB; b_XR = [Buf()] * NXB
                U = [sbt(s1, "U0", [128, 8, 512], BF16)] * NXB; b_U = [Buf()] * NXB
                YB = [sbt(s1, "YB0", [128, 4, 8, 128], F32)] * NXB; b_YB = [Buf()] * NXB
                GB = [sbt(s1, "GB0", [128, 4, 8, 128], BF16)] * NXB; b_GB = [Buf()] * NXB
                gsem = [c.dma_sem("gs")] * NXB
                Hr = [[sbt(s1, f"Hr{i}_{k}", [128, 512], BF16) for k in range(2)] for i in range(4)]
                b_Hr = [[Buf() for k in range(2)] for i in range(4)]
                Hi = [[sbt(s1, f"Hi{i}_{k}", [128, 512], BF16) for k in range(2)] for i in range(4)]
                b_Hi = [[Buf() for k in range(2)] for i in range(4)]
                Hp = [sbt(s1, f"Hp{i}", [128, 2, 256], BF16) for i in range(4)]; b_Hp = [Buf() for _ in range(4)]
                Ysb = [sbt(s1, f"Ysb{i}", [128, 512], F32) for i in range(4)]; b_Ysb = [Buf() for _ in range(4)]
                for i in range(4):
                    c.op("pool", lambda e, i=i: e.memset(Hp[i][:], 0.0), writes=[b_Hp[i]])
                g_ev = []
                for gb in range(8):
                    s = gb % NXB
                    for ct in range(4):
                        c.dma("sp" if ct % 2 == 0 else "act", XB[s][:, ct, :, :], xv[:, ct, :, gb * 128:(gb + 1) * 128],
                              writes=[b_XB[s]], sem=xsem[s])
                    for ct in range(4):
                        c.op("pool", lambda e, ct=ct, s=s: e.tensor_copy(
                            out=XR[s][:, ct, :, :].rearrange("p gl (j c) -> p gl j c", j=8),
                            in_=XB[s][:, ct, :, :].rearrange("p j (gl c) -> p gl j c", c=16)),
                             reads=[b_XB[s]], writes=[b_XR[s]])
                    for gl in range(8):
                        pb = 6 + (gl % 2)
                        psb16 = PS[pb][:].bitcast(BF16)
                        for ct in range(4):
                            c.op("pe", lambda e, ct=ct, gl=gl, s=s, psb16=psb16: e.transpose(
                                out=psb16[:, ct * 128:(ct + 1) * 128], in_=XR[s][:, ct, gl, :], identity=identb[:]),
                                 reads=[b_XR[s], b_identb], writes=[PSB[pb]], inc=(ct == 3))
                        c.op("act", lambda e, gl=gl, s=s, psb16=psb16: e.copy(out=U[s][:, gl, :], in_=psb16[:, 0:512]),
                             reads=[PSB[pb]], writes=[b_U[s]])
                    for quad in range(2):
                        for gg in range(4):
                            gl = quad * 4 + gg
                            g = gb * 8 + gl
                            c.op("pe", lambda e, g=g, gl=gl, gg=gg, s=s: e.matmul(PS[gg][:], lhsT=MinT[:, g, :], rhs=U[s][:, gl, :], start=True, stop=True),
                                 reads=[b_MinT, b_U[s]], writes=[PSB[gg]])
                        for k in range(8):
                            sh = 1 << k
                            for gg in range(4):
                                g = gb * 8 + quad * 4 + gg
                                c.op("act", lambda e, gg=gg, k=k, g=g: e.activation(out=Hr[gg][k % 2][:], in_=PS[gg][:], func=AF.Identity, scale=PA[:, k, g:g + 1]),
                                     reads=[PSB[gg], b_PA], writes=[b_Hr[gg][k % 2]])
                                c.op("dve", lambda e, gg=gg, k=k, g=g: e.tensor_scalar(out=Hi[gg][k % 2][:], in0=PS[gg][:], scalar1=PB[:, k, g:g + 1], scalar2=None, op0=ALU.mult),
                                     reads=[PSB[gg], b_PB], writes=[b_Hi[gg][k % 2]])
                            for gg in range(4):
                                for (Dm, b_Dm, Hx, b_Hx, last) in ((D1b, b_D1, Hr, b_Hr, False), (D2b, b_D2, Hi, b_Hi, True)):
                                    for sq in range(2):
                                        c.op("pe", lambda e, gg=gg, k=k, sh=sh, sq=sq, Dm=Dm, Hx=Hx: e.matmul(
                                            PS[gg][:, sq * 256 + sh:(sq + 1) * 256],
                                            lhsT=Dm[:],
                                            rhs=Hx[gg][k % 2][:, sq * 256:(sq + 1) * 256 - sh],
                                            start=False, stop=True, skip_group_check=True),
                                             reads=[b_Dm, b_Hx[gg][k % 2]], writes=[PSB[gg]], inc=(last and sq == 1))
                        for gg in range(4):
                            eng = "act" if gg % 2 == 0 else "dve"
                            src = PS[gg][:].rearrange("p (s n) -> p s n", s=2)[:, :, 0:255]
                            if eng == "act":
                                c.op("act", lambda e, gg=gg, src=src: e.copy(out=Hp[gg][:, :, 1:256], in_=src), reads=[PSB[gg]], writes=[b_Hp[gg]])
                            else:
                                c.op("dve", lambda e, gg=gg, src=src: e.tensor_copy(out=Hp[gg][:, :, 1:256], in_=src), reads=[PSB[gg]], writes=[b_Hp[gg]])
                        for gg in range(4):
                            gl = quad * 4 + gg
                            g = gb * 8 + gl
                            pb = 4 + (gg % 2)
                            c.op("pe", lambda e, g=g, gl=gl, pb=pb, s=s: e.matmul(PS[pb][:], lhsT=Tm[:, g, :], rhs=U[s][:, gl, :], start=True, stop=False),
                                 reads=[b_Tm, b_U[s]], writes=[PSB[pb]], inc=False)
                            c.op("pe", lambda e, g=g, gg=gg, pb=pb: e.matmul(PS[pb][:], lhsT=Mout[:, g, :], rhs=Hp[gg][:].rearrange("p s n -> p (s n)"), start=False, stop=True),
                                 reads=[b_Mout, b_Hp[gg]], writes=[PSB[pb]])
                            if gg % 2 == 0:
                                c.op("dve", lambda e, gg=gg, pb=pb: e.tensor_copy(out=Ysb[gg][:], in_=PS[pb][:]), reads=[PSB[pb]], writes=[b_Ysb[gg]])
                            else:
                                c.op("act", lambda e, gg=gg, pb=pb: e.copy(out=Ysb[gg][:], in_=PS[pb][:]), reads=[PSB[pb]], writes=[b_Ysb[gg]])
                            pt = 6 + (gg % 2)
                            for ct in range(4):
                                c.op("pe", lambda e, gg=gg, ct=ct, pt=pt: e.transpose(out=PS[pt][:, ct * 128:(ct + 1) * 128],
                                                                                      in_=Ysb[gg][:, ct * 128:(ct + 1) * 128], identity=identf[:]),
                                     reads=[b_Ysb[gg], b_identf], writes=[PSB[pt]], inc=(ct == 3))
                            ydst = YB[s][:, :, :, gl * 16:(gl + 1) * 16]
                            ysrc = PS[pt][:].rearrange("p (ct i o) -> p ct i o", ct=4, i=8)
                            if gg % 2 == 0:
                                c.op("act", lambda e, ydst=ydst, ysrc=ysrc: e.copy(out=ydst, in_=ysrc), reads=[PSB[pt]], writes=[b_YB[s]])
                            else:
                                c.op("dve", lambda e, ydst=ydst, ysrc=ysrc: e.tensor_copy(out=ydst, in_=ysrc), reads=[PSB[pt]], writes=[b_YB[s]])
                    xbv = XB[s][:].rearrange("p ct j d -> p (ct j) d")
                    ybv = YB[s][:].rearrange("p ct j d -> p (ct j) d")
                    dbv = dB[:, gb * 128:(gb + 1) * 128].unsqueeze(1).broadcast_to([128, 32, 128])
                    c.op("pool", lambda e, xbv=xbv, dbv=dbv: e.tensor_tensor(out=xbv, in0=xbv, in1=dbv, op=ALU.mult),
                         reads=[b_XB[s], b_dB], writes=[b_XB[s]])
                    c.op("pool", lambda e, xbv=xbv, ybv=ybv: e.tensor_tensor(out=ybv, in0=ybv, in1=xbv, op=ALU.add),
                         reads=[b_XB[s], b_YB[s]], writes=[b_YB[s]])
                    c.op("act", lambda e, s=s: e.activation(out=GB[s][:], in_=YB[s][:], func=AF.Gelu_apprx_tanh),
                         reads=[b_YB[s]], writes=[b_GB[s]])
                    for ct in range(4):
                        g_ev.append(c.dma("sp", Gv[:, ct, :, gb * 128:(gb + 1) * 128], GB[s][:, ct, :, :], reads=[b_GB[s]], sem=gsem[s]))
                c.barrier()
        if stop_after == "A":
            nc.sync.wait_ge(gsem[0][0], gsem[0][1])
            return nc
        class Ring:
            def __init__(self, stack, name, n, shape, dt):
                self.t = [sbt(stack, f"{name}{i}", shape, dt) for i in range(n)]
                self.b = [Buf(f"{name}{i}") for i in range(n)]
                self.i = 0
                self.n = n

            def next(self):
                k = self.i % self.n
                self.i += 1
                return self.t[k], self.b[k]

        bank_ctr = [0]

        def nbank():
            k = bank_ctr[0] % 8
            bank_ctr[0] += 1
            return k

        def alloc_weight(stack, name, src, K, N, queue="pool"):
            kc = K // 128
            t = sbt(stack, name, [128, kc, N], BF16)
            b = Buf(name)
            sem = c.dma_sem(name)
            srcv = src.rearrange("(kc p) n -> p kc n", p=128)

            def issue():
                for k in range(kc):
                    for n0 in range(0, N, 2048):
                        n1 = min(N, n0 + 2048)
                        c.dma(queue, t[:, k, n0:n1], srcv[:, k, n0:n1], writes=[b], sem=sem)
                        yield
            return t, b, issue

        def load_weight(stack, name, src, K, N, queue="pool"):
            t, b, issue = alloc_weight(stack, name, src, K, N, queue)
            for _ in issue():
                pass
            return t, b

        def load_ln(stack, idx):
            g = sbt(stack, f"lng{idx}", [128, D], F32); bg = Buf()
            bt = sbt(stack, f"lnb{idx}", [128, D], F32); bb = Buf()
            sem = c.dma_sem("ln")
            c.dma("sp", g[:], lnG_d[:, idx, :], writes=[bg], sem=sem)
            c.dma("sp", bt[:], lnB_d[:, idx, :], writes=[bb], sem=sem)
            return g, bg, bt, bb

        def layernorm(stack_rings, r, b_r, lng, b_lng, lnb, b_lnb):
            stats, b_st = stack_rings["stats"].next()
            mv, b_mv = stack_rings["mv"].next()
            sd, b_sd = stack_rings["sd"].next()
            for hh in range(2):
                c.op("dve", lambda e, hh=hh: e.bn_stats(out=stats[:, hh, :], in_=r[:, hh * 512:(hh + 1) * 512]), reads=[b_r], writes=[b_st])
            c.op("dve", lambda e: e.bn_aggr(out=mv[:], in_=stats[:].rearrange("p a b -> p (a b)")), reads=[b_st], writes=[b_mv])
            c.op("dve", lambda e: e.tensor_scalar(out=sd[:, 0:1], in0=mv[:, 1:2], scalar1=float(LN_EPS), scalar2=None, op0=ALU.add), reads=[b_mv], writes=[b_sd])
            c.op("pool", lambda e: e.tensor_tensor(out=sd[:, 2:3], in0=sd[:, 0:1], in1=mhalf[:, 0:1], op=ALU.pow), reads=[b_sd, b_mhalf], writes=[b_sd])
            c.op("dve", lambda e: e.tensor_scalar(out=sd[:, 3:4], in0=mv[:, 0:1], scalar1=sd[:, 2:3], scalar2=-1.0, op0=ALU.mult, op1=ALU.mult), reads=[b_mv, b_sd], writes=[b_sd])
            c.op("dve", lambda e: e.tensor_scalar(out=r[:], in0=r[:], scalar1=sd[:, 2:3], scalar2=sd[:, 3:4], op0=ALU.mult, op1=ALU.add), reads=[b_r, b_sd], writes=[b_r])
            c.op("pool", lambda e: e.tensor_tensor(out=r[:], in0=r[:], in1=lng[:], op=ALU.mult), reads=[b_r, b_lng], writes=[b_r])
            c.op("dve", lambda e: e.tensor_tensor(out=r[:], in0=r[:], in1=lnb[:], op=ALU.add), reads=[b_r, b_lnb], writes=[b_r])

        def ln_rings(stack):
            return {"stats": Ring(stack, "lnst", 3, [128, 2, 6], F32), "mv": Ring(stack, "lnmv", 3, [128, 2], F32),
                    "sd": Ring(stack, "lnsd", 3, [128, 4], F32)}

        def transpose_tile(src, b_src, dst_fn, b_dst, evac_eng="act"):
            pb = nbank()
            p16 = PS[pb][:].bitcast(BF16)
            for kc in range(8):
                c.op("pe", lambda e, kc=kc: e.transpose(out=p16[:, kc * 128:(kc + 1) * 128], in_=src[:, kc * 128:(kc + 1) * 128], identity=identb[:]),
                     reads=[b_src, b_identb], writes=[PSB[pb]], inc=(kc == 7))
            dst = dst_fn()
            if evac_eng == "act":
                c.op("act", lambda e: e.copy(out=dst, in_=p16[:].rearrange("p (k t) -> p k t", k=8)), reads=[PSB[pb]], writes=[b_dst])
            else:
                c.op("dve", lambda e: e.tensor_copy(out=dst, in_=p16[:].rearrange("p (k t) -> p k t", k=8)), reads=[PSB[pb]], writes=[b_dst])

        sBC = contextlib.ExitStack()
        sBC.__enter__()
        Wg, b_Wg, issue_Wg = alloc_weight(sBC, "Wg", fg_d, D, FF)
        Wu, b_Wu, issue_Wu = alloc_weight(sBC, "Wu", fu_d, D, FF)
        Wd, b_Wd, issue_Wd = alloc_weight(sBC, "Wd", fd_d, FF, D)
        with contextlib.ExitStack() as sB:
            Wglu, b_Wglu = load_weight(sB, "Wglu", wglu_d, D, 2 * D)
            import itertools
            ffn_dmas = itertools.chain(issue_Wg(), issue_Wu(), issue_Wd())
            lng, b_lng, lnb, b_lnb = load_ln(sB, 0)
            rings = ln_rings(sB)
            gt_r = Ring(sB, "gt", 2, [128, D], BF16); gsemB = [c.dma_sem("gB") for _ in range(2)]
            xt_r = Ring(sB, "xt", 2, [128, D], F32); xsemB = [c.dma_sem("xB") for _ in range(3)]
            gT_r = Ring(sB, "gT", 2, [128, 8, 128], BF16)
            sg_r = Ring(sB, "sg", 2, [128, D], F32)
            r_r = Ring(sB, "rB", 2, [128, D], F32); osemB = [c.dma_sem("oB") for _ in range(3)]
            def B_stage1(t):
                gt, b_gt = gt_r.next(); xt, b_xt = xt_r.next(); gT, b_gT = gT_r.next(); sg, b_sg = sg_r.next()
                c.dma("sp", gt[:], G_d[t * 128:(t + 1) * 128, :], writes=[b_gt], sem=gsemB[t % 2])
                c.dma("act", xt[:], x_d[t * 128:(t + 1) * 128, :], writes=[b_xt], sem=xsemB[t % 3])
                transpose_tile(gt, b_gt, lambda: gT[:], b_gT)
                banks = [nbank() for _ in range(4)]
                for nb_ in range(4):
                    for kc in range(8):
                        c.op("pe", lambda e, nb_=nb_, kc=kc: e.matmul(PS[banks[nb_]][:], lhsT=gT[:, kc, :], rhs=Wglu[:, kc, nb_ * 512:(nb_ + 1) * 512],
                                                                  start=(kc == 0), stop=(kc == 7)),
                             reads=[b_gT, b_Wglu], writes=[PSB[banks[nb_]]], inc=(kc == 7))
                for hh in range(2):
                    c.op("act", lambda e, hh=hh: e.activation(out=sg[:, hh * 512:(hh + 1) * 512], in_=PS[banks[2 + hh]][:], func=AF.Sigmoid),
                         reads=[PSB[banks[2 + hh]]], writes=[b_sg])
                    c.op("dve", lambda e, hh=hh: e.tensor_tensor(out=sg[:, hh * 512:(hh + 1) * 512], in0=PS[banks[hh]][:], in1=sg[:, hh * 512:(hh + 1) * 512], op=ALU.mult),
                         reads=[PSB[banks[hh]], b_sg], writes=[b_sg])
                return (t, xt, b_xt, sg, b_sg)

            def B_stage2(ctx):
                t, xt, b_xt, sg, b_sg = ctx
                r, b_r = r_r.next()
                c.op("dve", lambda e: e.scalar_tensor_tensor(out=r[:], in0=xt[:], scalar=ALPHA, in1=sg[:], op0=ALU.mult, op1=ALU.add),
                     reads=[b_xt, b_sg], writes=[b_r])
                layernorm(rings, r, b_r, lng, b_lng, lnb, b_lnb)
                c.dma("pool", H1_d[t * 128:(t + 1) * 128, :], r[:], reads=[b_r], sem=osemB[t % 3])

            pend = None
            for t in range(NT):
                ctx = B_stage1(t)
                for _ in range(2):
                    next(ffn_dmas, None)
                if pend is not None:
                    B_stage2(pend)
                pend = ctx
            B_stage2(pend)
            for _ in ffn_dmas:
                pass
            c.barrier()
        if stop_after == "B":
            sBC.__exit__(None, None, None)
            return nc

        with contextlib.ExitStack() as sC:
            lng, b_lng, lnb, b_lnb = load_ln(sC, 1)
            rings = ln_rings(sC)
            ht_r = Ring(sC, "htC", 2, [128, D], BF16); hsemC = [c.dma_sem("hC") for _ in range(2)]
            hres_r = Ring(sC, "hresC", 2, [128, D], F32); rsemC = [c.dma_sem("rC") for _ in range(2)]
            hT = sbt(sC, "hTC", [128, 8, 512], BF16); b_hT = Buf()
            h1T = sbt(sC, "h1TC", [128, FF // 128, 512], BF16); b_h1T = Buf()
            s_r = Ring(sC, "sC", 2, [128, 512], BF16)
            r_r = Ring(sC, "rC", 2, [128, D], F32); osemC = [c.dma_sem("oC") for _ in range(2)]
            for st_ in range(TOK // 512):
                for sub in range(4):
                    t = st_ * 4 + sub
                    ht, b_ht = ht_r.next()
                    c.dma("pool", ht[:], H1_d[t * 128:(t + 1) * 128, :], writes=[b_ht], sem=hsemC[t % 2])
                    transpose_tile(ht, b_ht, lambda sub=sub: hT[:, :, sub * 128:(sub + 1) * 128], b_hT, evac_eng="act" if sub % 2 == 0 else "dve")
                for fc in range(FF // 128):
                    pg, pu = nbank(), nbank()
                    for (W, bW, pb) in ((Wg, b_Wg, pg), (Wu, b_Wu, pu)):
                        for kc in range(8):
                            c.op("pe", lambda e, W=W, pb=pb, kc=kc, fc=fc: e.matmul(PS[pb][:], lhsT=W[:, kc, fc * 128:(fc + 1) * 128], rhs=hT[:, kc, :],
                                                                              start=(kc == 0), stop=(kc == 7)),
                                 reads=[bW, b_hT], writes=[PSB[pb]], inc=(kc == 7))
                    sb_, b_s = s_r.next()
                    c.op("act", lambda e, pg=pg, sb_=sb_: e.activation(out=sb_[:], in_=PS[pg][:], func=AF.Silu), reads=[PSB[pg]], writes=[b_s])
                    c.op("dve", lambda e, pu=pu, sb_=sb_, fc=fc: e.tensor_tensor(out=h1T[:, fc, :], in0=PS[pu][:], in1=sb_[:], op=ALU.mult),
                         reads=[PSB[pu], b_s], writes=[b_h1T])
                for sub in range(4):
                    t = st_ * 4 + sub
                    hres, b_hres = hres_r.next(); r, b_r = r_r.next()
                    c.dma("act", hres[:], H1_d[t * 128:(t + 1) * 128, :], writes=[b_hres], sem=rsemC[t % 2])
                    po = [nbank(), nbank()]
                    for nb_ in range(2):
                        for fc in range(FF // 128):
                            c.op("pe", lambda e, nb_=nb_, fc=fc, sub=sub: e.matmul(PS[po[nb_]][:], lhsT=h1T[:, fc, sub * 128:(sub + 1) * 128],
                                                                             rhs=Wd[:, fc, nb_ * 512:(nb_ + 1) * 512], start=(fc == 0), stop=(fc == FF // 128 - 1)),
                                 reads=[b_h1T, b_Wd], writes=[PSB[po[nb_]]], inc=(fc == FF // 128 - 1))
                        c.op("dve", lambda e, nb_=nb_: e.scalar_tensor_tensor(out=r[:, nb_ * 512:(nb_ + 1) * 512], in0=hres[:, nb_ * 512:(nb_ + 1) * 512], scalar=ALPHA,
                                                                         in1=PS[po[nb_]][:], op0=ALU.mult, op1=ALU.add),
                             reads=[b_hres, PSB[po[nb_]]], writes=[b_r])
                    layernorm(rings, r, b_r, lng, b_lng, lnb, b_lnb)
                    c.dma("sp", H2_d[t * 128:(t + 1) * 128, :], r[:], reads=[b_r], sem=osemC[t % 2])
            c.barrier()
        sBC.__exit__(None, None, None)
        if stop_after == "C":
            return nc

        with contextlib.ExitStack() as sE:
            Wq, b_Wq = load_weight(sE, "Wq", wq_d, D, D)
            Wkv, b_Wkv = load_weight(sE, "Wkv", kvw_d, D, 512)
            Wo, b_Wo = load_weight(sE, "Wo", wo_d, D, D)
            lng, b_lng, lnb, b_lnb = load_ln(sE, 2)
            rings = ln_rings(sE)
            semE = c.dma_sem("cE")
            posi = sbt(sE, "posi", [128, NT], I32); b_posi = Buf()
            invf = sbt(sE, "invf", [128, 8], F32); b_invf = Buf()
            sk = sbt(sE, "sk", [128, 16], F32); b_sk = Buf()
            c.dma("sp", posi[:], pos_d[:, :], writes=[b_posi], sem=semE)
            c.dma("sp", invf[:], invf_d[:, :], writes=[b_invf], sem=semE)
            c.dma("sp", sk[:], sink_d[:, :], writes=[b_sk], sem=semE)
            posf = sbt(sE, "posf", [128, NT], F32); b_posf = Buf()
            angE = sbt(sE, "angE", [128, NT, 8], F32); b_angE = Buf()
            kqE = sbt(sE, "kqE", [128, NT, 8], F32); b_kqE = Buf()
            kiE = sbt(sE, "kiE", [128, NT, 8], I32); b_kiE = Buf()
            twE = sbt(sE, "twE", [128, NT, 8], F32); b_twE = Buf()
            ycE = sbt(sE, "ycE", [128, NT, 8], F32); b_ycE = Buf()
            cosT = sbt(sE, "cosT", [128, NT, 8], F32); b_cosT = Buf()
            sinT = sbt(sE, "sinT", [128, NT, 8], F32); b_sinT = Buf()
            esk = sbt(sE, "esk", [128, 16], F32); b_esk = Buf()
            c.op("act", lambda e: e.activation(out=esk[:], in_=sk[:], func=AF.Exp), reads=[b_sk], writes=[b_esk])
            c.op("dve", lambda e: e.tensor_copy(out=posf[:], in_=posi[:]), reads=[b_posi], writes=[b_posf])
            c.op("dve", lambda e: e.tensor_tensor(out=angE[:], in0=posf[:].unsqueeze(2).broadcast_to([128, NT, 8]),
                                                  in1=invf[:].unsqueeze(1).broadcast_to([128, NT, 8]), op=ALU.mult), reads=[b_posf, b_invf], writes=[b_angE])
            c.op("dve", lambda e: e.tensor_scalar(out=kqE[:], in0=angE[:], scalar1=float(1.0 / (2 * math.pi)), scalar2=None, op0=ALU.mult), reads=[b_angE], writes=[b_kqE])
            c.op("dve", lambda e: e.tensor_copy(out=kiE[:], in_=kqE[:]), reads=[b_kqE], writes=[b_kiE])
            c.op("dve", lambda e: e.tensor_copy(out=kqE[:], in_=kiE[:]), reads=[b_kiE], writes=[b_kqE])
            c.op("dve", lambda e: e.scalar_tensor_tensor(out=angE[:], in0=kqE[:], scalar=-6.28125, in1=angE[:], op0=ALU.mult, op1=ALU.add), reads=[b_kqE, b_angE], writes=[b_angE])
            c.op("dve", lambda e: e.scalar_tensor_tensor(out=angE[:], in0=kqE[:], scalar=-float(2.0 * math.pi - 6.28125), in1=angE[:], op0=ALU.mult, op1=ALU.add), reads=[b_kqE, b_angE], writes=[b_angE])

            def wrapE(t, b_t):
                PI_ = float(math.pi)
                c.op("dve", lambda e: e.tensor_scalar(out=twE[:], in0=t[:], scalar1=-PI_, scalar2=2 * PI_, op0=ALU.is_lt, op1=ALU.mult), reads=[b_t], writes=[b_twE])
                c.op("dve", lambda e: e.tensor_tensor(out=t[:], in0=t[:], in1=twE[:], op=ALU.add), reads=[b_t, b_twE], writes=[b_t])
                c.op("dve", lambda e: e.tensor_scalar(out=twE[:], in0=t[:], scalar1=PI_, scalar2=-2 * PI_, op0=ALU.is_gt, op1=ALU.mult), reads=[b_t], writes=[b_twE])
                c.op("dve", lambda e: e.tensor_tensor(out=t[:], in0=t[:], in1=twE[:], op=ALU.add), reads=[b_t, b_twE], writes=[b_t])
            wrapE(angE, b_angE)
            wrapE(angE, b_angE)
            c.op("act", lambda e: e.activation(out=sinT[:], in_=angE[:], func=AF.Sin), reads=[b_angE], writes=[b_sinT])
            c.op("dve", lambda e: e.tensor_scalar(out=ycE[:], in0=angE[:], scalar1=float(math.pi / 2), scalar2=None, op0=ALU.add), reads=[b_angE], writes=[b_ycE])
            wrapE(ycE, b_ycE)
            c.op("act", lambda e: e.activation(out=cosT[:], in_=ycE[:], func=AF.Sin), reads=[b_ycE], writes=[b_cosT])
            mprev = sbt(sE, "mprev", [128, 128], BF16); b_mprev = Buf()
            mcur = sbt(sE, "mcur", [128, 128], BF16); b_mcur = Buf()
            c.op("pool", lambda e: e.memset(mprev[:], 1.0), writes=[b_mprev])
            c.op("pool", lambda e: e.affine_select(out=mprev[:], in_=mprev[:], pattern=[[-1, 128]], compare_op=ALU.is_gt, fill=0.0, base=0, channel_multiplier=1),
                 reads=[b_mprev], writes=[b_mprev])
            c.op("pool", lambda e: e.memset(mcur[:], 1.0), writes=[b_mcur])
            c.op("pool", lambda e: e.affine_select(out=mcur[:], in_=mcur[:], pattern=[[1, 128]], compare_op=ALU.is_ge, fill=0.0, base=0, channel_multiplier=-1),
                 reads=[b_mcur], writes=[b_mcur])
            KT = [sbt(sE, f"KT{i}", [128, 8, 128], BF16) for i in range(3)]; b_KT = [Buf() for _ in range(3)]
            Ksp = [sbt(sE, f"Ksp{i}", [128, 4, 2, 128], BF16) for i in range(3)]; b_Ksp = [Buf() for _ in range(3)]
            for i in range(3):
                c.op("pool", lambda e, i=i: e.memset(Ksp[i][:], 0.0), writes=[b_Ksp[i]])
            Va = [sbt(sE, f"Va{i}", [128, 4, 65], BF16) for i in range(3)]; b_Va = [Buf() for _ in range(3)]
            for i in range(3):
                c.op("pool", lambda e, i=i: e.memset(Va[i][:], 1.0), writes=[b_Va[i]])
            ht_r = Ring(sE, "htE", 3, [128, D], BF16); hsemE = [c.dma_sem("hE") for _ in range(3)]
            hres_r = Ring(sE, "hresE", 4, [128, D], F32); rsemE = [c.dma_sem("rE") for _ in range(4)]
            hT_r = Ring(sE, "hTE", 2, [128, 8, 128], BF16)
            Qs_r = Ring(sE, "Qs", 2, [128, 16, 64], BF16)
            Ks_r = Ring(sE, "Ks", 2, [128, 4, 64], BF16)
            QT_r = Ring(sE, "QT", 3, [128, 8, 128], BF16)
            ro_r = Ring(sE, "ro", 4, [128, 8, 8], F32)
            E_r = Ring(sE, "E", 6, [128, 512], BF16)
            O_r = Ring(sE, "O", 3, [128, 16, 64], BF16)
            OT_r = Ring(sE, "OT", 2, [128, 8, 128], BF16)
            dn_r = Ring(sE, "dn", 4, [128, 8], F32)
            r_r = Ring(sE, "rE", 3, [128, D], F32); osemE = [c.dma_sem("oE") for _ in range(3)]

            def rope(psrc, nh, dst, b_dst, pbuf, t):
                cb = cosT[:, t, :].unsqueeze(1).broadcast_to([128, nh, 8])
                sb2 = sinT[:, t, :].unsqueeze(1).broadcast_to([128, nh, 8])
                q1 = psrc[:, :, 0:8]; q2 = psrc[:, :, 8:16]
                ta, b_ta = ro_r.next(); tb, b_tb = ro_r.next()
                c.op("dve", lambda e: e.tensor_tensor(out=ta[:, 0:nh, :], in0=q1, in1=cb, op=ALU.mult), reads=[pbuf, b_cosT], writes=[b_ta])
                c.op("dve", lambda e: e.tensor_tensor(out=tb[:, 0:nh, :], in0=q2, in1=sb2, op=ALU.mult), reads=[pbuf, b_sinT], writes=[b_tb])
                c.op("dve", lambda e: e.tensor_tensor(out=dst[:, :, 0:8], in0=ta[:, 0:nh, :], in1=tb[:, 0:nh, :], op=ALU.subtract), reads=[b_ta, b_tb], writes=[b_dst])
                tc_, b_tc = ro_r.next(); td, b_td = ro_r.next()
                c.op("dve", lambda e: e.tensor_tensor(out=tc_[:, 0:nh, :], in0=q2, in1=cb, op=ALU.mult), reads=[pbuf, b_cosT], writes=[b_tc])
                c.op("dve", lambda e: e.tensor_tensor(out=td[:, 0:nh, :], in0=q1, in1=sb2, op=ALU.mult), reads=[pbuf, b_sinT], writes=[b_td])
                c.op("dve", lambda e: e.tensor_tensor(out=dst[:, :, 8:16], in0=tc_[:, 0:nh, :], in1=td[:, 0:nh, :], op=ALU.add), reads=[b_tc, b_td], writes=[b_dst])

            def E_A1(t):
                ht, b_ht = ht_r.next(); hres, b_hres = hres_r.next(); hT, b_hT = hT_r.next()
                c.dma("pool", ht[:], H2_d[t * 128:(t + 1) * 128, :], writes=[b_ht], sem=hsemE[t % 3])
                c.dma("act", hres[:], H2_d[t * 128:(t + 1) * 128, :], writes=[b_hres], sem=rsemE[t % 4])
                transpose_tile(ht, b_ht, lambda: hT[:], b_hT)
                pq = [nbank(), nbank()]; pkv = nbank()
                for nb_ in range(2):
                    for kc in range(8):
                        c.op("pe", lambda e, nb_=nb_, kc=kc: e.matmul(PS[pq[nb_]][:], lhsT=hT[:, kc, :], rhs=Wq[:, kc, nb_ * 512:(nb_ + 1) * 512], start=(kc == 0), stop=(kc == 7)),
                             reads=[b_hT, b_Wq], writes=[PSB[pq[nb_]]], inc=(kc == 7))
                for kc in range(8):
                    c.op("pe", lambda e, kc=kc: e.matmul(PS[pkv][:], lhsT=hT[:, kc, :], rhs=Wkv[:, kc, :], start=(kc == 0), stop=(kc == 7)),
                         reads=[b_hT, b_Wkv], writes=[PSB[pkv]], inc=(kc == 7))
                return dict(t=t, nblk=t % 16, hres=hres, b_hres=b_hres, pq=pq, pkv=pkv)

            def E_A2a(cx):
                t = cx['t']; pq = cx['pq']; pkv = cx['pkv']
                Qs, b_Qs = Qs_r.next(); Ks, b_Ks = Ks_r.next()
                slot = t % 3
                for nb_ in range(2):
                    pv_ = PS[pq[nb_]][:].rearrange("p (h d) -> p h d", h=8)
                    c.op("act", lambda e, nb_=nb_, pv_=pv_: e.copy(out=Qs[:, nb_ * 8:(nb_ + 1) * 8, :], in_=pv_), reads=[PSB[pq[nb_]]], writes=[b_Qs])
                    rope(pv_, 8, Qs[:, nb_ * 8:(nb_ + 1) * 8, :], b_Qs, PSB[pq[nb_]], t)
                kvv = PS[pkv][:].rearrange("p (h d) -> p h d", h=8)
                c.op("act", lambda e: e.copy(out=Ks[:], in_=kvv[:, 0:4, :]), reads=[PSB[pkv]], writes=[b_Ks])
                rope(kvv[:, 0:4, :], 4, Ks[:], b_Ks, PSB[pkv], t)
                c.op("act", lambda e: e.copy(out=Va[slot][:, :, 0:64], in_=kvv[:, 4:8, :]), reads=[PSB[pkv]], writes=[b_Va[slot]])
                c.op("pool", lambda e: e.tensor_copy(out=Ksp[slot][:, :, 0, 0:64], in_=Ks[:]), reads=[b_Ks], writes=[b_Ksp[slot]])
                c.op("pool", lambda e: e.tensor_copy(out=Ksp[slot][:, :, 1, 64:128], in_=Ks[:]), reads=[b_Ks], writes=[b_Ksp[slot]])
                cx['Qs'] = Qs; cx['b_Qs'] = b_Qs

            def E_A2b(cx):
                t = cx['t']; Qs = cx['Qs']; b_Qs = cx['b_Qs']
                slot = t % 3
                QT, b_QT = QT_r.next()
                transpose_tile(Qs[:].rearrange("p h d -> p (h d)"), b_Qs, lambda: QT[:], b_QT, evac_eng="act")
                transpose_tile(Ksp[slot][:].rearrange("p g v d -> p (g v d)"), b_Ksp[slot], lambda: KT[slot][:], b_KT[slot], evac_eng="dve")
                cx['QT'] = QT; cx['b_QT'] = b_QT

            def E_B(cx):
                t = cx['t']; nblk = cx['nblk']; QT = cx['QT']; b_QT = cx['b_QT']
                O, b_O = O_r.next()
                kbs = ([((t - 1) % 3, mprev, b_mprev)] if nblk > 0 else []) + [(t % 3, mcur, b_mcur)]

                def scores(g):
                    Es = []
                    for (sl, mk, b_mk) in kbs:
                        ps_ = nbank()
                        for var in range(2):
                            c.op("pe", lambda e, ps_=ps_, sl=sl, g=g, var=var: e.matmul(PS[ps_][:, var * 256:(var + 1) * 256], lhsT=KT[sl][:, g * 2 + var, :],
                                                                                  rhs=QT[:, 2 * g:2 * g + 2, :].rearrange("p h t -> p (h t)"), start=True, stop=True),
                                 reads=[b_KT[sl], b_QT], writes=[PSB[ps_]], inc=(var == 1))
                        E, b_E = E_r.next()
                        c.op("act", lambda e, ps_=ps_, E=E: e.activation(out=E[:], in_=PS[ps_][:], func=AF.Exp, scale=0.125), reads=[PSB[ps_]], writes=[b_E])
                        c.op("pool", lambda e, E=E, mk=mk: e.tensor_tensor(out=E[:].rearrange("p (h t) -> p h t", h=4), in0=E[:].rearrange("p (h t) -> p h t", h=4),
                                                                       in1=mk[:].unsqueeze(1).broadcast_to([128, 4, 128]), op=ALU.mult), reads=[b_E, b_mk], writes=[b_E])
                        Es.append((E, b_E, sl))
                    return Es

                def pv(g, Es):
                    po_ = nbank()
                    for hl in range(4):
                        for i, (E, b_E, sl) in enumerate(Es):
                            c.op("pe", lambda e, po_=po_, hl=hl, E=E, sl=sl, g=g, i=i: e.matmul(PS[po_][:, hl * 65:(hl + 1) * 65], lhsT=E[:, hl * 128:(hl + 1) * 128], rhs=Va[sl][:, g, :],
                                                                                     start=(i == 0), stop=(i == len(Es) - 1)),
                                 reads=[b_E, b_Va[sl]], writes=[PSB[po_]], inc=(hl == 3 and i == len(Es) - 1))
                    dn, b_dn = dn_r.next()
                    ov = PS[po_][:, 0:260].rearrange("p (v pr d) -> p v pr d", v=2, pr=2)
                    c.op("dve", lambda e: e.tensor_tensor(out=dn[:, 0:4].rearrange("p (v pr) -> p v pr", v=2), in0=ov[:, :, :, 64],
                                                          in1=esk[:, 4 * g:4 * g + 4].rearrange("p (pr v) -> p v pr", v=2), op=ALU.add), reads=[PSB[po_], b_esk], writes=[b_dn])
                    c.op("dve", lambda e: e.reciprocal(out=dn[:, 4:8], in_=dn[:, 0:4]), reads=[b_dn], writes=[b_dn])
                    for var in range(2):
                        c.op("dve", lambda e, var=var: e.tensor_tensor(
                            out=O[:, 4 * g:4 * g + 4, :].rearrange("p (pr v) d -> p v pr d", v=2)[:, var, :, :], in0=ov[:, var, :, 0:64],
                            in1=dn[:, 4 + 2 * var:6 + 2 * var].unsqueeze(2).broadcast_to([128, 2, 64]), op=ALU.mult),
                             reads=[PSB[po_], b_dn], writes=[b_O])

                prev = None
                for g in range(4):
                    Es = scores(g)
                    if prev is not None:
                        pv(*prev)
                    prev = (g, Es)
                pv(*prev)
                cx['O'] = O; cx['b_O'] = b_O

            def E_C(cx):
                t = cx['t']; hres = cx['hres']; b_hres = cx['b_hres']; O = cx['O']; b_O = cx['b_O']
                OT, b_OT = OT_r.next()
                transpose_tile(O[:].rearrange("p h d -> p (h d)"), b_O, lambda: OT[:], b_OT, evac_eng="act")
                r, b_r = r_r.next()
                po = [nbank(), nbank()]
                for nb_ in range(2):
                    for kc in range(8):
                        c.op("pe", lambda e, nb_=nb_, kc=kc: e.matmul(PS[po[nb_]][:], lhsT=OT[:, kc, :], rhs=Wo[:, kc, nb_ * 512:(nb_ + 1) * 512], start=(kc == 0), stop=(kc == 7)),
                             reads=[b_OT, b_Wo], writes=[PSB[po[nb_]]], inc=(kc == 7))
                    c.op("dve", lambda e, nb_=nb_: e.scalar_tensor_tensor(out=r[:, nb_ * 512:(nb_ + 1) * 512], in0=hres[:, nb_ * 512:(nb_ + 1) * 512], scalar=ALPHA,
                                                                     in1=PS[po[nb_]][:], op0=ALU.mult, op1=ALU.add), reads=[b_hres, PSB[po[nb_]]], writes=[b_r])
                layernorm(rings, r, b_r, lng, b_lng, lnb, b_lnb)
                c.dma("sp", H3_d[t * 128:(t + 1) * 128, :], r[:], reads=[b_r], sem=osemE[t % 3])

            cxs = {}
            for i in range(NT + 2):
                if i < NT:
                    cxs[i] = E_A1(i)
                    E_A2a(cxs[i])
                if 0 <= i - 2 < NT:
                    E_C(cxs.pop(i - 2))
                if 0 <= i - 1 < NT:
                    E_B(cxs[i - 1])
                if i < NT:
                    E_A2b(cxs[i])
            c.barrier()
        if stop_after == "E":
            return nc

        with contextlib.ExitStack() as sF:
            lng, b_lng, lnb, b_lnb = load_ln(sF, 3)
            rings = ln_rings(sF)
            semF = c.dma_sem("cF")
            Wr = sbt(sF, "Wr", [128, 8, NE], F32); b_Wr = Buf()
            brt = sbt(sF, "brt", [128, NE], F32); b_brt = Buf()
            c.dma("sp", Wr[:], wr_d.rearrange("(kc p) n -> p kc n", p=128), writes=[b_Wr], sem=semF)
            c.dma("sp", brt[:], br_d[:, :], writes=[b_brt], sem=semF)
            STK = 1024
            NSUB = STK // 128
            WG = [sbt(sF, f"WGe{i}", [128, 8, ED], BF16) for i in range(2)]; b_WG = [Buf() for _ in range(2)]
            WU = [sbt(sF, f"WUe{i}", [128, 8, ED], BF16) for i in range(2)]; b_WU = [Buf() for _ in range(2)]
            WD = [sbt(sF, f"WDe{i}", [128, 8, D], BF16) for i in range(2)]; b_WD = [Buf() for _ in range(2)]
            wsem = [c.dma_sem("wF") for _ in range(2)]
            hTm = [sbt(sF, f"hTm{i}", [128, 8, STK], BF16) for i in range(2)]; b_hTm = [Buf() for _ in range(2)]
            h1m = sbt(sF, "h1m", [128, 8, STK], BF16); b_h1m = Buf()
            acc = sbt(sF, "acc", [128, NSUB, D], F32); b_acc = [Buf() for _ in range(NSUB)]
            comb = [sbt(sF, f"comb{i}", [128, NSUB, NE], F32) for i in range(2)]; b_comb = [[Buf() for _ in range(NSUB)] for _ in range(2)]
            h3_r = Ring(sF, "h3F", 2, [128, D], F32); h3sem = [c.dma_sem("h3F") for _ in range(2)]
            h3T_r = Ring(sF, "h3T", 2, [128, 8, 128], F32)
            lg_r = Ring(sF, "lg", 3, [128, 4, NE], F32)
            s_r = Ring(sF, "sF", 3, [128, 512], BF16)
            osemF = [c.dma_sem("oF") for _ in range(2)]
            n_w = 0

            def issue_weights(e_idx, slot):
                if wsem[slot][1] >= 2500:
                    wsem[slot] = c.dma_sem("wF")
                for kc in range(8):
                    c.dma("pool", WG[slot][:, kc, :], mg_d[e_idx, kc * 128:(kc + 1) * 128, :], writes=[b_WG[slot]], sem=wsem[slot])
                    c.dma("pool", WU[slot][:, kc, :], mu_d[e_idx, kc * 128:(kc + 1) * 128, :], writes=[b_WU[slot]], sem=wsem[slot])
                for kc in range(8):
                    c.dma("pool", WD[slot][:, kc, :], md_d[e_idx, kc * 128:(kc + 1) * 128, :], writes=[b_WD[slot]], sem=wsem[slot])

            def router_sub(st_, sub):
                t = st_ * NSUB + sub
                h3, b_h3 = h3_r.next(); h3T, b_h3T = h3T_r.next()
                c.dma("sp", h3[:], H3_d[t * 128:(t + 1) * 128, :], writes=[b_h3], sem=h3semA[t % 2])
                pt = [nbank(), nbank()]
                for kc in range(8):
                    c.op("pe", lambda e, kc=kc: e.transpose(out=PS[pt[kc // 4]][:, (kc % 4) * 128:(kc % 4 + 1) * 128], in_=h3[:, kc * 128:(kc + 1) * 128], identity=identf[:]),
                         reads=[b_h3, b_identf], writes=[PSB[pt[kc // 4]]], inc=(kc % 4 == 3))
                for hh in range(2):
                    src = PS[pt[hh]][:].rearrange("p (k t) -> p k t", k=4)
                    c.op("act", lambda e, hh=hh, src=src: e.copy(out=h3T[:, hh * 4:(hh + 1) * 4, :], in_=src), reads=[PSB[pt[hh]]], writes=[b_h3T])
                    c.op("dve", lambda e, hh=hh, src=src, sub=sub: e.tensor_copy(out=hTm[st_ % 2][:, hh * 4:(hh + 1) * 4, sub * 128:(sub + 1) * 128], in_=src), reads=[PSB[pt[hh]]], writes=[b_hTm[st_ % 2]])
                pl = nbank()
                for kc in range(8):
                    c.op("pe", lambda e, kc=kc: e.matmul(PS[pl][:, 0:NE], lhsT=h3T[:, kc, :], rhs=Wr[:, kc, :], start=(kc == 0), stop=(kc == 7)),
                         reads=[b_h3T, b_Wr], writes=[PSB[pl]], inc=(kc == 7))
                lg, b_lg = lg_r.next()
                c.op("dve", lambda e: e.tensor_tensor(out=lg[:, 0, :], in0=PS[pl][:, 0:NE], in1=brt[:], op=ALU.add), reads=[PSB[pl], b_brt], writes=[b_lg])
                c.op("dve", lambda e: e.max(out=lg[:, 1, :], in_=lg[:, 0, :]), reads=[b_lg], writes=[b_lg])
                c.op("dve", lambda e: e.tensor_scalar(out=lg[:, 2, :], in0=lg[:, 0, :], scalar1=lg[:, 1, 1:2], scalar2=None, op0=ALU.is_ge), reads=[b_lg], writes=[b_lg])
                c.op("dve", lambda e: e.tensor_scalar(out=lg[:, 1, 2:3], in0=lg[:, 1, 0:1], scalar1=-1.0, scalar2=None, op0=ALU.mult), reads=[b_lg], writes=[b_lg])
                c.op("act", lambda e: e.activation(out=lg[:, 3, :], in_=lg[:, 0, :], func=AF.Exp, bias=lg[:, 1, 2:3], scale=1.0), reads=[b_lg], writes=[b_lg])
                c.op("dve", lambda e: e.tensor_tensor(out=lg[:, 3, :], in0=lg[:, 3, :], in1=lg[:, 2, :], op=ALU.mult), reads=[b_lg], writes=[b_lg])
                c.op("dve", lambda e: e.tensor_reduce(out=lg[:, 1, 3:4], in_=lg[:, 3, :], axis=AX.X, op=ALU.add), reads=[b_lg], writes=[b_lg])
                c.op("dve", lambda e: e.reciprocal(out=lg[:, 1, 4:5], in_=lg[:, 1, 3:4]), reads=[b_lg], writes=[b_lg])
                c.op("dve", lambda e, sub=sub: e.tensor_scalar(out=comb[st_ % 2][:, sub, :], in0=lg[:, 3, :], scalar1=lg[:, 1, 4:5], scalar2=None, op0=ALU.mult), reads=[b_lg], writes=[b_comb[st_ % 2][sub]])

            def part1(st_, ex, slot):
                for half in range(STK // 512):
                    for fc in range(8):
                        pg, pu = nbank(), nbank()
                        for (W, bW, pb) in ((WG[slot], b_WG[slot], pg), (WU[slot], b_WU[slot], pu)):
                            for kc in range(8):
                                c.op("pe", lambda e, W=W, pb=pb, kc=kc, fc=fc, half=half: e.matmul(PS[pb][:], lhsT=W[:, kc, fc * 128:(fc + 1) * 128], rhs=hTm[st_ % 2][:, kc, half * 512:(half + 1) * 512],
                                                                                             start=(kc == 0), stop=(kc == 7)),
                                     reads=[bW, b_hTm[st_ % 2]], writes=[PSB[pb]], inc=(kc == 7))
                        sb_, b_s = s_r.next()
                        c.op("act", lambda e, pg=pg, sb_=sb_: e.activation(out=sb_[:], in_=PS[pg][:], func=AF.Silu), reads=[PSB[pg]], writes=[b_s])
                        c.op("dve", lambda e, pu=pu, sb_=sb_, fc=fc, half=half: e.tensor_tensor(out=h1m[:, fc, half * 512:(half + 1) * 512], in0=PS[pu][:], in1=sb_[:], op=ALU.mult),
                             reads=[PSB[pu], b_s], writes=[b_h1m])

            def part2(st_, ex, slot):
                for sub in range(NSUB):
                    for nb_ in range(2):
                        po_ = nbank()
                        for fc in range(8):
                            c.op("pe", lambda e, po_=po_, fc=fc, sub=sub, nb_=nb_, slot=slot: e.matmul(PS[po_][:], lhsT=h1m[:, fc, sub * 128:(sub + 1) * 128],
                                                                                             rhs=WD[slot][:, fc, nb_ * 512:(nb_ + 1) * 512], start=(fc == 0), stop=(fc == 7)),
                                 reads=[b_h1m, b_WD[slot]], writes=[PSB[po_]], inc=(fc == 7))
                        av = acc[:, sub, nb_ * 512:(nb_ + 1) * 512]
                        if ex == 0:
                            c.op("dve", lambda e, po_=po_, av=av, sub=sub, ex=ex: e.tensor_scalar(out=av, in0=PS[po_][:], scalar1=comb[st_ % 2][:, sub, ex:ex + 1], scalar2=None, op0=ALU.mult),
                                 reads=[PSB[po_], b_comb[st_ % 2][sub]], writes=[b_acc[sub]])
                        else:
                            c.op("dve", lambda e, po_=po_, av=av, sub=sub, ex=ex: e.scalar_tensor_tensor(out=av, in0=PS[po_][:], scalar=comb[st_ % 2][:, sub, ex:ex + 1], in1=av, op0=ALU.mult, op1=ALU.add),
                                 reads=[PSB[po_], b_comb[st_ % 2][sub], b_acc[sub]], writes=[b_acc[sub]])

            def finish(st_):
                for sub in range(NSUB):
                    t = st_ * NSUB + sub
                    h3, b_h3 = h3_r.next()
                    c.dma("sp", h3[:], H3_d[t * 128:(t + 1) * 128, :], writes=[b_h3], sem=h3semB[t % 2])
                    c.op("dve", lambda e, sub=sub: e.scalar_tensor_tensor(out=acc[:, sub, :], in0=h3[:], scalar=ALPHA, in1=acc[:, sub, :], op0=ALU.mult, op1=ALU.add),
                         reads=[b_h3, b_acc[sub]], writes=[b_acc[sub]])
                    layernorm(rings, acc[:, sub, :], b_acc[sub], lng, b_lng, lnb, b_lnb)
                    c.dma("act", out_d[t * 128:(t + 1) * 128, :], acc[:, sub, :], reads=[b_acc[sub]], sem=osemF[t % 2])

            NST = TOK // STK
            issue_weights(0, 0)
            h3semA = [c.dma_sem("h3A") for _ in range(2)]
            for sub in range(NSUB):
                router_sub(0, sub)
            for st_ in range(NST):
                h3semA = [c.dma_sem("h3A") for _ in range(2)]
                h3semB = [c.dma_sem("h3B") for _ in range(2)]
                for ex in range(NE):
                    slot = n_w % 2
                    n_w += 1
                    nxt = (st_ * NE + ex + 1)
                    if nxt < NST * NE:
                        issue_weights(nxt % NE, 1 - slot)
                    part1(st_, ex, slot)
                    if ex == 0 and st_ > 0:
                        finish(st_ - 1)
                    part2(st_, ex, slot)
                    if st_ + 1 < NST:
                        router_sub(st_ + 1, ex)
            finish(NST - 1)
            for k in ("sp",):
                for s_ in osemF:
                    nc.sync.wait_ge(s_[0], s_[1])
    return nc


def make_in_maps(inp):
    f = lambda a: np.ascontiguousarray(np.asarray(a), dtype=np.float32)
    x = f(inp["x"])
    pos = np.asarray(inp["positions"]).astype(np.int32)
    ln_g = f(inp["ln_g"]).reshape(4, D)
    ln_b = f(inp["ln_b"]).reshape(4, D)
    bcast = lambda a: np.ascontiguousarray(np.broadcast_to(a[None], (128,) + a.shape))
    lam_re = f(inp["ssm_lambda_re"])[0]; lam_im = f(inp["ssm_lambda_im"])[0]
    two = lambda a: np.ascontiguousarray(np.concatenate([a, a], axis=0))
    b_re = f(inp["ssm_b_re"])[0].transpose(1, 0, 2); b_im = f(inp["ssm_b_im"])[0].transpose(1, 0, 2)
    c_re = f(inp["ssm_c_re"])[0].transpose(2, 0, 1); c_im = f(inp["ssm_c_im"])[0].transpose(2, 0, 1)
    cat = lambda a, b: np.ascontiguousarray(np.concatenate([a, b], axis=0))
    inv_freq = (500000.0 ** (-np.arange(0, 16, 2, dtype=np.float32) / 16.0)).astype(np.float32)
    shared = {
        "ln_g": bcast(ln_g), "ln_b": bcast(ln_b),
        "lam_re": two(lam_re.T), "lam_im": two(lam_im.T), "lstep": bcast(f(inp["ssm_log_step"])[0]),
        "s5_ba": cat(b_re, b_im), "s5_bb": cat(b_im, b_re), "s5_ca": cat(c_re, c_im), "s5_cb": cat(c_im, c_re),
        "s5_d": bcast(f(inp["ssm_d"])[0]),
        "w_glu": f(inp["ssm_w_glu"])[0], "kv_w": f(inp["kv_w"]), "w_q": f(inp["attn_w_q"])[0],
        "sinks": bcast(f(inp["attn_sinks"])[0]), "w_out": f(inp["attn_w_out"])[0],
        "ffn_g": f(inp["ffn_w_gate"])[0], "ffn_u": f(inp["ffn_w_up"])[0], "ffn_d": f(inp["ffn_w_down"])[0],
        "w_router": f(inp["moe_w_router"])[0], "b_router": bcast(f(inp["moe_b_router"])[0]),
        "moe_g": f(inp["moe_w_gate"])[0], "moe_u": f(inp["moe_w_up"])[0], "moe_d": f(inp["moe_w_down"])[0],
        "inv_freq": bcast(inv_freq),
    }
    maps = []
    for i in range(NCORES):
        m = dict(shared)
        m["x"] = np.ascontiguousarray(x[2 * i:2 * i + 2].reshape(TOK, D))
        m["pos"] = np.ascontiguousarray(pos[2 * i:2 * i + 2].reshape(NT, 128).T)
        maps.append(m)
    return maps


def kernel(**inputs):
    nc = build()
    maps = make_in_maps(inputs)
    res = run_bass_kernel_spmd(nc, maps, core_ids=list(range(NCORES)))
    out = np.stack([np.asarray(r["out"]).reshape(NSEQ, L, D) for r in res.results], axis=0)
    return out.reshape(NCORES * NSEQ, L, D).astype(np.float32)
```

```python
import contextlib
import math
import numpy as np
import concourse.bass as bass
import concourse.mybir as mybir
from concourse.alu_op_type import AluOpType as ALU
from concourse.bass_utils import run_bass_kernel_spmd

F32 = mybir.dt.float32
BF16 = mybir.dt.bfloat16
I32 = mybir.dt.int32
AF = mybir.ActivationFunctionType
AX = mybir.AxisListType

NCORES = 8
D = 1024
L = 2048
NSEQ = 2
TOK = NSEQ * L
NT = TOK // 128
FF = 2816
NE = 8
ED = 1024
ALPHA = float(4.0 ** 0.25)
LN_EPS = 1e-5
MLIST = [0, -1, -2, -3, -4, -5, -6, -7] + list(range(16)) + [16, 32, 64, 128, 256, 512, 1024]
NM = len(MLIST)
SEM_ROLL = 6000


class Ev:
    __slots__ = ("sem", "val", "eng", "ref")

    def __init__(self, sem, val, eng, ref=None):
        self.sem = sem
        self.val = val
        self.eng = eng
        self.ref = ref


class Buf:
    __slots__ = ("name", "w", "r", "excl")

    def __init__(self, name="", excl=False):
        self.name = name
        self.w = None
        self.r = {}
        self.excl = excl


class Ctx:
    def __init__(self, nc, stack):
        self.nc = nc
        self.stack = stack
        self.engs = {"pe": nc.tensor, "act": nc.scalar, "dve": nc.vector, "pool": nc.gpsimd, "sp": nc.sync}
        self.sem = {}
        self.cnt = {}
        self.nsem = 0
        self.dsems = []
        for k in self.engs:
            self._new_eng_sem(k)
        self.waited = {k: {} for k in self.engs}
        self.pending = {k: [] for k in self.engs}
        self.ninst = {k: 0 for k in self.engs}
        self.last = {k: None for k in self.engs}

    def _new_sem(self, name):
        self.nsem += 1
        return self.stack.enter_context(self.nc.semaphore(f"{name}_{self.nsem}"))

    def _new_eng_sem(self, k):
        self.sem[k] = self._new_sem("e" + k)
        self.cnt[k] = 0

    def dma_sem(self, name="d"):
        s = [self._new_sem(name), 0]
        self.dsems.append(s)
        return s

    def _wait(self, k, ev):
        if ev is None:
            return
        if ev.eng == "pe" and k == "pe":
            return
        if ev.val is None:
            raise RuntimeError("dependency on an instruction without inc")
        w = self.waited[k]
        sid = id(ev.sem)
        val = ev.ref[1] if ev.ref is not None else ev.val
        if w.get(sid, 0) >= val:
            return
        self.engs[k].wait_ge(ev.sem, val)
        self.ninst[k] += 1
        w[sid] = val

    def _deps(self, k, reads, writes):
        for b in reads:
            self._wait(k, b.w)
            if b.excl:
                for kk, e in b.r.items():
                    if kk != k:
                        self._wait(k, e)
        for b in writes:
            self._wait(k, b.w)
            for e in b.r.values():
                self._wait(k, e)

    def _commit(self, ev, reads, writes):
        key = id(ev.sem) if ev.eng == "dma" else ev.eng
        for b in reads:
            b.r[key] = ev
        for b in writes:
            b.w = ev
            b.r = {}

    def op(self, k, fn, reads=(), writes=(), inc=True):
        self._deps(k, reads, writes)
        inst = fn(self.engs[k])
        self.ninst[k] += 1
        if inc:
            if self.cnt[k] >= SEM_ROLL:
                self._new_eng_sem(k)
            self.cnt[k] += 1
            inst.then_inc(self.sem[k], 1)
            ev = Ev(self.sem[k], self.cnt[k], k)
            for p in self.pending[k]:
                p.sem = ev.sem
                p.val = ev.val
            self.pending[k] = []
            self.last[k] = ev
        else:
            ev = Ev(None, None, k)
            self.pending[k].append(ev)
        self._commit(ev, reads, writes)
        return ev

    def dma(self, k, out, in_, reads=(), writes=(), sem=None, **kw):
        self._deps(k, reads, writes)
        inst = self.engs[k].dma_start(out=out, in_=in_, **kw)
        self.ninst[k] += 1
        sem[1] += 16
        inst.then_inc(sem[0], 16)
        ev = Ev(sem[0], sem[1], "dma", sem)
        self._commit(ev, reads, writes)
        return ev

    def barrier(self, engs=("pe", "act", "dve", "pool", "sp")):
        for k in engs:
            assert not self.pending[k]
        for k in engs:
            for k2 in engs:
                if k2 != k and self.last[k2] is not None:
                    self._wait(k, self.last[k2])
            for s in self.dsems:
                if s[1] > 0:
                    self._wait(k, Ev(s[0], s[1], "dma", s))


LAST_CTX = None


def build(stop_after=None):
    global LAST_CTX
    nc = bass.Bass("TRN2", target_bir_lowering=False)
    dram_in = lambda name, shape, dt=F32: nc.dram_tensor(name, list(shape), dt, kind="ExternalInput").ap()
    x_d = dram_in("x", [TOK, D])
    pos_d = dram_in("pos", [128, NT], I32)
    lnG_d = dram_in("ln_g", [128, 4, D])
    lnB_d = dram_in("ln_b", [128, 4, D])
    lamr_d = dram_in("lam_re", [128, 64])
    lami_d = dram_in("lam_im", [128, 64])
    lstep_d = dram_in("lstep", [128, 64])
    BA_d = dram_in("s5_ba", [128, 64, 16])
    BB_d = dram_in("s5_bb", [128, 64, 16])
    CA_d = dram_in("s5_ca", [128, 64, 16])
    CB_d = dram_in("s5_cb", [128, 64, 16])
    dsk_d = dram_in("s5_d", [128, D])
    wglu_d = dram_in("w_glu", [D, 2 * D])
    kvw_d = dram_in("kv_w", [D, 512])
    wq_d = dram_in("w_q", [D, D])
    sink_d = dram_in("sinks", [128, 16])
    wo_d = dram_in("w_out", [D, D])
    fg_d = dram_in("ffn_g", [D, FF])
    fu_d = dram_in("ffn_u", [D, FF])
    fd_d = dram_in("ffn_d", [FF, D])
    wr_d = dram_in("w_router", [D, NE])
    br_d = dram_in("b_router", [128, NE])
    mg_d = dram_in("moe_g", [NE, D, ED])
    mu_d = dram_in("moe_u", [NE, D, ED])
    md_d = dram_in("moe_d", [NE, ED, D])
    invf_d = dram_in("inv_freq", [128, 8])
    out_d = nc.dram_tensor("out", [TOK, D], F32, kind="ExternalOutput").ap()
    dbg = stop_after is not None
    G_d = nc.dram_tensor("G", [TOK, D], BF16, kind="ExternalOutput" if stop_after == "A" else "Internal").ap()
    H1_d = nc.dram_tensor("H1", [TOK, D], F32, kind="ExternalOutput" if stop_after == "B" else "Internal").ap()
    H2_d = nc.dram_tensor("H2", [TOK, D], F32, kind="ExternalOutput" if stop_after == "C" else "Internal").ap()
    H3_d = nc.dram_tensor("H3", [TOK, D], F32, kind="ExternalOutput" if stop_after == "E" else "Internal").ap()
    if stop_after == "P":
        dbgP = nc.dram_tensor("dbgP", [128, 2 * NM + 2, 64], F32, kind="ExternalOutput").ap()
        dbgT = nc.dram_tensor("dbgT", [128, 3, 64, 128], BF16, kind="ExternalOutput").ap()

    with contextlib.ExitStack() as st:
        c = Ctx(nc, st)
        LAST_CTX = c
        uniq = [0]

        def sbt(stack, name, shape, dt):
            uniq[0] += 1
            return stack.enter_context(nc.sbuf_tensor(f"{name}_{uniq[0]}", list(shape), dt))
        PS = [st.enter_context(nc.psum_tensor(f"ps{i}", [128, 512], F32)) for i in range(8)]
        PSB = [Buf(f"ps{i}", excl=True) for i in range(8)]

        identf = sbt(st, "identf", [128, 128], F32); b_identf = Buf()
        identb = sbt(st, "identb", [128, 128], BF16); b_identb = Buf()
        c.op("pool", lambda e: e.memset(identf[:], 0.0), writes=[b_identf])
        c.op("pool", lambda e: e.affine_select(out=identf[:], in_=identf[:], pattern=[[-1, 128]], compare_op=ALU.not_equal,
                                               fill=1.0, base=0, channel_multiplier=1), reads=[b_identf], writes=[b_identf])
        c.op("pool", lambda e: e.tensor_copy(out=identb[:], in_=identf[:]), reads=[b_identf], writes=[b_identb])
        out_sem = c.dma_sem("out")
        mhalf = sbt(st, "mhalf", [128, 1], F32); b_mhalf = Buf()
        c.op("pool", lambda e: e.memset(mhalf[:], -0.5), writes=[b_mhalf])

        with contextlib.ExitStack() as sa:
            Tm = sbt(sa, "Tm", [128, 64, 128], BF16); b_Tm = Buf()
            MinT = sbt(sa, "MinT", [128, 64, 128], BF16); b_MinT = Buf()
            Mout = sbt(sa, "Mout", [128, 64, 128], BF16); b_Mout = Buf()
            PA = sbt(sa, "PA", [128, 8, 64], F32); b_PA = Buf()
            PB = sbt(sa, "PB", [128, 8, 64], F32); b_PB = Buf()
            D1b = sbt(sa, "D1b", [128, 128], BF16); b_D1 = Buf()
            D2b = sbt(sa, "D2b", [128, 128], BF16); b_D2 = Buf()
            dB = sbt(sa, "dB", [128, D], F32); b_dB = Buf()
            ld0 = c.dma_sem("ld0")
            c.dma("sp", dB[:], dsk_d[:, :], writes=[b_dB], sem=ld0)
            with contextlib.ExitStack() as s0:
                lr = sbt(s0, "lr", [128, 64], F32); b_lr = Buf()
                li = sbt(s0, "li", [128, 64], F32); b_li = Buf()
                ls = sbt(s0, "ls", [128, 64], F32); b_ls = Buf()
                BAt = sbt(s0, "BAt", [128, 64, 16], F32); b_BA = Buf()
                BBt = sbt(s0, "BBt", [128, 64, 16], F32); b_BB = Buf()
                CAt = sbt(s0, "CAt", [128, 64, 16], F32); b_CA = Buf()
                CBt = sbt(s0, "CBt", [128, 64, 16], F32); b_CB = Buf()
                c.dma("sp", lr[:], lamr_d[:, :], writes=[b_lr], sem=ld0)
                c.dma("sp", li[:], lami_d[:, :], writes=[b_li], sem=ld0)
                c.dma("sp", ls[:], lstep_d[:, :], writes=[b_ls], sem=ld0)
                c.dma("act", BAt[:], BA_d[:, :, :], writes=[b_BA], sem=ld0)
                c.dma("act", BBt[:], BB_d[:, :, :], writes=[b_BB], sem=ld0)
                c.dma("act", CAt[:], CA_d[:, :, :], writes=[b_CA], sem=ld0)
                c.dma("act", CBt[:], CB_d[:, :, :], writes=[b_CB], sem=ld0)

                sgn = sbt(s0, "sgn", [128, 1], F32); b_sgn = Buf()
                sgn2 = sbt(s0, "sgn2", [128, 1], F32); b_sgn2 = Buf()
                c.op("pool", lambda e: e.memset(sgn[0:64, :], -1.0), writes=[b_sgn])
                c.op("pool", lambda e: e.memset(sgn[64:128, :], 1.0), writes=[b_sgn])
                c.op("pool", lambda e: e.memset(sgn2[0:64, :], 1.0), writes=[b_sgn2])
                c.op("pool", lambda e: e.memset(sgn2[64:128, :], -1.0), writes=[b_sgn2])
                D2f = sbt(s0, "D2f", [128, 128], F32); b_D2f = Buf()
                cmask = sbt(s0, "cmask", [128, 128], F32); b_cm = Buf()
                c.op("pool", lambda e: e.memset(D2f[:], 0.0), writes=[b_D2f])
                c.op("pool", lambda e: e.affine_select(out=D2f[:], in_=D2f[:], pattern=[[-1, 128]], compare_op=ALU.not_equal,
                                                       fill=1.0, base=64, channel_multiplier=1), reads=[b_D2f], writes=[b_D2f])
                c.op("pool", lambda e: e.affine_select(out=D2f[:], in_=D2f[:], pattern=[[-1, 128]], compare_op=ALU.not_equal,
                                                       fill=1.0, base=-64, channel_multiplier=1), reads=[b_D2f], writes=[b_D2f])
                c.op("pool", lambda e: e.tensor_copy(out=D2b[:], in_=D2f[:]), reads=[b_D2f], writes=[b_D2])
                c.op("pool", lambda e: e.tensor_copy(out=D1b[:], in_=identf[:]), reads=[b_identf], writes=[b_D1])
                c.op("pool", lambda e: e.memset(cmask[:], 1.0), writes=[b_cm])
                c.op("pool", lambda e: e.affine_select(out=cmask[:].rearrange("p (i o) -> p i o", i=8),
                                                       in_=cmask[:].rearrange("p (i o) -> p i o", i=8),
                                                       pattern=[[16, 8], [0, 16]], compare_op=ALU.is_ge,
                                                       fill=0.0, base=15, channel_multiplier=-1), reads=[b_cm], writes=[b_cm])

                s0a = contextlib.ExitStack()
                s0a.__enter__()
                def T3(name, stk=None):
                    return sbt(s0a if stk is None else stk, name, [128, NM, 64], F32), Buf(name)
                Pr, b_Pr = T3("Pr", s0)
                Pi, b_Pi = T3("Pi", s0)
                Mtab, b_Mt = T3("Mtab")
                for j, m in enumerate(MLIST):
                    c.op("pool", lambda e, j=j, m=m: e.memset(Mtab[:, j, :], float(m)), writes=[b_Mt])
                dt_ = sbt(s0a, "dt", [128, 64], F32); b_dt = Buf()
                lrdt = sbt(s0a, "lrdt", [128, 64], F32); b_lrdt = Buf()
                lidt = sbt(s0a, "lidt", [128, 64], F32); b_lidt = Buf()
                c.op("act", lambda e: e.activation(out=dt_[:], in_=ls[:], func=AF.Exp), reads=[b_ls], writes=[b_dt])
                c.op("dve", lambda e: e.tensor_tensor(out=lrdt[:], in0=lr[:], in1=dt_[:], op=ALU.mult), reads=[b_lr, b_dt], writes=[b_lrdt])
                c.op("dve", lambda e: e.tensor_tensor(out=lidt[:], in0=li[:], in1=dt_[:], op=ALU.mult), reads=[b_li, b_dt], writes=[b_lidt])
                bc3 = lambda t: t[:].unsqueeze(1).broadcast_to([128, NM, 64])
                Et, b_E = T3("Et")
                mag, b_mag = T3("mag")
                Ang, b_Ang = T3("Ang")
                c.op("dve", lambda e: e.tensor_tensor(out=Et[:], in0=Mtab[:], in1=bc3(lrdt), op=ALU.mult), reads=[b_Mt, b_lrdt], writes=[b_E])
                c.op("act", lambda e: e.activation(out=mag[:], in_=Et[:], func=AF.Exp), reads=[b_E], writes=[b_mag])
                c.op("dve", lambda e: e.tensor_tensor(out=Ang[:], in0=Mtab[:], in1=bc3(lidt), op=ALU.mult), reads=[b_Mt, b_lidt], writes=[b_Ang])
                kq, b_kq = T3("kq")
                ki_ = sbt(s0a, "ki", [128, NM, 64], I32); b_ki = Buf()
                kf, b_kf = T3("kf")
                yv, b_y = T3("yv")
                tw, b_tw = T3("tw")
                C1 = 6.28125
                C2 = float(2.0 * math.pi - 6.28125)
                PI = float(math.pi)
                TWO_PI = float(2.0 * math.pi)
                c.op("dve", lambda e: e.tensor_scalar(out=kq[:], in0=Ang[:], scalar1=float(1.0 / TWO_PI), scalar2=None, op0=ALU.mult), reads=[b_Ang], writes=[b_kq])
                c.op("dve", lambda e: e.tensor_copy(out=ki_[:], in_=kq[:]), reads=[b_kq], writes=[b_ki])
                c.op("dve", lambda e: e.tensor_copy(out=kf[:], in_=ki_[:]), reads=[b_ki], writes=[b_kf])
                c.op("dve", lambda e: e.scalar_tensor_tensor(out=yv[:], in0=kf[:], scalar=-C1, in1=Ang[:], op0=ALU.mult, op1=ALU.add), reads=[b_kf, b_Ang], writes=[b_y])
                c.op("dve", lambda e: e.scalar_tensor_tensor(out=yv[:], in0=kf[:], scalar=-C2, in1=yv[:], op0=ALU.mult, op1=ALU.add), reads=[b_kf, b_y], writes=[b_y])

                def wrap(t, b_t):
                    c.op("dve", lambda e: e.tensor_scalar(out=tw[:], in0=t[:], scalar1=-PI, scalar2=TWO_PI, op0=ALU.is_lt, op1=ALU.mult), reads=[b_t], writes=[b_tw])
                    c.op("dve", lambda e: e.tensor_tensor(out=t[:], in0=t[:], in1=tw[:], op=ALU.add), reads=[b_t, b_tw], writes=[b_t])
                    c.op("dve", lambda e: e.tensor_scalar(out=tw[:], in0=t[:], scalar1=PI, scalar2=-TWO_PI, op0=ALU.is_gt, op1=ALU.mult), reads=[b_t], writes=[b_tw])
                    c.op("dve", lambda e: e.tensor_tensor(out=t[:], in0=t[:], in1=tw[:], op=ALU.add), reads=[b_t, b_tw], writes=[b_t])
                wrap(yv, b_y)
                wrap(yv, b_y)
                sn, b_sn = T3("sn")
                cs, b_cs = T3("cs")
                yc, b_yc = T3("yc")
                c.op("act", lambda e: e.activation(out=sn[:], in_=yv[:], func=AF.Sin), reads=[b_y], writes=[b_sn])
                c.op("dve", lambda e: e.tensor_scalar(out=yc[:], in0=yv[:], scalar1=float(PI / 2), scalar2=None, op0=ALU.add), reads=[b_y], writes=[b_yc])
                wrap(yc, b_yc)
                c.op("act", lambda e: e.activation(out=cs[:], in_=yc[:], func=AF.Sin), reads=[b_yc], writes=[b_cs])
                c.op("dve", lambda e: e.tensor_tensor(out=Pr[:], in0=mag[:], in1=cs[:], op=ALU.mult), reads=[b_mag, b_cs], writes=[b_Pr])
                c.op("dve", lambda e: e.tensor_tensor(out=Pi[:], in0=mag[:], in1=sn[:], op=ALU.mult), reads=[b_mag, b_sn], writes=[b_Pi])
                c.barrier()
                s0a.__exit__(None, None, None)
                J1 = 9
                t64 = lambda name: (sbt(s0, name, [128, 64], F32), Buf(name))
                nr, b_nr = t64("nr"); den, b_den = t64("den"); t1, b_t1 = t64("t1"); t2, b_t2 = t64("t2")
                kr, b_kr = t64("kr"); kim, b_kim = t64("kim"); rden, b_rden = t64("rden")
                c.op("dve", lambda e: e.tensor_scalar(out=nr[:], in0=Pr[:, J1, :], scalar1=-1.0, scalar2=None, op0=ALU.add), reads=[b_Pr], writes=[b_nr])
                c.op("dve", lambda e: e.tensor_tensor(out=t1[:], in0=lr[:], in1=lr[:], op=ALU.mult), reads=[b_lr], writes=[b_t1])
                c.op("dve", lambda e: e.tensor_tensor(out=t2[:], in0=li[:], in1=li[:], op=ALU.mult), reads=[b_li], writes=[b_t2])
                c.op("dve", lambda e: e.tensor_tensor(out=den[:], in0=t1[:], in1=t2[:], op=ALU.add), reads=[b_t1, b_t2], writes=[b_den])
                c.op("dve", lambda e: e.reciprocal(out=rden[:], in_=den[:]), reads=[b_den], writes=[b_rden])
                c.op("dve", lambda e: e.tensor_tensor(out=t1[:], in0=nr[:], in1=lr[:], op=ALU.mult), reads=[b_nr, b_lr, b_den], writes=[b_t1])
                c.op("dve", lambda e: e.tensor_tensor(out=t2[:], in0=Pi[:, J1, :], in1=li[:], op=ALU.mult), reads=[b_Pi, b_li, b_den], writes=[b_t2])
                c.op("dve", lambda e: e.tensor_tensor(out=t1[:], in0=t1[:], in1=t2[:], op=ALU.add), reads=[b_t1, b_t2], writes=[b_t1])
                c.op("dve", lambda e: e.tensor_tensor(out=kr[:], in0=t1[:], in1=rden[:], op=ALU.mult), reads=[b_t1, b_rden], writes=[b_kr])
                c.op("dve", lambda e: e.tensor_tensor(out=t1[:], in0=Pi[:, J1, :], in1=lr[:], op=ALU.mult), reads=[b_Pi, b_lr, b_kr], writes=[b_t1])
                c.op("dve", lambda e: e.tensor_tensor(out=t2[:], in0=nr[:], in1=li[:], op=ALU.mult), reads=[b_nr, b_li, b_kr], writes=[b_t2])
                c.op("dve", lambda e: e.tensor_tensor(out=t1[:], in0=t1[:], in1=t2[:], op=ALU.subtract), reads=[b_t1, b_t2], writes=[b_t1])
                c.op("dve", lambda e: e.tensor_tensor(out=kim[:], in0=t1[:], in1=rden[:], op=ALU.mult), reads=[b_t1, b_rden], writes=[b_kim])
                Qr = sbt(s0, "Qr", [128, 8, 64], F32); b_Qr = Buf()
                Qi = sbt(s0, "Qi", [128, 8, 64], F32); b_Qi = Buf()
                q1 = sbt(s0, "q1", [128, 8, 64], F32); b_q1 = Buf()
                bc8 = lambda t: t[:].unsqueeze(1).broadcast_to([128, 8, 64])
                c.op("dve", lambda e: e.tensor_tensor(out=Qr[:], in0=Pr[:, 0:8, :], in1=bc8(kr), op=ALU.mult), reads=[b_Pr, b_kr], writes=[b_Qr])
                c.op("dve", lambda e: e.tensor_tensor(out=q1[:], in0=Pi[:, 0:8, :], in1=bc8(kim), op=ALU.mult), reads=[b_Pi, b_kim], writes=[b_q1])
                c.op("dve", lambda e: e.tensor_tensor(out=Qr[:], in0=Qr[:], in1=q1[:], op=ALU.subtract), reads=[b_Qr, b_q1], writes=[b_Qr])
                c.op("dve", lambda e: e.tensor_tensor(out=Qi[:], in0=Pi[:, 0:8, :], in1=bc8(kr), op=ALU.mult), reads=[b_Pi, b_kr], writes=[b_Qi])
                c.op("dve", lambda e: e.tensor_tensor(out=q1[:], in0=Pr[:, 0:8, :], in1=bc8(kim), op=ALU.mult), reads=[b_Pr, b_kim, b_Qr], writes=[b_q1])
                c.op("dve", lambda e: e.tensor_tensor(out=Qi[:], in0=Qi[:], in1=q1[:], op=ALU.add), reads=[b_Qi, b_q1], writes=[b_Qi])
                c.op("dve", lambda e: e.tensor_scalar(out=Qi[:], in0=Qi[:], scalar1=sgn[:, 0:1], scalar2=None, op0=ALU.mult), reads=[b_Qi, b_sgn], writes=[b_Qi])
                SIDX = [16, 24, 25, 26, 27, 28, 29, 30]
                for k, ix in enumerate(SIDX):
                    c.op("pool", lambda e, k=k, ix=ix: e.tensor_copy(out=PA[:, k, :], in_=Pr[:, ix, :]), reads=[b_Pr], writes=[b_PA])
                    c.op("dve", lambda e, k=k, ix=ix: e.tensor_scalar(out=PB[:, k, :], in0=Pi[:, ix, :], scalar1=sgn2[:, 0:1], scalar2=None, op0=ALU.mult), reads=[b_Pi, b_sgn2], writes=[b_PB])
                Prs = sbt(s0, "Prs", [128, 16, 64], F32); b_Prs = Buf()
                c.op("dve", lambda e: e.tensor_scalar(out=Prs[:], in0=Pr[:, 8:24, :], scalar1=sgn2[:, 0:1], scalar2=None, op0=ALU.mult), reads=[b_Pr, b_sgn2], writes=[b_Prs])
                X = sbt(s0, "X", [128, 32, 8, 16], F32); b_X = Buf()
                X2 = sbt(s0, "X2", [128, 32, 8, 16], F32); b_X2 = Buf()
                YY = sbt(s0, "YY", [128, 32, 16, 16], F32); b_YY = Buf()
                Y2 = sbt(s0, "Y2", [128, 32, 16, 16], F32); b_Y2 = Buf()
                for hg in range(2):
                    gs = slice(hg * 32, hg * 32 + 32)
                    qv = lambda t: t[:, :, gs].rearrange("p j g -> p g j").unsqueeze(3).broadcast_to([128, 32, 8, 16])
                    bv = lambda t: t[:, gs, :].unsqueeze(2).broadcast_to([128, 32, 8, 16])
                    pv = lambda ap: ap.rearrange("p m g -> p g m").unsqueeze(3).broadcast_to([128, 32, 16, 16])
                    cv = lambda t: t[:, gs, :].unsqueeze(2).broadcast_to([128, 32, 16, 16])
                    c.op("dve", lambda e: e.tensor_tensor(out=X[:], in0=qv(Qr), in1=bv(BAt), op=ALU.mult), reads=[b_Qr, b_BA], writes=[b_X])
                    c.op("pool", lambda e: e.tensor_tensor(out=X2[:], in0=qv(Qi), in1=bv(BBt), op=ALU.mult), reads=[b_Qi, b_BB], writes=[b_X2])
                    c.op("dve", lambda e: e.tensor_tensor(out=X[:], in0=X[:], in1=X2[:], op=ALU.add), reads=[b_X, b_X2], writes=[b_X])
                    c.op("dve", lambda e: e.tensor_tensor(out=YY[:], in0=pv(Prs[:, :, gs]), in1=cv(CAt), op=ALU.mult), reads=[b_Prs, b_CA], writes=[b_YY])
                    c.op("pool", lambda e: e.tensor_tensor(out=Y2[:], in0=pv(Pi[:, 8:24, gs]), in1=cv(CBt), op=ALU.mult), reads=[b_Pi, b_CB], writes=[b_Y2])
                    c.op("dve", lambda e: e.tensor_tensor(out=YY[:], in0=YY[:], in1=Y2[:], op=ALU.subtract), reads=[b_YY, b_Y2], writes=[b_YY])
                    c.op("act", lambda e: e.copy(out=Mout[:, gs, :].rearrange("p g (m o) -> p g m o", m=8), in_=YY[:, :, 8:16, :]), reads=[b_YY], writes=[b_Mout])
                    for q4 in range(8):
                        pa, pb = q4 % 2, 2 + (q4 % 2)
                        G0 = hg * 32 + q4 * 4
                        for gg in range(4):
                            g = q4 * 4 + gg
                            c.op("pe", lambda e, g=g, gg=gg, pa=pa: e.matmul(PS[pa][:, gg * 128:(gg + 1) * 128],
                                                                             lhsT=X[:, g, :, :].rearrange("p j c -> p (j c)"),
                                                                             rhs=YY[:, g, 0:8, :].rearrange("p m o -> p (m o)"),
                                                                             start=True, stop=True),
                                 reads=[b_X, b_YY], writes=[PSB[pa]], inc=(gg == 3))
                        c.op("dve", lambda e, G0=G0, pa=pa: e.tensor_tensor(out=Tm[:, G0:G0 + 4, :],
                                                                            in0=PS[pa][:].rearrange("p (g n) -> p g n", g=4),
                                                                            in1=cmask[:].unsqueeze(1).broadcast_to([128, 4, 128]), op=ALU.mult),
                             reads=[PSB[pa], b_cm], writes=[b_Tm])
                        for gg in range(4):
                            g = q4 * 4 + gg
                            c.op("pe", lambda e, g=g, gg=gg, pb=pb: e.transpose(out=PS[pb][:, gg * 128:(gg + 1) * 128],
                                                                                in_=X[:, g, :, :].rearrange("p j c -> p (j c)"),
                                                                                identity=identf[:]),
                                 reads=[b_X, b_identf], writes=[PSB[pb]], inc=(gg == 3))
                        c.op("act", lambda e, G0=G0, pb=pb: e.copy(out=MinT[:, G0:G0 + 4, :],
                                                                   in_=PS[pb][:].rearrange("p (g n) -> p g n", g=4)),
                             reads=[PSB[pb]], writes=[b_MinT])
                if stop_after == "P":
                    ds = c.dma_sem("dbg")
                    c.dma("sp", dbgP[:, 0:NM, :], Pr[:], reads=[b_Pr], sem=ds)
                    c.dma("sp", dbgP[:, NM:2 * NM, :], Pi[:], reads=[b_Pi], sem=ds)
                    c.dma("sp", dbgP[:, 2 * NM, :], kr[:], reads=[b_kr], sem=ds)
                    c.dma("sp", dbgP[:, 2 * NM + 1, :], kim[:], reads=[b_kim], sem=ds)
                    c.dma("sp", dbgT[:, 0, :, :], Tm[:], reads=[b_Tm], sem=ds)
                    c.dma("sp", dbgT[:, 1, :, :], MinT[:], reads=[b_MinT], sem=ds)
                    c.dma("sp", dbgT[:, 2, :, :], Mout[:], reads=[b_Mout], sem=ds)
                    nc.sync.wait_ge(ds[0], ds[1])
                    return nc
                c.barrier()
            with contextlib.ExitStack() as s1:
                xv = x_d.rearrange("(ct ch j) d -> ch ct j d", ct=4, ch=128, j=8)
                Gv = G_d.rearrange("(ct ch j) d -> ch ct j d", ct=4, ch=128, j=8)
                NXB = 2
                XB = [sbt(s1, f"XB{i}", [128, 4, 8, 128], F32) for i in range(NXB)]; b_XB = [Buf() for _ in range(NXB)]
                xsem = [c.dma_sem("xs") for _ in range(NXB)]
                XR = [sbt(s1, f"XR{i}", [128, 4, 8, 128], BF16) for i in range(NXB)]; b_XR = [Buf() for _ in range(NXB)]
                U = [sbt(s1, f"U{i}", [128, 8, 512], BF16) for i in range(NXB)]; b_U = [Buf() for _ in range(NXB)]
                YB = [sbt(s1, f"YB{i}", [128, 4, 8, 128], F32) for i in range(NXB)]; b_YB = [Buf() for _ in range(NXB)]
                GB = [sbt(s1, f"GB{i}", [128, 4, 8, 128], BF16) for i in range(NXB)]; b_GB = [Buf() for _ in range(NXB)]
                gsem = [c.dma_sem("gs") for _ in range(NXB)]
                Hr = [[sbt(s1, f"Hr{i}_{k}", [128, 512], BF16) for k in range(2)] for i in range(4)]
                b_Hr = [[Buf() for k in range(2)] for i in range(4)]
                Hi = [[sbt(s1, f"Hi{i}_{k}", [128, 512], BF16) for k in range(2)] for i in range(4)]
                b_Hi = [[Buf() for k in range(2)] for i in range(4)]
                Hp = [sbt(s1, f"Hp{i}", [128, 2, 256], BF16) for i in range(4)]; b_Hp = [Buf() for _ in range(4)]
                Ysb = [sbt(s1, f"Ysb{i}", [128, 512], F32) for i in range(4)]; b_Ysb = [Buf() for _ in range(4)]
                for i in range(4):
                    c.op("pool", lambda e, i=i: e.memset(Hp[i][:], 0.0), writes=[b_Hp[i]])
                g_ev = []
                def A_head(gb):
                    s = gb % NXB
                    for ct in range(4):
                        c.dma("sp" if ct % 2 == 0 else "act", XB[s][:, ct, :, :], xv[:, ct, :, gb * 128:(gb + 1) * 128],
                              writes=[b_XB[s]], sem=xsem[s])
                    for ct in range(4):
                        c.op("pool", lambda e, ct=ct, s=s: e.tensor_copy(
                            out=XR[s][:, ct, :, :].rearrange("p gl (j c) -> p gl j c", j=8),
                            in_=XB[s][:, ct, :, :].rearrange("p j (gl c) -> p gl j c", c=16)),
                             reads=[b_XB[s]], writes=[b_XR[s]])
                    for gl in range(8):
                        pb = 6 + (gl % 2)
                        psb16 = PS[pb][:].bitcast(BF16)
                        for ct in range(4):
                            c.op("pe", lambda e, ct=ct, gl=gl, s=s, psb16=psb16: e.transpose(
                                out=psb16[:, ct * 128:(ct + 1) * 128], in_=XR[s][:, ct, gl, :], identity=identb[:]),
                                 reads=[b_XR[s], b_identb], writes=[PSB[pb]], inc=(ct == 3))
                        c.op("act", lambda e, gl=gl, s=s, psb16=psb16: e.copy(out=U[s][:, gl, :], in_=psb16[:, 0:512]),
                             reads=[PSB[pb]], writes=[b_U[s]])

                A_head(0)
                for gb in range(8):
                    s = gb % NXB
                    for quad in range(2):
                        if quad == 1 and gb + 1 < 8:
                            A_head(gb + 1)
                        for gg in range(4):
                            gl = quad * 4 + gg
                            g = gb * 8 + gl
                            c.op("pe", lambda e, g=g, gl=gl, gg=gg, s=s: e.matmul(PS[gg][:], lhsT=MinT[:, g, :], rhs=U[s][:, gl, :], start=True, stop=True),
                                 reads=[b_MinT, b_U[s]], writes=[PSB[gg]])
                        for k in range(8):
                            sh = 1 << k
                            for gg in range(4):
                                for (eng, g2) in (("act", gg), ("dve", gg)):
                                    g = gb * 8 + quad * 4 + g2
                                    if eng == "act":
                                        c.op("act", lambda e, g2=g2, k=k, g=g: e.activation(out=Hr[g2][k % 2][:], in_=PS[g2][:], func=AF.Identity, scale=PA[:, k, g:g + 1]),
                                             reads=[PSB[g2], b_PA], writes=[b_Hr[g2][k % 2]])
                                    else:
                                        c.op("dve", lambda e, g2=g2, k=k, g=g: e.tensor_scalar(out=Hi[g2][k % 2][:], in0=PS[g2][:], scalar1=PB[:, k, g:g + 1], scalar2=None, op0=ALU.mult),
                                             reads=[PSB[g2], b_PB], writes=[b_Hi[g2][k % 2]])
                            for gg in range(4):
                                for (Dm, b_Dm, Hx, b_Hx, last) in ((D1b, b_D1, Hr, b_Hr, False), (D2b, b_D2, Hi, b_Hi, True)):
                                    for sq in range(2):
                                        c.op("pe", lambda e, gg=gg, k=k, sh=sh, sq=sq, Dm=Dm, Hx=Hx: e.matmul(
                                            PS[gg][:, sq * 256 + sh:(sq + 1) * 256],
                                            lhsT=Dm[:],
                                            rhs=Hx[gg][k % 2][:, sq * 256:(sq + 1) * 256 - sh],
                                            start=False, stop=True, skip_group_check=True),
                                             reads=[b_Dm, b_Hx[gg][k % 2]], writes=[PSB[gg]], inc=(last and sq == 1))
                        for gg in range(4):
                            eng = "act" if gg % 2 == 0 else "dve"
                            src = PS[gg][:].rearrange("p (s n) -> p s n", s=2)[:, :, 0:255]
                            if eng == "act":
                                c.op("act", lambda e, gg=gg, src=src: e.copy(out=Hp[gg][:, :, 1:256], in_=src), reads=[PSB[gg]], writes=[b_Hp[gg]])
                            else:
                                c.op("dve", lambda e, gg=gg, src=src: e.tensor_copy(out=Hp[gg][:, :, 1:256], in_=src), reads=[PSB[gg]], writes=[b_Hp[gg]])
                        for gg in range(4):
                            gl = quad * 4 + gg
                            g = gb * 8 + gl
                            pb = 4 + (gg % 2)
                            c.op("pe", lambda e, g=g, gl=gl, pb=pb, s=s: e.matmul(PS[pb][:], lhsT=Tm[:, g, :], rhs=U[s][:, gl, :], start=True, stop=False),
                                 reads=[b_Tm, b_U[s]], writes=[PSB[pb]], inc=False)
                            c.op("pe", lambda e, g=g, gg=gg, pb=pb: e.matmul(PS[pb][:], lhsT=Mout[:, g, :], rhs=Hp[gg][:].rearrange("p s n -> p (s n)"), start=False, stop=True),
                                 reads=[b_Mout, b_Hp[gg]], writes=[PSB[pb]])
                            if gg % 2 == 0:
                                c.op("dve", lambda e, gg=gg, pb=pb: e.tensor_copy(out=Ysb[gg][:], in_=PS[pb][:]), reads=[PSB[pb]], writes=[b_Ysb[gg]])
                            else:
                                c.op("act", lambda e, gg=gg, pb=pb: e.copy(out=Ysb[gg][:], in_=PS[pb][:]), reads=[PSB[pb]], writes=[b_Ysb[gg]])
                            pt = 6 + (gg % 2)
                            for ct in range(4):
                                c.op("pe", lambda e, gg=gg, ct=ct, pt=pt: e.transpose(out=PS[pt][:, ct * 128:(ct + 1) * 128],
                                                                                      in_=Ysb[gg][:, ct * 128:(ct + 1) * 128], identity=identf[:]),
                                     reads=[b_Ysb[gg], b_identf], writes=[PSB[pt]], inc=(ct == 3))
                            ydst = YB[s][:, :, :, gl * 16:(gl + 1) * 16]
                            ysrc = PS[pt][:].rearrange("p (ct i o) -> p ct i o", ct=4, i=8)
                            if gg % 2 == 0:
                                c.op("act", lambda e, ydst=ydst, ysrc=ysrc: e.copy(out=ydst, in_=ysrc), reads=[PSB[pt]], writes=[b_YB[s]])
                            else:
                                c.op("dve", lambda e, ydst=ydst, ysrc=ysrc: e.tensor_copy(out=ydst, in_=ysrc), reads=[PSB[pt]], writes=[b_YB[s]])
                    xbv = XB[s][:].rearrange("p ct j d -> p (ct j) d")
                    ybv = YB[s][:].rearrange("p ct j d -> p (ct j) d")
                    dbv = dB[:, gb * 128:(gb + 1) * 128].unsqueeze(1).broadcast_to([128, 32, 128])
                    c.op("pool", lambda e, xbv=xbv, dbv=dbv: e.tensor_tensor(out=xbv, in0=xbv, in1=dbv, op=ALU.mult),
                         reads=[b_XB[s], b_dB], writes=[b_XB[s]])
                    c.op("pool", lambda e, xbv=xbv, ybv=ybv: e.tensor_tensor(out=ybv, in0=ybv, in1=xbv, op=ALU.add),
                         reads=[b_XB[s], b_YB[s]], writes=[b_YB[s]])
                    c.op("act", lambda e, s=s: e.activation(out=GB[s][:], in_=YB[s][:], func=AF.Gelu_apprx_tanh),
                         reads=[b_YB[s]], writes=[b_GB[s]])
                    for ct in range(4):
                        g_ev.append(c.dma("sp", Gv[:, ct, :, gb * 128:(gb + 1) * 128], GB[s][:, ct, :, :], reads=[b_GB[s]], sem=gsem[s]))
                c.barrier()
        if stop_after == "A":
            nc.sync.wait_ge(gsem[0][0], gsem[0][1])
            nc.sync.wait_ge(gsem[1][0], gsem[1][1])
            return nc
        class Ring:
            def __init__(self, stack, name, n, shape, dt):
                self.t = [sbt(stack, f"{name}{i}", shape, dt) for i in range(n)]
                self.b = [Buf(f"{name}{i}") for i in range(n)]
                self.i = 0
                self.n = n

            def next(self):
                k = self.i % self.n
                self.i += 1
                return self.t[k], self.b[k]

        bank_ctr = [0]

        def nbank():
            k = bank_ctr[0] % 8
            bank_ctr[0] += 1
            return k

        def alloc_weight(stack, name, src, K, N, queue="pool"):
            kc = K // 128
            t = sbt(stack, name, [128, kc, N], BF16)
            b = Buf(name)
            sem = c.dma_sem(name)
            srcv = src.rearrange("(kc p) n -> p kc n", p=128)

            def issue():
                for k in range(kc):
                    for n0 in range(0, N, 2048):
                        n1 = min(N, n0 + 2048)
                        c.dma(queue, t[:, k, n0:n1], srcv[:, k, n0:n1], writes=[b], sem=sem)
                        yield
            return t, b, issue

        def load_weight(stack, name, src, K, N, queue="pool"):
            t, b, issue = alloc_weight(stack, name, src, K, N, queue)
            for _ in issue():
                pass
            return t, b

        def load_ln(stack, idx):
            g = sbt(stack, f"lng{idx}", [128, D], F32); bg = Buf()
            bt = sbt(stack, f"lnb{idx}", [128, D], F32); bb = Buf()
            sem = c.dma_sem("ln")
            c.dma("sp", g[:], lnG_d[:, idx, :], writes=[bg], sem=sem)
            c.dma("sp", bt[:], lnB_d[:, idx, :], writes=[bb], sem=sem)
            return g, bg, bt, bb

        def layernorm(stack_rings, r, b_r, lng, b_lng, lnb, b_lnb):
            stats, b_st = stack_rings["stats"].next()
            mv, b_mv = stack_rings["mv"].next()
            sd, b_sd = stack_rings["sd"].next()
            for hh in range(2):
                c.op("dve", lambda e, hh=hh: e.bn_stats(out=stats[:, hh, :], in_=r[:, hh * 512:(hh + 1) * 512]), reads=[b_r], writes=[b_st])
            c.op("dve", lambda e: e.bn_aggr(out=mv[:], in_=stats[:].rearrange("p a b -> p (a b)")), reads=[b_st], writes=[b_mv])
            c.op("dve", lambda e: e.tensor_scalar(out=sd[:, 0:1], in0=mv[:, 1:2], scalar1=float(LN_EPS), scalar2=None, op0=ALU.add), reads=[b_mv], writes=[b_sd])
            c.op("pool", lambda e: e.tensor_tensor(out=sd[:, 2:3], in0=sd[:, 0:1], in1=mhalf[:, 0:1], op=ALU.pow), reads=[b_sd, b_mhalf], writes=[b_sd])
            c.op("dve", lambda e: e.tensor_scalar(out=sd[:, 3:4], in0=mv[:, 0:1], scalar1=sd[:, 2:3], scalar2=-1.0, op0=ALU.mult, op1=ALU.mult), reads=[b_mv, b_sd], writes=[b_sd])
            c.op("dve", lambda e: e.tensor_scalar(out=r[:], in0=r[:], scalar1=sd[:, 2:3], scalar2=sd[:, 3:4], op0=ALU.mult, op1=ALU.add), reads=[b_r, b_sd], writes=[b_r])
            c.op("pool", lambda e: e.tensor_tensor(out=r[:], in0=r[:], in1=lng[:], op=ALU.mult), reads=[b_r, b_lng], writes=[b_r])
            c.op("dve", lambda e: e.tensor_tensor(out=r[:], in0=r[:], in1=lnb[:], op=ALU.add), reads=[b_r, b_lnb], writes=[b_r])

        def ln_rings(stack):
            return {"stats": Ring(stack, "lnst", 3, [128, 2, 6], F32), "mv": Ring(stack, "lnmv", 3, [128, 2], F32),
                    "sd": Ring(stack, "lnsd", 3, [128, 4], F32)}

        def transpose_tile(src, b_src, dst_fn, b_dst, evac_eng="act"):
            pb = nbank()
            p16 = PS[pb][:].bitcast(BF16)
            for kc in range(8):
                c.op("pe", lambda e, kc=kc: e.transpose(out=p16[:, kc * 128:(kc + 1) * 128], in_=src[:, kc * 128:(kc + 1) * 128], identity=identb[:]),
                     reads=[b_src, b_identb], writes=[PSB[pb]], inc=(kc == 7))
            dst = dst_fn()
            if evac_eng == "act":
                c.op("act", lambda e: e.copy(out=dst, in_=p16[:].rearrange("p (k t) -> p k t", k=8)), reads=[PSB[pb]], writes=[b_dst])
            else:
                c.op("dve", lambda e: e.tensor_copy(out=dst, in_=p16[:].rearrange("p (k t) -> p k t", k=8)), reads=[PSB[pb]], writes=[b_dst])

        sBC = contextlib.ExitStack()
        sBC.__enter__()
        Wg, b_Wg, issue_Wg = alloc_weight(sBC, "Wg", fg_d, D, FF)
        Wu, b_Wu, issue_Wu = alloc_weight(sBC, "Wu", fu_d, D, FF)
        Wd, b_Wd, issue_Wd = alloc_weight(sBC, "Wd", fd_d, FF, D)
        with contextlib.ExitStack() as sB:
            Wglu, b_Wglu = load_weight(sB, "Wglu", wglu_d, D, 2 * D)
            import itertools
            ffn_dmas = itertools.chain(issue_Wg(), issue_Wu(), issue_Wd())
            lng, b_lng, lnb, b_lnb = load_ln(sB, 0)
            rings = ln_rings(sB)
            gt_r = Ring(sB, "gt", 2, [128, D], BF16); gsemB = [c.dma_sem("gB") for _ in range(2)]
            xt_r = Ring(sB, "xt", 2, [128, D], F32); xsemB = [c.dma_sem("xB") for _ in range(3)]
            gT_r = Ring(sB, "gT", 2, [128, 8, 128], BF16)
            sg_r = Ring(sB, "sg", 2, [128, D], F32)
            r_r = Ring(sB, "rB", 2, [128, D], F32); osemB = [c.dma_sem("oB") for _ in range(3)]
            def B_stage1(t):
                gt, b_gt = gt_r.next(); xt, b_xt = xt_r.next(); gT, b_gT = gT_r.next(); sg, b_sg = sg_r.next()
                c.dma("sp", gt[:], G_d[t * 128:(t + 1) * 128, :], writes=[b_gt], sem=gsemB[t % 2])
                c.dma("act", xt[:], x_d[t * 128:(t + 1) * 128, :], writes=[b_xt], sem=xsemB[t % 3])
                transpose_tile(gt, b_gt, lambda: gT[:], b_gT)
                banks = [nbank() for _ in range(4)]
                for nb_ in range(4):
                    for kc in range(8):
                        c.op("pe", lambda e, nb_=nb_, kc=kc: e.matmul(PS[banks[nb_]][:], lhsT=gT[:, kc, :], rhs=Wglu[:, kc, nb_ * 512:(nb_ + 1) * 512],
                                                                  start=(kc == 0), stop=(kc == 7)),
                             reads=[b_gT, b_Wglu], writes=[PSB[banks[nb_]]], inc=(kc == 7))
                for hh in range(2):
                    c.op("act", lambda e, hh=hh: e.activation(out=sg[:, hh * 512:(hh + 1) * 512], in_=PS[banks[2 + hh]][:], func=AF.Sigmoid),
                         reads=[PSB[banks[2 + hh]]], writes=[b_sg])
                    c.op("dve", lambda e, hh=hh: e.tensor_tensor(out=sg[:, hh * 512:(hh + 1) * 512], in0=PS[banks[hh]][:], in1=sg[:, hh * 512:(hh + 1) * 512], op=ALU.mult),
                         reads=[PSB[banks[hh]], b_sg], writes=[b_sg])
                return (t, xt, b_xt, sg, b_sg)

            def B_stage2(ctx):
                t, xt, b_xt, sg, b_sg = ctx
                r, b_r = r_r.next()
                c.op("dve", lambda e: e.scalar_tensor_tensor(out=r[:], in0=xt[:], scalar=ALPHA, in1=sg[:], op0=ALU.mult, op1=ALU.add),
                     reads=[b_xt, b_sg], writes=[b_r])
                layernorm(rings, r, b_r, lng, b_lng, lnb, b_lnb)
                c.dma("pool", H1_d[t * 128:(t + 1) * 128, :], r[:], reads=[b_r], sem=osemB[t % 3])

            pend = None
            for t in range(NT):
                ctx = B_stage1(t)
                for _ in range(2):
                    next(ffn_dmas, None)
                if pend is not None:
                    B_stage2(pend)
                pend = ctx
            B_stage2(pend)
            for _ in ffn_dmas:
                pass
            c.barrier()
        if stop_after == "B":
            sBC.__exit__(None, None, None)
            return nc

        with contextlib.ExitStack() as sC:
            lng, b_lng, lnb, b_lnb = load_ln(sC, 1)
            rings = ln_rings(sC)
            ht_r = Ring(sC, "htC", 2, [128, D], BF16); hsemC = [c.dma_sem("hC") for _ in range(2)]
            hres_r = Ring(sC, "hresC", 2, [128, D], F32); rsemC = [c.dma_sem("rC") for _ in range(2)]
            hT = sbt(sC, "hTC", [128, 8, 512], BF16); b_hT = Buf()
            h1T = sbt(sC, "h1TC", [128, FF // 128, 512], BF16); b_h1T = Buf()
            s_r = Ring(sC, "sC", 2, [128, 512], BF16)
            r_r = Ring(sC, "rC", 2, [128, D], F32); osemC = [c.dma_sem("oC") for _ in range(2)]
            for st_ in range(TOK // 512):
                for sub in range(4):
                    t = st_ * 4 + sub
                    ht, b_ht = ht_r.next()
                    c.dma("pool", ht[:], H1_d[t * 128:(t + 1) * 128, :], writes=[b_ht], sem=hsemC[t % 2])
                    transpose_tile(ht, b_ht, lambda sub=sub: hT[:, :, sub * 128:(sub + 1) * 128], b_hT, evac_eng="act" if sub % 2 == 0 else "dve")
                for fc in range(FF // 128):
                    pg, pu = nbank(), nbank()
                    for (W, bW, pb) in ((Wg, b_Wg, pg), (Wu, b_Wu, pu)):
                        for kc in range(8):
                            c.op("pe", lambda e, W=W, pb=pb, kc=kc, fc=fc: e.matmul(PS[pb][:], lhsT=W[:, kc, fc * 128:(fc + 1) * 128], rhs=hT[:, kc, :],
                                                                              start=(kc == 0), stop=(kc == 7)),
                                 reads=[bW, b_hT], writes=[PSB[pb]], inc=(kc == 7))
                    sb_, b_s = s_r.next()
                    c.op("act", lambda e, pg=pg, sb_=sb_: e.activation(out=sb_[:], in_=PS[pg][:], func=AF.Silu), reads=[PSB[pg]], writes=[b_s])
                    c.op("dve", lambda e, pu=pu, sb_=sb_, fc=fc: e.tensor_tensor(out=h1T[:, fc, :], in0=PS[pu][:], in1=sb_[:], op=ALU.mult),
                         reads=[PSB[pu], b_s], writes=[b_h1T])
                for sub in range(4):
                    t = st_ * 4 + sub
                    hres, b_hres = hres_r.next(); r, b_r = r_r.next()
                    c.dma("act", hres[:], H1_d[t * 128:(t + 1) * 128, :], writes=[b_hres], sem=rsemC[t % 2])
                    po = [nbank(), nbank()]
                    for nb_ in range(2):
                        for fc in range(FF // 128):
                            c.op("pe", lambda e, nb_=nb_, fc=fc, sub=sub: e.matmul(PS[po[nb_]][:], lhsT=h1T[:, fc, sub * 128:(sub + 1) * 128],
                                                                             rhs=Wd[:, fc, nb_ * 512:(nb_ + 1) * 512], start=(fc == 0), stop=(fc == FF // 128 - 1)),
                                 reads=[b_h1T, b_Wd], writes=[PSB[po[nb_]]], inc=(fc == FF // 128 - 1))
                        c.op("dve", lambda e, nb_=nb_: e.scalar_tensor_tensor(out=r[:, nb_ * 512:(nb_ + 1) * 512], in0=hres[:, nb_ * 512:(nb_ + 1) * 512], scalar=ALPHA,
                                                                         in1=PS[po[nb_]][:], op0=ALU.mult, op1=ALU.add),
                             reads=[b_hres, PSB[po[nb_]]], writes=[b_r])
                    layernorm(rings, r, b_r, lng, b_lng, lnb, b_lnb)
                    c.dma("sp", H2_d[t * 128:(t + 1) * 128, :], r[:], reads=[b_r], sem=osemC[t % 2])
            c.barrier()
        sBC.__exit__(None, None, None)
        if stop_after == "C":
            return nc

        with contextlib.ExitStack() as sE:
            Wq, b_Wq = load_weight(sE, "Wq", wq_d, D, D)
            Wkv, b_Wkv = load_weight(sE, "Wkv", kvw_d, D, 512)
            Wo, b_Wo = load_weight(sE, "Wo", wo_d, D, D)
            lng, b_lng, lnb, b_lnb = load_ln(sE, 2)
            rings = ln_rings(sE)
            semE = c.dma_sem("cE")
            posi = sbt(sE, "posi", [128, NT], I32); b_posi = Buf()
            invf = sbt(sE, "invf", [128, 8], F32); b_invf = Buf()
            sk = sbt(sE, "sk", [128, 16], F32); b_sk = Buf()
            c.dma("sp", posi[:], pos_d[:, :], writes=[b_posi], sem=semE)
            c.dma("sp", invf[:], invf_d[:, :], writes=[b_invf], sem=semE)
            c.dma("sp", sk[:], sink_d[:, :], writes=[b_sk], sem=semE)
            posf = sbt(sE, "posf", [128, NT], F32); b_posf = Buf()
            angE = sbt(sE, "angE", [128, NT, 8], F32); b_angE = Buf()
            kqE = sbt(sE, "kqE", [128, NT, 8], F32); b_kqE = Buf()
            kiE = sbt(sE, "kiE", [128, NT, 8], I32); b_kiE = Buf()
            twE = sbt(sE, "twE", [128, NT, 8], F32); b_twE = Buf()
            ycE = sbt(sE, "ycE", [128, NT, 8], F32); b_ycE = Buf()
            cosT = sbt(sE, "cosT", [128, NT, 8], F32); b_cosT = Buf()
            sinT = sbt(sE, "sinT", [128, NT, 8], F32); b_sinT = Buf()
            esk = sbt(sE, "esk", [128, 16], F32); b_esk = Buf()
            c.op("act", lambda e: e.activation(out=esk[:], in_=sk[:], func=AF.Exp), reads=[b_sk], writes=[b_esk])
            c.op("dve", lambda e: e.tensor_copy(out=posf[:], in_=posi[:]), reads=[b_posi], writes=[b_posf])
            c.op("dve", lambda e: e.tensor_tensor(out=angE[:], in0=posf[:].unsqueeze(2).broadcast_to([128, NT, 8]),
                                                  in1=invf[:].unsqueeze(1).broadcast_to([128, NT, 8]), op=ALU.mult), reads=[b_posf, b_invf], writes=[b_angE])
            c.op("dve", lambda e: e.tensor_scalar(out=kqE[:], in0=angE[:], scalar1=float(1.0 / (2 * math.pi)), scalar2=None, op0=ALU.mult), reads=[b_angE], writes=[b_kqE])
            c.op("dve", lambda e: e.tensor_copy(out=kiE[:], in_=kqE[:]), reads=[b_kqE], writes=[b_kiE])
            c.op("dve", lambda e: e.tensor_copy(out=kqE[:], in_=kiE[:]), reads=[b_kiE], writes=[b_kqE])
            c.op("dve", lambda e: e.scalar_tensor_tensor(out=angE[:], in0=kqE[:], scalar=-6.28125, in1=angE[:], op0=ALU.mult, op1=ALU.add), reads=[b_kqE, b_angE], writes=[b_angE])
            c.op("dve", lambda e: e.scalar_tensor_tensor(out=angE[:], in0=kqE[:], scalar=-float(2.0 * math.pi - 6.28125), in1=angE[:], op0=ALU.mult, op1=ALU.add), reads=[b_kqE, b_angE], writes=[b_angE])

            def wrapE(t, b_t):
                PI_ = float(math.pi)
                c.op("dve", lambda e: e.tensor_scalar(out=twE[:], in0=t[:], scalar1=-PI_, scalar2=2 * PI_, op0=ALU.is_lt, op1=ALU.mult), reads=[b_t], writes=[b_twE])
                c.op("dve", lambda e: e.tensor_tensor(out=t[:], in0=t[:], in1=twE[:], op=ALU.add), reads=[b_t, b_twE], writes=[b_t])
                c.op("dve", lambda e: e.tensor_scalar(out=twE[:], in0=t[:], scalar1=PI_, scalar2=-2 * PI_, op0=ALU.is_gt, op1=ALU.mult), reads=[b_t], writes=[b_twE])
                c.op("dve", lambda e: e.tensor_tensor(out=t[:], in0=t[:], in1=twE[:], op=ALU.add), reads=[b_t, b_twE], writes=[b_t])
            wrapE(angE, b_angE)
            wrapE(angE, b_angE)
            c.op("act", lambda e: e.activation(out=sinT[:], in_=angE[:], func=AF.Sin), reads=[b_angE], writes=[b_sinT])
            c.op("dve", lambda e: e.tensor_scalar(out=ycE[:], in0=angE[:], scalar1=float(math.pi / 2), scalar2=None, op0=ALU.add), reads=[b_angE], writes=[b_ycE])
            wrapE(ycE, b_ycE)
            c.op("act", lambda e: e.activation(out=cosT[:], in_=ycE[:], func=AF.Sin), reads=[b_ycE], writes=[b_cosT])
            mprev = sbt(sE, "mprev", [128, 128], BF16); b_mprev = Buf()
            mcur = sbt(sE, "mcur", [128, 128], BF16); b_mcur = Buf()
            c.op("pool", lambda e: e.memset(mprev[:], 1.0), writes=[b_mprev])
            c.op("pool", lambda e: e.affine_select(out=mprev[:], in_=mprev[:], pattern=[[-1, 128]], compare_op=ALU.is_gt, fill=0.0, base=0, channel_multiplier=1),
                 reads=[b_mprev], writes=[b_mprev])
            c.op("pool", lambda e: e.memset(mcur[:], 1.0), writes=[b_mcur])
            c.op("pool", lambda e: e.affine_select(out=mcur[:], in_=mcur[:], pattern=[[1, 128]], compare_op=ALU.is_ge, fill=0.0, base=0, channel_multiplier=-1),
                 reads=[b_mcur], writes=[b_mcur])
            KT = [sbt(sE, f"KT{i}", [128, 8, 128], BF16) for i in range(3)]; b_KT = [Buf() for _ in range(3)]
            Ksp = [sbt(sE, f"Ksp{i}", [128, 4, 2, 128], BF16) for i in range(3)]; b_Ksp = [Buf() for _ in range(3)]
            for i in range(3):
                c.op("pool", lambda e, i=i: e.memset(Ksp[i][:], 0.0), writes=[b_Ksp[i]])
            Va = [sbt(sE, f"Va{i}", [128, 4, 65], BF16) for i in range(3)]; b_Va = [Buf() for _ in range(3)]
            for i in range(3):
                c.op("pool", lambda e, i=i: e.memset(Va[i][:], 1.0), writes=[b_Va[i]])
            ht_r = Ring(sE, "htE", 3, [128, D], BF16); hsemE = [c.dma_sem("hE") for _ in range(3)]
            hres_r = Ring(sE, "hresE", 4, [128, D], F32); rsemE = [c.dma_sem("rE") for _ in range(4)]
            hT_r = Ring(sE, "hTE", 2, [128, 8, 128], BF16)
            Qs_r = Ring(sE, "Qs", 2, [128, 16, 64], BF16)
            Ks_r = Ring(sE, "Ks", 2, [128, 4, 64], BF16)
            QT_r = Ring(sE, "QT", 3, [128, 8, 128], BF16)
            ro_r = Ring(sE, "ro", 4, [128, 8, 8], F32)
            E_r = Ring(sE, "E", 6, [128, 512], BF16)
            O_r = Ring(sE, "O", 3, [128, 16, 64], BF16)
            OT_r = Ring(sE, "OT", 2, [128, 8, 128], BF16)
            dn_r = Ring(sE, "dn", 4, [128, 8], F32)
            r_r = Ring(sE, "rE", 3, [128, D], F32); osemE = [c.dma_sem("oE") for _ in range(3)]

            def rope(psrc, nh, dst, b_dst, pbuf, t):
                cb = cosT[:, t, :].unsqueeze(1).broadcast_to([128, nh, 8])
                sb2 = sinT[:, t, :].unsqueeze(1).broadcast_to([128, nh, 8])
                q1 = psrc[:, :, 0:8]; q2 = psrc[:, :, 8:16]
                ta, b_ta = ro_r.next(); tb, b_tb = ro_r.next()
                c.op("dve", lambda e: e.tensor_tensor(out=ta[:, 0:nh, :], in0=q1, in1=cb, op=ALU.mult), reads=[pbuf, b_cosT], writes=[b_ta])
                c.op("dve", lambda e: e.tensor_tensor(out=tb[:, 0:nh, :], in0=q2, in1=sb2, op=ALU.mult), reads=[pbuf, b_sinT], writes=[b_tb])
                c.op("dve", lambda e: e.tensor_tensor(out=dst[:, :, 0:8], in0=ta[:, 0:nh, :], in1=tb[:, 0:nh, :], op=ALU.subtract), reads=[b_ta, b_tb], writes=[b_dst])
                tc_, b_tc = ro_r.next(); td, b_td = ro_r.next()
                c.op("dve", lambda e: e.tensor_tensor(out=tc_[:, 0:nh, :], in0=q2, in1=cb, op=ALU.mult), reads=[pbuf, b_cosT], writes=[b_tc])
                c.op("dve", lambda e: e.tensor_tensor(out=td[:, 0:nh, :], in0=q1, in1=sb2, op=ALU.mult), reads=[pbuf, b_sinT], writes=[b_td])
                c.op("dve", lambda e: e.tensor_tensor(out=dst[:, :, 8:16], in0=tc_[:, 0:nh, :], in1=td[:, 0:nh, :], op=ALU.add), reads=[b_tc, b_td], writes=[b_dst])

            def E_A1(t):
                ht, b_ht = ht_r.next(); hres, b_hres = hres_r.next(); hT, b_hT = hT_r.next()
                c.dma("pool", ht[:], H2_d[t * 128:(t + 1) * 128, :], writes=[b_ht], sem=hsemE[t % 3])
                c.dma("act", hres[:], H2_d[t * 128:(t + 1) * 128, :], writes=[b_hres], sem=rsemE[t % 4])
                transpose_tile(ht, b_ht, lambda: hT[:], b_hT)
                pq = [nbank(), nbank()]; pkv = nbank()
                for nb_ in range(2):
                    for kc in range(8):
                        c.op("pe", lambda e, nb_=nb_, kc=kc: e.matmul(PS[pq[nb_]][:], lhsT=hT[:, kc, :], rhs=Wq[:, kc, nb_ * 512:(nb_ + 1) * 512], start=(kc == 0), stop=(kc == 7)),
                             reads=[b_hT, b_Wq], writes=[PSB[pq[nb_]]], inc=(kc == 7))
                for kc in range(8):
                    c.op("pe", lambda e, kc=kc: e.matmul(PS[pkv][:], lhsT=hT[:, kc, :], rhs=Wkv[:, kc, :], start=(kc == 0), stop=(kc == 7)),
                         reads=[b_hT, b_Wkv], writes=[PSB[pkv]], inc=(kc == 7))
                return dict(t=t, nblk=t % 16, hres=hres, b_hres=b_hres, pq=pq, pkv=pkv)

            def E_A2a(cx):
                t = cx['t']; pq = cx['pq']; pkv = cx['pkv']
                Qs, b_Qs = Qs_r.next(); Ks, b_Ks = Ks_r.next()
                slot = t % 3
                for nb_ in range(2):
                    pv_ = PS[pq[nb_]][:].rearrange("p (h d) -> p h d", h=8)
                    c.op("act", lambda e, nb_=nb_, pv_=pv_: e.copy(out=Qs[:, nb_ * 8:(nb_ + 1) * 8, :], in_=pv_), reads=[PSB[pq[nb_]]], writes=[b_Qs])
                    rope(pv_, 8, Qs[:, nb_ * 8:(nb_ + 1) * 8, :], b_Qs, PSB[pq[nb_]], t)
                kvv = PS[pkv][:].rearrange("p (h d) -> p h d", h=8)
                c.op("act", lambda e: e.copy(out=Ks[:], in_=kvv[:, 0:4, :]), reads=[PSB[pkv]], writes=[b_Ks])
                rope(kvv[:, 0:4, :], 4, Ks[:], b_Ks, PSB[pkv], t)
                c.op("act", lambda e: e.copy(out=Va[slot][:, :, 0:64], in_=kvv[:, 4:8, :]), reads=[PSB[pkv]], writes=[b_Va[slot]])
                c.op("pool", lambda e: e.tensor_copy(out=Ksp[slot][:, :, 0, 0:64], in_=Ks[:]), reads=[b_Ks], writes=[b_Ksp[slot]])
                c.op("pool", lambda e: e.tensor_copy(out=Ksp[slot][:, :, 1, 64:128], in_=Ks[:]), reads=[b_Ks], writes=[b_Ksp[slot]])
                cx['Qs'] = Qs; cx['b_Qs'] = b_Qs

            def E_A2b(cx):
                t = cx['t']; Qs = cx['Qs']; b_Qs = cx['b_Qs']
                slot = t % 3
                QT, b_QT = QT_r.next()
                transpose_tile(Qs[:].rearrange("p h d -> p (h d)"), b_Qs, lambda: QT[:], b_QT, evac_eng="act")
                transpose_tile(Ksp[slot][:].rearrange("p g v d -> p (g v d)"), b_Ksp[slot], lambda: KT[slot][:], b_KT[slot], evac_eng="dve")
                cx['QT'] = QT; cx['b_QT'] = b_QT

            def E_B(cx):
                t = cx['t']; nblk = cx['nblk']; QT = cx['QT']; b_QT = cx['b_QT']
                O, b_O = O_r.next()
                kbs = ([((t - 1) % 3, mprev, b_mprev)] if nblk > 0 else []) + [(t % 3, mcur, b_mcur)]

                def scores(g):
                    Es = []
                    for (sl, mk, b_mk) in kbs:
                        ps_ = nbank()
                        for var in range(2):
                            c.op("pe", lambda e, ps_=ps_, sl=sl, g=g, var=var: e.matmul(PS[ps_][:, var * 256:(var + 1) * 256], lhsT=KT[sl][:, g * 2 + var, :],
                                                                                  rhs=QT[:, 2 * g:2 * g + 2, :].rearrange("p h t -> p (h t)"), start=True, stop=True),
                                 reads=[b_KT[sl], b_QT], writes=[PSB[ps_]], inc=(var == 1))
                        E, b_E = E_r.next()
                        c.op("act", lambda e, ps_=ps_, E=E: e.activation(out=E[:], in_=PS[ps_][:], func=AF.Exp, scale=0.125), reads=[PSB[ps_]], writes=[b_E])
                        c.op("pool", lambda e, E=E, mk=mk: e.tensor_tensor(out=E[:].rearrange("p (h t) -> p h t", h=4), in0=E[:].rearrange("p (h t) -> p h t", h=4),
                                                                       in1=mk[:].unsqueeze(1).broadcast_to([128, 4, 128]), op=ALU.mult), reads=[b_E, b_mk], writes=[b_E])
                        Es.append((E, b_E, sl))
                    return Es

                def pv(g, Es):
                    po_ = nbank()
                    for hl in range(4):
                        for i, (E, b_E, sl) in enumerate(Es):
                            c.op("pe", lambda e, po_=po_, hl=hl, E=E, sl=sl, g=g, i=i: e.matmul(PS[po_][:, hl * 65:(hl + 1) * 65], lhsT=E[:, hl * 128:(hl + 1) * 128], rhs=Va[sl][:, g, :],
                                                                                     start=(i == 0), stop=(i == len(Es) - 1)),
                                 reads=[b_E, b_Va[sl]], writes=[PSB[po_]], inc=(hl == 3 and i == len(Es) - 1))
                    dn, b_dn = dn_r.next()
                    ov = PS[po_][:, 0:260].rearrange("p (v pr d) -> p v pr d", v=2, pr=2)
                    c.op("dve", lambda e: e.tensor_tensor(out=dn[:, 0:4].rearrange("p (v pr) -> p v pr", v=2), in0=ov[:, :, :, 64],
                                                          in1=esk[:, 4 * g:4 * g + 4].rearrange("p (pr v) -> p v pr", v=2), op=ALU.add), reads=[PSB[po_], b_esk], writes=[b_dn])
                    c.op("dve", lambda e: e.reciprocal(out=dn[:, 4:8], in_=dn[:, 0:4]), reads=[b_dn], writes=[b_dn])
                    for var in range(2):
                        c.op("dve", lambda e, var=var: e.tensor_tensor(
                            out=O[:, 4 * g:4 * g + 4, :].rearrange("p (pr v) d -> p v pr d", v=2)[:, var, :, :], in0=ov[:, var, :, 0:64],
                            in1=dn[:, 4 + 2 * var:6 + 2 * var].unsqueeze(2).broadcast_to([128, 2, 64]), op=ALU.mult),
                             reads=[PSB[po_], b_dn], writes=[b_O])

                prev = None
                for g in range(4):
                    Es = scores(g)
                    if prev is not None:
                        pv(*prev)
                    prev = (g, Es)
                pv(*prev)
                cx['O'] = O; cx['b_O'] = b_O

            def E_C(cx):
                t = cx['t']; hres = cx['hres']; b_hres = cx['b_hres']; O = cx['O']; b_O = cx['b_O']
                OT, b_OT = OT_r.next()
                transpose_tile(O[:].rearrange("p h d -> p (h d)"), b_O, lambda: OT[:], b_OT, evac_eng="act")
                r, b_r = r_r.next()
                po = [nbank(), nbank()]
                for nb_ in range(2):
                    for kc in range(8):
                        c.op("pe", lambda e, nb_=nb_, kc=kc: e.matmul(PS[po[nb_]][:], lhsT=OT[:, kc, :], rhs=Wo[:, kc, nb_ * 512:(nb_ + 1) * 512], start=(kc == 0), stop=(kc == 7)),
                             reads=[b_OT, b_Wo], writes=[PSB[po[nb_]]], inc=(kc == 7))
                    c.op("dve", lambda e, nb_=nb_: e.scalar_tensor_tensor(out=r[:, nb_ * 512:(nb_ + 1) * 512], in0=hres[:, nb_ * 512:(nb_ + 1) * 512], scalar=ALPHA,
                                                                     in1=PS[po[nb_]][:], op0=ALU.mult, op1=ALU.add), reads=[b_hres, PSB[po[nb_]]], writes=[b_r])
                layernorm(rings, r, b_r, lng, b_lng, lnb, b_lnb)
                c.dma("sp", H3_d[t * 128:(t + 1) * 128, :], r[:], reads=[b_r], sem=osemE[t % 3])

            cxs = {}
            for i in range(NT + 2):
                if i < NT:
                    cxs[i] = E_A1(i)
                    E_A2a(cxs[i])
                if 0 <= i - 2 < NT:
                    E_C(cxs.pop(i - 2))
                if 0 <= i - 1 < NT:
                    E_B(cxs[i - 1])
                if i < NT:
                    E_A2b(cxs[i])
            c.barrier()
        if stop_after == "E":
            return nc

        with contextlib.ExitStack() as sF:
            lng, b_lng, lnb, b_lnb = load_ln(sF, 3)
            rings = ln_rings(sF)
            semF = c.dma_sem("cF")
            Wr = sbt(sF, "Wr", [128, 8, NE], F32); b_Wr = Buf()
            brt = sbt(sF, "brt", [128, NE], F32); b_brt = Buf()
            c.dma("sp", Wr[:], wr_d.rearrange("(kc p) n -> p kc n", p=128), writes=[b_Wr], sem=semF)
            c.dma("sp", brt[:], br_d[:, :], writes=[b_brt], sem=semF)
            STK = 1024
            NSUB = STK // 128
            WG = [sbt(sF, f"WGe{i}", [128, 8, ED], BF16) for i in range(2)]; b_WG = [Buf() for _ in range(2)]
            WU = [sbt(sF, f"WUe{i}", [128, 8, ED], BF16) for i in range(2)]; b_WU = [Buf() for _ in range(2)]
            WD = [sbt(sF, f"WDe{i}", [128, 8, D], BF16) for i in range(2)]; b_WD = [Buf() for _ in range(2)]
            wsem = [c.dma_sem("wF") for _ in range(2)]
            hTm = [sbt(sF, f"hTm{i}", [128, 8, STK], BF16) for i in range(2)]; b_hTm = [Buf() for _ in range(2)]
            h1m = sbt(sF, "h1m", [128, 8, STK], BF16); b_h1m = Buf()
            acc = sbt(sF, "acc", [128, NSUB, D], F32); b_acc = [Buf() for _ in range(NSUB)]
            comb = [sbt(sF, f"comb{i}", [128, NSUB, NE], F32) for i in range(2)]; b_comb = [[Buf() for _ in range(NSUB)] for _ in range(2)]
            h3_r = Ring(sF, "h3F", 2, [128, D], F32); h3sem = [c.dma_sem("h3F") for _ in range(2)]
            h3T_r = Ring(sF, "h3T", 2, [128, 8, 128], F32)
            lg_r = Ring(sF, "lg", 3, [128, 4, NE], F32)
            s_r = Ring(sF, "sF", 3, [128, 512], BF16)
            osemF = [c.dma_sem("oF") for _ in range(2)]
            n_w = 0

            def issue_weights(e_idx, slot):
                if wsem[slot][1] >= 2500:
                    wsem[slot] = c.dma_sem("wF")
                for kc in range(8):
                    c.dma("pool", WG[slot][:, kc, :], mg_d[e_idx, kc * 128:(kc + 1) * 128, :], writes=[b_WG[slot]], sem=wsem[slot])
                    c.dma("pool", WU[slot][:, kc, :], mu_d[e_idx, kc * 128:(kc + 1) * 128, :], writes=[b_WU[slot]], sem=wsem[slot])
                for kc in range(8):
                    c.dma("pool", WD[slot][:, kc, :], md_d[e_idx, kc * 128:(kc + 1) * 128, :], writes=[b_WD[slot]], sem=wsem[slot])

            def router_sub(st_, sub):
                t = st_ * NSUB + sub
                h3, b_h3 = h3_r.next(); h3T, b_h3T = h3T_r.next()
                c.dma("sp", h3[:], H3_d[t * 128:(t + 1) * 128, :], writes=[b_h3], sem=h3semA[t % 2])
                pt = [nbank(), nbank()]
                for kc in range(8):
                    c.op("pe", lambda e, kc=kc: e.transpose(out=PS[pt[kc // 4]][:, (kc % 4) * 128:(kc % 4 + 1) * 128], in_=h3[:, kc * 128:(kc + 1) * 128], identity=identf[:]),
                         reads=[b_h3, b_identf], writes=[PSB[pt[kc // 4]]], inc=(kc % 4 == 3))
                for hh in range(2):
                    src = PS[pt[hh]][:].rearrange("p (k t) -> p k t", k=4)
                    c.op("act", lambda e, hh=hh, src=src: e.copy(out=h3T[:, hh * 4:(hh + 1) * 4, :], in_=src), reads=[PSB[pt[hh]]], writes=[b_h3T])
                    c.op("dve", lambda e, hh=hh, src=src, sub=sub: e.tensor_copy(out=hTm[st_ % 2][:, hh * 4:(hh + 1) * 4, sub * 128:(sub + 1) * 128], in_=src), reads=[PSB[pt[hh]]], writes=[b_hTm[st_ % 2]])
                pl = nbank()
                for kc in range(8):
                    c.op("pe", lambda e, kc=kc: e.matmul(PS[pl][:, 0:NE], lhsT=h3T[:, kc, :], rhs=Wr[:, kc, :], start=(kc == 0), stop=(kc == 7)),
                         reads=[b_h3T, b_Wr], writes=[PSB[pl]], inc=(kc == 7))
                lg, b_lg = lg_r.next()
                c.op("dve", lambda e: e.tensor_tensor(out=lg[:, 0, :], in0=PS[pl][:, 0:NE], in1=brt[:], op=ALU.add), reads=[PSB[pl], b_brt], writes=[b_lg])
                c.op("dve", lambda e: e.max(out=lg[:, 1, :], in_=lg[:, 0, :]), reads=[b_lg], writes=[b_lg])
                c.op("dve", lambda e: e.tensor_scalar(out=lg[:, 2, :], in0=lg[:, 0, :], scalar1=lg[:, 1, 1:2], scalar2=None, op0=ALU.is_ge), reads=[b_lg], writes=[b_lg])
                c.op("dve", lambda e: e.tensor_scalar(out=lg[:, 1, 2:3], in0=lg[:, 1, 0:1], scalar1=-1.0, scalar2=None, op0=ALU.mult), reads=[b_lg], writes=[b_lg])
                c.op("act", lambda e: e.activation(out=lg[:, 3, :], in_=lg[:, 0, :], func=AF.Exp, bias=lg[:, 1, 2:3], scale=1.0), reads=[b_lg], writes=[b_lg])
                c.op("dve", lambda e: e.tensor_tensor(out=lg[:, 3, :], in0=lg[:, 3, :], in1=lg[:, 2, :], op=ALU.mult), reads=[b_lg], writes=[b_lg])
                c.op("dve", lambda e: e.tensor_reduce(out=lg[:, 1, 3:4], in_=lg[:, 3, :], axis=AX.X, op=ALU.add), reads=[b_lg], writes=[b_lg])
                c.op("dve", lambda e: e.reciprocal(out=lg[:, 1, 4:5], in_=lg[:, 1, 3:4]), reads=[b_lg], writes=[b_lg])
                c.op("dve", lambda e, sub=sub: e.tensor_scalar(out=comb[st_ % 2][:, sub, :], in0=lg[:, 3, :], scalar1=lg[:, 1, 4:5], scalar2=None, op0=ALU.mult), reads=[b_lg], writes=[b_comb[st_ % 2][sub]])

            def part1(st_, ex, slot):
                for half in range(STK // 512):
                    for fc in range(8):
                        pg, pu = nbank(), nbank()
                        for (W, bW, pb) in ((WG[slot], b_WG[slot], pg), (WU[slot], b_WU[slot], pu)):
                            for kc in range(8):
                                c.op("pe", lambda e, W=W, pb=pb, kc=kc, fc=fc, half=half: e.matmul(PS[pb][:], lhsT=W[:, kc, fc * 128:(fc + 1) * 128], rhs=hTm[st_ % 2][:, kc, half * 512:(half + 1) * 512],
                                                                                             start=(kc == 0), stop=(kc == 7)),
                                     reads=[bW, b_hTm[st_ % 2]], writes=[PSB[pb]], inc=(kc == 7))
                        sb_, b_s = s_r.next()
                        c.op("act", lambda e, pg=pg, sb_=sb_: e.activation(out=sb_[:], in_=PS[pg][:], func=AF.Silu), reads=[PSB[pg]], writes=[b_s])
                        c.op("dve", lambda e, pu=pu, sb_=sb_, fc=fc, half=half: e.tensor_tensor(out=h1m[:, fc, half * 512:(half + 1) * 512], in0=PS[pu][:], in1=sb_[:], op=ALU.mult),
                             reads=[PSB[pu], b_s], writes=[b_h1m])

            def part2(st_, ex, slot):
                for sub in range(NSUB):
                    for nb_ in range(2):
                        po_ = nbank()
                        for fc in range(8):
                            c.op("pe", lambda e, po_=po_, fc=fc, sub=sub, nb_=nb_, slot=slot: e.matmul(PS[po_][:], lhsT=h1m[:, fc, sub * 128:(sub + 1) * 128],
                                                                                             rhs=WD[slot][:, fc, nb_ * 512:(nb_ + 1) * 512], start=(fc == 0), stop=(fc == 7)),
                                 reads=[b_h1m, b_WD[slot]], writes=[PSB[po_]], inc=(fc == 7))
                        av = acc[:, sub, nb_ * 512:(nb_ + 1) * 512]
                        if ex == 0:
                            c.op("dve", lambda e, po_=po_, av=av, sub=sub, ex=ex: e.tensor_scalar(out=av, in0=PS[po_][:], scalar1=comb[st_ % 2][:, sub, ex:ex + 1], scalar2=None, op0=ALU.mult),
                                 reads=[PSB[po_], b_comb[st_ % 2][sub]], writes=[b_acc[sub]])
                        else:
                            c.op("dve", lambda e, po_=po_, av=av, sub=sub, ex=ex: e.scalar_tensor_tensor(out=av, in0=PS[po_][:], scalar=comb[st_ % 2][:, sub, ex:ex + 1], in1=av, op0=ALU.mult, op1=ALU.add),
                                 reads=[PSB[po_], b_comb[st_ % 2][sub], b_acc[sub]], writes=[b_acc[sub]])

            def finish(st_):
                for sub in range(NSUB):
                    t = st_ * NSUB + sub
                    h3, b_h3 = h3_r.next()
                    c.dma("sp", h3[:], H3_d[t * 128:(t + 1) * 128, :], writes=[b_h3], sem=h3semB[t % 2])
                    c.op("dve", lambda e, sub=sub: e.scalar_tensor_tensor(out=acc[:, sub, :], in0=h3[:], scalar=ALPHA, in1=acc[:, sub, :], op0=ALU.mult, op1=ALU.add),
                         reads=[b_h3, b_acc[sub]], writes=[b_acc[sub]])
                    layernorm(rings, acc[:, sub, :], b_acc[sub], lng, b_lng, lnb, b_lnb)
                    c.dma("act", out_d[t * 128:(t + 1) * 128, :], acc[:, sub, :], reads=[b_acc[sub]], sem=osemF[t % 2])

            NST = TOK // STK
            issue_weights(0, 0)
            h3semA = [c.dma_sem("h3A") for _ in range(2)]
            for sub in range(NSUB):
                router_sub(0, sub)
            for st_ in range(NST):
                h3semA = [c.dma_sem("h3A") for _ in range(2)]
                h3semB = [c.dma_sem("h3B") for _ in range(2)]
                for ex in range(NE):
                    slot = n_w % 2
                    n_w += 1
                    nxt = (st_ * NE + ex + 1)
                    if nxt < NST * NE:
                        issue_weights(nxt % NE, 1 - slot)
                    part1(st_, ex, slot)
                    if ex == 0 and st_ > 0:
                        finish(st_ - 1)
                    part2(st_, ex, slot)
                    if st_ + 1 < NST:
                        router_sub(st_ + 1, ex)
            finish(NST - 1)
            for k in ("sp",):
                for s_ in osemF:
                    nc.sync.wait_ge(s_[0], s_[1])
    return nc


def make_in_maps(inp):
    f = lambda a: np.ascontiguousarray(np.asarray(a), dtype=np.float32)
    x = f(inp["x"])
    pos = np.asarray(inp["positions"]).astype(np.int32)
    ln_g = f(inp["ln_g"]).reshape(4, D)
    ln_b = f(inp["ln_b"]).reshape(4, D)
    bcast = lambda a: np.ascontiguousarray(np.broadcast_to(a[None], (128,) + a.shape))
    lam_re = f(inp["ssm_lambda_re"])[0]; lam_im = f(inp["ssm_lambda_im"])[0]
    two = lambda a: np.ascontiguousarray(np.concatenate([a, a], axis=0))
    b_re = f(inp["ssm_b_re"])[0].transpose(1, 0, 2); b_im = f(inp["ssm_b_im"])[0].transpose(1, 0, 2)
    c_re = f(inp["ssm_c_re"])[0].transpose(2, 0, 1); c_im = f(inp["ssm_c_im"])[0].transpose(2, 0, 1)
    cat = lambda a, b: np.ascontiguousarray(np.concatenate([a, b], axis=0))
    inv_freq = (500000.0 ** (-np.arange(0, 16, 2, dtype=np.float32) / 16.0)).astype(np.float32)
    shared = {
        "ln_g": bcast(ln_g), "ln_b": bcast(ln_b),
        "lam_re": two(lam_re.T), "lam_im": two(lam_im.T), "lstep": bcast(f(inp["ssm_log_step"])[0]),
        "s5_ba": cat(b_re, b_im), "s5_bb": cat(b_im, b_re), "s5_ca": cat(c_re, c_im), "s5_cb": cat(c_im, c_re),
        "s5_d": bcast(f(inp["ssm_d"])[0]),
        "w_glu": f(inp["ssm_w_glu"])[0], "kv_w": f(inp["kv_w"]), "w_q": f(inp["attn_w_q"])[0],
        "sinks": bcast(f(inp["attn_sinks"])[0]), "w_out": f(inp["attn_w_out"])[0],
        "ffn_g": f(inp["ffn_w_gate"])[0], "ffn_u": f(inp["ffn_w_up"])[0], "ffn_d": f(inp["ffn_w_down"])[0],
        "w_router": f(inp["moe_w_router"])[0], "b_router": bcast(f(inp["moe_b_router"])[0]),
        "moe_g": f(inp["moe_w_gate"])[0], "moe_u": f(inp["moe_w_up"])[0], "moe_d": f(inp["moe_w_down"])[0],
        "inv_freq": bcast(inv_freq),
    }
    maps = []
    for i in range(NCORES):
        m = dict(shared)
        m["x"] = np.ascontiguousarray(x[2 * i:2 * i + 2].reshape(TOK, D))
        m["pos"] = np.ascontiguousarray(pos[2 * i:2 * i + 2].reshape(NT, 128).T)
        maps.append(m)
    return maps


def kernel(**inputs):
    nc = build()
    maps = make_in_maps(inputs)
    res = run_bass_kernel_spmd(nc, maps, core_ids=list(range(NCORES)))
    out = np.stack([np.asarray(r["out"]).reshape(NSEQ, L, D) for r in res.results], axis=0)
    return out.reshape(NCORES * NSEQ, L, D).astype(np.float32)
```

```python
import contextlib
import math
import numpy as np
import concourse.bass as bass
import concourse.mybir as mybir
from concourse.alu_op_type import AluOpType as ALU
from concourse.bass_utils import run_bass_kernel_spmd

F32 = mybir.dt.float32
BF16 = mybir.dt.bfloat16
I32 = mybir.dt.int32
AF = mybir.ActivationFunctionType
AX = mybir.AxisListType

NCORES = 8
D = 1024
L = 2048
NSEQ = 2
TOK = NSEQ * L
NT = TOK // 128
FF = 2816
NE = 8
ED = 1024
ALPHA = float(4.0 ** 0.25)
LN_EPS = 1e-5
MLIST = [0, -1, -2, -3, -4, -5, -6, -7] + list(range(16)) + [16, 32, 64, 128, 256, 512, 1024]
NM = len(MLIST)
SEM_ROLL = 6000


class Ev:
    __slots__ = ("sem", "val", "eng", "ref")

    def __init__(self, sem, val, eng, ref=None):
        self.sem = sem
        self.val = val
        self.eng = eng
        self.ref = ref


class Buf:
    __slots__ = ("name", "w", "r", "excl")

    def __init__(self, name="", excl=False):
        self.name = name
        self.w = None
        self.r = {}
        self.excl = excl


class Ctx:
    def __init__(self, nc, stack):
        self.nc = nc
        self.stack = stack
        self.engs = {"pe": nc.tensor, "act": nc.scalar, "dve": nc.vector, "pool": nc.gpsimd, "sp": nc.sync}
        self.sem = {}
        self.cnt = {}
        self.nsem = 0
        self.dsems = []
        for k in self.engs:
            self._new_eng_sem(k)
        self.waited = {k: {} for k in self.engs}
        self.pending = {k: [] for k in self.engs}
        self.ninst = {k: 0 for k in self.engs}
        self.last = {k: None for k in self.engs}

    def _new_sem(self, name):
        self.nsem += 1
        return self.stack.enter_context(self.nc.semaphore(f"{name}_{self.nsem}"))

    def _new_eng_sem(self, k):
        self.sem[k] = self._new_sem("e" + k)
        self.cnt[k] = 0

    def dma_sem(self, name="d"):
        s = [self._new_sem(name), 0]
        self.dsems.append(s)
        return s

    def _wait(self, k, ev):
        if ev is None:
            return
        if ev.eng == "pe" and k == "pe":
            return
        if ev.val is None:
            raise RuntimeError("dependency on an instruction without inc")
        w = self.waited[k]
        sid = id(ev.sem)
        val = ev.ref[1] if ev.ref is not None else ev.val
        if w.get(sid, 0) >= val:
            return
        self.engs[k].wait_ge(ev.sem, val)
        self.ninst[k] += 1
        w[sid] = val

    def _deps(self, k, reads, writes):
        for b in reads:
            self._wait(k, b.w)
            if b.excl:
                for kk, e in b.r.items():
                    if kk != k:
                        self._wait(k, e)
        for b in writes:
            self._wait(k, b.w)
            for e in b.r.values():
                self._wait(k, e)

    def _commit(self, ev, reads, writes):
        key = id(ev.sem) if ev.eng == "dma" else ev.eng
        for b in reads:
            b.r[key] = ev
        for b in writes:
            b.w = ev
            b.r = {}

    def op(self, k, fn, reads=(), writes=(), inc=True):
        self._deps(k, reads, writes)
        inst = fn(self.engs[k])
        self.ninst[k] += 1
        if inc:
            if self.cnt[k] >= SEM_ROLL:
                self._new_eng_sem(k)
            self.cnt[k] += 1
            inst.then_inc(self.sem[k], 1)
            ev = Ev(self.sem[k], self.cnt[k], k)
            for p in self.pending[k]:
                p.sem = ev.sem
                p.val = ev.val
            self.pending[k] = []
            self.last[k] = ev
        else:
            ev = Ev(None, None, k)
            self.pending[k].append(ev)
        self._commit(ev, reads, writes)
        return ev

    def dma(self, k, out, in_, reads=(), writes=(), sem=None, **kw):
        self._deps(k, reads, writes)
        inst = self.engs[k].dma_start(out=out, in_=in_, **kw)
        self.ninst[k] += 1
        sem[1] += 16
        inst.then_inc(sem[0], 16)
        ev = Ev(sem[0], sem[1], "dma", sem)
        self._commit(ev, reads, writes)
        return ev

    def barrier(self, engs=("pe", "act", "dve", "pool", "sp")):
        for k in engs:
            assert not self.pending[k]
        for k in engs:
            for k2 in engs:
                if k2 != k and self.last[k2] is not None:
                    self._wait(k, self.last[k2])
            for s in self.dsems:
                if s[1] > 0:
                    self._wait(k, Ev(s[0], s[1], "dma", s))


LAST_CTX = None


def build(stop_after=None):
    global LAST_CTX
    nc = bass.Bass("TRN2", target_bir_lowering=False)
    dram_in = lambda name, shape, dt=F32: nc.dram_tensor(name, list(shape), dt, kind="ExternalInput").ap()
    x_d = dram_in("x", [TOK, D])
    pos_d = dram_in("pos", [128, NT], I32)
    lnG_d = dram_in("ln_g", [128, 4, D])
    lnB_d = dram_in("ln_b", [128, 4, D])
    lamr_d = dram_in("lam_re", [128, 64])
    lami_d = dram_in("lam_im", [128, 64])
    lstep_d = dram_in("lstep", [128, 64])
    BA_d = dram_in("s5_ba", [128, 64, 16])
    BB_d = dram_in("s5_bb", [128, 64, 16])
    CA_d = dram_in("s5_ca", [128, 64, 16])
    CB_d = dram_in("s5_cb", [128, 64, 16])
    dsk_d = dram_in("s5_d", [128, D])
    wglu_d = dram_in("w_glu", [D, 2 * D])
    kvw_d = dram_in("kv_w", [D, 512])
    wq_d = dram_in("w_q", [D, D])
    sink_d = dram_in("sinks", [128, 16])
    wo_d = dram_in("w_out", [D, D])
    fg_d = dram_in("ffn_g", [D, FF])
    fu_d = dram_in("ffn_u", [D, FF])
    fd_d = dram_in("ffn_d", [FF, D])
    wr_d = dram_in("w_router", [D, NE])
    br_d = dram_in("b_router", [128, NE])
    mg_d = dram_in("moe_g", [NE, D, ED])
    mu_d = dram_in("moe_u", [NE, D, ED])
    md_d = dram_in("moe_d", [NE, ED, D])
    invf_d = dram_in("inv_freq", [128, 8])
    out_d = nc.dram_tensor("out", [TOK, D], F32, kind="ExternalOutput").ap()
    dbg = stop_after is not None
    G_d = nc.dram_tensor("G", [TOK, D], BF16, kind="ExternalOutput" if stop_after == "A" else "Internal").ap()
    H1_d = nc.dram_tensor("H1", [TOK, D], F32, kind="ExternalOutput" if stop_after == "B" else "Internal").ap()
    H2_d = nc.dram_tensor("H2", [TOK, D], F32, kind="ExternalOutput" if stop_after == "C" else "Internal").ap()
    H3_d = nc.dram_tensor("H3", [TOK, D], F32, kind="ExternalOutput" if stop_after == "E" else "Internal").ap()
    if stop_after == "P":
        dbgP = nc.dram_tensor("dbgP", [128, 2 * NM + 2, 64], F32, kind="ExternalOutput").ap()
        dbgT = nc.dram_tensor("dbgT", [128, 3, 64, 128], BF16, kind="ExternalOutput").ap()

    with contextlib.ExitStack() as st:
        c = Ctx(nc, st)
        LAST_CTX = c
        uniq = [0]

        def sbt(stack, name, shape, dt):
            uniq[0] += 1
            return stack.enter_context(nc.sbuf_tensor(f"{name}_{uniq[0]}", list(shape), dt))
        PS = [st.enter_context(nc.psum_tensor(f"ps{i}", [128, 512], F32)) for i in range(8)]
        PSB = [Buf(f"ps{i}", excl=True) for i in range(8)]

        identf = sbt(st, "identf", [128, 128], F32); b_identf = Buf()
        identb = sbt(st, "identb", [128, 128], BF16); b_identb = Buf()
        c.op("pool", lambda e: e.memset(identf[:], 0.0), writes=[b_identf])
        c.op("pool", lambda e: e.affine_select(out=identf[:], in_=identf[:], pattern=[[-1, 128]], compare_op=ALU.not_equal,
                                               fill=1.0, base=0, channel_multiplier=1), reads=[b_identf], writes=[b_identf])
        c.op("pool", lambda e: e.tensor_copy(out=identb[:], in_=identf[:]), reads=[b_identf], writes=[b_identb])
        out_sem = c.dma_sem("out")
        mhalf = sbt(st, "mhalf", [128, 1], F32); b_mhalf = Buf()
        c.op("pool", lambda e: e.memset(mhalf[:], -0.5), writes=[b_mhalf])

        with contextlib.ExitStack() as sa:
            Tm = sbt(sa, "Tm", [128, 64, 128], BF16); b_Tm = Buf()
            MinT = sbt(sa, "MinT", [128, 64, 128], BF16); b_MinT = Buf()
            Mout = sbt(sa, "Mout", [128, 64, 128], BF16); b_Mout = Buf()
            PA = sbt(sa, "PA", [128, 8, 64], F32); b_PA = Buf()
            PB = sbt(sa, "PB", [128, 8, 64], F32); b_PB = Buf()
            D1b = sbt(sa, "D1b", [128, 128], BF16); b_D1 = Buf()
            D2b = sbt(sa, "D2b", [128, 128], BF16); b_D2 = Buf()
            dB = sbt(sa, "dB", [128, D], F32); b_dB = Buf()
            ld0 = c.dma_sem("ld0")
            c.dma("sp", dB[:], dsk_d[:, :], writes=[b_dB], sem=ld0)
            with contextlib.ExitStack() as s0:
                lr = sbt(s0, "lr", [128, 64], F32); b_lr = Buf()
                li = sbt(s0, "li", [128, 64], F32); b_li = Buf()
                ls = sbt(s0, "ls", [128, 64], F32); b_ls = Buf()
                BAt = sbt(s0, "BAt", [128, 64, 16], F32); b_BA = Buf()
                BBt = sbt(s0, "BBt", [128, 64, 16], F32); b_BB = Buf()
                CAt = sbt(s0, "CAt", [128, 64, 16], F32); b_CA = Buf()
                CBt = sbt(s0, "CBt", [128, 64, 16], F32); b_CB = Buf()
                c.dma("sp", lr[:], lamr_d[:, :], writes=[b_lr], sem=ld0)
                c.dma("sp", li[:], lami_d[:, :], writes=[b_li], sem=ld0)
                c.dma("sp", ls[:], lstep_d[:, :], writes=[b_ls], sem=ld0)
                c.dma("act", BAt[:], BA_d[:, :, :], writes=[b_BA], sem=ld0)
                c.dma("act", BBt[:], BB_d[:, :, :], writes=[b_BB], sem=ld0)
                c.dma("act", CAt[:], CA_d[:, :, :], writes=[b_CA], sem=ld0)
                c.dma("act", CBt[:], CB_d[:, :, :], writes=[b_CB], sem=ld0)

                sgn = sbt(s0, "sgn", [128, 1], F32); b_sgn = Buf()
                sgn2 = sbt(s0, "sgn2", [128, 1], F32); b_sgn2 = Buf()
                c.op("pool", lambda e: e.memset(sgn[0:64, :], -1.0), writes=[b_sgn])
                c.op("pool", lambda e: e.memset(sgn[64:128, :], 1.0), writes=[b_sgn])
                c.op("pool", lambda e: e.memset(sgn2[0:64, :], 1.0), writes=[b_sgn2])
                c.op("pool", lambda e: e.memset(sgn2[64:128, :], -1.0), writes=[b_sgn2])
                D2f = sbt(s0, "D2f", [128, 128], F32); b_D2f = Buf()
                cmask = sbt(s0, "cmask", [128, 128], F32); b_cm = Buf()
                c.op("pool", lambda e: e.memset(D2f[:], 0.0), writes=[b_D2f])
                c.op("pool", lambda e: e.affine_select(out=D2f[:], in_=D2f[:], pattern=[[-1, 128]], compare_op=ALU.not_equal,
                                                       fill=1.0, base=64, channel_multiplier=1), reads=[b_D2f], writes=[b_D2f])
                c.op("pool", lambda e: e.affine_select(out=D2f[:], in_=D2f[:], pattern=[[-1, 128]], compare_op=ALU.not_equal,
                                                       fill=1.0, base=-64, channel_multiplier=1), reads=[b_D2f], writes=[b_D2f])
                c.op("pool", lambda e: e.tensor_copy(out=D2b[:], in_=D2f[:]), reads=[b_D2f], writes=[b_D2])
                c.op("pool", lambda e: e.tensor_copy(out=D1b[:], in_=identf[:]), reads=[b_identf], writes=[b_D1])
                c.op("pool", lambda e: e.memset(cmask[:], 1.0), writes=[b_cm])
                c.op("pool", lambda e: e.affine_select(out=cmask[:].rearrange("p (i o) -> p i o", i=8),
                                                       in_=cmask[:].rearrange("p (i o) -> p i o", i=8),
                                                       pattern=[[16, 8], [0, 16]], compare_op=ALU.is_ge,
                                                       fill=0.0, base=15, channel_multiplier=-1), reads=[b_cm], writes=[b_cm])

                s0a = contextlib.ExitStack()
                s0a.__enter__()
                def T3(name, stk=None):
                    return sbt(s0a if stk is None else stk, name, [128, NM, 64], F32), Buf(name)
                Pr, b_Pr = T3("Pr", s0)
                Pi, b_Pi = T3("Pi", s0)
                Mtab, b_Mt = T3("Mtab")
                for j, m in enumerate(MLIST):
                    c.op("pool", lambda e, j=j, m=m: e.memset(Mtab[:, j, :], float(m)), writes=[b_Mt])
                dt_ = sbt(s0a, "dt", [128, 64], F32); b_dt = Buf()
                lrdt = sbt(s0a, "lrdt", [128, 64], F32); b_lrdt = Buf()
                lidt = sbt(s0a, "lidt", [128, 64], F32); b_lidt = Buf()
                c.op("act", lambda e: e.activation(out=dt_[:], in_=ls[:], func=AF.Exp), reads=[b_ls], writes=[b_dt])
                c.op("dve", lambda e: e.tensor_tensor(out=lrdt[:], in0=lr[:], in1=dt_[:], op=ALU.mult), reads=[b_lr, b_dt], writes=[b_lrdt])
                c.op("dve", lambda e: e.tensor_tensor(out=lidt[:], in0=li[:], in1=dt_[:], op=ALU.mult), reads=[b_li, b_dt], writes=[b_lidt])
                bc3 = lambda t: t[:].unsqueeze(1).broadcast_to([128, NM, 64])
                Et, b_E = T3("Et")
                mag, b_mag = T3("mag")
                Ang, b_Ang = T3("Ang")
                c.op("dve", lambda e: e.tensor_tensor(out=Et[:], in0=Mtab[:], in1=bc3(lrdt), op=ALU.mult), reads=[b_Mt, b_lrdt], writes=[b_E])
                c.op("act", lambda e: e.activation(out=mag[:], in_=Et[:], func=AF.Exp), reads=[b_E], writes=[b_mag])
                c.op("dve", lambda e: e.tensor_tensor(out=Ang[:], in0=Mtab[:], in1=bc3(lidt), op=ALU.mult), reads=[b_Mt, b_lidt], writes=[b_Ang])
                kq, b_kq = T3("kq")
                ki_ = sbt(s0a, "ki", [128, NM, 64], I32); b_ki = Buf()
                kf, b_kf = T3("kf")
                yv, b_y = T3("yv")
                tw, b_tw = T3("tw")
                C1 = 6.28125
                C2 = float(2.0 * math.pi - 6.28125)
                PI = float(math.pi)
                TWO_PI = float(2.0 * math.pi)
                c.op("dve", lambda e: e.tensor_scalar(out=kq[:], in0=Ang[:], scalar1=float(1.0 / TWO_PI), scalar2=None, op0=ALU.mult), reads=[b_Ang], writes=[b_kq])
                c.op("dve", lambda e: e.tensor_copy(out=ki_[:], in_=kq[:]), reads=[b_kq], writes=[b_ki])
                c.op("dve", lambda e: e.tensor_copy(out=kf[:], in_=ki_[:]), reads=[b_ki], writes=[b_kf])
                c.op("dve", lambda e: e.scalar_tensor_tensor(out=yv[:], in0=kf[:], scalar=-C1, in1=Ang[:], op0=ALU.mult, op1=ALU.add), reads=[b_kf, b_Ang], writes=[b_y])
                c.op("dve", lambda e: e.scalar_tensor_tensor(out=yv[:], in0=kf[:], scalar=-C2, in1=yv[:], op0=ALU.mult, op1=ALU.add), reads=[b_kf, b_y], writes=[b_y])

                def wrap(t, b_t):
                    c.op("dve", lambda e: e.tensor_scalar(out=tw[:], in0=t[:], scalar1=-PI, scalar2=TWO_PI, op0=ALU.is_lt, op1=ALU.mult), reads=[b_t], writes=[b_tw])
                    c.op("dve", lambda e: e.tensor_tensor(out=t[:], in0=t[:], in1=tw[:], op=ALU.add), reads=[b_t, b_tw], writes=[b_t])
                    c.op("dve", lambda e: e.tensor_scalar(out=tw[:], in0=t[:], scalar1=PI, scalar2=-TWO_PI, op0=ALU.is_gt, op1=ALU.mult), reads=[b_t], writes=[b_tw])
                    c.op("dve", lambda e: e.tensor_tensor(out=t[:], in0=t[:], in1=tw[:], op=ALU.add), reads=[b_t, b_tw], writes=[b_t])
                wrap(yv, b_y)
                wrap(yv, b_y)
                sn, b_sn = T3("sn")
                cs, b_cs = T3("cs")
                yc, b_yc = T3("yc")
                c.op("act", lambda e: e.activation(out=sn[:], in_=yv[:], func=AF.Sin), reads=[b_y], writes=[b_sn])
                c.op("dve", lambda e: e.tensor_scalar(out=yc[:], in0=yv[:], scalar1=float(PI / 2), scalar2=None, op0=ALU.add), reads=[b_y], writes=[b_yc])
                wrap(yc, b_yc)
                c.op("act", lambda e: e.activation(out=cs[:], in_=yc[:], func=AF.Sin), reads=[b_yc], writes=[b_cs])
                c.op("dve", lambda e: e.tensor_tensor(out=Pr[:], in0=mag[:], in1=cs[:], op=ALU.mult), reads=[b_mag, b_cs], writes=[b_Pr])
                c.op("dve", lambda e: e.tensor_tensor(out=Pi[:], in0=mag[:], in1=sn[:], op=ALU.mult), reads=[b_mag, b_sn], writes=[b_Pi])
                c.barrier()
                s0a.__exit__(None, None, None)
                J1 = 9
                t64 = lambda name: (sbt(s0, name, [128, 64], F32), Buf(name))
                nr, b_nr = t64("nr"); den, b_den = t64("den"); t1, b_t1 = t64("t1"); t2, b_t2 = t64("t2")
                kr, b_kr = t64("kr"); kim, b_kim = t64("kim"); rden, b_rden = t64("rden")
                c.op("dve", lambda e: e.tensor_scalar(out=nr[:], in0=Pr[:, J1, :], scalar1=-1.0, scalar2=None, op0=ALU.add), reads=[b_Pr], writes=[b_nr])
                c.op("dve", lambda e: e.tensor_tensor(out=t1[:], in0=lr[:], in1=lr[:], op=ALU.mult), reads=[b_lr], writes=[b_t1])
                c.op("dve", lambda e: e.tensor_tensor(out=t2[:], in0=li[:], in1=li[:], op=ALU.mult), reads=[b_li], writes=[b_t2])
                c.op("dve", lambda e: e.tensor_tensor(out=den[:], in0=t1[:], in1=t2[:], op=ALU.add), reads=[b_t1, b_t2], writes=[b_den])
                c.op("dve", lambda e: e.reciprocal(out=rden[:], in_=den[:]), reads=[b_den], writes=[b_rden])
                c.op("dve", lambda e: e.tensor_tensor(out=t1[:], in0=nr[:], in1=lr[:], op=ALU.mult), reads=[b_nr, b_lr, b_den], writes=[b_t1])
                c.op("dve", lambda e: e.tensor_tensor(out=t2[:], in0=Pi[:, J1, :], in1=li[:], op=ALU.mult), reads=[b_Pi, b_li, b_den], writes=[b_t2])
                c.op("dve", lambda e: e.tensor_tensor(out=t1[:], in0=t1[:], in1=t2[:], op=ALU.add), reads=[b_t1, b_t2], writes=[b_t1])
                c.op("dve", lambda e: e.tensor_tensor(out=kr[:], in0=t1[:], in1=rden[:], op=ALU.mult), reads=[b_t1, b_rden], writes=[b_kr])
                c.op("dve", lambda e: e.tensor_tensor(out=t1[:], in0=Pi[:, J1, :], in1=lr[:], op=ALU.mult), reads=[b_Pi, b_lr, b_kr], writes=[b_t1])
                c.op("dve", lambda e: e.tensor_tensor(out=t2[:], in0=nr[:], in1=li[:], op=ALU.mult), reads=[b_nr, b_li, b_kr], writes=[b_t2])
                c.op("dve", lambda e: e.tensor_tensor(out=t1[:], in0=t1[:], in1=t2[:], op=ALU.subtract), reads=[b_t1, b_t2], writes=[b_t1])
                c.op("dve", lambda e: e.tensor_tensor(out=kim[:], in0=t1[:], in1=rden[:], op=ALU.mult), reads=[b_t1, b_rden], writes=[b_kim])
                Qr = sbt(s0, "Qr", [128, 8, 64], F32); b_Qr = Buf()
                Qi = sbt(s0, "Qi", [128, 8, 64], F32); b_Qi = Buf()
                q1 = sbt(s0, "q1", [128, 8, 64], F32); b_q1 = Buf()
                bc8 = lambda t: t[:].unsqueeze(1).broadcast_to([128, 8, 64])
                c.op("dve", lambda e: e.tensor_tensor(out=Qr[:], in0=Pr[:, 0:8, :], in1=bc8(kr), op=ALU.mult), reads=[b_Pr, b_kr], writes=[b_Qr])
                c.op("dve", lambda e: e.tensor_tensor(out=q1[:], in0=Pi[:, 0:8, :], in1=bc8(kim), op=ALU.mult), reads=[b_Pi, b_kim], writes=[b_q1])
                c.op("dve", lambda e: e.tensor_tensor(out=Qr[:], in0=Qr[:], in1=q1[:], op=ALU.subtract), reads=[b_Qr, b_q1], writes=[b_Qr])
                c.op("dve", lambda e: e.tensor_tensor(out=Qi[:], in0=Pi[:, 0:8, :], in1=bc8(kr), op=ALU.mult), reads=[b_Pi, b_kr], writes=[b_Qi])
                c.op("dve", lambda e: e.tensor_tensor(out=q1[:], in0=Pr[:, 0:8, :], in1=bc8(kim), op=ALU.mult), reads=[b_Pr, b_kim, b_Qr], writes=[b_q1])
                c.op("dve", lambda e: e.tensor_tensor(out=Qi[:], in0=Qi[:], in1=q1[:], op=ALU.add), reads=[b_Qi, b_q1], writes=[b_Qi])
                c.op("dve", lambda e: e.tensor_scalar(out=Qi[:], in0=Qi[:], scalar1=sgn[:, 0:1], scalar2=None, op0=ALU.mult), reads=[b_Qi, b_sgn], writes=[b_Qi])
                SIDX = [16, 24, 25, 26, 27, 28, 29, 30]
                for k, ix in enumerate(SIDX):
                    c.op("pool", lambda e, k=k, ix=ix: e.tensor_copy(out=PA[:, k, :], in_=Pr[:, ix, :]), reads=[b_Pr], writes=[b_PA])
                    c.op("dve", lambda e, k=k, ix=ix: e.tensor_scalar(out=PB[:, k, :], in0=Pi[:, ix, :], scalar1=sgn2[:, 0:1], scalar2=None, op0=ALU.mult), reads=[b_Pi, b_sgn2], writes=[b_PB])
                Prs = sbt(s0, "Prs", [128, 16, 64], F32); b_Prs = Buf()
                c.op("dve", lambda e: e.tensor_scalar(out=Prs[:], in0=Pr[:, 8:24, :], scalar1=sgn2[:, 0:1], scalar2=None, op0=ALU.mult), reads=[b_Pr, b_sgn2], writes=[b_Prs])
                X = sbt(s0, "X", [128, 32, 8, 16], F32); b_X = Buf()
                X2 = sbt(s0, "X2", [128, 32, 8, 16], F32); b_X2 = Buf()
                YY = sbt(s0, "YY", [128, 32, 16, 16], F32); b_YY = Buf()
                Y2 = sbt(s0, "Y2", [128, 32, 16, 16], F32); b_Y2 = Buf()
                for hg in range(2):
                    gs = slice(hg * 32, hg * 32 + 32)
                    qv = lambda t: t[:, :, gs].rearrange("p j g -> p g j").unsqueeze(3).broadcast_to([128, 32, 8, 16])
                    bv = lambda t: t[:, gs, :].unsqueeze(2).broadcast_to([128, 32, 8, 16])
                    pv = lambda ap: ap.rearrange("p m g -> p g m").unsqueeze(3).broadcast_to([128, 32, 16, 16])
                    cv = lambda t: t[:, gs, :].unsqueeze(2).broadcast_to([128, 32, 16, 16])
                    c.op("dve", lambda e: e.tensor_tensor(out=X[:], in0=qv(Qr), in1=bv(BAt), op=ALU.mult), reads=[b_Qr, b_BA], writes=[b_X])
                    c.op("pool", lambda e: e.tensor_tensor(out=X2[:], in0=qv(Qi), in1=bv(BBt), op=ALU.mult), reads=[b_Qi, b_BB], writes=[b_X2])
                    c.op("dve", lambda e: e.tensor_tensor(out=X[:], in0=X[:], in1=X2[:], op=ALU.add), reads=[b_X, b_X2], writes=[b_X])
                    c.op("dve", lambda e: e.tensor_tensor(out=YY[:], in0=pv(Prs[:, :, gs]), in1=cv(CAt), op=ALU.mult), reads=[b_Prs, b_CA], writes=[b_YY])
                    c.op("pool", lambda e: e.tensor_tensor(out=Y2[:], in0=pv(Pi[:, 8:24, gs]), in1=cv(CBt), op=ALU.mult), reads=[b_Pi, b_CB], writes=[b_Y2])
                    c.op("dve", lambda e: e.tensor_tensor(out=YY[:], in0=YY[:], in1=Y2[:], op=ALU.subtract), reads=[b_YY, b_Y2], writes=[b_YY])
                    c.op("act", lambda e: e.copy(out=Mout[:, gs, :].rearrange("p g (m o) -> p g m o", m=8), in_=YY[:, :, 8:16, :]), reads=[b_YY], writes=[b_Mout])
                    for q4 in range(8):
                        pa, pb = q4 % 2, 2 + (q4 % 2)
                        G0 = hg * 32 + q4 * 4
                        for gg in range(4):
                            g = q4 * 4 + gg
                            c.op("pe", lambda e, g=g, gg=gg, pa=pa: e.matmul(PS[pa][:, gg * 128:(gg + 1) * 128],
                                                                             lhsT=X[:, g, :, :].rearrange("p j c -> p (j c)"),
                                                                             rhs=YY[:, g, 0:8, :].rearrange("p m o -> p (m o)"),
                                                                             start=True, stop=True),
                                 reads=[b_X, b_YY], writes=[PSB[pa]], inc=(gg == 3))
                        c.op("dve", lambda e, G0=G0, pa=pa: e.tensor_tensor(out=Tm[:, G0:G0 + 4, :],
                                                                            in0=PS[pa][:].rearrange("p (g n) -> p g n", g=4),
                                                                            in1=cmask[:].unsqueeze(1).broadcast_to([128, 4, 128]), op=ALU.mult),
                             reads=[PSB[pa], b_cm], writes=[b_Tm])
                        for gg in range(4):
                            g = q4 * 4 + gg
                            c.op("pe", lambda e, g=g, gg=gg, pb=pb: e.transpose(out=PS[pb][:, gg * 128:(gg + 1) * 128],
                                                                                in_=X[:, g, :, :].rearrange("p j c -> p (j c)"),
                                                                                identity=identf[:]),
                                 reads=[b_X, b_identf], writes=[PSB[pb]], inc=(gg == 3))
                        c.op("act", lambda e, G0=G0, pb=pb: e.copy(out=MinT[:, G0:G0 + 4, :],
                                                                   in_=PS[pb][:].rearrange("p (g n) -> p g n", g=4)),
                             reads=[PSB[pb]], writes=[b_MinT])
                if stop_after == "P":
                    ds = c.dma_sem("dbg")
                    c.dma("sp", dbgP[:, 0:NM, :], Pr[:], reads=[b_Pr], sem=ds)
                    c.dma("sp", dbgP[:, NM:2 * NM, :], Pi[:], reads=[b_Pi], sem=ds)
                    c.dma("sp", dbgP[:, 2 * NM, :], kr[:], reads=[b_kr], sem=ds)
                    c.dma("sp", dbgP[:, 2 * NM + 1, :], kim[:], reads=[b_kim], sem=ds)
                    c.dma("sp", dbgT[:, 0, :, :], Tm[:], reads=[b_Tm], sem=ds)
                    c.dma("sp", dbgT[:, 1, :, :], MinT[:], reads=[b_MinT], sem=ds)
                    c.dma("sp", dbgT[:, 2, :, :], Mout[:], reads=[b_Mout], sem=ds)
                    nc.sync.wait_ge(ds[0], ds[1])
                    return nc
                c.barrier()
            with contextlib.ExitStack() as s1:
                xv = x_d.rearrange("(ct ch j) d -> ch ct j d", ct=4, ch=128, j=8)
                Gv = G_d.rearrange("(ct ch j) d -> ch ct j d", ct=4, ch=128, j=8)
                NXB = 2
                XB = [sbt(s1, f"XB{i}", [128, 4, 8, 128], F32) for i in range(NXB)]; b_XB = [Buf() for _ in range(NXB)]
                xsem = [c.dma_sem("xs") for _ in range(NXB)]
                XR = [sbt(s1, f"XR{i}", [128, 4, 8, 128], BF16) for i in range(NXB)]; b_XR = [Buf() for _ in range(NXB)]
                U = [sbt(s1, f"U{i}", [128, 8, 512], BF16) for i in range(NXB)]; b_U = [Buf() for _ in range(NXB)]
                YB = [sbt(s1, f"YB{i}", [128, 4, 8, 128], F32) for i in range(NXB)]; b_YB = [Buf() for _ in range(NXB)]
                GB = [sbt(s1, f"GB{i}", [128, 4, 8, 128], BF16) for i in range(NXB)]; b_GB = [Buf() for _ in range(NXB)]
                gsem = [c.dma_sem("gs") for _ in range(NXB)]
                Hr = [[sbt(s1, f"Hr{i}_{k}", [128, 512], BF16) for k in range(2)] for i in range(4)]
                b_Hr = [[Buf() for k in range(2)] for i in range(4)]
                Hi = [[sbt(s1, f"Hi{i}_{k}", [128, 512], BF16) for k in range(2)] for i in range(4)]
                b_Hi = [[Buf() for k in range(2)] for i in range(4)]
                Hp = [sbt(s1, f"Hp{i}", [128, 2, 256], BF16) for i in range(4)]; b_Hp = [Buf() for _ in range(4)]
                Ysb = [sbt(s1, f"Ysb{i}", [128, 512], F32) for i in range(4)]; b_Ysb = [Buf() for _ in range(4)]
                for i in range(4):
                    c.op("pool", lambda e, i=i: e.memset(Hp[i][:], 0.0), writes=[b_Hp[i]])
                g_ev = []
                def A_head(gb):
                    s = gb % NXB
                    for ct in range(4):
                        c.dma("sp" if ct % 2 == 0 else "act", XB[s][:, ct, :, :], xv[:, ct, :, gb * 128:(gb + 1) * 128],
                              writes=[b_XB[s]], sem=xsem[s])
                    for ct in range(4):
                        c.op("pool", lambda e, ct=ct, s=s: e.tensor_copy(
                            out=XR[s][:, ct, :, :].rearrange("p gl (j c) -> p gl j c", j=8),
                            in_=XB[s][:, ct, :, :].rearrange("p j (gl c) -> p gl j c", c=16)),
                             reads=[b_XB[s]], writes=[b_XR[s]])
                    for gl in range(8):
                        pb = 6 + (gl % 2)
                        psb16 = PS[pb][:].bitcast(BF16)
                        for ct in range(4):
                            c.op("pe", lambda e, ct=ct, gl=gl, s=s, psb16=psb16: e.transpose(
                                out=psb16[:, ct * 128:(ct + 1) * 128], in_=XR[s][:, ct, gl, :], identity=identb[:]),
                                 reads=[b_XR[s], b_identb], writes=[PSB[pb]], inc=(ct == 3))
                        c.op("act", lambda e, gl=gl, s=s, psb16=psb16: e.copy(out=U[s][:, gl, :], in_=psb16[:, 0:512]),
                             reads=[PSB[pb]], writes=[b_U[s]])

                A_head(0)
                for gb in range(8):
                    s = gb % NXB
                    for quad in range(2):
                        if quad == 1 and gb + 1 < 8:
                            A_head(gb + 1)
                        for gg in range(4):
                            gl = quad * 4 + gg
                            g = gb * 8 + gl
                            c.op("pe", lambda e, g=g, gl=gl, gg=gg, s=s: e.matmul(PS[gg][:], lhsT=MinT[:, g, :], rhs=U[s][:, gl, :], start=True, stop=True),
                                 reads=[b_MinT, b_U[s]], writes=[PSB[gg]])
                        for k in range(8):
                            sh = 1 << k
                            for gg in range(4):
                                for (eng, g2) in (("act", gg), ("dve", gg)):
                                    g = gb * 8 + quad * 4 + g2
                                    if eng == "act":
                                        c.op("act", lambda e, g2=g2, k=k, g=g: e.activation(out=Hr[g2][k % 2][:], in_=PS[g2][:], func=AF.Identity, scale=PA[:, k, g:g + 1]),
                                             reads=[PSB[g2], b_PA], writes=[b_Hr[g2][k % 2]])
                                    else:
                                        c.op("dve", lambda e, g2=g2, k=k, g=g: e.tensor_scalar(out=Hi[g2][k % 2][:], in0=PS[g2][:], scalar1=PB[:, k, g:g + 1], scalar2=None, op0=ALU.mult),
                                             reads=[PSB[g2], b_PB], writes=[b_Hi[g2][k % 2]])
                            for gg in range(4):
                                for (Dm, b_Dm, Hx, b_Hx, last) in ((D1b, b_D1, Hr, b_Hr, False), (D2b, b_D2, Hi, b_Hi, True)):
                                    for sq in range(2):
                                        c.op("pe", lambda e, gg=gg, k=k, sh=sh, sq=sq, Dm=Dm, Hx=Hx: e.matmul(
                                            PS[gg][:, sq * 256 + sh:(sq + 1) * 256],
                                            lhsT=Dm[:],
                                            rhs=Hx[gg][k % 2][:, sq * 256:(sq + 1) * 256 - sh],
                                            start=False, stop=True, skip_group_check=True),
                                             reads=[b_Dm, b_Hx[gg][k % 2]], writes=[PSB[gg]], inc=(last and sq == 1))
                        for gg in range(4):
                            eng = "act" if gg % 2 == 0 else "dve"
                            src = PS[gg][:].rearrange("p (s n) -> p s n", s=2)[:, :, 0:255]
                            if eng == "act":
                                c.op("act", lambda e, gg=gg, src=src: e.copy(out=Hp[gg][:, :, 1:256], in_=src), reads=[PSB[gg]], writes=[b_Hp[gg]])
                            else:
                                c.op("dve", lambda e, gg=gg, src=src: e.tensor_copy(out=Hp[gg][:, :, 1:256], in_=src), reads=[PSB[gg]], writes=[b_Hp[gg]])
                        for gg in range(4):
                            gl = quad * 4 + gg
                            g = gb * 8 + gl
                            pb = 4 + (gg % 2)
                            c.op("pe", lambda e, g=g, gl=gl, pb=pb, s=s: e.matmul(PS[pb][:], lhsT=Tm[:, g, :], rhs=U[s][:, gl, :], start=True, stop=False),
                                 reads=[b_Tm, b_U[s]], writes=[PSB[pb]], inc=False)
                            c.op("pe", lambda e, g=g, gg=gg, pb=pb: e.matmul(PS[pb][:], lhsT=Mout[:, g, :], rhs=Hp[gg][:].rearrange("p s n -> p (s n)"), start=False, stop=True),
                                 reads=[b_Mout, b_Hp[gg]], writes=[PSB[pb]])
                            if gg % 2 == 0:
                                c.op("dve", lambda e, gg=gg, pb=pb: e.tensor_copy(out=Ysb[gg][:], in_=PS[pb][:]), reads=[PSB[pb]], writes=[b_Ysb[gg]])
                            else:
                                c.op("act", lambda e, gg=gg, pb=pb: e.copy(out=Ysb[gg][:], in_=PS[pb][:]), reads=[PSB[pb]], writes=[b_Ysb[gg]])
                            pt = 6 + (gg % 2)
                            for ct in range(4):
                                c.op("pe", lambda e, gg=gg, ct=ct, pt=pt: e.transpose(out=PS[pt][:, ct * 128:(ct + 1) * 128],
                                                                                      in_=Ysb[gg][:, ct * 128:(ct + 1) * 128], identity=identf[:]),
                                     reads=[b_Ysb[gg], b_identf], writes=[PSB[pt]], inc=(ct == 3))
                            ydst = YB[s][:, :, :, gl * 16:(gl + 1) * 16]
                            ysrc = PS[pt][:].rearrange("p (ct i o) -> p ct i o", ct=4, i=8)
                            if gg % 2 == 0:
                                c.op("act", lambda e, ydst=ydst, ysrc=ysrc: e.copy(out=ydst, in_=ysrc), reads=[PSB[pt]], writes=[b_YB[s]])
                            else:
                                c.op("dve", lambda e, ydst=ydst, ysrc=ysrc: e.tensor_copy(out=ydst, in_=ysrc), reads=[PSB[pt]], writes=[b_YB[s]])
                    xbv = XB[s][:].rearrange("p ct j d -> p (ct j) d")
                    ybv = YB[s][:].rearrange("p ct j d -> p (ct j) d")
                    dbv = dB[:, gb * 128:(gb + 1) * 128].unsqueeze(1).broadcast_to([128, 32, 128])
                    c.op("pool", lambda e, xbv=xbv, dbv=dbv: e.tensor_tensor(out=xbv, in0=xbv, in1=dbv, op=ALU.mult),
                         reads=[b_XB[s], b_dB], writes=[b_XB[s]])
                    c.op("pool", lambda e, xbv=xbv, ybv=ybv: e.tensor_tensor(out=ybv, in0=ybv, in1=xbv, op=ALU.add),
                         reads=[b_XB[s], b_YB[s]], writes=[b_YB[s]])
                    c.op("act", lambda e, s=s: e.activation(out=GB[s][:], in_=YB[s][:], func=AF.Gelu_apprx_tanh),
                         reads=[b_YB[s]], writes=[b_GB[s]])
                    for ct in range(4):
                        g_ev.append(c.dma("sp", Gv[:, ct, :, gb * 128:(gb + 1) * 128], GB[s][:, ct, :, :], reads=[b_GB[s]], sem=gsem[s]))
                c.barrier()
        if stop_after == "A":
            nc.sync.wait_ge(gsem[0][0], gsem[0][1])
            nc.sync.wait_ge(gsem[1][0], gsem[1][1])
            return nc
        class Ring:
            def __init__(self, stack, name, n, shape, dt):
                self.t = [sbt(stack, f"{name}{i}", shape, dt) for i in range(n)]
                self.b = [Buf(f"{name}{i}") for i in range(n)]
                self.i = 0
                self.n = n

            def next(self):
                k = self.i % self.n
                self.i += 1
                return self.t[k], self.b[k]

        bank_ctr = [0]

        def nbank():
            k = bank_ctr[0] % 8
            bank_ctr[0] += 1
            return k

        def alloc_weight(stack, name, src, K, N, queue="pool"):
            kc = K // 128
            t = sbt(stack, name, [128, kc, N], BF16)
            b = Buf(name)
            sem = c.dma_sem(name)
            srcv = src.rearrange("(kc p) n -> p kc n", p=128)

            def issue():
                for k in range(kc):
                    for n0 in range(0, N, 2048):
                        n1 = min(N, n0 + 2048)
                        c.dma(queue, t[:, k, n0:n1], srcv[:, k, n0:n1], writes=[b], sem=sem)
                        yield
            return t, b, issue

        def load_weight(stack, name, src, K, N, queue="pool"):
            t, b, issue = alloc_weight(stack, name, src, K, N, queue)
            for _ in issue():
                pass
            return t, b

        def load_ln(stack, idx):
            g = sbt(stack, f"lng{idx}", [128, D], F32); bg = Buf()
            bt = sbt(stack, f"lnb{idx}", [128, D], F32); bb = Buf()
            sem = c.dma_sem("ln")
            c.dma("sp", g[:], lnG_d[:, idx, :], writes=[bg], sem=sem)
            c.dma("sp", bt[:], lnB_d[:, idx, :], writes=[bb], sem=sem)
            return g, bg, bt, bb

        def layernorm(stack_rings, r, b_r, lng, b_lng, lnb, b_lnb):
            stats, b_st = stack_rings["stats"].next()
            mv, b_mv = stack_rings["mv"].next()
            sd, b_sd = stack_rings["sd"].next()
            for hh in range(2):
                c.op("dve", lambda e, hh=hh: e.bn_stats(out=stats[:, hh, :], in_=r[:, hh * 512:(hh + 1) * 512]), reads=[b_r], writes=[b_st])
            c.op("dve", lambda e: e.bn_aggr(out=mv[:], in_=stats[:].rearrange("p a b -> p (a b)")), reads=[b_st], writes=[b_mv])
            c.op("dve", lambda e: e.tensor_scalar(out=sd[:, 0:1], in0=mv[:, 1:2], scalar1=float(LN_EPS), scalar2=None, op0=ALU.add), reads=[b_mv], writes=[b_sd])
            c.op("pool", lambda e: e.tensor_tensor(out=sd[:, 2:3], in0=sd[:, 0:1], in1=mhalf[:, 0:1], op=ALU.pow), reads=[b_sd, b_mhalf], writes=[b_sd])
            c.op("dve", lambda e: e.tensor_scalar(out=sd[:, 3:4], in0=mv[:, 0:1], scalar1=sd[:, 2:3], scalar2=-1.0, op0=ALU.mult, op1=ALU.mult), reads=[b_mv, b_sd], writes=[b_sd])
            c.op("dve", lambda e: e.tensor_scalar(out=r[:], in0=r[:], scalar1=sd[:, 2:3], scalar2=sd[:, 3:4], op0=ALU.mult, op1=ALU.add), reads=[b_r, b_sd], writes=[b_r])
            c.op("pool", lambda e: e.tensor_tensor(out=r[:], in0=r[:], in1=lng[:], op=ALU.mult), reads=[b_r, b_lng], writes=[b_r])
            c.op("dve", lambda e: e.tensor_tensor(out=r[:], in0=r[:], in1=lnb[:], op=ALU.add), reads=[b_r, b_lnb], writes=[b_r])

        def ln_rings(stack):
            return {"stats": Ring(stack, "lnst", 3, [128, 2, 6], F32), "mv": Ring(stack, "lnmv", 3, [128, 2], F32),
                    "sd": Ring(stack, "lnsd", 3, [128, 4], F32)}

        def transpose_tile(src, b_src, dst_fn, b_dst, evac_eng="act"):
            pb = nbank()
            p16 = PS[pb][:].bitcast(BF16)
            for kc in range(8):
                c.op("pe", lambda e, kc=kc: e.transpose(out=p16[:, kc * 128:(kc + 1) * 128], in_=src[:, kc * 128:(kc + 1) * 128], identity=identb[:]),
                     reads=[b_src, b_identb], writes=[PSB[pb]], inc=(kc == 7))
            dst = dst_fn()
            if evac_eng == "act":
                c.op("act", lambda e: e.copy(out=dst, in_=p16[:].rearrange("p (k t) -> p k t", k=8)), reads=[PSB[pb]], writes=[b_dst])
            else:
                c.op("dve", lambda e: e.tensor_copy(out=dst, in_=p16[:].rearrange("p (k t) -> p k t", k=8)), reads=[PSB[pb]], writes=[b_dst])

        sBC = contextlib.ExitStack()
        sBC.__enter__()
        Wg, b_Wg, issue_Wg = alloc_weight(sBC, "Wg", fg_d, D, FF)
        Wu, b_Wu, issue_Wu = alloc_weight(sBC, "Wu", fu_d, D, FF)
        Wd, b_Wd, issue_Wd = alloc_weight(sBC, "Wd", fd_d, FF, D)
        with contextlib.ExitStack() as sB:
            Wglu, b_Wglu = load_weight(sB, "Wglu", wglu_d, D, 2 * D)
            import itertools
            ffn_dmas = itertools.chain(issue_Wg(), issue_Wu(), issue_Wd())
            lng, b_lng, lnb, b_lnb = load_ln(sB, 0)
            rings = ln_rings(sB)
            gt_r = Ring(sB, "gt", 2, [128, D], BF16); gsemB = [c.dma_sem("gB") for _ in range(2)]
            xt_r = Ring(sB, "xt", 2, [128, D], F32); xsemB = [c.dma_sem("xB") for _ in range(3)]
            gT_r = Ring(sB, "gT", 2, [128, 8, 128], BF16)
            sg_r = Ring(sB, "sg", 2, [128, D], F32)
            r_r = Ring(sB, "rB", 2, [128, D], F32); osemB = [c.dma_sem("oB") for _ in range(3)]
            def B_stage1(t):
                gt, b_gt = gt_r.next(); xt, b_xt = xt_r.next(); gT, b_gT = gT_r.next(); sg, b_sg = sg_r.next()
                c.dma("sp", gt[:], G_d[t * 128:(t + 1) * 128, :], writes=[b_gt], sem=gsemB[t % 2])
                c.dma("act", xt[:], x_d[t * 128:(t + 1) * 128, :], writes=[b_xt], sem=xsemB[t % 3])
                transpose_tile(gt, b_gt, lambda: gT[:], b_gT)
                banks = [nbank() for _ in range(4)]
                for nb_ in range(4):
                    for kc in range(8):
                        c.op("pe", lambda e, nb_=nb_, kc=kc: e.matmul(PS[banks[nb_]][:], lhsT=gT[:, kc, :], rhs=Wglu[:, kc, nb_ * 512:(nb_ + 1) * 512],
                                                                  start=(kc == 0), stop=(kc == 7)),
                             reads=[b_gT, b_Wglu], writes=[PSB[banks[nb_]]], inc=(kc == 7))
                for hh in range(2):
                    c.op("act", lambda e, hh=hh: e.activation(out=sg[:, hh * 512:(hh + 1) * 512], in_=PS[banks[2 + hh]][:], func=AF.Sigmoid),
                         reads=[PSB[banks[2 + hh]]], writes=[b_sg])
                    c.op("dve", lambda e, hh=hh: e.tensor_tensor(out=sg[:, hh * 512:(hh + 1) * 512], in0=PS[banks[hh]][:], in1=sg[:, hh * 512:(hh + 1) * 512], op=ALU.mult),
                         reads=[PSB[banks[hh]], b_sg], writes=[b_sg])
                return (t, xt, b_xt, sg, b_sg)

            def B_stage2(ctx):
                t, xt, b_xt, sg, b_sg = ctx
                r, b_r = r_r.next()
                c.op("dve", lambda e: e.scalar_tensor_tensor(out=r[:], in0=xt[:], scalar=ALPHA, in1=sg[:], op0=ALU.mult, op1=ALU.add),
                     reads=[b_xt, b_sg], writes=[b_r])
                layernorm(rings, r, b_r, lng, b_lng, lnb, b_lnb)
                c.dma("pool", H1_d[t * 128:(t + 1) * 128, :], r[:], reads=[b_r], sem=osemB[t % 3])

            pend = None
            for t in range(NT):
                ctx = B_stage1(t)
                for _ in range(2):
                    next(ffn_dmas, None)
                if pend is not None:
                    B_stage2(pend)
                pend = ctx
            B_stage2(pend)
            for _ in ffn_dmas:
                pass
            c.barrier()
        if stop_after == "B":
            sBC.__exit__(None, None, None)
            return nc

        with contextlib.ExitStack() as sC:
            lng, b_lng, lnb, b_lnb = load_ln(sC, 1)
            rings = ln_rings(sC)
            ht_r = Ring(sC, "htC", 2, [128, D], BF16); hsemC = [c.dma_sem("hC") for _ in range(2)]
            hres_r = Ring(sC, "hresC", 2, [128, D], F32); rsemC = [c.dma_sem("rC") for _ in range(2)]
            hT = sbt(sC, "hTC", [128, 8, 512], BF16); b_hT = Buf()
            h1T = sbt(sC, "h1TC", [128, FF // 128, 512], BF16); b_h1T = Buf()
            s_r = Ring(sC, "sC", 2, [128, 512], BF16)
            r_r = Ring(sC, "rC", 2, [128, D], F32); osemC = [c.dma_sem("oC") for _ in range(2)]
            for st_ in range(TOK // 512):
                for sub in range(4):
                    t = st_ * 4 + sub
                    ht, b_ht = ht_r.next()
                    c.dma("pool", ht[:], H1_d[t * 128:(t + 1) * 128, :], writes=[b_ht], sem=hsemC[t % 2])
                    transpose_tile(ht, b_ht, lambda sub=sub: hT[:, :, sub * 128:(sub + 1) * 128], b_hT, evac_eng="act" if sub % 2 == 0 else "dve")
                for fc in range(FF // 128):
                    pg, pu = nbank(), nbank()
                    for (W, bW, pb) in ((Wg, b_Wg, pg), (Wu, b_Wu, pu)):
                        for kc in range(8):
                            c.op("pe", lambda e, W=W, pb=pb, kc=kc, fc=fc: e.matmul(PS[pb][:], lhsT=W[:, kc, fc * 128:(fc + 1) * 128], rhs=hT[:, kc, :],
                                                                              start=(kc == 0), stop=(kc == 7)),
                                 reads=[bW, b_hT], writes=[PSB[pb]], inc=(kc == 7))
                    sb_, b_s = s_r.next()
                    c.op("act", lambda e, pg=pg, sb_=sb_: e.activation(out=sb_[:], in_=PS[pg][:], func=AF.Silu), reads=[PSB[pg]], writes=[b_s])
                    c.op("dve", lambda e, pu=pu, sb_=sb_, fc=fc: e.tensor_tensor(out=h1T[:, fc, :], in0=PS[pu][:], in1=sb_[:], op=ALU.mult),
                         reads=[PSB[pu], b_s], writes=[b_h1T])
                for sub in range(4):
                    t = st_ * 4 + sub
                    hres, b_hres = hres_r.next(); r, b_r = r_r.next()
                    c.dma("act", hres[:], H1_d[t * 128:(t + 1) * 128, :], writes=[b_hres], sem=rsemC[t % 2])
                    po = [nbank(), nbank()]
                    for nb_ in range(2):
                        for fc in range(FF // 128):
                            c.op("pe", lambda e, nb_=nb_, fc=fc, sub=sub: e.matmul(PS[po[nb_]][:], lhsT=h1T[:, fc, sub * 128:(sub + 1) * 128],
                                                                             rhs=Wd[:, fc, nb_ * 512:(nb_ + 1) * 512], start=(fc == 0), stop=(fc == FF // 128 - 1)),
                                 reads=[b_h1T, b_Wd], writes=[PSB[po[nb_]]], inc=(fc == FF // 128 - 1))
                        c.op("dve", lambda e, nb_=nb_: e.scalar_tensor_tensor(out=r[:, nb_ * 512:(nb_ + 1) * 512], in0=hres[:, nb_ * 512:(nb_ + 1) * 512], scalar=ALPHA,
                                                                         in1=PS[po[nb_]][:], op0=ALU.mult, op1=ALU.add),
                             reads=[b_hres, PSB[po[nb_]]], writes=[b_r])
                    layernorm(rings, r, b_r, lng, b_lng, lnb, b_lnb)
                    c.dma("sp", H2_d[t * 128:(t + 1) * 128, :], r[:], reads=[b_r], sem=osemC[t % 2])
            c.barrier()
        sBC.__exit__(None, None, None)
        if stop_after == "C":
            return nc

        with contextlib.ExitStack() as sE:
            Wq, b_Wq = load_weight(sE, "Wq", wq_d, D, D)
            Wkv, b_Wkv = load_weight(sE, "Wkv", kvw_d, D, 512)
            Wo, b_Wo = load_weight(sE, "Wo", wo_d, D, D)
            lng, b_lng, lnb, b_lnb = load_ln(sE, 2)
            rings = ln_rings(sE)
            semE = c.dma_sem("cE")
            posi = sbt(sE, "posi", [128, NT], I32); b_posi = Buf()
            invf = sbt(sE, "invf", [128, 8], F32); b_invf = Buf()
            sk = sbt(sE, "sk", [128, 16], F32); b_sk = Buf()
            c.dma("sp", posi[:], pos_d[:, :], writes=[b_posi], sem=semE)
            c.dma("sp", invf[:], invf_d[:, :], writes=[b_invf], sem=semE)
            c.dma("sp", sk[:], sink_d[:, :], writes=[b_sk], sem=semE)
            posf = sbt(sE, "posf", [128, NT], F32); b_posf = Buf()
            angE = sbt(sE, "angE", [128, NT, 8], F32); b_angE = Buf()
            kqE = sbt(sE, "kqE", [128, NT, 8], F32); b_kqE = Buf()
            kiE = sbt(sE, "kiE", [128, NT, 8], I32); b_kiE = Buf()
            twE = sbt(sE, "twE", [128, NT, 8], F32); b_twE = Buf()
            ycE = sbt(sE, "ycE", [128, NT, 8], F32); b_ycE = Buf()
            cosT = sbt(sE, "cosT", [128, NT, 8], F32); b_cosT = Buf()
            sinT = sbt(sE, "sinT", [128, NT, 8], F32); b_sinT = Buf()
            esk = sbt(sE, "esk", [128, 16], F32); b_esk = Buf()
            c.op("act", lambda e: e.activation(out=esk[:], in_=sk[:], func=AF.Exp), reads=[b_sk], writes=[b_esk])
            c.op("dve", lambda e: e.tensor_copy(out=posf[:], in_=posi[:]), reads=[b_posi], writes=[b_posf])
            c.op("dve", lambda e: e.tensor_tensor(out=angE[:], in0=posf[:].unsqueeze(2).broadcast_to([128, NT, 8]),
                                                  in1=invf[:].unsqueeze(1).broadcast_to([128, NT, 8]), op=ALU.mult), reads=[b_posf, b_invf], writes=[b_angE])
            c.op("dve", lambda e: e.tensor_scalar(out=kqE[:], in0=angE[:], scalar1=float(1.0 / (2 * math.pi)), scalar2=None, op0=ALU.mult), reads=[b_angE], writes=[b_kqE])
            c.op("dve", lambda e: e.tensor_copy(out=kiE[:], in_=kqE[:]), reads=[b_kqE], writes=[b_kiE])
            c.op("dve", lambda e: e.tensor_copy(out=kqE[:], in_=kiE[:]), reads=[b_kiE], writes=[b_kqE])
            c.op("dve", lambda e: e.scalar_tensor_tensor(out=angE[:], in0=kqE[:], scalar=-6.28125, in1=angE[:], op0=ALU.mult, op1=ALU.add), reads=[b_kqE, b_angE], writes=[b_angE])
            c.op("dve", lambda e: e.scalar_tensor_tensor(out=angE[:], in0=kqE[:], scalar=-float(2.0 * math.pi - 6.28125), in1=angE[:], op0=ALU.mult, op1=ALU.add), reads=[b_kqE, b_angE], writes=[b_angE])

            def wrapE(t, b_t):
                PI_ = float(math.pi)
                c.op("dve", lambda e: e.tensor_scalar(out=twE[:], in0=t[:], scalar1=-PI_, scalar2=2 * PI_, op0=ALU.is_lt, op1=ALU.mult), reads=[b_t], writes=[b_twE])
                c.op("dve", lambda e: e.tensor_tensor(out=t[:], in0=t[:], in1=twE[:], op=ALU.add), reads=[b_t, b_twE], writes=[b_t])
                c.op("dve", lambda e: e.tensor_scalar(out=twE[:], in0=t[:], scalar1=PI_, scalar2=-2 * PI_, op0=ALU.is_gt, op1=ALU.mult), reads=[b_t], writes=[b_twE])
                c.op("dve", lambda e: e.tensor_tensor(out=t[:], in0=t[:], in1=twE[:], op=ALU.add), reads=[b_t, b_twE], writes=[b_t])
            wrapE(angE, b_angE)
            wrapE(angE, b_angE)
            c.op("act", lambda e: e.activation(out=sinT[:], in_=angE[:], func=AF.Sin), reads=[b_angE], writes=[b_sinT])
            c.op("dve", lambda e: e.tensor_scalar(out=ycE[:], in0=angE[:], scalar1=float(math.pi / 2), scalar2=None, op0=ALU.add), reads=[b_angE], writes=[b_ycE])
            wrapE(ycE, b_ycE)
            c.op("act", lambda e: e.activation(out=cosT[:], in_=ycE[:], func=AF.Sin), reads=[b_ycE], writes=[b_cosT])
            mprev = sbt(sE, "mprev", [128, 128], BF16); b_mprev = Buf()
            mcur = sbt(sE, "mcur", [128, 128], BF16); b_mcur = Buf()
            c.op("pool", lambda e: e.memset(mprev[:], 1.0), writes=[b_mprev])
            c.op("pool", lambda e: e.affine_select(out=mprev[:], in_=mprev[:], pattern=[[-1, 128]], compare_op=ALU.is_gt, fill=0.0, base=0, channel_multiplier=1),
                 reads=[b_mprev], writes=[b_mprev])
            c.op("pool", lambda e: e.memset(mcur[:], 1.0), writes=[b_mcur])
            c.op("pool", lambda e: e.affine_select(out=mcur[:], in_=mcur[:], pattern=[[1, 128]], compare_op=ALU.is_ge, fill=0.0, base=0, channel_multiplier=-1),
                 reads=[b_mcur], writes=[b_mcur])
            KT = [sbt(sE, f"KT{i}", [128, 8, 128], BF16) for i in range(3)]; b_KT = [Buf() for _ in range(3)]
            Ksp = [sbt(sE, f"Ksp{i}", [128, 4, 2, 128], BF16) for i in range(3)]; b_Ksp = [Buf() for _ in range(3)]
            for i in range(3):
                c.op("pool", lambda e, i=i: e.memset(Ksp[i][:], 0.0), writes=[b_Ksp[i]])
            Va = [sbt(sE, f"Va{i}", [128, 4, 65], BF16) for i in range(3)]; b_Va = [Buf() for _ in range(3)]
            for i in range(3):
                c.op("pool", lambda e, i=i: e.memset(Va[i][:], 1.0), writes=[b_Va[i]])
            ht_r = Ring(sE, "htE", 3, [128, D], BF16); hsemE = [c.dma_sem("hE") for _ in range(3)]
            hres_r = Ring(sE, "hresE", 4, [128, D], F32); rsemE = [c.dma_sem("rE") for _ in range(4)]
            hT_r = Ring(sE, "hTE", 2, [128, 8, 128], BF16)
            Qs_r = Ring(sE, "Qs", 2, [128, 16, 64], BF16)
            Ks_r = Ring(sE, "Ks", 2, [128, 4, 64], BF16)
            QT_r = Ring(sE, "QT", 3, [128, 8, 128], BF16)
            ro_r = Ring(sE, "ro", 4, [128, 8, 8], F32)
            E_r = Ring(sE, "E", 8, [128, 512], BF16)
            O_r = Ring(sE, "O", 3, [128, 16, 64], BF16)
            OT_r = Ring(sE, "OT", 2, [128, 8, 128], BF16)
            dn_r = Ring(sE, "dn", 4, [128, 8], F32)
            r_r = Ring(sE, "rE", 3, [128, D], F32); osemE = [c.dma_sem("oE") for _ in range(3)]

            def rope(psrc, nh, dst, b_dst, pbuf, t):
                cb = cosT[:, t, :].unsqueeze(1).broadcast_to([128, nh, 8])
                sb2 = sinT[:, t, :].unsqueeze(1).broadcast_to([128, nh, 8])
                q1 = psrc[:, :, 0:8]; q2 = psrc[:, :, 8:16]
                ta, b_ta = ro_r.next(); tb, b_tb = ro_r.next()
                c.op("dve", lambda e: e.tensor_tensor(out=ta[:, 0:nh, :], in0=q1, in1=cb, op=ALU.mult), reads=[pbuf, b_cosT], writes=[b_ta])
                c.op("dve", lambda e: e.tensor_tensor(out=tb[:, 0:nh, :], in0=q2, in1=sb2, op=ALU.mult), reads=[pbuf, b_sinT], writes=[b_tb])
                c.op("dve", lambda e: e.tensor_tensor(out=dst[:, :, 0:8], in0=ta[:, 0:nh, :], in1=tb[:, 0:nh, :], op=ALU.subtract), reads=[b_ta, b_tb], writes=[b_dst])
                tc_, b_tc = ro_r.next(); td, b_td = ro_r.next()
                c.op("dve", lambda e: e.tensor_tensor(out=tc_[:, 0:nh, :], in0=q2, in1=cb, op=ALU.mult), reads=[pbuf, b_cosT], writes=[b_tc])
                c.op("dve", lambda e: e.tensor_tensor(out=td[:, 0:nh, :], in0=q1, in1=sb2, op=ALU.mult), reads=[pbuf, b_sinT], writes=[b_td])
                c.op("dve", lambda e: e.tensor_tensor(out=dst[:, :, 8:16], in0=tc_[:, 0:nh, :], in1=td[:, 0:nh, :], op=ALU.add), reads=[b_tc, b_td], writes=[b_dst])

            def E_A1(t):
                ht, b_ht = ht_r.next(); hres, b_hres = hres_r.next(); hT, b_hT = hT_r.next()
                c.dma("pool", ht[:], H2_d[t * 128:(t + 1) * 128, :], writes=[b_ht], sem=hsemE[t % 3])
                c.dma("act", hres[:], H2_d[t * 128:(t + 1) * 128, :], writes=[b_hres], sem=rsemE[t % 4])
                transpose_tile(ht, b_ht, lambda: hT[:], b_hT)
                pq = [nbank(), nbank()]; pkv = nbank()
                for nb_ in range(2):
                    for kc in range(8):
                        c.op("pe", lambda e, nb_=nb_, kc=kc: e.matmul(PS[pq[nb_]][:], lhsT=hT[:, kc, :], rhs=Wq[:, kc, nb_ * 512:(nb_ + 1) * 512], start=(kc == 0), stop=(kc == 7)),
                             reads=[b_hT, b_Wq], writes=[PSB[pq[nb_]]], inc=(kc == 7))
                for kc in range(8):
                    c.op("pe", lambda e, kc=kc: e.matmul(PS[pkv][:], lhsT=hT[:, kc, :], rhs=Wkv[:, kc, :], start=(kc == 0), stop=(kc == 7)),
                         reads=[b_hT, b_Wkv], writes=[PSB[pkv]], inc=(kc == 7))
                return dict(t=t, nblk=t % 16, hres=hres, b_hres=b_hres, pq=pq, pkv=pkv)

            def E_A2a(cx):
                t = cx['t']; pq = cx['pq']; pkv = cx['pkv']
                Qs, b_Qs = Qs_r.next(); Ks, b_Ks = Ks_r.next()
                slot = t % 3
                for nb_ in range(2):
                    pv_ = PS[pq[nb_]][:].rearrange("p (h d) -> p h d", h=8)
                    c.op("act", lambda e, nb_=nb_, pv_=pv_: e.copy(out=Qs[:, nb_ * 8:(nb_ + 1) * 8, :], in_=pv_), reads=[PSB[pq[nb_]]], writes=[b_Qs])
                    rope(pv_, 8, Qs[:, nb_ * 8:(nb_ + 1) * 8, :], b_Qs, PSB[pq[nb_]], t)
                kvv = PS[pkv][:].rearrange("p (h d) -> p h d", h=8)
                c.op("act", lambda e: e.copy(out=Ks[:], in_=kvv[:, 0:4, :]), reads=[PSB[pkv]], writes=[b_Ks])
                rope(kvv[:, 0:4, :], 4, Ks[:], b_Ks, PSB[pkv], t)
                c.op("act", lambda e: e.copy(out=Va[slot][:, :, 0:64], in_=kvv[:, 4:8, :]), reads=[PSB[pkv]], writes=[b_Va[slot]])
                c.op("pool", lambda e: e.tensor_copy(out=Ksp[slot][:, :, 0, 0:64], in_=Ks[:]), reads=[b_Ks], writes=[b_Ksp[slot]])
                c.op("pool", lambda e: e.tensor_copy(out=Ksp[slot][:, :, 1, 64:128], in_=Ks[:]), reads=[b_Ks], writes=[b_Ksp[slot]])
                cx['Qs'] = Qs; cx['b_Qs'] = b_Qs

            def E_A2b(cx):
                t = cx['t']; Qs = cx['Qs']; b_Qs = cx['b_Qs']
                slot = t % 3
                QT, b_QT = QT_r.next()
                transpose_tile(Qs[:].rearrange("p h d -> p (h d)"), b_Qs, lambda: QT[:], b_QT, evac_eng="act")
                transpose_tile(Ksp[slot][:].rearrange("p g v d -> p (g v d)"), b_Ksp[slot], lambda: KT[slot][:], b_KT[slot], evac_eng="dve")
                cx['QT'] = QT; cx['b_QT'] = b_QT

            def E_B(cx):
                t = cx['t']; nblk = cx['nblk']; QT = cx['QT']; b_QT = cx['b_QT']
                O, b_O = O_r.next()
                kbs = ([((t - 1) % 3, mprev, b_mprev)] if nblk > 0 else []) + [(t % 3, mcur, b_mcur)]

                def scores(g):
                    Es = []
                    for (sl, mk, b_mk) in kbs:
                        ps_ = nbank()
                        for var in range(2):
                            c.op("pe", lambda e, ps_=ps_, sl=sl, g=g, var=var: e.matmul(PS[ps_][:, var * 256:(var + 1) * 256], lhsT=KT[sl][:, g * 2 + var, :],
                                                                                  rhs=QT[:, 2 * g:2 * g + 2, :].rearrange("p h t -> p (h t)"), start=True, stop=True),
                                 reads=[b_KT[sl], b_QT], writes=[PSB[ps_]], inc=(var == 1))
                        E, b_E = E_r.next()
                        c.op("act", lambda e, ps_=ps_, E=E: e.activation(out=E[:], in_=PS[ps_][:], func=AF.Exp, scale=0.125), reads=[PSB[ps_]], writes=[b_E])
                        meng = "pool" if len(Es) == 0 else "dve"
                        c.op(meng, lambda e, E=E, mk=mk: e.tensor_tensor(out=E[:].rearrange("p (h t) -> p h t", h=4), in0=E[:].rearrange("p (h t) -> p h t", h=4),
                                                                     in1=mk[:].unsqueeze(1).broadcast_to([128, 4, 128]), op=ALU.mult), reads=[b_E, b_mk], writes=[b_E])
                        Es.append((E, b_E, sl))
                    return Es

                def pv(g, Es):
                    po_ = nbank()
                    for hl in range(4):
                        for i, (E, b_E, sl) in enumerate(Es):
                            c.op("pe", lambda e, po_=po_, hl=hl, E=E, sl=sl, g=g, i=i: e.matmul(PS[po_][:, hl * 65:(hl + 1) * 65], lhsT=E[:, hl * 128:(hl + 1) * 128], rhs=Va[sl][:, g, :],
                                                                                     start=(i == 0), stop=(i == len(Es) - 1)),
                                 reads=[b_E, b_Va[sl]], writes=[PSB[po_]], inc=(hl == 3 and i == len(Es) - 1))
                    dn, b_dn = dn_r.next()
                    ov = PS[po_][:, 0:260].rearrange("p (v pr d) -> p v pr d", v=2, pr=2)
                    c.op("dve", lambda e: e.tensor_tensor(out=dn[:, 0:4].rearrange("p (v pr) -> p v pr", v=2), in0=ov[:, :, :, 64],
                                                          in1=esk[:, 4 * g:4 * g + 4].rearrange("p (pr v) -> p v pr", v=2), op=ALU.add), reads=[PSB[po_], b_esk], writes=[b_dn])
                    c.op("dve", lambda e: e.reciprocal(out=dn[:, 4:8], in_=dn[:, 0:4]), reads=[b_dn], writes=[b_dn])
                    for var in range(2):
                        c.op("dve", lambda e, var=var: e.tensor_tensor(
                            out=O[:, 4 * g:4 * g + 4, :].rearrange("p (pr v) d -> p v pr d", v=2)[:, var, :, :], in0=ov[:, var, :, 0:64],
                            in1=dn[:, 4 + 2 * var:6 + 2 * var].unsqueeze(2).broadcast_to([128, 2, 64]), op=ALU.mult),
                             reads=[PSB[po_], b_dn], writes=[b_O])

                pend_g = []
                for g in range(4):
                    pend_g.append((g, scores(g)))
                    if len(pend_g) > 2:
                        pv(*pend_g.pop(0))
                while pend_g:
                    pv(*pend_g.pop(0))
                cx['O'] = O; cx['b_O'] = b_O

            def E_C(cx):
                t = cx['t']; hres = cx['hres']; b_hres = cx['b_hres']; O = cx['O']; b_O = cx['b_O']
                OT, b_OT = OT_r.next()
                transpose_tile(O[:].rearrange("p h d -> p (h d)"), b_O, lambda: OT[:], b_OT, evac_eng="act")
                r, b_r = r_r.next()
                po = [nbank(), nbank()]
                for nb_ in range(2):
                    for kc in range(8):
                        c.op("pe", lambda e, nb_=nb_, kc=kc: e.matmul(PS[po[nb_]][:], lhsT=OT[:, kc, :], rhs=Wo[:, kc, nb_ * 512:(nb_ + 1) * 512], start=(kc == 0), stop=(kc == 7)),
                             reads=[b_OT, b_Wo], writes=[PSB[po[nb_]]], inc=(kc == 7))
                    c.op("dve", lambda e, nb_=nb_: e.scalar_tensor_tensor(out=r[:, nb_ * 512:(nb_ + 1) * 512], in0=hres[:, nb_ * 512:(nb_ + 1) * 512], scalar=ALPHA,
                                                                     in1=PS[po[nb_]][:], op0=ALU.mult, op1=ALU.add), reads=[b_hres, PSB[po[nb_]]], writes=[b_r])
                layernorm(rings, r, b_r, lng, b_lng, lnb, b_lnb)
                c.dma("sp", H3_d[t * 128:(t + 1) * 128, :], r[:], reads=[b_r], sem=osemE[t % 3])

            cxs = {}
            for i in range(NT + 2):
                if i < NT:
                    cxs[i] = E_A1(i)
                    E_A2a(cxs[i])
                if 0 <= i - 2 < NT:
                    E_C(cxs.pop(i - 2))
                if 0 <= i - 1 < NT:
                    E_B(cxs[i - 1])
                if i < NT:
                    E_A2b(cxs[i])
            c.barrier()
        if stop_after == "E":
            return nc

        with contextlib.ExitStack() as sF:
            lng, b_lng, lnb, b_lnb = load_ln(sF, 3)
            rings = ln_rings(sF)
            semF = c.dma_sem("cF")
            Wr = sbt(sF, "Wr", [128, 8, NE], F32); b_Wr = Buf()
            brt = sbt(sF, "brt", [128, NE], F32); b_brt = Buf()
            c.dma("sp", Wr[:], wr_d.rearrange("(kc p) n -> p kc n", p=128), writes=[b_Wr], sem=semF)
            c.dma("sp", brt[:], br_d[:, :], writes=[b_brt], sem=semF)
            STK = 1024
            NSUB = STK // 128
            WG = [sbt(sF, f"WGe{i}", [128, 8, ED], BF16) for i in range(2)]; b_WG = [Buf() for _ in range(2)]
            WU = [sbt(sF, f"WUe{i}", [128, 8, ED], BF16) for i in range(2)]; b_WU = [Buf() for _ in range(2)]
            WD = [sbt(sF, f"WDe{i}", [128, 8, D], BF16) for i in range(2)]; b_WD = [Buf() for _ in range(2)]
            wsem = [c.dma_sem("wF") for _ in range(2)]
            hTm = [sbt(sF, f"hTm{i}", [128, 8, STK], BF16) for i in range(2)]; b_hTm = [Buf() for _ in range(2)]
            h1m = sbt(sF, "h1m", [128, 8, STK], BF16); b_h1m = Buf()
            acc = sbt(sF, "acc", [128, NSUB, D], F32); b_acc = [Buf() for _ in range(NSUB)]
            comb = [sbt(sF, f"comb{i}", [128, NSUB, NE], F32) for i in range(2)]; b_comb = [[Buf() for _ in range(NSUB)] for _ in range(2)]
            h3_r = Ring(sF, "h3F", 2, [128, D], F32); h3sem = [c.dma_sem("h3F") for _ in range(2)]
            h3T_r = Ring(sF, "h3T", 2, [128, 8, 128], F32)
            lg_r = Ring(sF, "lg", 3, [128, 4, NE], F32)
            s_r = Ring(sF, "sF", 3, [128, 512], BF16)
            osemF = [c.dma_sem("oF") for _ in range(2)]
            n_w = 0

            def issue_weights(e_idx, slot):
                if wsem[slot][1] >= 2500:
                    wsem[slot] = c.dma_sem("wF")
                for kc in range(8):
                    c.dma("pool", WG[slot][:, kc, :], mg_d[e_idx, kc * 128:(kc + 1) * 128, :], writes=[b_WG[slot]], sem=wsem[slot])
                    c.dma("pool", WU[slot][:, kc, :], mu_d[e_idx, kc * 128:(kc + 1) * 128, :], writes=[b_WU[slot]], sem=wsem[slot])
                for kc in range(8):
                    c.dma("pool", WD[slot][:, kc, :], md_d[e_idx, kc * 128:(kc + 1) * 128, :], writes=[b_WD[slot]], sem=wsem[slot])

            def router_sub(st_, sub):
                t = st_ * NSUB + sub
                h3, b_h3 = h3_r.next(); h3T, b_h3T = h3T_r.next()
                c.dma("sp", h3[:], H3_d[t * 128:(t + 1) * 128, :], writes=[b_h3], sem=h3semA[t % 2])
                pt = [nbank(), nbank()]
                for kc in range(8):
                    c.op("pe", lambda e, kc=kc: e.transpose(out=PS[pt[kc // 4]][:, (kc % 4) * 128:(kc % 4 + 1) * 128], in_=h3[:, kc * 128:(kc + 1) * 128], identity=identf[:]),
                         reads=[b_h3, b_identf], writes=[PSB[pt[kc // 4]]], inc=(kc % 4 == 3))
                for hh in range(2):
                    src = PS[pt[hh]][:].rearrange("p (k t) -> p k t", k=4)
                    c.op("act", lambda e, hh=hh, src=src: e.copy(out=h3T[:, hh * 4:(hh + 1) * 4, :], in_=src), reads=[PSB[pt[hh]]], writes=[b_h3T])
                    c.op("dve", lambda e, hh=hh, src=src, sub=sub: e.tensor_copy(out=hTm[st_ % 2][:, hh * 4:(hh + 1) * 4, sub * 128:(sub + 1) * 128], in_=src), reads=[PSB[pt[hh]]], writes=[b_hTm[st_ % 2]])
                pl = nbank()
                for kc in range(8):
                    c.op("pe", lambda e, kc=kc: e.matmul(PS[pl][:, 0:NE], lhsT=h3T[:, kc, :], rhs=Wr[:, kc, :], start=(kc == 0), stop=(kc == 7)),
                         reads=[b_h3T, b_Wr], writes=[PSB[pl]], inc=(kc == 7))
                lg, b_lg = lg_r.next()
                c.op("dve", lambda e: e.tensor_tensor(out=lg[:, 0, :], in0=PS[pl][:, 0:NE], in1=brt[:], op=ALU.add), reads=[PSB[pl], b_brt], writes=[b_lg])
                c.op("dve", lambda e: e.max(out=lg[:, 1, :], in_=lg[:, 0, :]), reads=[b_lg], writes=[b_lg])
                c.op("dve", lambda e: e.tensor_scalar(out=lg[:, 2, :], in0=lg[:, 0, :], scalar1=lg[:, 1, 1:2], scalar2=None, op0=ALU.is_ge), reads=[b_lg], writes=[b_lg])
                c.op("dve", lambda e: e.tensor_scalar(out=lg[:, 1, 2:3], in0=lg[:, 1, 0:1], scalar1=-1.0, scalar2=None, op0=ALU.mult), reads=[b_lg], writes=[b_lg])
                c.op("act", lambda e: e.activation(out=lg[:, 3, :], in_=lg[:, 0, :], func=AF.Exp, bias=lg[:, 1, 2:3], scale=1.0), reads=[b_lg], writes=[b_lg])
                c.op("dve", lambda e: e.tensor_tensor(out=lg[:, 3, :], in0=lg[:, 3, :], in1=lg[:, 2, :], op=ALU.mult), reads=[b_lg], writes=[b_lg])
                c.op("dve", lambda e: e.tensor_reduce(out=lg[:, 1, 3:4], in_=lg[:, 3, :], axis=AX.X, op=ALU.add), reads=[b_lg], writes=[b_lg])
                c.op("dve", lambda e: e.reciprocal(out=lg[:, 1, 4:5], in_=lg[:, 1, 3:4]), reads=[b_lg], writes=[b_lg])
                c.op("dve", lambda e, sub=sub: e.tensor_scalar(out=comb[st_ % 2][:, sub, :], in0=lg[:, 3, :], scalar1=lg[:, 1, 4:5], scalar2=None, op0=ALU.mult), reads=[b_lg], writes=[b_comb[st_ % 2][sub]])

            def part1(st_, ex, slot):
                for half in range(STK // 512):
                    for fc in range(8):
                        pg, pu = nbank(), nbank()
                        for (W, bW, pb) in ((WG[slot], b_WG[slot], pg), (WU[slot], b_WU[slot], pu)):
                            for kc in range(8):
                                c.op("pe", lambda e, W=W, pb=pb, kc=kc, fc=fc, half=half: e.matmul(PS[pb][:], lhsT=W[:, kc, fc * 128:(fc + 1) * 128], rhs=hTm[st_ % 2][:, kc, half * 512:(half + 1) * 512],
                                                                                             start=(kc == 0), stop=(kc == 7)),
                                     reads=[bW, b_hTm[st_ % 2]], writes=[PSB[pb]], inc=(kc == 7))
                        sb_, b_s = s_r.next()
                        c.op("act", lambda e, pg=pg, sb_=sb_: e.activation(out=sb_[:], in_=PS[pg][:], func=AF.Silu), reads=[PSB[pg]], writes=[b_s])
                        c.op("dve", lambda e, pu=pu, sb_=sb_, fc=fc, half=half: e.tensor_tensor(out=h1m[:, fc, half * 512:(half + 1) * 512], in0=PS[pu][:], in1=sb_[:], op=ALU.mult),
                             reads=[PSB[pu], b_s], writes=[b_h1m])

            def part2(st_, ex, slot):
                for sub in range(NSUB):
                    for nb_ in range(2):
                        po_ = nbank()
                        for fc in range(8):
                            c.op("pe", lambda e, po_=po_, fc=fc, sub=sub, nb_=nb_, slot=slot: e.matmul(PS[po_][:], lhsT=h1m[:, fc, sub * 128:(sub + 1) * 128],
                                                                                             rhs=WD[slot][:, fc, nb_ * 512:(nb_ + 1) * 512], start=(fc == 0), stop=(fc == 7)),
                                 reads=[b_h1m, b_WD[slot]], writes=[PSB[po_]], inc=(fc == 7))
                        av = acc[:, sub, nb_ * 512:(nb_ + 1) * 512]
                        if ex == 0:
                            c.op("dve", lambda e, po_=po_, av=av, sub=sub, ex=ex: e.tensor_scalar(out=av, in0=PS[po_][:], scalar1=comb[st_ % 2][:, sub, ex:ex + 1], scalar2=None, op0=ALU.mult),
                                 reads=[PSB[po_], b_comb[st_ % 2][sub]], writes=[b_acc[sub]])
                        else:
                            c.op("dve", lambda e, po_=po_, av=av, sub=sub, ex=ex: e.scalar_tensor_tensor(out=av, in0=PS[po_][:], scalar=comb[st_ % 2][:, sub, ex:ex + 1], in1=av, op0=ALU.mult, op1=ALU.add),
                                 reads=[PSB[po_], b_comb[st_ % 2][sub], b_acc[sub]], writes=[b_acc[sub]])

            def finish(st_):
                for sub in range(NSUB):
                    t = st_ * NSUB + sub
                    h3, b_h3 = h3_r.next()
                    c.dma("sp", h3[:], H3_d[t * 128:(t + 1) * 128, :], writes=[b_h3], sem=h3semB[t % 2])
                    c.op("dve", lambda e, sub=sub: e.scalar_tensor_tensor(out=acc[:, sub, :], in0=h3[:], scalar=ALPHA, in1=acc[:, sub, :], op0=ALU.mult, op1=ALU.add),
                         reads=[b_h3, b_acc[sub]], writes=[b_acc[sub]])
                    layernorm(rings, acc[:, sub, :], b_acc[sub], lng, b_lng, lnb, b_lnb)
                    c.dma("act", out_d[t * 128:(t + 1) * 128, :], acc[:, sub, :], reads=[b_acc[sub]], sem=osemF[t % 2])

            NST = TOK // STK
            issue_weights(0, 0)
            h3semA = [c.dma_sem("h3A") for _ in range(2)]
            for sub in range(NSUB):
                router_sub(0, sub)
            for st_ in range(NST):
                h3semA = [c.dma_sem("h3A") for _ in range(2)]
                h3semB = [c.dma_sem("h3B") for _ in range(2)]
                for ex in range(NE):
                    slot = n_w % 2
                    n_w += 1
                    nxt = (st_ * NE + ex + 1)
                    if nxt < NST * NE:
                        issue_weights(nxt % NE, 1 - slot)
                    part1(st_, ex, slot)
                    if ex == 0 and st_ > 0:
                        finish(st_ - 1)
                    part2(st_, ex, slot)
                    if st_ + 1 < NST:
                        router_sub(st_ + 1, ex)
            finish(NST - 1)
            for k in ("sp",):
                for s_ in osemF:
                    nc.sync.wait_ge(s_[0], s_[1])
    return nc


def make_in_maps(inp):
    f = lambda a: np.ascontiguousarray(np.asarray(a), dtype=np.float32)
    x = f(inp["x"])
    pos = np.asarray(inp["positions"]).astype(np.int32)
    ln_g = f(inp["ln_g"]).reshape(4, D)
    ln_b = f(inp["ln_b"]).reshape(4, D)
    bcast = lambda a: np.ascontiguousarray(np.broadcast_to(a[None], (128,) + a.shape))
    lam_re = f(inp["ssm_lambda_re"])[0]; lam_im = f(inp["ssm_lambda_im"])[0]
    two = lambda a: np.ascontiguousarray(np.concatenate([a, a], axis=0))
    b_re = f(inp["ssm_b_re"])[0].transpose(1, 0, 2); b_im = f(inp["ssm_b_im"])[0].transpose(1, 0, 2)
    c_re = f(inp["ssm_c_re"])[0].transpose(2, 0, 1); c_im = f(inp["ssm_c_im"])[0].transpose(2, 0, 1)
    cat = lambda a, b: np.ascontiguousarray(np.concatenate([a, b], axis=0))
    inv_freq = (500000.0 ** (-np.arange(0, 16, 2, dtype=np.float32) / 16.0)).astype(np.float32)
    shared = {
        "ln_g": bcast(ln_g), "ln_b": bcast(ln_b),
        "lam_re": two(lam_re.T), "lam_im": two(lam_im.T), "lstep": bcast(f(inp["ssm_log_step"])[0]),
        "s5_ba": cat(b_re, b_im), "s5_bb": cat(b_im, b_re), "s5_ca": cat(c_re, c_im), "s5_cb": cat(c_im, c_re),
        "s5_d": bcast(f(inp["ssm_d"])[0]),
        "w_glu": f(inp["ssm_w_glu"])[0], "kv_w": f(inp["kv_w"]), "w_q": f(inp["attn_w_q"])[0],
        "sinks": bcast(f(inp["attn_sinks"])[0]), "w_out": f(inp["attn_w_out"])[0],
        "ffn_g": f(inp["ffn_w_gate"])[0], "ffn_u": f(inp["ffn_w_up"])[0], "ffn_d": f(inp["ffn_w_down"])[0],
        "w_router": f(inp["moe_w_router"])[0], "b_router": bcast(f(inp["moe_b_router"])[0]),
        "moe_g": f(inp["moe_w_gate"])[0], "moe_u": f(inp["moe_w_up"])[0], "moe_d": f(inp["moe_w_down"])[0],
        "inv_freq": bcast(inv_freq),
    }
    maps = []
    for i in range(NCORES):
        m = dict(shared)
        m["x"] = np.ascontiguousarray(x[2 * i:2 * i + 2].reshape(TOK, D))
        m["pos"] = np.ascontiguousarray(pos[2 * i:2 * i + 2].reshape(NT, 128).T)
        maps.append(m)
    return maps


def kernel(**inputs):
    nc = build()
    maps = make_in_maps(inputs)
    res = run_bass_kernel_spmd(nc, maps, core_ids=list(range(NCORES)))
    out = np.stack([np.asarray(r["out"]).reshape(NSEQ, L, D) for r in res.results], axis=0)
    return out.reshape(NCORES * NSEQ, L, D).astype(np.float32)
```
